# Optimizing a Trainium2 kernel written in Bass

```python
import math
import jax, jax.numpy as jnp
from jax import lax
import numpy as np

D_MODEL = 1024
BATCH = 4
SEQ = 4096
DEPTH = 2

HEAD_DIM = 64
N_MIXERS = 4
N_HEADS_PER_MIXER = D_MODEL // (N_MIXERS * HEAD_DIM)
GROUP_W = N_HEADS_PER_MIXER * HEAD_DIM
MLA_NOPE_DIM = 64
MLA_ROPE_DIM = 32
MLA_V_DIM = HEAD_DIM
MLA_Q_LORA = D_MODEL // 4
MLA_KV_LORA = D_MODEL // 8
MIX_W = N_MIXERS * GROUP_W
IN_SIZES = (GROUP_W, GROUP_W, GROUP_W, N_HEADS_PER_MIXER,
            GROUP_W, GROUP_W, GROUP_W,
            GROUP_W, GROUP_W, GROUP_W,
            MLA_Q_LORA, MLA_KV_LORA, MLA_ROPE_DIM,
            MIX_W)
IN_W = sum(IN_SIZES)
Q_BLOCK = 128
MOBA_BLOCK = 256
MOBA_TOPK = 3
MOBA_Q_CHUNK = 64
DIL_PATTERNS = ((128, 1), (512, 4), (2048, 16))
DIL_BLOCK = 128
N_BUCKETS = 32
MAX_DISTANCE = 2048
N_BIAS_HEADS = 2 * N_HEADS_PER_MIXER
PLE_DIM = 256
ROPE_THETA = 10000.0
EPS = 1e-6
NEG = -1e30

kernel_name = "hymba_fox_moba_dilated_mla_block"


def rms_norm(x, g):
    xf = x.astype(jnp.float32)
    y = xf * lax.rsqrt(jnp.mean(xf * xf, axis=-1, keepdims=True) + EPS)
    return y * g.astype(jnp.float32)


def to_heads(t, n):
    b, s, _ = t.shape
    return t.reshape(b, s, n, -1).transpose(0, 2, 1, 3)


def from_heads(o):
    b, h, s, d = o.shape
    return o.transpose(0, 2, 1, 3).reshape(b, s, h * d)


def t5_bucket(dist):
    dist = jnp.maximum(dist, 0)
    max_exact = N_BUCKETS // 2
    d_f = jnp.maximum(dist, 1).astype(jnp.float32)
    large = max_exact + (jnp.log(d_f / max_exact) / math.log(MAX_DISTANCE / max_exact)
                         * (N_BUCKETS - max_exact)).astype(jnp.int32)
    large = jnp.minimum(large, N_BUCKETS - 1)
    return jnp.where(dist < max_exact, dist, large)


def rotary(x, pos):
    half = MLA_ROPE_DIM // 2
    inv = 1.0 / (ROPE_THETA ** (jnp.arange(half, dtype=jnp.float32) * 2.0 / MLA_ROPE_DIM))
    ang = pos.astype(jnp.float32)[:, None] * inv[None, :]
    cos, sin = jnp.cos(ang), jnp.sin(ang)
    x1, x2 = x[..., :half], x[..., half:]
    return jnp.concatenate([x1 * cos - x2 * sin, x1 * sin + x2 * cos], axis=-1)


def causal_block_attention(q, k, v, log_f_cum=None):
    b, h, s, _ = q.shape
    key_pos = jnp.arange(s)

    def one_block(i):
        start = i * Q_BLOCK
        qb = lax.dynamic_slice_in_dim(q, start, Q_BLOCK, axis=2)
        sc = jnp.einsum('bhqd,bhkd->bhqk', qb, k)
        if log_f_cum is not None:
            cq = lax.dynamic_slice_in_dim(log_f_cum, start, Q_BLOCK, axis=2)
            sc = sc + cq[..., :, None] - log_f_cum[..., None, :]
        q_pos = start + jnp.arange(Q_BLOCK)
        sc = jnp.where(key_pos[None, :] <= q_pos[:, None], sc, NEG)
        pr = jax.nn.softmax(sc, axis=-1)
        return jnp.einsum('bhqk,bhkd->bhqd', pr, v)

    out = lax.map(one_block, jnp.arange(s // Q_BLOCK))
    return out.transpose(1, 2, 0, 3, 4).reshape(b, h, s, -1)


def moba_attention(q, k, v, bias_table):
    b, h, s, hd = q.shape
    n_kb = -(-s // MOBA_BLOCK)
    pad = n_kb * MOBA_BLOCK - s
    kp = jnp.pad(k, ((0, 0), (0, 0), (0, pad), (0, 0))).reshape(b, h, n_kb, MOBA_BLOCK, hd)
    vp = jnp.pad(v, ((0, 0), (0, 0), (0, pad), (0, 0))).reshape(b, h, n_kb, MOBA_BLOCK, hd)
    k_mean = kp.mean(axis=3)
    n_sel = min(MOBA_TOPK, n_kb)
    bi = jnp.arange(b)[:, None, None, None]
    hi = jnp.arange(h)[None, :, None, None]
    blk_ids = jnp.arange(n_kb)
    in_blk = jnp.arange(MOBA_BLOCK)

    def one_chunk(i):
        start = i * MOBA_Q_CHUNK
        qc = lax.dynamic_slice_in_dim(q, start, MOBA_Q_CHUNK, axis=2)
        q_pos = start + jnp.arange(MOBA_Q_CHUNK)
        own = start // MOBA_BLOCK
        gate = jnp.einsum('bhqd,bhnd->bhqn', qc, k_mean)
        gate = jnp.where(blk_ids < own, gate, NEG)
        _, sel = lax.top_k(gate, n_sel)
        sel_ok = sel < own
        k_sel = kp[bi, hi, sel]
        v_sel = vp[bi, hi, sel]
        s_sel = jnp.einsum('bhqd,bhqnkd->bhqnk', qc, k_sel)
        pos_sel = sel[..., None] * MOBA_BLOCK + in_blk
        b_sel = bias_table[hi[..., None], t5_bucket(q_pos[:, None, None] - pos_sel)]
        s_sel = jnp.where(sel_ok[..., None], s_sel + b_sel, NEG)
        s_sel = s_sel.reshape(b, h, MOBA_Q_CHUNK, n_sel * MOBA_BLOCK)
        k_own = lax.dynamic_index_in_dim(kp, own, axis=2, keepdims=False)
        v_own = lax.dynamic_index_in_dim(vp, own, axis=2, keepdims=False)
        rel_own = q_pos[:, None] - (own * MOBA_BLOCK + in_blk)[None, :]
        s_own = jnp.einsum('bhqd,bhkd->bhqk', qc, k_own) + bias_table[:, t5_bucket(rel_own)]
        s_own = jnp.where(rel_own >= 0, s_own, NEG)
        pr = jax.nn.softmax(jnp.concatenate([s_sel, s_own], axis=-1), axis=-1)
        p_sel = pr[..., :n_sel * MOBA_BLOCK].reshape(b, h, MOBA_Q_CHUNK, n_sel, MOBA_BLOCK)
        p_own = pr[..., n_sel * MOBA_BLOCK:]
        return (jnp.einsum('bhqnk,bhqnkd->bhqd', p_sel, v_sel)
                + jnp.einsum('bhqk,bhkd->bhqd', p_own, v_own))

    out = lax.map(one_chunk, jnp.arange(s // MOBA_Q_CHUNK))
    return out.transpose(1, 2, 0, 3, 4).reshape(b, h, s, hd)


def dilated_branch(q, k, v, bias_table, window, dil):
    b, h, s, hd = q.shape
    L = s // dil
    n_back = window // dil
    nb = -(-L // DIL_BLOCK)
    Lp = nb * DIL_BLOCK

    def sub(t):
        return t.reshape(b, h, L, dil, hd).transpose(0, 1, 3, 2, 4)

    qs = jnp.pad(sub(q), ((0, 0), (0, 0), (0, 0), (0, Lp - L), (0, 0))).reshape(b, h, dil, nb, DIL_BLOCK, hd)

    def band(t):
        tp = jnp.pad(sub(t), ((0, 0), (0, 0), (0, 0), (DIL_BLOCK, Lp - L), (0, 0)))
        tp = tp.reshape(b, h, dil, nb + 1, DIL_BLOCK, hd)
        return jnp.concatenate([tp[:, :, :, :-1], tp[:, :, :, 1:]], axis=4)

    kb, vb = band(k), band(v)
    qi = jnp.arange(DIL_BLOCK)[:, None]
    kj = jnp.arange(2 * DIL_BLOCK)[None, :]
    rel = qi + DIL_BLOCK - kj
    key_idx = jnp.arange(nb)[:, None, None] * DIL_BLOCK - DIL_BLOCK + kj
    valid = (rel >= 0) & (rel <= n_back) & (key_idx >= 0)
    bias = bias_table[:, t5_bucket(rel * dil)]
    sc = jnp.einsum('bhrnqd,bhrnkd->bhrnqk', qs, kb) + bias[None, :, None, None]
    sc = jnp.where(valid, sc, NEG)
    lse = jax.nn.logsumexp(sc, axis=-1)
    pr = jnp.exp(sc - lse[..., None])
    o = jnp.einsum('bhrnqk,bhrnkd->bhrnqd', pr, vb)
    o = o.reshape(b, h, dil, Lp, hd)[:, :, :, :L].transpose(0, 1, 3, 2, 4).reshape(b, h, s, hd)
    lse = lse.reshape(b, h, dil, Lp)[..., :L].transpose(0, 1, 3, 2).reshape(b, h, s)
    return o, lse


def dilated_attention(q, k, v, bias_table):
    outs, lses = [], []
    for window, dil in DIL_PATTERNS:
        o, l = dilated_branch(q, k, v, bias_table, window, dil)
        outs.append(o)
        lses.append(l)
    w = jax.nn.softmax(jnp.stack(lses, axis=0), axis=0)
    return jnp.sum(w[..., None] * jnp.stack(outs, axis=0), axis=0)


def split_points():
    pts, acc = [], 0
    for sz in IN_SIZES[:-1]:
        acc += sz
        pts.append(acc)
    return pts


def hybrid_layer(x, p_i, ln_g, w_in, b_forget, qk_gain, mla_q_norm, mla_kv_norm,
                 mla_nope_gain, mla_rope_gain, w_uq, w_ukv, w_out, rel_bias,
                 ple_norm_g, w_ple_gate, w_ple_proj):
    b, s, _ = x.shape
    nh = N_HEADS_PER_MIXER
    scale = HEAD_DIM ** -0.5
    h = rms_norm(x, ln_g)
    proj = h @ w_in
    (fq, fk, fv, ff, mq, mk, mv, dq, dk, dv, cq, ckv, kr, gate) = jnp.split(proj, split_points(), axis=-1)

    q = rms_norm(to_heads(fq, nh), qk_gain[0]) * scale
    k = rms_norm(to_heads(fk, nh), qk_gain[1])
    log_f = jax.nn.log_sigmoid((ff + b_forget).astype(jnp.float32)).transpose(0, 2, 1)
    o_fox = causal_block_attention(q, k, to_heads(fv, nh), jnp.cumsum(log_f, axis=-1))

    q = rms_norm(to_heads(mq, nh), qk_gain[2]) * scale
    k = rms_norm(to_heads(mk, nh), qk_gain[3])
    o_moba = moba_attention(q, k, to_heads(mv, nh), rel_bias[:, :nh].T)

    q = rms_norm(to_heads(dq, nh), qk_gain[4]) * scale
    k = rms_norm(to_heads(dk, nh), qk_gain[5])
    o_dil = dilated_attention(q, k, to_heads(dv, nh), rel_bias[:, nh:].T)

    pos = jnp.arange(s)
    qf = to_heads(rms_norm(cq, mla_q_norm) @ w_uq, nh)
    kvf = to_heads(rms_norm(ckv, mla_kv_norm) @ w_ukv, nh)
    q_nope = rms_norm(qf[..., :MLA_NOPE_DIM], mla_nope_gain[0])
    q_rope = rotary(rms_norm(qf[..., MLA_NOPE_DIM:], mla_rope_gain[0]), pos)
    k_nope = rms_norm(kvf[..., :MLA_NOPE_DIM], mla_nope_gain[1])
    v_mla = kvf[..., MLA_NOPE_DIM:]
    k_rope = rotary(rms_norm(kr, mla_rope_gain[1]), pos)[:, None]
    q_m = jnp.concatenate([q_nope, q_rope], axis=-1) * (MLA_NOPE_DIM + MLA_ROPE_DIM) ** -0.5
    k_m = jnp.concatenate([k_nope, jnp.broadcast_to(k_rope, (b, nh, s, MLA_ROPE_DIM))], axis=-1)
    o_mla = causal_block_attention(q_m, k_m, v_mla)

    mix = jnp.concatenate([from_heads(o_fox), from_heads(o_moba), from_heads(o_dil), from_heads(o_mla)], axis=-1)
    x = x + (mix * jax.nn.silu(gate)) @ w_out

    ple_gate = jax.nn.sigmoid(rms_norm(x, ple_norm_g) @ w_ple_gate)
    return x + ple_gate * (p_i @ w_ple_proj)


def setup_inputs(seed: int = 0) -> dict:
    key = jax.random.key(seed)
    ks = jax.random.split(key, 18)

    def nrm(k, shape, sc):
        return sc * jax.random.normal(k, shape, jnp.float32)

    nh = N_HEADS_PER_MIXER
    return {
        "x": nrm(ks[0], (BATCH, SEQ, D_MODEL), 1.0),
        "p": nrm(ks[1], (DEPTH, BATCH, SEQ, PLE_DIM), 1.0),
        "ln_g": 1.0 + nrm(ks[2], (DEPTH, D_MODEL), 0.05),
        "w_in": nrm(ks[3], (DEPTH, D_MODEL, IN_W), D_MODEL ** -0.5),
        "b_forget": 2.0 + nrm(ks[4], (DEPTH, nh), 0.5),
        "qk_gain": 1.0 + nrm(ks[5], (DEPTH, 6, HEAD_DIM), 0.05),
        "mla_q_norm": 1.0 + nrm(ks[6], (DEPTH, MLA_Q_LORA), 0.05),
        "mla_kv_norm": 1.0 + nrm(ks[7], (DEPTH, MLA_KV_LORA), 0.05),
        "mla_nope_gain": 1.0 + nrm(ks[8], (DEPTH, 2, MLA_NOPE_DIM), 0.05),
        "mla_rope_gain": 1.0 + nrm(ks[9], (DEPTH, 2, MLA_ROPE_DIM), 0.05),
        "w_uq": nrm(ks[10], (DEPTH, MLA_Q_LORA, nh * (MLA_NOPE_DIM + MLA_ROPE_DIM)), MLA_Q_LORA ** -0.5),
        "w_ukv": nrm(ks[11], (DEPTH, MLA_KV_LORA, nh * (MLA_NOPE_DIM + MLA_V_DIM)), MLA_KV_LORA ** -0.5),
        "w_out": nrm(ks[12], (DEPTH, MIX_W, D_MODEL), MIX_W ** -0.5),
        "rel_bias": nrm(ks[13], (N_BUCKETS, N_BIAS_HEADS), 0.2),
        "ple_norm_g": 1.0 + nrm(ks[14], (DEPTH, D_MODEL), 0.05),
        "w_ple_gate": nrm(ks[15], (DEPTH, D_MODEL, D_MODEL), D_MODEL ** -0.5),
        "w_ple_proj": nrm(ks[16], (DEPTH, PLE_DIM, D_MODEL), PLE_DIM ** -0.5),
    }


def reference(x, p, ln_g, w_in, b_forget, qk_gain, mla_q_norm, mla_kv_norm,
              mla_nope_gain, mla_rope_gain, w_uq, w_ukv, w_out, rel_bias,
              ple_norm_g, w_ple_gate, w_ple_proj):
    for i in range(DEPTH):
        x = hybrid_layer(x, p[i], ln_g[i], w_in[i], b_forget[i], qk_gain[i],
                         mla_q_norm[i], mla_kv_norm[i], mla_nope_gain[i], mla_rope_gain[i],
                         w_uq[i], w_ukv[i], w_out[i], rel_bias,
                         ple_norm_g[i], w_ple_gate[i], w_ple_proj[i])
    return x
```

```python
import contextlib
import math
import numpy as np
import ml_dtypes
import concourse.bass as bass
import concourse.mybir as mybir
from concourse.bass_utils import run_bass_kernel_spmd

F32 = mybir.dt.float32
BF16 = mybir.dt.bfloat16
AF = mybir.ActivationFunctionType
ALU = mybir.AluOpType
AX = mybir.AxisListType

COMPUTE = ('pe', 'act', 'dve', 'pool')
NDMA_SEM = 8
SEQ = 4096
OWN = 2048
NEG = -30000.0
WG = 2688
FL = WG + 128
WC = 640
EPS = 1e-6


class Sched:
    def __init__(self, nc, stack):
        self.nc = nc
        self.ops = {e: [] for e in ('pe', 'act', 'dve', 'pool', 'sp')}
        self.psem = {e: stack.enter_context(nc.semaphore("pg_" + e)) for e in COMPUTE}
        self.pcnt = {e: 0 for e in COMPUTE}
        self.dsem = {q: [stack.enter_context(nc.semaphore("dq_%s%d" % (q, i))) for i in range(NDMA_SEM)]
                     for q in ('sp', 'pool')}
        self.dval = {q: [0] * NDMA_SEM for q in ('sp', 'pool')}
        self.didx = {q: 0 for q in ('sp', 'pool')}
        self.waited = {e: {} for e in self.ops}
        self.res = {}

    def _need(self, eng, tok, waits):
        if tok is None:
            return
        key, sem, val, prod = tok
        if prod == eng and eng == 'pe':
            return
        if self.waited[eng].get(key, 0) >= val:
            return
        self.waited[eng][key] = val
        waits.append((sem, val))

    def _deps(self, eng, reads, writes):
        waits = []
        for r in reads:
            st = self.res.get(r)
            if st is not None:
                self._need(eng, st[0], waits)
        for w in writes:
            st = self.res.get(w)
            if st is not None:
                self._need(eng, st[0], waits)
                for t in st[1]:
                    self._need(eng, t, waits)
        return waits

    def _record(self, tok, reads, writes):
        for r in reads:
            st = self.res.setdefault(r, [None, []])
            st[1].append(tok)
        for w in writes:
            self.res[w] = [tok, []]

    def op(self, eng, fn, reads=(), writes=(), sig=True):
        waits = self._deps(eng, reads, writes)
        tok = None
        if sig:
            self.pcnt[eng] += 1
            tok = ('p' + eng, self.psem[eng], self.pcnt[eng], eng)
            self._record(tok, reads, writes)
        self.ops[eng].append((waits, fn, (self.psem[eng], 1) if sig else None))
        return tok

    def dma(self, q, fn, reads=(), writes=()):
        waits = self._deps(q, reads, writes)
        i = self.didx[q]
        self.didx[q] = (i + 1) % NDMA_SEM
        sem = self.dsem[q][i]
        key = 'd%s%d' % (q, i)
        prev = self.dval[q][i]
        if prev > 0 and self.waited[q].get(key, 0) < prev:
            self.waited[q][key] = prev
            waits.append((sem, prev))
        self.dval[q][i] = prev + 16
        tok = (key, sem, prev + 16, 'dma')
        self._record(tok, reads, writes)
        self.ops[q].append((waits, fn, (sem, 16)))
        return tok

    def barrier(self):
        toks = []
        for e in COMPUTE:
            if self.pcnt[e] > 0:
                toks.append(('p' + e, self.psem[e], self.pcnt[e], e))
        for q in ('sp', 'pool'):
            for i in range(NDMA_SEM):
                if self.dval[q][i] > 0:
                    toks.append(('d%s%d' % (q, i), self.dsem[q][i], self.dval[q][i], 'dma'))
        for e in self.ops:
            waits = []
            for t in toks:
                key, sem, val, prod = t
                if self.waited[e].get(key, 0) >= val:
                    continue
                self.waited[e][key] = val
                waits.append((sem, val))
            if waits:
                self.ops[e].append((waits, None, None))
        self.res = {}

    def emit(self):
        nc = self.nc
        with nc.Block() as block:
            def mk(name):
                def body(eng):
                    for waits, fn, sig in self.ops[name]:
                        for sem, val in waits:
                            eng.wait_ge(sem, val)
                        if fn is None:
                            continue
                        ins = fn(eng)
                        if sig is not None:
                            ins.then_inc(sig[0], sig[1])
                return body
            block.tensor(mk('pe'))
            block.scalar(mk('act'))
            block.vector(mk('dve'))
            block.gpsimd(mk('pool'))
            block.sync(mk('sp'))


V_LNG, V_GK, V_GQ, V_GQN, V_GKV, V_GKN, V_GA, V_GB, V_GR, V_BF, V_PLEG, NV = 0, 8, 14, 20, 22, 23, 24, 25, 26, 28, 29, 40
C_BD64, C_B128, C_B256, C_BDMLA, C_B32, C_ONES, C_J, C_ID, NCB = 0, 128, 256, 384, 480, 512, 576, 704, 832
K_K, K_V, K_CKV, K_KR, K_KRS, K_FF, NWK = 0, 768, 1536, 1664, 1696, 1728, 1736
Q_Q, Q_CQ, Q_GATE, NWQ = 0, 768, 1024, 2048


def build_fused():
    nc = bass.Bass("TRN2", target_bir_lowering=False)

    def din(name, shape, dt=F32):
        return nc.dram_tensor(name, shape, dt, kind="ExternalInput").ap()

    x_all = din("x_all", [SEQ, 1024])
    identf_d = din("identf", [128, 128])
    CK_d = din("CK", [32, SEQ])
    SK_d = din("SK", [32, SEQ])
    oh16_d = din("oh16", [16, SEQ], BF16)
    ones3_d = din("ones3", [3, SEQ], BF16)
    cbf_d = din("cbf", [128, NCB], BF16)
    tabX_d = din("tabX", [34, 9])
    msel_d = din("msel", [128, 2])
    WL = []
    for l in range(2):
        sfx = "_l%d" % l
        WL.append({"wK": din("wK" + sfx, [128, 8, NWK]), "wQ": din("wQ" + sfx, [128, 8, NWQ]),
                   "wuqA": din("wuqA" + sfx, [128, 2, 384]), "wuqB": din("wuqB" + sfx, [128, 2, 384]),
                   "wukvK": din("wukvK" + sfx, [128, 1, 256]), "wukvV": din("wukvV" + sfx, [128, 1, 256]),
                   "wout": din("wout" + sfx, [128, 8, 1024]), "wpg": din("wpg" + sfx, [128, 8, 1024]),
                   "wpp": din("wpp" + sfx, [128, 2, 1024]), "vecs": din("vecs" + sfx, [128, NV])})
    CS = {}
    for c in ("a0", "a1", "m"):
        CS[c] = {"OH": din("OH_" + c, [34, FL]), "Sel": din("Sel_" + c, [128, 4, 256]), "VB": din("VB_" + c, [128, 16, 16]),
                 "VM": din("VM_" + c, [128, 16, 16]), "CTq": din("CTq_" + c, [96, OWN]), "STq": din("STq_" + c, [96, OWN]),
                 "p": din("p_" + c, [OWN, 256])}
    xown_d = [din("x_own_a%d" % a, [OWN, 1024]) for a in range(2)]
    out_d = nc.dram_tensor("out", [OWN, 1024], F32, kind="ExternalOutput").ap()

    KT_s = nc.dram_tensor("KT_s", [16, 128, SEQ], BF16, kind="Internal").ap()
    QT_s = nc.dram_tensor("QT_s", [16, 128, OWN], BF16, kind="Internal").ap()
    V_s = nc.dram_tensor("V_s", [SEQ, 1024], BF16, kind="Internal").ap()
    F_sL = {c: nc.dram_tensor("F_s_" + c, [9, FL], BF16, kind="Internal").ap() for c in ("a0", "a1", "m")}
    x1_s = [nc.dram_tensor("x1_s%d" % a, [OWN, 1024], F32, kind="Internal").ap() for a in range(2)]

    with contextlib.ExitStack() as top:
        S = Sched(nc, top)

        uid = [0]

        def sbt(st, n, shp, dt):
            uid[0] += 1
            return st.enter_context(nc.sbuf_tensor('s%d_%s' % (uid[0], n), shp, dt))

        def pst(st, n, shp, dt):
            uid[0] += 1
            return st.enter_context(nc.psum_tensor('p%d_%s' % (uid[0], n), shp, dt))

        vecsL = [sbt(top, "vecs%d" % l, [128, NV], F32) for l in range(2)]
        msel = sbt(top, "msel", [128, 2], F32)
        cbf = sbt(top, "cbf", [128, NCB], BF16)
        identf = sbt(top, "identf", [128, 128], F32)
        kmT = [sbt(top, "kmT%d" % i, [128, 16], BF16) for i in range(2)]
        kmBD = [sbt(top, "kmBD%d" % i, [128, 32], BF16) for i in range(2)]
        cTm = sbt(top, "cTm", [128, 128], F32)
        negb = sbt(top, "negb", [128, 1], F32)

        for l in range(2):
            S.dma('sp', lambda e, l=l: e.dma_start(out=vecsL[l][:], in_=WL[l]["vecs"][:]))
        S.dma('sp', lambda e: e.dma_start(out=msel[:], in_=msel_d[:]))
        S.dma('sp', lambda e: e.dma_start(out=cbf[:], in_=cbf_d[:]))
        S.dma('sp', lambda e: e.dma_start(out=identf[:], in_=identf_d[:]))
        S.barrier()
        ident = cbf[:, C_ID:C_ID + 128]
        bd64 = cbf[:, C_BD64:C_BD64 + 128]
        b128 = cbf[:, C_B128:C_B128 + 128]
        b256 = cbf[:, C_B256:C_B256 + 128]
        bdmla = cbf[0:96, C_BDMLA:C_BDMLA + 96]
        b32 = cbf[0:32, C_B32:C_B32 + 32]
        ones64 = cbf[:, C_ONES:C_ONES + 64]
        Jm = cbf[:, C_J:C_J + 128]

        def load_w(st, name, src, nk, ncols, gcol=None, vecs=None):
            w = sbt(st, name, [128, nk, ncols], BF16)
            stg = [sbt(st, name + "_stg%d" % i, [128, ncols], F32) for i in range(2)]
            for kc in range(nk):
                b = kc % 2
                S.dma('sp', lambda e, b=b, kc=kc: e.dma_start(out=stg[b][:, :], in_=src[:, kc, :]), writes=[name + 'stg%d' % b])
                if kc % 2 == 0:
                    if gcol is None:
                        S.op('dve', lambda e, b=b, kc=kc: e.tensor_copy(out=w[:, kc, :], in_=stg[b][:, :]), reads=[name + 'stg%d' % b])
                    else:
                        S.op('dve', lambda e, b=b, kc=kc: e.tensor_scalar(out=w[:, kc, :], in0=stg[b][:, :],
                                                                         scalar1=vecs[:, gcol + kc:gcol + kc + 1], scalar2=None, op0=ALU.mult),
                             reads=[name + 'stg%d' % b])
                else:
                    if gcol is None:
                        S.op('act', lambda e, b=b, kc=kc: e.activation(out=w[:, kc, :], in_=stg[b][:, :], func=AF.Copy), reads=[name + 'stg%d' % b])
                    else:
                        S.op('act', lambda e, b=b, kc=kc: e.activation(out=w[:, kc, :], in_=stg[b][:, :], func=AF.Copy,
                                                                       scale=vecs[:, gcol + kc:gcol + kc + 1]),
                             reads=[name + 'stg%d' % b])
            return w

        def norm_phase(st, xload, ntiles, hT, pfx):
            NB = 4
            xt = [sbt(st, pfx + "xt%d" % i, [128, 1024], F32) for i in range(NB)]
            junk = sbt(st, pfx + "junk", [128, 1024], BF16)
            xn = [sbt(st, pfx + "xn%d" % i, [128, 1024], BF16) for i in range(NB)]
            ss = [sbt(st, pfx + "ss%d" % i, [128, 4], F32) for i in range(NB)]
            pT = [pst(st, pfx + "pT%d" % i, [128, 1024], BF16) for i in range(2)]

            def front(t):
                b = t % NB
                X, N, SS = pfx + 'xt%d' % b, pfx + 'xn%d' % b, pfx + 'ss%d' % b
                xload(t, xt[b], X)
                S.op('act', lambda e: e.activation(out=junk[:], in_=xt[b][:], func=AF.Square, accum_out=ss[b][:, 0:1]),
                     reads=[X], writes=[pfx + 'junk', SS])
                S.op('act', lambda e: e.activation(out=ss[b][:, 1:2], in_=ss[b][:, 0:1], func=AF.Ln, scale=1.0 / 1024, bias=EPS),
                     reads=[SS], writes=[SS])
                S.op('act', lambda e: e.activation(out=ss[b][:, 2:3], in_=ss[b][:, 1:2], func=AF.Exp, scale=-0.5),
                     reads=[SS], writes=[SS])
                S.op('dve', lambda e: e.tensor_scalar(out=xn[b][:], in0=xt[b][:], scalar1=ss[b][:, 2:3], scalar2=None, op0=ALU.mult),
                     reads=[X, SS], writes=[N])

            def back(t):
                b, pb_ = t % NB, t % 2
                N, P = pfx + 'xn%d' % b, pfx + 'pT%d' % pb_
                for c in range(8):
                    S.op('pe', lambda e, c=c: e.transpose(out=pT[pb_][:, c * 128:(c + 1) * 128], in_=xn[b][:, c * 128:(c + 1) * 128], identity=ident),
                         reads=[N], writes=[P], sig=(c == 7))
                if t % 2 == 0:
                    S.op('dve', lambda e: e.tensor_copy(out=hT[:, :, t * 128:(t + 1) * 128], in_=pT[pb_][:].rearrange("p (c n) -> p c n", c=8)), reads=[P])
                else:
                    S.op('act', lambda e: e.activation(out=hT[:, :, t * 128:(t + 1) * 128], in_=pT[pb_][:].rearrange("p (c n) -> p c n", c=8), func=AF.Copy), reads=[P])

            LA = 2
            for i in range(ntiles + LA):
                if i < ntiles:
                    front(i)
                if i >= LA:
                    back(i - LA)

        class RmsFeat:
            def __init__(self, st):
                self.sq = [[sbt(st, "rf_sq%d_%d" % (i, a), [128, 512], BF16) for a in range(2)] for i in range(2)]
                self.lnv = [sbt(st, "rf_ln%d" % i, [128, 512], F32) for i in range(2)]
                self.rstd = [sbt(st, "rf_rs%d" % i, [128, 512], F32) for i in range(2)]
                self.psB = [pst(st, "rf_psB%d" % i, [128, 512], F32) for i in range(2)]
                self.cnt = 0

            def __call__(self, As, nstat, P, bm, gains, ebias, outs, n=512):
                i = self.cnt % 2
                self.cnt += 1
                for a in range(nstat):
                    S.op('act', lambda e, a=a: e.activation(out=self.sq[i][a][0:P, 0:n], in_=As[a][0], func=AF.Square),
                         reads=[As[a][1]], writes=['rf_sq%d_%d' % (i, a)])
                for a in range(nstat):
                    S.op('pe', lambda e, a=a: e.matmul(self.psB[i][0:P, 0:n], lhsT=bm, rhs=self.sq[i][a][0:P, 0:n], start=(a == 0), stop=(a == nstat - 1)),
                         reads=['rf_sq%d_%d' % (i, aa) for aa in range(nstat)], writes=['rf_psB%d' % i], sig=(a == nstat - 1))
                S.op('act', lambda e: e.activation(out=self.lnv[i][0:P, 0:n], in_=self.psB[i][0:P, 0:n], func=AF.Ln, bias=EPS),
                     reads=['rf_psB%d' % i], writes=['rf_ln%d' % i])
                S.op('act', lambda e: e.activation(out=self.rstd[i][0:P, 0:n], in_=self.lnv[i][0:P, 0:n], func=AF.Exp, scale=-0.5, bias=ebias),
                     reads=['rf_ln%d' % i], writes=['rf_rs%d' % i])
                for a in range(len(As)):
                    S.op('dve', lambda e, a=a: e.scalar_tensor_tensor(out=outs[a][0], in0=As[a][0], scalar=gains[a], in1=self.rstd[i][0:P, 0:n],
                                                                      op0=ALU.mult, op1=ALU.mult),
                         reads=[As[a][1], 'rf_rs%d' % i], writes=[outs[a][1]])

        class Pipe:
            def __init__(self):
                self.items = []

            def add(self, A, B=None, dep_prev=False):
                self.items.append((A, B, dep_prev))

            def run(self):
                n = len(self.items)
                done_a = 0
                for i in range(n):
                    while done_a < min(n, i + 2):
                        if done_a > i and self.items[done_a][2]:
                            break
                        self.items[done_a][0]()
                        done_a += 1
                    if self.items[i][1] is not None:
                        self.items[i][1]()
                self.items = []

        def split3(st, pfx, src, npart, n):
            hi = sbt(st, pfx + "hi", [npart, n], BF16)
            mid = sbt(st, pfx + "mid", [npart, n], BF16)
            lo = sbt(st, pfx + "lo", [npart, n], BF16)
            r1 = sbt(st, pfx + "r1", [npart, n], F32)
            S.op('dve', lambda e: e.tensor_copy(out=hi[:], in_=src), reads=[pfx + 'src'], writes=[pfx + 'hi'])
            S.op('dve', lambda e: e.tensor_tensor(out=r1[:], in0=src, in1=hi[:], op=ALU.subtract), reads=[pfx + 'src', pfx + 'hi'], writes=[pfx + 'r1'])
            S.op('dve', lambda e: e.tensor_copy(out=mid[:], in_=r1[:]), reads=[pfx + 'r1'], writes=[pfx + 'mid'])
            S.op('dve', lambda e: e.tensor_tensor(out=r1[:], in0=r1[:], in1=mid[:], op=ALU.subtract), reads=[pfx + 'r1', pfx + 'mid'], writes=[pfx + 'r1'])
            S.op('dve', lambda e: e.tensor_copy(out=lo[:], in_=r1[:]), reads=[pfx + 'r1'], writes=[pfx + 'lo'])
            return hi, mid, lo

        def phase0(OH_d, F_s):
            with contextlib.ExitStack() as st:
                tabX = sbt(st, "tabX", [34, 9], F32)
                OH = sbt(st, "OH", [34, FL], F32)
                Fsb = sbt(st, "Fsb", [9, FL], BF16)
                psF = [pst(st, "psF%d" % i, [128, 512], F32) for i in range(2)]
                S.dma('sp', lambda e: e.dma_start(out=tabX[:], in_=tabX_d[:]), writes=['tabX'])
                S.dma('sp', lambda e: e.dma_start(out=OH[:], in_=OH_d[:]), writes=['OH'])
                nch = (FL + 511) // 512
                for ci in range(nch):
                    c0 = ci * 512
                    cw = min(512, FL - c0)
                    b = ci % 2
                    S.op('pe', lambda e, b=b, c0=c0, cw=cw: e.matmul(psF[b][0:9, 0:cw], lhsT=tabX[0:34, 0:9], rhs=OH[0:34, c0:c0 + cw], start=True, stop=True),
                         reads=['tabX', 'OH'], writes=['psF%d' % b])
                    S.op('dve', lambda e, b=b, c0=c0, cw=cw: e.tensor_copy(out=Fsb[0:9, c0:c0 + cw], in_=psF[b][0:9, 0:cw]),
                         reads=['psF%d' % b], writes=['Fsb'])
                S.dma('sp', lambda e: e.dma_start(out=F_s[:], in_=Fsb[:]), reads=['Fsb'])

        def phaseK(xload, W, vecs):
            stKC = contextlib.ExitStack()
            ffT = sbt(stKC, "ffT", [4, SEQ], F32)
            kmsum = [sbt(stKC, "kmsum%d" % i, [128, 16], F32) for i in range(2)]
            with contextlib.ExitStack() as st:
                hTa = sbt(st, "hTa", [128, 8, SEQ], BF16)
                with contextlib.ExitStack() as st1:
                    norm_phase(st1, xload, SEQ // 128, hTa, "na_")
                    S.barrier()
                wK = load_w(st, "wK", W["wK"], 8, NWK, V_LNG, vecs)
                wukvK_ = load_w(st, "wukvK", W["wukvK"], 1, 256)
                wukvK = wukvK_[:, 0, :]
                wukvV_ = load_w(st, "wukvV", W["wukvV"], 1, 256)
                wukvV = wukvV_[:, 0, :]
                S.barrier()
                rfk = RmsFeat(st)
                psAk = [pst(st, "psAk%d" % i, [128, 512], F32) for i in range(4)]
                psV = [pst(st, "psV%d" % i, [128, 512], F32) for i in range(2)]
                kst = [sbt(st, "kst%d" % i, [128, 512], BF16) for i in range(3)]
                vst = [sbt(st, "vst%d" % i, [128, 768], BF16) for i in range(2)]
                vst2 = [sbt(st, "vst2_%d" % i, [128, 256], BF16) for i in range(2)]
                ckvn = [sbt(st, "ckvn%d" % i, [128, 512], BF16) for i in range(2)]
                xa = sbt(st, "xa", [32, 512], F32)
                xb = sbt(st, "xb", [32, 512], F32)
                ckt = [sbt(st, "ckt%d" % i, [32, 512], F32) for i in range(2)]
                skt = [sbt(st, "skt%d" % i, [32, 512], F32) for i in range(2)]
                rst = [sbt(st, "rst%d" % i, [32, 512], BF16) for i in range(2)]
                acnt = [0]
                kcnt = [0]

                def nextAk():
                    i = acnt[0] % 4
                    acnt[0] += 1
                    return i

                def proj(ai, col0, ncol, T0, n=512):
                    for kc in range(8):
                        S.op('pe', lambda e, kc=kc: e.matmul(psAk[ai][0:ncol, 0:n], lhsT=wK[:, kc, col0:col0 + ncol], rhs=hTa[:, kc, T0:T0 + n],
                                                             start=(kc == 0), stop=(kc == 7)),
                             writes=['psAk%d' % ai], sig=(kc == 7))

                pipe = Pipe()
                for ch in range(8):
                    T0 = ch * 512
                    cb = ch % 2
                    tb = ch % 2
                    for pi in range(6):
                        ai = nextAk()
                        kb = kcnt[0] % 3
                        kcnt[0] += 1

                        def A(ai=ai, pi=pi, T0=T0):
                            proj(ai, K_K + pi * 128, 128, T0)

                        def B(ai=ai, pi=pi, T0=T0, kb=kb, ch=ch):
                            rfk([(psAk[ai][:, :], 'psAk%d' % ai)], 1, 128, bd64, [vecs[:, V_GK + pi:V_GK + pi + 1]], 0.0,
                                [(kst[kb][:, :], 'kst%d' % kb)])
                            hd = 4 * (pi // 2) + 2 * (pi % 2)
                            for hh in range(2):
                                S.dma('sp', lambda e, hh=hh: e.dma_start(out=KT_s[hd + hh, 0:64, T0:T0 + 512], in_=kst[kb][hh * 64:(hh + 1) * 64, :]),
                                      reads=['kst%d' % kb])
                            if pi // 2 == 1:
                                pp = pi % 2
                                S.op('dve', lambda e: e.tensor_reduce(out=kmsum[pp][:, 2 * ch:2 * ch + 2],
                                                                      in_=kst[kb][:, :].rearrange("p (a b) -> p a b", a=2), axis=AX.X, op=ALU.add),
                                     reads=['kst%d' % kb], writes=['kmsum%d' % pp])
                        pipe.add(A, B)
                    for tt in range(4):
                        def A(tt=tt, T0=T0, ch=ch):
                            tok = T0 + tt * 128
                            vb = (ch * 4 + tt) % 2
                            for half, (c0, cw) in enumerate(((0, 512), (512, 256))):
                                for kc in range(8):
                                    S.op('pe', lambda e, kc=kc, half=half, c0=c0, cw=cw: e.matmul(
                                        psV[half][:, 0:cw], lhsT=hTa[:, kc, tok:tok + 128], rhs=wK[:, kc, K_V + c0:K_V + c0 + cw],
                                        start=(kc == 0), stop=(kc == 7)), writes=['psV%d' % half], sig=(kc == 7))
                            S.op('act', lambda e: e.activation(out=vst[vb][:, 0:512], in_=psV[0][:, 0:512], func=AF.Copy), reads=['psV0'], writes=['vst%d' % vb])
                            S.op('dve', lambda e: e.tensor_copy(out=vst[vb][:, 512:768], in_=psV[1][:, 0:256]), reads=['psV1'], writes=['vst%d' % vb])
                            S.dma('sp', lambda e: e.dma_start(out=V_s[tok:tok + 128, 0:768], in_=vst[vb][:, :]), reads=['vst%d' % vb])
                        pipe.add(A)
                    ai = nextAk()

                    def A(ai=ai, T0=T0):
                        proj(ai, K_CKV, 128, T0)

                    def B(ai=ai, cb=cb):
                        rfk([(psAk[ai][:, :], 'psAk%d' % ai)], 1, 128, b128, [vecs[:, V_GKV:V_GKV + 1]], 0.0, [(ckvn[cb][:, :], 'ckvn%d' % cb)])
                    pipe.add(A, B)
                    for pp in range(2):
                        ai = nextAk()
                        kb = kcnt[0] % 3
                        kcnt[0] += 1

                        def A(ai=ai, pp=pp, cb=cb):
                            S.op('pe', lambda e: e.matmul(psAk[ai][:, :], lhsT=wukvK[:, pp * 128:(pp + 1) * 128], rhs=ckvn[cb][:, :], start=True, stop=True),
                                 reads=['ckvn%d' % cb], writes=['psAk%d' % ai])

                        def B(ai=ai, pp=pp, kb=kb, T0=T0):
                            rfk([(psAk[ai][:, :], 'psAk%d' % ai)], 1, 128, bd64, [vecs[:, V_GKN:V_GKN + 1]], 0.0, [(kst[kb][:, :], 'kst%d' % kb)])
                            for hh in range(2):
                                S.dma('sp', lambda e, hh=hh: e.dma_start(out=KT_s[12 + 2 * pp + hh, 0:64, T0:T0 + 512], in_=kst[kb][hh * 64:(hh + 1) * 64, :]),
                                      reads=['kst%d' % kb])
                        pipe.add(A, B, dep_prev=(pp == 0))
                    for tt in range(4):
                        def A(tt=tt, T0=T0, ch=ch, cb=cb):
                            tok = T0 + tt * 128
                            vb = (ch * 4 + tt) % 2
                            S.op('pe', lambda e: e.matmul(psV[1][:, 0:256], lhsT=ckvn[cb][:, tt * 128:(tt + 1) * 128], rhs=wukvV, start=True, stop=True),
                                 reads=['ckvn%d' % cb], writes=['psV1'])
                            S.op('dve', lambda e: e.tensor_copy(out=vst2[vb][:, :], in_=psV[1][:, 0:256]), reads=['psV1'], writes=['vst2_%d' % vb])
                            S.dma('sp', lambda e: e.dma_start(out=V_s[tok:tok + 128, 768:1024], in_=vst2[vb][:, :]), reads=['vst2_%d' % vb])
                        pipe.add(A)
                    aiA = nextAk()
                    aiB = nextAk()

                    def A(aiA=aiA, aiB=aiB, T0=T0, tb=tb):
                        proj(aiA, K_KR, 32, T0)
                        proj(aiB, K_KRS, 32, T0)
                        S.dma('sp', lambda e: e.dma_start(out=ckt[tb][:, :], in_=CK_d[:, T0:T0 + 512]), writes=['ckt%d' % tb])
                        S.dma('sp', lambda e: e.dma_start(out=skt[tb][:, :], in_=SK_d[:, T0:T0 + 512]), writes=['skt%d' % tb])

                    def B(aiA=aiA, aiB=aiB, T0=T0, tb=tb):
                        rfk([(psAk[aiA][0:32, :], 'psAk%d' % aiA), (psAk[aiB][0:32, :], 'psAk%d' % aiB)], 1, 32, b32,
                            [vecs[0:32, V_GR:V_GR + 1], vecs[0:32, V_GR + 1:V_GR + 2]], 0.0, [(xa[:, :], 'xa'), (xb[:, :], 'xb')])
                        S.op('pool', lambda e: e.tensor_tensor(out=xa[:, :], in0=xa[:, :], in1=ckt[tb][:, :], op=ALU.mult), reads=['xa', 'ckt%d' % tb], writes=['xa'])
                        S.op('pool', lambda e: e.tensor_tensor(out=xb[:, :], in0=xb[:, :], in1=skt[tb][:, :], op=ALU.mult), reads=['xb', 'skt%d' % tb], writes=['xb'])
                        S.op('pool', lambda e: e.tensor_tensor(out=rst[tb][:, :], in0=xa[:, :], in1=xb[:, :], op=ALU.add), reads=['xa', 'xb'], writes=['rst%d' % tb])
                        for h in range(4):
                            S.dma('sp', lambda e, h=h: e.dma_start(out=KT_s[12 + h, 64:96, T0:T0 + 512], in_=rst[tb][:, :]), reads=['rst%d' % tb])
                    pipe.add(A, B)
                    ai = nextAk()

                    def A(ai=ai, T0=T0):
                        proj(ai, K_FF, 4, T0)

                    def B(ai=ai, T0=T0):
                        S.op('act', lambda e: e.activation(out=ffT[0:4, T0:T0 + 512], in_=psAk[ai][0:4, :], func=AF.Copy), reads=['psAk%d' % ai], writes=['ffT'])
                    pipe.add(A, B)
                pipe.run()
                S.barrier()
            with stKC as st:
                onesb = sbt(st, "onesb", [4, SEQ], BF16)
                cT4 = sbt(st, "cT4", [4, SEQ], F32)
                S.op('pool', lambda e: e.memset(onesb[:], 1.0), writes=['onesb'])
                S.op('dve', lambda e: e.tensor_scalar(out=negb[0:4, :], in0=vecs[0:4, V_BF:V_BF + 1], scalar1=-1.0, scalar2=None, op0=ALU.mult), writes=['negb'])
                S.op('act', lambda e: e.activation(out=ffT[:, :], in_=ffT[:, :], func=AF.Exp, scale=-1.0, bias=negb[0:4, 0:1]), reads=['ffT', 'negb'], writes=['ffT'])
                S.op('act', lambda e: e.activation(out=ffT[:, :], in_=ffT[:, :], func=AF.Ln, bias=1.0), reads=['ffT'], writes=['ffT'])
                S.op('dve', lambda e: e.tensor_tensor_scan(out=cT4[:, :], data0=onesb[:, :], data1=ffT[:, :], initial=0.0, op0=ALU.mult, op1=ALU.add),
                     reads=['onesb', 'ffT'], writes=['ncsrc'])
                hi, mid, lo = split3(st, "nc", cT4[:, :], 4, SEQ)
                for part, row in ((hi, 67), (mid, 68), (lo, 69)):
                    S.dma('sp', lambda e, part=part, row=row: e.dma_start(out=KT_s[0:4, row, :], in_=part[:, :]), reads=['nchi', 'ncmid', 'nclo'])
                for h in range(4):
                    S.dma('sp', lambda e, h=h: e.dma_start(out=KT_s[h, 64:67, :], in_=ones3_d[:, :]))
                    S.dma('sp', lambda e, h=h: e.dma_start(out=KT_s[4 + h, 64:80, :], in_=oh16_d[:, :]))
                psC = pst(st, "psC", [128, 128], F32)
                for blk in range(32):
                    S.op('pe', lambda e, blk=blk: e.transpose(out=psC[:, blk * 4:(blk + 1) * 4], in_=cT4[0:4, blk * 128:(blk + 1) * 128], identity=identf[0:4, 0:4]),
                         reads=['ncsrc'], writes=['psC'], sig=(blk == 31))
                S.op('dve', lambda e: e.tensor_copy(out=cTm[:, :], in_=psC[:, :]), reads=['psC'], writes=['cTm'])
                for pp in range(2):
                    S.op('dve', lambda e, pp=pp: e.tensor_scalar(out=kmT[pp][:, :], in0=kmsum[pp][:, :], scalar1=1.0 / 256, scalar2=None, op0=ALU.mult),
                         reads=['kmsum%d' % pp], writes=['kmT%d' % pp])
                    S.op('dve', lambda e, pp=pp: e.memset(kmBD[pp][:, :], 0.0), writes=['kmBD%d' % pp])
                    S.op('dve', lambda e, pp=pp: e.tensor_copy(out=kmBD[pp][0:64, 0:16], in_=kmT[pp][0:64, :]), reads=['kmT%d' % pp], writes=['kmBD%d' % pp])
                    S.op('dve', lambda e, pp=pp: e.tensor_copy(out=kmBD[pp][64:128, 16:32], in_=kmT[pp][64:128, :]), reads=['kmT%d' % pp], writes=['kmBD%d' % pp])
                S.barrier()

        def phaseQ(xload, W, C, vecs, GT):
            with contextlib.ExitStack() as st:
                hTo = sbt(st, "hTo", [128, 8, OWN], BF16)
                with contextlib.ExitStack() as st1:
                    norm_phase(st1, xload, OWN // 128, hTo, "no_")
                    S.barrier()
                wQ = load_w(st, "wQ", W["wQ"], 8, NWQ, V_LNG, vecs)
                wuqA = load_w(st, "wuqA", W["wuqA"], 2, 384)
                wuqB = load_w(st, "wuqB", W["wuqB"], 2, 384)
                VB = sbt(st, "VB", [128, 16, 16], F32)
                VM = sbt(st, "VM", [128, 16, 16], F32)
                S.dma('sp', lambda e: e.dma_start(out=VB[:], in_=C["VB"][:]))
                S.dma('sp', lambda e: e.dma_start(out=VM[:], in_=C["VM"][:]))
                S.barrier()
                rfq = RmsFeat(st)
                psAq = [pst(st, "qpsA%d" % i, [128, 512], F32) for i in range(4)]
                psG = pst(st, "psG", [128, 512], F32)
                psM = pst(st, "psM", [128, 1024], BF16)
                qst = [sbt(st, "qst%d" % i, [128, 512], BF16) for i in range(3)]
                cqn = sbt(st, "cqn", [128, 2, 512], BF16)
                qa = sbt(st, "qa", [96, 512], F32)
                qb = sbt(st, "qb", [96, 512], F32)
                ctt = [sbt(st, "ctt%d" % i, [96, 512], F32) for i in range(2)]
                stt = [sbt(st, "stt%d" % i, [96, 512], F32) for i in range(2)]
                qmst = [sbt(st, "qmst%d" % i, [96, 512], BF16) for i in range(2)]
                gvs = sbt(st, "gvs", [128, 2, 4, 16], F32)
                m8 = sbt(st, "m8", [128, 2, 4, 8], F32)
                Mf = sbt(st, "Mf", [128, 2, 4, 16], F32)
                Mb = [sbt(st, "Mb%d" % i, [128, 2, 4, 16], BF16) for i in range(2)]
                mst = [sbt(st, "mst%d" % i, [16, 1024], BF16) for i in range(2)]
                acnt = [0]
                qcnt = [0]
                mcnt = [0]

                def nextAq():
                    i = acnt[0] % 4
                    acnt[0] += 1
                    return i

                def projq(ai, col0, ncol, T0):
                    for kc in range(8):
                        S.op('pe', lambda e, kc=kc: e.matmul(psAq[ai][0:ncol, :], lhsT=wQ[:, kc, col0:col0 + ncol], rhs=hTo[:, kc, T0:T0 + 512],
                                                             start=(kc == 0), stop=(kc == 7)),
                             writes=['qpsA%d' % ai], sig=(kc == 7))

                pipe = Pipe()
                for ch in range(4):
                    T0 = ch * 512
                    tb = ch % 2
                    fins = []
                    for pi in range(6):
                        ai = nextAq()
                        qbuf = qcnt[0] % 3
                        qcnt[0] += 1
                        mbs = None
                        if pi // 2 == 1:
                            mbs = (mcnt[0] % 2, mcnt[0] % 2)
                            mcnt[0] += 1

                        def A(ai=ai, pi=pi, T0=T0):
                            projq(ai, Q_Q + pi * 128, 128, T0)

                        def B(ai=ai, pi=pi, T0=T0, qbuf=qbuf, ch=ch, mbs=mbs):
                            rfq([(psAq[ai][:, :], 'qpsA%d' % ai)], 1, 128, bd64, [vecs[:, V_GQ + pi:V_GQ + pi + 1]], math.log(0.125),
                                [(qst[qbuf][:, :], 'qst%d' % qbuf)])
                            hd = 4 * (pi // 2) + 2 * (pi % 2)
                            for hh in range(2):
                                S.dma('sp', lambda e, hh=hh: e.dma_start(out=QT_s[hd + hh, 0:64, T0:T0 + 512], in_=qst[qbuf][hh * 64:(hh + 1) * 64, :]),
                                      reads=['qst%d' % qbuf])
                            if pi // 2 == 1:
                                pp = pi % 2
                                mbi = mbs[0]
                                for tt in range(4):
                                    S.op('pe', lambda e, tt=tt: e.matmul(
                                        psG[:, tt * 32:(tt + 1) * 32], lhsT=qst[qbuf][:, tt * 128:(tt + 1) * 128],
                                        rhs=kmBD[pp][:, 0:32], start=True, stop=True),
                                        reads=['qst%d' % qbuf], writes=['psG'], sig=(tt == 3))
                                for hh in range(2):
                                    S.op('dve', lambda e, hh=hh: e.tensor_tensor(
                                        out=gvs[:, hh, :, :], in0=psG[:, 0:128].rearrange("p (t h n) -> p t h n", t=4, h=2)[:, :, hh, :],
                                        in1=VB[:, ch * 4:(ch + 1) * 4, :], op=ALU.add), reads=['psG'], writes=['gvs'])
                                    for tt in range(4):
                                        S.op('dve', lambda e, hh=hh, tt=tt: e.max(out=m8[:, hh, tt, :], in_=gvs[:, hh, tt, :]), reads=['gvs'], writes=['m8'])
                                    S.op('dve', lambda e, hh=hh: e.tensor_tensor(out=Mf[:, hh, :, :], in0=gvs[:, hh, :, :],
                                                                               in1=m8[:, hh, :, 2:3].to_broadcast([128, 4, 16]), op=ALU.is_ge),
                                         reads=['gvs', 'm8'], writes=['Mf'])
                                    S.op('dve', lambda e, hh=hh: e.tensor_scalar(out=Mf[:, hh, :, :], in0=Mf[:, hh, :, :], scalar1=1.0, scalar2=-NEG,
                                                                               op0=ALU.subtract, op1=ALU.mult), reads=['Mf'], writes=['Mf'])
                                    S.op('dve', lambda e, hh=hh: e.tensor_tensor(out=Mb[mbi][:, hh, :, :], in0=Mf[:, hh, :, :], in1=VM[:, ch * 4:(ch + 1) * 4, :], op=ALU.mult),
                                         reads=['Mf'], writes=['Mb%d' % mbi])

                        fin = None
                        if pi // 2 == 1:
                            def fin(pi=pi, T0=T0, mbs=mbs):
                                pp = pi % 2
                                mbi = mbs[0]
                                for hh in range(2):
                                    for tt in range(4):
                                        g = hh * 4 + tt
                                        S.op('pe', lambda e, hh=hh, tt=tt, g=g: e.transpose(out=psM[0:16, g * 128:(g + 1) * 128], in_=Mb[mbi][:, hh, tt, :], identity=ident),
                                             reads=['Mb%d' % mbi], writes=['psM'], sig=(g == 7))
                                S.op('dve', lambda e: e.tensor_copy(out=mst[mbi][:, :], in_=psM[0:16, :]), reads=['psM'], writes=['mst%d' % mbi])
                                for hh in range(2):
                                    S.dma('sp', lambda e, hh=hh: e.dma_start(out=QT_s[4 + 2 * pp + hh, 64:80, T0:T0 + 512], in_=mst[mbi][:, hh * 512:(hh + 1) * 512]),
                                          reads=['mst%d' % mbi])
                        fins.append(fin)
                        pipe.add(A, B)
                        if pi >= 2 and fins[-3] is not None:
                            pipe.add(lambda: None, fins[-3])
                    for f_ in fins[-2:]:
                        if f_ is not None:
                            pipe.add(lambda: None, f_)
                    a0 = nextAq()
                    a1 = nextAq()

                    def A(a0=a0, a1=a1, T0=T0, tb=tb):
                        projq(a0, Q_CQ, 128, T0)
                        projq(a1, Q_CQ + 128, 128, T0)
                        S.dma('sp', lambda e: e.dma_start(out=ctt[tb][:, :], in_=C["CTq"][:, T0:T0 + 512]), writes=['ctt%d' % tb])
                        S.dma('sp', lambda e: e.dma_start(out=stt[tb][:, :], in_=C["STq"][:, T0:T0 + 512]), writes=['stt%d' % tb])

                    def B(a0=a0, a1=a1):
                        rfq([(psAq[a0][:, :], 'qpsA%d' % a0), (psAq[a1][:, :], 'qpsA%d' % a1)], 2, 128, b256,
                            [vecs[:, V_GQN:V_GQN + 1], vecs[:, V_GQN + 1:V_GQN + 2]], 0.0,
                            [(cqn[:, 0, :], 'cqn'), (cqn[:, 1, :], 'cqn')])
                    pipe.add(A, B)
                    for h in range(4):
                        aA = nextAq()
                        aB = nextAq()
                        qmb = (ch * 4 + h) % 2

                        def A(aA=aA, aB=aB, h=h):
                            for c in range(2):
                                S.op('pe', lambda e, c=c: e.matmul(psAq[aA][0:96, :], lhsT=wuqA[:, c, 96 * h:96 * h + 96], rhs=cqn[:, c, :], start=(c == 0), stop=(c == 1)),
                                     reads=['cqn'], writes=['qpsA%d' % aA], sig=(c == 1))
                            for c in range(2):
                                S.op('pe', lambda e, c=c: e.matmul(psAq[aB][0:96, :], lhsT=wuqB[:, c, 96 * h:96 * h + 96], rhs=cqn[:, c, :], start=(c == 0), stop=(c == 1)),
                                     reads=['cqn'], writes=['qpsA%d' % aB], sig=(c == 1))

                        def B(aA=aA, aB=aB, h=h, qmb=qmb, tb=tb, T0=T0):
                            rfq([(psAq[aA][0:96, :], 'qpsA%d' % aA), (psAq[aB][0:96, :], 'qpsA%d' % aB)], 1, 96, bdmla,
                                [vecs[0:96, V_GA:V_GA + 1], vecs[0:96, V_GB:V_GB + 1]], 0.0, [(qa[:, :], 'qa'), (qb[:, :], 'qb')])
                            S.op('pool', lambda e: e.tensor_tensor(out=qa[:, :], in0=qa[:, :], in1=ctt[tb][:, :], op=ALU.mult), reads=['qa', 'ctt%d' % tb], writes=['qa'])
                            S.op('pool', lambda e: e.tensor_tensor(out=qb[:, :], in0=qb[:, :], in1=stt[tb][:, :], op=ALU.mult), reads=['qb', 'stt%d' % tb], writes=['qb'])
                            S.op('pool', lambda e: e.tensor_tensor(out=qmst[qmb][:, :], in0=qa[:, :], in1=qb[:, :], op=ALU.add), reads=['qa', 'qb'], writes=['qmst%d' % qmb])
                            S.dma('sp', lambda e: e.dma_start(out=QT_s[12 + h, 0:96, T0:T0 + 512], in_=qmst[qmb][:, :]), reads=['qmst%d' % qmb])
                        pipe.add(A, B, dep_prev=(h == 0))
                    for g in range(8):
                        ai = nextAq()

                        def A(ai=ai, g=g, T0=T0):
                            projq(ai, Q_GATE + g * 128, 128, T0)

                        def B(ai=ai, g=g, T0=T0):
                            S.op('act', lambda e: e.activation(out=GT[:, g, T0:T0 + 512], in_=psAq[ai][:, :], func=AF.Silu), reads=['qpsA%d' % ai])
                        pipe.add(A, B)
                pipe.run()
                S.barrier()
            with contextlib.ExitStack() as st:
                Sel = sbt(st, "Sel", [128, 4, 256], F32)
                S.dma("sp", lambda e: e.dma_start(out=Sel[:], in_=C["Sel"][:]), writes=["Sel"])
                psG2 = pst(st, "psG2", [128, 512], F32)
                cown = sbt(st, "cown", [4, OWN], F32)
                for s in range(8):
                    for jj in range(4):
                        blk = 4 * s + jj
                        S.op('pe', lambda e, blk=blk, jj=jj: e.matmul(psG2[0:4, 0:256], lhsT=cTm[:, blk * 4:(blk + 1) * 4], rhs=Sel[:, jj, :], start=(jj == 0), stop=(jj == 3)),
                             reads=['Sel'], writes=['psG2'], sig=(jj == 3))
                    S.op('dve', lambda e, s=s: e.tensor_scalar(out=cown[0:4, s * 256:(s + 1) * 256], in0=psG2[0:4, 0:256], scalar1=-1.0, scalar2=None, op0=ALU.mult),
                         reads=['psG2'], writes=['cosrc'])
                hi, mid, lo = split3(st, "co", cown[:, :], 4, OWN)
                for part, row in ((hi, 64), (mid, 65), (lo, 66)):
                    S.dma('sp', lambda e, part=part, row=row: e.dma_start(out=QT_s[0:4, row, :], in_=part[:, :]), reads=['cohi', 'comid', 'colo'])
                for h in range(4):
                    S.dma('sp', lambda e, h=h: e.dma_start(out=QT_s[h, 67:70, :], in_=ones3_d[:, 0:OWN]))
                S.barrier()

        def phaseA(GT, mixT, F_s):
            with contextlib.ExitStack() as st:
                kt = [[sbt(st, "kt%d_%d" % (b, hh), [128, SEQ], BF16) for hh in range(2)] for b in range(2)]
                qt_ = [[sbt(st, "qt%d_%d" % (b, hh), [128, OWN], BF16) for hh in range(2)] for b in range(2)]
                vt = [sbt(st, "vt%d" % b, [128, 32, 192], BF16) for b in range(2)]
                gstage = [sbt(st, "gstage%d" % hh, [128, WG], BF16) for hh in range(2)]
                gtab = [sbt(st, "gtab%d" % b, [128, 2, WG], BF16) for b in range(2)]
                gc = sbt(st, "gc", [128, WC], BF16)
                NPB = 4
                pb = [sbt(st, "pb%d" % i, [128, 512], BF16) for i in range(NPB)]
                rd = sbt(st, "rd", [128, 512], F32)
                tmpo = sbt(st, "tmpo", [128, 512], F32)
                psS = [pst(st, "psS%d" % i, [128, 512], F32) for i in range(NPB)]
                psO = [[pst(st, "psO%d_%d" % (i, hh), [128, 512], F32) for hh in range(2)] for i in range(2)]
                Vv = V_s.rearrange("(j p) c -> p j c", p=128)
                S.dma('sp', lambda e: e.dma_start(out=gc[:, :], in_=bass.AP(tensor=F_s.tensor, offset=8 * FL, ap=[[1, 128], [1, WC]])), writes=['gc'])
                for b in range(2):
                    S.op('pool', lambda e, b=b: e.memset(vt[b][:, :, 64:128], 1.0), writes=['vt%d' % b])
                KDs = (70, 80, 64, 96)
                KDM = (70, 80, 128, 96)
                for b_ in range(2):
                    for hh_ in range(2):
                        S.op('dve', lambda e, b_=b_, hh_=hh_: e.memset(kt[b_][hh_][64:128, :], 0.0), writes=['kt%d_%d' % (b_, hh_)])
                        S.op('dve', lambda e, b_=b_, hh_=hh_: e.memset(qt_[b_][hh_][64:128, :], 0.0), writes=['qt%d_%d' % (b_, hh_)])
                scnt = [0]
                ocnt = [0]
                for p8 in range(8):
                    m, pp = p8 // 2, p8 % 2
                    KD = KDs[m]
                    KDq = KDM[m]
                    b = p8 % 2
                    if m == 2:
                        for hh in range(2):
                            S.op('dve', lambda e, b=b, hh=hh: e.memset(qt_[b][hh][64:128, :], 0.0), writes=['qt%d_%d' % (b, hh)])
                    for hh in range(2):
                        hd = 4 * m + 2 * pp + hh
                        for half in range(2):
                            S.dma('sp', lambda e, b=b, hh=hh, hd=hd, half=half, KD=KD: e.dma_start(
                                out=kt[b][hh][0:KD, half * 2048:(half + 1) * 2048], in_=KT_s[hd, 0:KD, half * 2048:(half + 1) * 2048]),
                                writes=['kt%d_%d' % (b, hh)])
                        S.dma('sp', lambda e, b=b, hh=hh, hd=hd, KD=KD: e.dma_start(out=qt_[b][hh][0:KD, :], in_=QT_s[hd, 0:KD, :]), writes=['qt%d_%d' % (b, hh)])
                    for q4 in range(4):
                        for hh in range(2):
                            c0 = m * 256 + pp * 128 + hh * 64
                            S.dma('sp', lambda e, b=b, q4=q4, hh=hh, c0=c0: e.dma_start(
                                out=vt[b][:, q4 * 8:(q4 + 1) * 8, hh * 128:hh * 128 + 64], in_=Vv[:, q4 * 8:(q4 + 1) * 8, c0:c0 + 64]),
                                writes=['vt%d' % b])
                    if m in (1, 2):
                        for hh in range(2):
                            row = (m - 1) * 4 + 2 * pp + hh
                            S.dma('sp', lambda e, hh=hh, row=row: e.dma_start(
                                out=gstage[hh][:, :], in_=bass.AP(tensor=F_s.tensor, offset=row * FL, ap=[[1, 128], [1, WG]])), writes=['gstage%d' % hh])
                            for c0 in range(0, WG, 512):
                                cw = min(512, WG - c0)
                                bi = scnt[0] % NPB
                                scnt[0] += 1
                                S.op('pe', lambda e, bi=bi, hh=hh, c0=c0, cw=cw: e.matmul(psS[bi][:, 0:cw], lhsT=Jm, rhs=gstage[hh][:, c0:c0 + cw], start=True, stop=True),
                                     reads=['gstage%d' % hh], writes=['psS%d' % bi])
                                S.op('act', lambda e, bi=bi, hh=hh, c0=c0, cw=cw, b=b: e.activation(out=gtab[b][:, hh, c0:c0 + cw], in_=psS[bi][:, 0:cw], func=AF.Exp),
                                     reads=['psS%d' % bi], writes=['gtab%d' % b])
                    RK = ['kt%d_%d' % (b, hh) for hh in range(2)] + ['qt%d_%d' % (b, hh) for hh in range(2)]
                    for u in range(4):
                        s0, s1 = 2 * u, 2 * u + 1
                        nk = (4 * s0 + 4, 4 * s1 + 4)
                        jlo = (max(0, 4 * s0 - 16), max(0, 4 * s1 - 16)) if m == 2 else (0, 0)
                        js = list(range(jlo[0], nk[1]))
                        ob = ocnt[0] % 2
                        ocnt[0] += 1
                        sbufs = {}

                        def active(j):
                            a0 = (jlo[0] <= j < nk[0])
                            a1 = (jlo[1] <= j < nk[1])
                            c0 = 0 if a0 else 256
                            c1 = 512 if a1 else 256
                            return a0, a1, c0, c1

                        def emit_S(j):
                            a0, a1, c0, c1 = active(j)
                            bis = []
                            for hh in range(2):
                                bi = scnt[0] % NPB
                                scnt[0] += 1
                                bis.append(bi)
                                jadd = None
                                if m in (0, 3):
                                    for si, sl in enumerate((s0, s1)):
                                        o = 512 * sl - 128 * j + 384
                                        if (a0, a1)[si] and o < 512:
                                            jadd = (si, o)
                                S.op('pe', lambda e, bi=bi, hh=hh, j=j, b=b, KD=KDq, c0=c0, c1=c1, jadd=jadd, s0=s0: e.matmul(
                                    psS[bi][:, c0:c1], lhsT=kt[b][hh][0:KD, j * 128:(j + 1) * 128],
                                    rhs=qt_[b][hh][0:KD, s0 * 256 + c0:s0 * 256 + c1], start=True, stop=(jadd is None)),
                                    reads=RK, writes=['psS%d' % bi], sig=(jadd is None))
                                if jadd is not None:
                                    si, o = jadd
                                    S.op('pe', lambda e, bi=bi, si=si, o=o: e.matmul(psS[bi][:, si * 256:(si + 1) * 256], lhsT=Jm, rhs=gc[:, o:o + 256],
                                                                                   start=False, stop=True),
                                         reads=RK + ['gc'], writes=['psS%d' % bi])
                            sbufs[j] = bis

                        def emit_PV(j):
                            a0, a1, c0, c1 = active(j)
                            bis = sbufs[j]
                            first, last = (j == js[0]), (j == js[-1])
                            for hh in range(2):
                                bi = bis[hh]
                                S.op('act', lambda e, bi=bi, c0=c0, c1=c1: e.activation(out=pb[bi][:, c0:c1], in_=psS[bi][:, c0:c1], func=AF.Exp),
                                     reads=['psS%d' % bi], writes=['pb%d' % bi])
                                if m in (1, 2):
                                    for si, sl in enumerate((s0, s1)):
                                        if not (a0, a1)[si]:
                                            continue
                                        o = 512 * sl - 128 * j + 384
                                        oe = min(o, 2048) if m == 1 else o
                                        S.op('dve', lambda e, bi=bi, oe=oe, b=b, hh=hh, si=si: e.tensor_tensor(
                                            out=pb[bi][:, si * 256:(si + 1) * 256], in0=pb[bi][:, si * 256:(si + 1) * 256],
                                            in1=gtab[b][:, hh, oe:oe + 256], op=ALU.mult),
                                            reads=['pb%d' % bi, 'gtab%d' % b], writes=['pb%d' % bi])
                                S.op('pe', lambda e, bi=bi, hh=hh, j=j, ob=ob, b=b, first=first, last=last, c0=c0, c1=c1: e.matmul(
                                    psO[ob][hh][:, c0:c1], lhsT=vt[b][:, j, hh * 64:hh * 64 + 128],
                                    rhs=pb[bi][:, c0:c1], start=first, stop=last),
                                    reads=['pb%d' % bi, 'vt%d' % b], writes=['psO%d_%d' % (ob, hh)], sig=True)

                        emit_S(js[0])
                        for idx, j in enumerate(js):
                            if idx + 1 < len(js):
                                emit_S(js[idx + 1])
                            emit_PV(j)
                        S.op('act', lambda e, ob=ob: e.activation(out=rd[0:64, :], in_=psO[ob][0][64:128, :], func=AF.Ln), reads=['psO%d_0' % ob], writes=['rd'])
                        S.op('act', lambda e, ob=ob: e.activation(out=rd[64:128, :], in_=psO[ob][1][0:64, :], func=AF.Ln), reads=['psO%d_1' % ob], writes=['rd'])
                        S.op('act', lambda e: e.activation(out=rd[:, :], in_=rd[:, :], func=AF.Exp, scale=-1.0), reads=['rd'], writes=['rd'])
                        S.op('dve', lambda e, ob=ob: e.tensor_tensor(out=tmpo[0:64, :], in0=psO[ob][0][0:64, :], in1=rd[0:64, :], op=ALU.mult),
                             reads=['psO%d_0' % ob, 'rd'], writes=['tmpo'])
                        S.op('dve', lambda e, ob=ob: e.tensor_tensor(out=tmpo[64:128, :], in0=psO[ob][1][64:128, :], in1=rd[64:128, :], op=ALU.mult),
                             reads=['psO%d_1' % ob, 'psO%d_0' % ob, 'rd'], writes=['tmpo'])
                        S.op('pool', lambda e, p8=p8, s0=s0: e.tensor_tensor(out=mixT[:, p8, s0 * 256:s0 * 256 + 512], in0=tmpo[:, :], in1=GT[:, p8, s0 * 256:s0 * 256 + 512], op=ALU.mult),
                             reads=['tmpo'])
                S.barrier()

        def phaseO(xload, pown_d, W, vecs, mixT, out_ap):
            with contextlib.ExitStack() as st:
                wout = load_w(st, "wout", W["wout"], 8, 1024)
                wpp = load_w(st, "wpp", W["wpp"], 2, 1024)
                wpg = load_w(st, "wpg", W["wpg"], 8, 1024, V_PLEG, vecs)
                S.barrier()
                NX = 3
                xt = [sbt(st, "oxt%d" % i, [128, 1024], F32) for i in range(NX)]
                pt = [sbt(st, "opt%d" % i, [128, 256], F32) for i in range(2)]
                ptb = [sbt(st, "optb%d" % i, [128, 256], BF16) for i in range(2)]
                pTs = [sbt(st, "opTs%d" % i, [128, 2, 128], BF16) for i in range(2)]
                x1 = [sbt(st, "ox1%d" % i, [128, 1024], F32) for i in range(NX)]
                junk = sbt(st, "ojunk", [128, 1024], BF16)
                xn = [sbt(st, "oxn%d" % i, [128, 1024], BF16) for i in range(2)]
                ss = [sbt(st, "oss%d" % i, [128, 4], F32) for i in range(2)]
                gTs = [sbt(st, "ogT%d" % i, [128, 8, 128], BF16) for i in range(2)]
                sg = [sbt(st, "osg%d" % i, [128, 1024], F32) for i in range(2)]
                psY = [pst(st, "psY%d" % i, [128, 512], F32) for i in range(2)]
                psT = pst(st, "opsT", [128, 1024], BF16)
                psP = pst(st, "opsP", [128, 512], BF16)
                psGt = [pst(st, "psGt%d" % i, [128, 512], F32) for i in range(2)]
                psPP = [pst(st, "psPP%d" % i, [128, 512], F32) for i in range(2)]
                NTO = OWN // 128

                def stage1(t):
                    b, b3, tok = t % 2, t % NX, t * 128
                    xload(t, xt[b3], 'oxt%d' % b3)
                    S.dma('sp', lambda e: e.dma_start(out=pt[b][:, :], in_=pown_d[tok:tok + 128, :]), writes=['opt%d' % b])
                    for half in range(2):
                        for kc in range(8):
                            S.op('pe', lambda e, half=half, kc=kc: e.matmul(psY[half][:, :], lhsT=mixT[:, kc, tok:tok + 128], rhs=wout[:, kc, half * 512:(half + 1) * 512],
                                                                          start=(kc == 0), stop=(kc == 7)), writes=['psY%d' % half], sig=(kc == 7))
                        S.op('dve', lambda e, half=half: e.tensor_tensor(out=x1[b3][:, half * 512:(half + 1) * 512], in0=psY[half][:, :], in1=xt[b3][:, half * 512:(half + 1) * 512], op=ALU.add),
                             reads=['psY%d' % half, 'oxt%d' % b3], writes=['ox1%d' % b3])
                    S.op('act', lambda e: e.activation(out=junk[:, :], in_=x1[b3][:, :], func=AF.Square, accum_out=ss[b][:, 0:1]), reads=['ox1%d' % b3], writes=['ojunk', 'oss%d' % b])
                    S.op('act', lambda e: e.activation(out=ss[b][:, 1:2], in_=ss[b][:, 0:1], func=AF.Ln, scale=1.0 / 1024, bias=EPS), reads=['oss%d' % b], writes=['oss%d' % b])
                    S.op('act', lambda e: e.activation(out=ss[b][:, 2:3], in_=ss[b][:, 1:2], func=AF.Exp, scale=-0.5), reads=['oss%d' % b], writes=['oss%d' % b])
                    S.op('act', lambda e: e.activation(out=xn[b][:, :], in_=x1[b3][:, :], func=AF.Copy, scale=ss[b][:, 2:3]),
                         reads=['ox1%d' % b3, 'oss%d' % b], writes=['oxn%d' % b])
                    S.op('dve', lambda e: e.tensor_copy(out=ptb[b][:, :], in_=pt[b][:, :]), reads=['opt%d' % b], writes=['optb%d' % b])

                def stage2(t):
                    b = t % 2
                    for c in range(8):
                        S.op('pe', lambda e, c=c: e.transpose(out=psT[:, c * 128:(c + 1) * 128], in_=xn[b][:, c * 128:(c + 1) * 128], identity=ident),
                             reads=['oxn%d' % b], writes=['opsT'], sig=(c == 7))
                    S.op('dve', lambda e: e.tensor_copy(out=gTs[b][:, :, :], in_=psT[:, :].rearrange("p (c n) -> p c n", c=8)), reads=['opsT'], writes=['ogT%d' % b])
                    for c in range(2):
                        S.op('pe', lambda e, c=c: e.transpose(out=psP[:, c * 128:(c + 1) * 128], in_=ptb[b][:, c * 128:(c + 1) * 128], identity=ident),
                             reads=['optb%d' % b], writes=['opsP'], sig=(c == 1))
                    S.op('act', lambda e: e.activation(out=pTs[b][:, :, :], in_=psP[:, 0:256].rearrange("p (c n) -> p c n", c=2), func=AF.Copy), reads=['opsP'], writes=['opTs%d' % b])

                def stage3(t):
                    b, b3, tok = t % 2, t % NX, t * 128
                    for half in range(2):
                        for kc in range(8):
                            S.op('pe', lambda e, half=half, kc=kc: e.matmul(psGt[half][:, :], lhsT=gTs[b][:, kc, :], rhs=wpg[:, kc, half * 512:(half + 1) * 512],
                                                                          start=(kc == 0), stop=(kc == 7)), reads=['ogT%d' % b], writes=['psGt%d' % half], sig=(kc == 7))
                        S.op('act', lambda e, half=half: e.activation(out=sg[b][:, half * 512:(half + 1) * 512], in_=psGt[half][:, :], func=AF.Sigmoid),
                             reads=['psGt%d' % half], writes=['osg%d' % b])
                        for c in range(2):
                            S.op('pe', lambda e, half=half, c=c: e.matmul(psPP[half][:, :], lhsT=pTs[b][:, c, :], rhs=wpp[:, c, half * 512:(half + 1) * 512],
                                                                        start=(c == 0), stop=(c == 1)), reads=['opTs%d' % b], writes=['psPP%d' % half], sig=(c == 1))
                        S.op('dve', lambda e, half=half: e.tensor_tensor(out=sg[b][:, half * 512:(half + 1) * 512], in0=psPP[half][:, :], in1=sg[b][:, half * 512:(half + 1) * 512], op=ALU.mult),
                             reads=['psPP%d' % half, 'osg%d' % b], writes=['osg%d' % b])
                    S.op('dve', lambda e: e.tensor_tensor(out=sg[b][:, :], in0=sg[b][:, :], in1=x1[b3][:, :], op=ALU.add), reads=['osg%d' % b, 'ox1%d' % b3], writes=['osg%d' % b])
                    S.dma('sp', lambda e: e.dma_start(out=out_ap[tok:tok + 128, :], in_=sg[b][:, :]), reads=['osg%d' % b])

                for i in range(NTO + 2):
                    if i < NTO:
                        stage1(i)
                    if 1 <= i <= NTO:
                        stage2(i - 1)
                    if i >= 2:
                        stage3(i - 2)
                S.barrier()

        def dram_loader(src):
            def f(t, dst, res):
                S.dma('sp', lambda e: e.dma_start(out=dst[:, :], in_=src[t * 128:(t + 1) * 128, :]), writes=[res])
            return f

        def x1_nat_loader(t, dst, res):
            a = (t % 4) // 2
            r0 = (t // 4) * 256 + (t % 2) * 128
            S.dma('sp', lambda e: e.dma_start(out=dst[:, :], in_=x1_s[a][r0:r0 + 128, :]), writes=[res])

        for c in ("a0", "a1", "m"):
            phase0(CS[c]["OH"], F_sL[c])
        phaseK(dram_loader(x_all), WL[0], vecsL[0])
        for a in range(2):
            C = CS["a%d" % a]
            with contextlib.ExitStack() as stg:
                GT = sbt(stg, "GT", [128, 8, OWN], BF16)
                phaseQ(dram_loader(xown_d[a]), WL[0], C, vecsL[0], GT)
                mixT = sbt(stg, "mixT", [128, 8, OWN], BF16)
                phaseA(GT, mixT, F_sL["a%d" % a])
                phaseO(dram_loader(xown_d[a]), C["p"], WL[0], vecsL[0], mixT, x1_s[a])
        phaseK(x1_nat_loader, WL[1], vecsL[1])
        C = CS["m"]
        with contextlib.ExitStack() as stg:
            selt = [sbt(stg, "selt%d" % i, [128, 1024], F32) for i in range(2)]

            def x1_own_loader(t, dst, res):
                for a in range(2):
                    S.dma('sp', lambda e, a=a: e.dma_start(out=selt[a][:, :], in_=x1_s[a][t * 128:(t + 1) * 128, :]), writes=['selt%d' % a])
                S.op('act', lambda e: e.activation(out=selt[0][:, :], in_=selt[0][:, :], func=AF.Copy, scale=msel[:, 0:1]),
                     reads=['selt0'], writes=['selt0'])
                S.op('dve', lambda e: e.scalar_tensor_tensor(out=dst[:, :], in0=selt[1][:, :], scalar=msel[:, 1:2], in1=selt[0][:, :], op0=ALU.mult, op1=ALU.add),
                     reads=['selt0', 'selt1'], writes=[res])

            GT = sbt(stg, "GT", [128, 8, OWN], BF16)
            phaseQ(x1_own_loader, WL[1], C, vecsL[1], GT)
            mixT = sbt(stg, "mixT", [128, 8, OWN], BF16)
            phaseA(GT, mixT, F_sL["m"])
            phaseO(x1_own_loader, C["p"], WL[1], vecsL[1], mixT, out_d)
        S.emit()
    return nc


def _t5_bucket(d):
    d = np.maximum(d, 0)
    df = np.maximum(d, 1).astype(np.float32)
    large = 16 + (np.log(df / np.float32(16)) / np.float32(math.log(2048 / 16)) * np.float32(16)).astype(np.int32)
    large = np.minimum(large, 31)
    return np.where(d < 16, d, large)


def _core_consts(par):
    bf = ml_dtypes.bfloat16
    c = {}
    i = np.arange(FL)
    d = i + 256 * par - 511
    OH = np.zeros((34, FL), np.float32)
    bk = _t5_bucket(d)
    valid = d >= 0
    OH[bk[valid], i[valid]] = 1.0
    OH[32, ~valid] = NEG
    mult = ((d <= 128).astype(np.float32) + ((d % 4 == 0) & (d <= 512)).astype(np.float32)
            + ((d % 16 == 0) & (d <= 2048)).astype(np.float32))
    ok = valid & (mult > 0)
    OH[33, :] = NEG
    OH[33, ok] = np.log(mult[ok]).astype(np.float32)
    c["OH"] = OH
    Sel = np.zeros((128, 4, 256), np.float32)
    for jj in range(4):
        for k in range(128):
            q = 128 * jj + k - 256 * par
            if 0 <= q < 256:
                Sel[k, jj, q] = 1.0
    c["Sel"] = Sel
    VB = np.zeros((128, 16, 16), np.float32)
    VM = np.zeros((128, 16, 16), np.float32)
    for qt in range(16):
        own = 2 * (qt // 2) + par
        VB[:, qt, own:] = -1e9
        VM[:, qt, :own] = 1.0
    c["VB"], c["VM"] = VB, VM
    half = 16
    inv = (1.0 / (np.float32(10000.0) ** (np.arange(half, dtype=np.float32) * np.float32(2.0) / np.float32(32)))).astype(np.float32)
    pos = np.arange(SEQ).astype(np.float32)
    ang = pos[:, None] * inv[None, :]
    cos, sin = np.cos(ang).astype(np.float32).T, np.sin(ang).astype(np.float32).T
    c["CK"] = np.ascontiguousarray(np.concatenate([cos, cos], 0))
    c["SK"] = np.ascontiguousarray(np.concatenate([-sin, sin], 0))
    own_idx = np.concatenate([512 * s + 256 * par + np.arange(256) for s in range(8)])
    sc = np.float32(96 ** -0.5)
    CT = np.full((96, OWN), sc, np.float32)
    ST = np.zeros((96, OWN), np.float32)
    CT[64:96] = np.concatenate([cos, cos], 0)[:, own_idx] * sc
    ST[64:96] = np.concatenate([-sin, sin], 0)[:, own_idx] * sc
    c["CTq"], c["STq"] = CT, ST
    c["own_idx"] = own_idx
    return c


def _shared_consts():
    bf = ml_dtypes.bfloat16
    cb = np.zeros((128, NCB), np.float32)
    for g in range(2):
        cb[g * 64:(g + 1) * 64, C_BD64 + g * 64:C_BD64 + (g + 1) * 64] = 1.0 / 64
    cb[:, C_B128:C_B128 + 128] = 1.0 / 128
    cb[:, C_B256:C_B256 + 128] = 1.0 / 256
    cb[0:64, C_BDMLA:C_BDMLA + 64] = 1.0 / 64
    cb[64:96, C_BDMLA + 64:C_BDMLA + 96] = 1.0 / 32
    cb[0:32, C_B32:C_B32 + 32] = 1.0 / 32
    cb[:, C_ONES:C_ONES + 64] = 1.0
    cb[:, C_J:C_J + 128] = np.eye(128, dtype=np.float32)[::-1]
    cb[:, C_ID:C_ID + 128] = np.eye(128, dtype=np.float32)
    oh16 = np.zeros((16, SEQ), np.float32)
    for n in range(16):
        oh16[n, n * 256:(n + 1) * 256] = 1.0
    return {"cbf": cb.astype(bf), "oh16": oh16.astype(bf), "ones3": np.ones((3, SEQ), bf),
            "identf": np.eye(128, dtype=np.float32)}


def _kc(w):
    return np.ascontiguousarray(w.reshape(8, 128, -1).transpose(1, 0, 2))


def _layer_weights(l, ln_g, w_in, b_forget, qk_gain, mla_q_norm, mla_kv_norm, mla_nope_gain, mla_rope_gain,
                   w_uq, w_ukv, w_out, rel_bias, ple_norm_g, w_ple_gate, w_ple_proj):
    W = w_in[l]
    fq, fk, fv, ff = W[:, 0:256], W[:, 256:512], W[:, 512:768], W[:, 768:772]
    mq, mk, mv = W[:, 772:1028], W[:, 1028:1284], W[:, 1284:1540]
    dq, dk, dv = W[:, 1540:1796], W[:, 1796:2052], W[:, 2052:2308]
    cq, ckv, kr, gate = W[:, 2308:2564], W[:, 2564:2692], W[:, 2692:2724], W[:, 2724:3748]
    kr_sw = np.concatenate([kr[:, 16:32], kr[:, 0:16]], 1)
    wK = np.concatenate([fk, mk, dk, fv, mv, dv, ckv, kr, kr_sw, ff, np.zeros((1024, 4), np.float32)], 1)
    wQ = np.concatenate([fq, mq, dq, cq, gate], 1)
    uq = w_uq[l]
    uqB = uq.copy()
    for h in range(4):
        uqB[:, 96 * h + 64:96 * h + 80] = uq[:, 96 * h + 80:96 * h + 96]
        uqB[:, 96 * h + 80:96 * h + 96] = uq[:, 96 * h + 64:96 * h + 80]
    ukv = w_ukv[l]
    ukvK = np.concatenate([ukv[:, 128 * h:128 * h + 64] for h in range(4)], 1)
    ukvV = np.concatenate([ukv[:, 128 * h + 64:128 * h + 128] for h in range(4)], 1)
    vec = np.zeros((128, NV), np.float32)
    vec[:, V_LNG:V_LNG + 8] = ln_g[l].reshape(8, 128).T
    for pi in range(6):
        m = pi // 2
        vec[:, V_GK + pi] = np.tile(qk_gain[l, 2 * m + 1], 2)
        vec[:, V_GQ + pi] = np.tile(qk_gain[l, 2 * m], 2)
    vec[:, V_GQN:V_GQN + 2] = mla_q_norm[l].reshape(2, 128).T
    vec[:, V_GKV] = mla_kv_norm[l]
    vec[:, V_GKN] = np.tile(mla_nope_gain[l, 1], 2)
    rg0, rg1 = mla_rope_gain[l, 0], mla_rope_gain[l, 1]
    vec[0:96, V_GA] = np.concatenate([mla_nope_gain[l, 0], rg0])
    vec[0:96, V_GB] = np.concatenate([mla_nope_gain[l, 0], rg0[16:32], rg0[0:16]])
    vec[0:32, V_GR] = rg1
    vec[0:32, V_GR + 1] = np.concatenate([rg1[16:32], rg1[0:16]])
    vec[0:4, V_BF] = b_forget[l]
    vec[:, V_PLEG:V_PLEG + 8] = ple_norm_g[l].reshape(8, 128).T
    tabX = np.zeros((34, 9), np.float32)
    tabX[0:32, 0:8] = rel_bias
    tabX[32, 0:4] = 1.0
    tabX[33, 4:8] = 1.0
    tabX[32, 8] = 1.0
    return {"wK": _kc(wK), "wQ": _kc(wQ),
            "wuqA": np.ascontiguousarray(uq.reshape(2, 128, 384).transpose(1, 0, 2)),
            "wuqB": np.ascontiguousarray(uqB.reshape(2, 128, 384).transpose(1, 0, 2)),
            "wukvK": np.ascontiguousarray(ukvK.reshape(128, 1, 256)), "wukvV": np.ascontiguousarray(ukvV.reshape(128, 1, 256)),
            "wout": _kc(w_out[l]), "wpg": _kc(w_ple_gate[l]),
            "wpp": np.ascontiguousarray(w_ple_proj[l].reshape(2, 128, 1024).transpose(1, 0, 2)),
            "vecs": vec, "tabX": tabX}


_NC = None


def kernel(x, p, ln_g, w_in, b_forget, qk_gain, mla_q_norm, mla_kv_norm, mla_nope_gain, mla_rope_gain,
           w_uq, w_ukv, w_out, rel_bias, ple_norm_g, w_ple_gate, w_ple_proj):
    global _NC
    args = [np.asarray(a, dtype=np.float32) for a in (ln_g, w_in, b_forget, qk_gain, mla_q_norm, mla_kv_norm, mla_nope_gain,
                                                     mla_rope_gain, w_uq, w_ukv, w_out, rel_bias, ple_norm_g, w_ple_gate, w_ple_proj)]
    x = np.asarray(x, dtype=np.float32)
    p = np.asarray(p, dtype=np.float32)
    if _NC is None:
        _NC = build_fused()
    shared = _shared_consts()
    cc = [_core_consts(par) for par in range(2)]
    base = dict(shared)
    for l in range(2):
        lw = _layer_weights(l, *args)
        base["tabX"] = lw.pop("tabX")
        for k, v in lw.items():
            base[k + "_l%d" % l] = v
    base["CK"], base["SK"] = cc[0]["CK"], cc[0]["SK"]
    for a in range(2):
        for k in ("OH", "Sel", "VB", "VM", "CTq", "STq"):
            base[k + "_a%d" % a] = cc[a][k]
    in_maps = []
    for core in range(8):
        b, par = core // 2, core % 2
        mp = dict(base)
        mp["x_all"] = np.ascontiguousarray(x[b])
        for a in range(2):
            mp["x_own_a%d" % a] = np.ascontiguousarray(x[b][cc[a]["own_idx"]])
            mp["p_a%d" % a] = np.ascontiguousarray(p[0, b][cc[a]["own_idx"]])
        mp["p_m"] = np.ascontiguousarray(p[1, b][cc[par]["own_idx"]])
        for k in ("OH", "Sel", "VB", "VM", "CTq", "STq"):
            mp[k + "_m"] = cc[par][k]
        ms = np.zeros((128, 2), np.float32)
        ms[:, par] = 1.0
        mp["msel"] = ms
        in_maps.append(mp)
    res = run_bass_kernel_spmd(_NC, in_maps, core_ids=list(range(8)))
    out = np.empty_like(x)
    for core in range(8):
        b, par = core // 2, core % 2
        out[b][cc[par]["own_idx"]] = np.asarray(res.results[core]["out"], dtype=np.float32)
    return out
```

```python
import contextlib
import math
import numpy as np
import ml_dtypes
import concourse.bass as bass
import concourse.mybir as mybir
from concourse.bass_utils import run_bass_kernel_spmd

F32 = mybir.dt.float32
BF16 = mybir.dt.bfloat16
AF = mybir.ActivationFunctionType
ALU = mybir.AluOpType
AX = mybir.AxisListType

COMPUTE = ('pe', 'act', 'dve', 'pool')
NDMA_SEM = 8
SEQ = 4096
OWN = 2048
NEG = -30000.0
WG = 2688
FL = WG + 128
WC = 640
EPS = 1e-6


class Sched:
    def __init__(self, nc, stack):
        self.nc = nc
        self.ops = {e: [] for e in ('pe', 'act', 'dve', 'pool', 'sp')}
        self.psem = {e: stack.enter_context(nc.semaphore("pg_" + e)) for e in COMPUTE}
        self.pcnt = {e: 0 for e in COMPUTE}
        self.dsem = {q: [stack.enter_context(nc.semaphore("dq_%s%d" % (q, i))) for i in range(NDMA_SEM)]
                     for q in ('sp', 'pool')}
        self.dval = {q: [0] * NDMA_SEM for q in ('sp', 'pool')}
        self.didx = {q: 0 for q in ('sp', 'pool')}
        self.waited = {e: {} for e in self.ops}
        self.res = {}

    def _need(self, eng, tok, waits):
        if tok is None:
            return
        key, sem, val, prod = tok
        if prod == eng and eng == 'pe':
            return
        if self.waited[eng].get(key, 0) >= val:
            return
        self.waited[eng][key] = val
        waits.append((sem, val))

    def _deps(self, eng, reads, writes):
        waits = []
        for r in reads:
            st = self.res.get(r)
            if st is not None:
                self._need(eng, st[0], waits)
        for w in writes:
            st = self.res.get(w)
            if st is not None:
                self._need(eng, st[0], waits)
                for t in st[1]:
                    self._need(eng, t, waits)
        return waits

    def _record(self, tok, reads, writes):
        for r in reads:
            st = self.res.setdefault(r, [None, []])
            st[1].append(tok)
        for w in writes:
            self.res[w] = [tok, []]

    def op(self, eng, fn, reads=(), writes=(), sig=True):
        waits = self._deps(eng, reads, writes)
        tok = None
        if sig:
            self.pcnt[eng] += 1
            tok = ('p' + eng, self.psem[eng], self.pcnt[eng], eng)
            self._record(tok, reads, writes)
        self.ops[eng].append((waits, fn, (self.psem[eng], 1) if sig else None))
        return tok

    def dma(self, q, fn, reads=(), writes=()):
        waits = self._deps(q, reads, writes)
        i = self.didx[q]
        self.didx[q] = (i + 1) % NDMA_SEM
        sem = self.dsem[q][i]
        key = 'd%s%d' % (q, i)
        prev = self.dval[q][i]
        if prev > 0 and self.waited[q].get(key, 0) < prev:
            self.waited[q][key] = prev
            waits.append((sem, prev))
        self.dval[q][i] = prev + 16
        tok = (key, sem, prev + 16, 'dma')
        self._record(tok, reads, writes)
        self.ops[q].append((waits, fn, (sem, 16)))
        return tok

    def barrier(self):
        toks = []
        for e in COMPUTE:
            if self.pcnt[e] > 0:
                toks.append(('p' + e, self.psem[e], self.pcnt[e], e))
        for q in ('sp', 'pool'):
            for i in range(NDMA_SEM):
                if self.dval[q][i] > 0:
                    toks.append(('d%s%d' % (q, i), self.dsem[q][i], self.dval[q][i], 'dma'))
        for e in self.ops:
            waits = []
            for t in toks:
                key, sem, val, prod = t
                if self.waited[e].get(key, 0) >= val:
                    continue
                self.waited[e][key] = val
                waits.append((sem, val))
            if waits:
                self.ops[e].append((waits, None, None))
        self.res = {}

    def emit(self):
        nc = self.nc
        with nc.Block() as block:
            def mk(name):
                def body(eng):
                    for waits, fn, sig in self.ops[name]:
                        for sem, val in waits:
                            eng.wait_ge(sem, val)
                        if fn is None:
                            continue
                        ins = fn(eng)
                        if sig is not None:
                            ins.then_inc(sig[0], sig[1])
                return body
            block.tensor(mk('pe'))
            block.scalar(mk('act'))
            block.vector(mk('dve'))
            block.gpsimd(mk('pool'))
            block.sync(mk('sp'))


V_LNG, V_GK, V_GQ, V_GQN, V_GKV, V_GKN, V_GA, V_GB, V_GR, V_BF, V_PLEG, NV = 0, 8, 14, 20, 22, 23, 24, 25, 26, 28, 29, 40
C_BD64, C_B128, C_B256, C_BDMLA, C_B32, C_ONES, C_J, C_ID, NCB = 0, 128, 256, 384, 480, 512, 576, 704, 832
K_K, K_V, K_CKV, K_KR, K_KRS, K_FF, NWK = 0, 768, 1536, 1664, 1696, 1728, 1736
Q_Q, Q_CQ, Q_GATE, NWQ = 0, 768, 1024, 2048


def build_fused():
    nc = bass.Bass("TRN2", target_bir_lowering=False)

    def din(name, shape, dt=F32):
        return nc.dram_tensor(name, shape, dt, kind="ExternalInput").ap()

    x_all = din("x_all", [SEQ, 1024])
    identf_d = din("identf", [128, 128])
    CK_d = din("CK", [32, SEQ])
    SK_d = din("SK", [32, SEQ])
    oh16_d = din("oh16", [16, SEQ], BF16)
    ones3_d = din("ones3", [3, SEQ], BF16)
    cbf_d = din("cbf", [128, NCB], BF16)
    tabX_d = din("tabX", [34, 9])
    msel_d = din("msel", [128, 2])
    WL = []
    for l in range(2):
        sfx = "_l%d" % l
        WL.append({"wK": din("wK" + sfx, [128, 8, NWK]), "wQ": din("wQ" + sfx, [128, 8, NWQ]),
                   "wuqA": din("wuqA" + sfx, [128, 2, 384]), "wuqB": din("wuqB" + sfx, [128, 2, 384]),
                   "wukvK": din("wukvK" + sfx, [128, 1, 256]), "wukvV": din("wukvV" + sfx, [128, 1, 256]),
                   "wout": din("wout" + sfx, [128, 8, 1024]), "wpg": din("wpg" + sfx, [128, 8, 1024]),
                   "wpp": din("wpp" + sfx, [128, 2, 1024]), "vecs": din("vecs" + sfx, [128, NV])})
    CS = {}
    for c in ("a0", "a1", "m"):
        CS[c] = {"OH": din("OH_" + c, [34, FL]), "Sel": din("Sel_" + c, [128, 4, 256]), "VB": din("VB_" + c, [128, 16, 16]),
                 "VM": din("VM_" + c, [128, 16, 16]), "CTq": din("CTq_" + c, [96, OWN]), "STq": din("STq_" + c, [96, OWN]),
                 "p": din("p_" + c, [OWN, 256])}
    xown_d = [din("x_own_a%d" % a, [OWN, 1024]) for a in range(2)]
    out_d = nc.dram_tensor("out", [OWN, 1024], F32, kind="ExternalOutput").ap()

    KT_s = nc.dram_tensor("KT_s", [16, 128, SEQ], BF16, kind="Internal").ap()
    QT_s = nc.dram_tensor("QT_s", [16, 128, OWN], BF16, kind="Internal").ap()
    V_s = nc.dram_tensor("V_s", [SEQ, 1024], BF16, kind="Internal").ap()
    F_sL = {c: nc.dram_tensor("F_s_" + c, [9, FL], BF16, kind="Internal").ap() for c in ("a0", "a1", "m")}
    x1_s = [nc.dram_tensor("x1_s%d" % a, [OWN, 1024], F32, kind="Internal").ap() for a in range(2)]

    with contextlib.ExitStack() as top:
        S = Sched(nc, top)

        uid = [0]

        def sbt(st, n, shp, dt):
            uid[0] += 1
            return st.enter_context(nc.sbuf_tensor('s%d_%s' % (uid[0], n), shp, dt))

        def pst(st, n, shp, dt):
            uid[0] += 1
            return st.enter_context(nc.psum_tensor('p%d_%s' % (uid[0], n), shp, dt))

        vecsL = [sbt(top, "vecs%d" % l, [128, NV], F32) for l in range(2)]
        msel = sbt(top, "msel", [128, 2], F32)
        cbf = sbt(top, "cbf", [128, NCB], BF16)
        identf = sbt(top, "identf", [128, 128], F32)
        kmT = [sbt(top, "kmT%d" % i, [128, 16], BF16) for i in range(2)]
        kmBD = [sbt(top, "kmBD%d" % i, [128, 32], BF16) for i in range(2)]
        cTm = sbt(top, "cTm", [128, 128], F32)
        negb = sbt(top, "negb", [128, 1], F32)

        for l in range(2):
            S.dma('sp', lambda e, l=l: e.dma_start(out=vecsL[l][:], in_=WL[l]["vecs"][:]))
        S.dma('sp', lambda e: e.dma_start(out=msel[:], in_=msel_d[:]))
        S.dma('sp', lambda e: e.dma_start(out=cbf[:], in_=cbf_d[:]))
        S.dma('sp', lambda e: e.dma_start(out=identf[:], in_=identf_d[:]))
        S.barrier()
        ident = cbf[:, C_ID:C_ID + 128]
        bd64 = cbf[:, C_BD64:C_BD64 + 128]
        b128 = cbf[:, C_B128:C_B128 + 128]
        b256 = cbf[:, C_B256:C_B256 + 128]
        bdmla = cbf[0:96, C_BDMLA:C_BDMLA + 96]
        b32 = cbf[0:32, C_B32:C_B32 + 32]
        ones64 = cbf[:, C_ONES:C_ONES + 64]
        Jm = cbf[:, C_J:C_J + 128]

        def load_w(st, name, src, nk, ncols, gcol=None, vecs=None):
            w = sbt(st, name, [128, nk, ncols], BF16)
            stg = [sbt(st, name + "_stg%d" % i, [128, ncols], F32) for i in range(2)]
            for kc in range(nk):
                b = kc % 2
                S.dma('sp', lambda e, b=b, kc=kc: e.dma_start(out=stg[b][:, :], in_=src[:, kc, :]), writes=[name + 'stg%d' % b])
                if kc % 2 == 0:
                    if gcol is None:
                        S.op('dve', lambda e, b=b, kc=kc: e.tensor_copy(out=w[:, kc, :], in_=stg[b][:, :]), reads=[name + 'stg%d' % b])
                    else:
                        S.op('dve', lambda e, b=b, kc=kc: e.tensor_scalar(out=w[:, kc, :], in0=stg[b][:, :],
                                                                         scalar1=vecs[:, gcol + kc:gcol + kc + 1], scalar2=None, op0=ALU.mult),
                             reads=[name + 'stg%d' % b])
                else:
                    if gcol is None:
                        S.op('act', lambda e, b=b, kc=kc: e.activation(out=w[:, kc, :], in_=stg[b][:, :], func=AF.Copy), reads=[name + 'stg%d' % b])
                    else:
                        S.op('act', lambda e, b=b, kc=kc: e.activation(out=w[:, kc, :], in_=stg[b][:, :], func=AF.Copy,
                                                                       scale=vecs[:, gcol + kc:gcol + kc + 1]),
                             reads=[name + 'stg%d' % b])
            return w

        def norm_phase(st, xload, ntiles, hT, pfx):
            NB = 4
            xt = [sbt(st, pfx + "xt%d" % i, [128, 1024], F32) for i in range(NB)]
            junk = sbt(st, pfx + "junk", [128, 1024], BF16)
            xn = [sbt(st, pfx + "xn%d" % i, [128, 1024], BF16) for i in range(NB)]
            ss = [sbt(st, pfx + "ss%d" % i, [128, 4], F32) for i in range(NB)]
            pT = [pst(st, pfx + "pT%d" % i, [128, 1024], BF16) for i in range(2)]

            def front(t):
                b = t % NB
                X, N, SS = pfx + 'xt%d' % b, pfx + 'xn%d' % b, pfx + 'ss%d' % b
                xload(t, xt[b], X)
                S.op('act', lambda e: e.activation(out=junk[:], in_=xt[b][:], func=AF.Square, accum_out=ss[b][:, 0:1]),
                     reads=[X], writes=[pfx + 'junk', SS])
                S.op('act', lambda e: e.activation(out=ss[b][:, 1:2], in_=ss[b][:, 0:1], func=AF.Ln, scale=1.0 / 1024, bias=EPS),
                     reads=[SS], writes=[SS])
                S.op('act', lambda e: e.activation(out=ss[b][:, 2:3], in_=ss[b][:, 1:2], func=AF.Exp, scale=-0.5),
                     reads=[SS], writes=[SS])
                S.op('dve', lambda e: e.tensor_scalar(out=xn[b][:], in0=xt[b][:], scalar1=ss[b][:, 2:3], scalar2=None, op0=ALU.mult),
                     reads=[X, SS], writes=[N])

            def back(t):
                b, pb_ = t % NB, t % 2
                N, P = pfx + 'xn%d' % b, pfx + 'pT%d' % pb_
                for c in range(8):
                    S.op('pe', lambda e, c=c: e.transpose(out=pT[pb_][:, c * 128:(c + 1) * 128], in_=xn[b][:, c * 128:(c + 1) * 128], identity=ident),
                         reads=[N], writes=[P], sig=(c == 7))
                if t % 2 == 0:
                    S.op('dve', lambda e: e.tensor_copy(out=hT[:, :, t * 128:(t + 1) * 128], in_=pT[pb_][:].rearrange("p (c n) -> p c n", c=8)), reads=[P])
                else:
                    S.op('act', lambda e: e.activation(out=hT[:, :, t * 128:(t + 1) * 128], in_=pT[pb_][:].rearrange("p (c n) -> p c n", c=8), func=AF.Copy), reads=[P])

            LA = 2
            for i in range(ntiles + LA):
                if i < ntiles:
                    front(i)
                if i >= LA:
                    back(i - LA)

        class RmsFeat:
            def __init__(self, st):
                self.sq = [[sbt(st, "rf_sq%d_%d" % (i, a), [128, 512], BF16) for a in range(2)] for i in range(2)]
                self.lnv = [sbt(st, "rf_ln%d" % i, [128, 512], F32) for i in range(2)]
                self.rstd = [sbt(st, "rf_rs%d" % i, [128, 512], F32) for i in range(2)]
                self.psB = [pst(st, "rf_psB%d" % i, [128, 512], F32) for i in range(2)]
                self.cnt = 0

            def __call__(self, As, nstat, P, bm, gains, ebias, outs, n=512):
                i = self.cnt % 2
                self.cnt += 1
                for a in range(nstat):
                    S.op('act', lambda e, a=a: e.activation(out=self.sq[i][a][0:P, 0:n], in_=As[a][0], func=AF.Square),
                         reads=[As[a][1]], writes=['rf_sq%d_%d' % (i, a)])
                for a in range(nstat):
                    S.op('pe', lambda e, a=a: e.matmul(self.psB[i][0:P, 0:n], lhsT=bm, rhs=self.sq[i][a][0:P, 0:n], start=(a == 0), stop=(a == nstat - 1)),
                         reads=['rf_sq%d_%d' % (i, aa) for aa in range(nstat)], writes=['rf_psB%d' % i], sig=(a == nstat - 1))
                S.op('act', lambda e: e.activation(out=self.lnv[i][0:P, 0:n], in_=self.psB[i][0:P, 0:n], func=AF.Ln, bias=EPS),
                     reads=['rf_psB%d' % i], writes=['rf_ln%d' % i])
                S.op('act', lambda e: e.activation(out=self.rstd[i][0:P, 0:n], in_=self.lnv[i][0:P, 0:n], func=AF.Exp, scale=-0.5, bias=ebias),
                     reads=['rf_ln%d' % i], writes=['rf_rs%d' % i])
                for a in range(len(As)):
                    S.op('dve', lambda e, a=a: e.scalar_tensor_tensor(out=outs[a][0], in0=As[a][0], scalar=gains[a], in1=self.rstd[i][0:P, 0:n],
                                                                      op0=ALU.mult, op1=ALU.mult),
                         reads=[As[a][1], 'rf_rs%d' % i], writes=[outs[a][1]])

        class Pipe:
            def __init__(self):
                self.items = []

            def add(self, A, B=None, dep_prev=False):
                self.items.append((A, B, dep_prev))

            def run(self):
                n = len(self.items)
                done_a = 0
                for i in range(n):
                    while done_a < min(n, i + 2):
                        if done_a > i and self.items[done_a][2]:
                            break
                        self.items[done_a][0]()
                        done_a += 1
                    if self.items[i][1] is not None:
                        self.items[i][1]()
                self.items = []

        def split3(st, pfx, src, npart, n):
            hi = sbt(st, pfx + "hi", [npart, n], BF16)
            mid = sbt(st, pfx + "mid", [npart, n], BF16)
            lo = sbt(st, pfx + "lo", [npart, n], BF16)
            r1 = sbt(st, pfx + "r1", [npart, n], F32)
            S.op('dve', lambda e: e.tensor_copy(out=hi[:], in_=src), reads=[pfx + 'src'], writes=[pfx + 'hi'])
            S.op('dve', lambda e: e.tensor_tensor(out=r1[:], in0=src, in1=hi[:], op=ALU.subtract), reads=[pfx + 'src', pfx + 'hi'], writes=[pfx + 'r1'])
            S.op('dve', lambda e: e.tensor_copy(out=mid[:], in_=r1[:]), reads=[pfx + 'r1'], writes=[pfx + 'mid'])
            S.op('dve', lambda e: e.tensor_tensor(out=r1[:], in0=r1[:], in1=mid[:], op=ALU.subtract), reads=[pfx + 'r1', pfx + 'mid'], writes=[pfx + 'r1'])
            S.op('dve', lambda e: e.tensor_copy(out=lo[:], in_=r1[:]), reads=[pfx + 'r1'], writes=[pfx + 'lo'])
            return hi, mid, lo

        def phase0(OH_d, F_s):
            with contextlib.ExitStack() as st:
                tabX = sbt(st, "tabX", [34, 9], F32)
                OH = sbt(st, "OH", [34, FL], F32)
                Fsb = sbt(st, "Fsb", [9, FL], BF16)
                psF = [pst(st, "psF%d" % i, [128, 512], F32) for i in range(2)]
                S.dma('sp', lambda e: e.dma_start(out=tabX[:], in_=tabX_d[:]), writes=['tabX'])
                S.dma('sp', lambda e: e.dma_start(out=OH[:], in_=OH_d[:]), writes=['OH'])
                nch = (FL + 511) // 512
                for ci in range(nch):
                    c0 = ci * 512
                    cw = min(512, FL - c0)
                    b = ci % 2
                    S.op('pe', lambda e, b=b, c0=c0, cw=cw: e.matmul(psF[b][0:9, 0:cw], lhsT=tabX[0:34, 0:9], rhs=OH[0:34, c0:c0 + cw], start=True, stop=True),
                         reads=['tabX', 'OH'], writes=['psF%d' % b])
                    S.op('dve', lambda e, b=b, c0=c0, cw=cw: e.tensor_copy(out=Fsb[0:9, c0:c0 + cw], in_=psF[b][0:9, 0:cw]),
                         reads=['psF%d' % b], writes=['Fsb'])
                S.dma('sp', lambda e: e.dma_start(out=F_s[:], in_=Fsb[:]), reads=['Fsb'])

        def phaseK(xload, W, vecs):
            stKC = contextlib.ExitStack()
            ffT = sbt(stKC, "ffT", [4, SEQ], F32)
            kmsum = [sbt(stKC, "kmsum%d" % i, [128, 16], F32) for i in range(2)]
            with contextlib.ExitStack() as st:
                hTa = sbt(st, "hTa", [128, 8, SEQ], BF16)
                with contextlib.ExitStack() as st1:
                    norm_phase(st1, xload, SEQ // 128, hTa, "na_")
                    S.barrier()
                wK = load_w(st, "wK", W["wK"], 8, NWK, V_LNG, vecs)
                wukvK_ = load_w(st, "wukvK", W["wukvK"], 1, 256)
                wukvK = wukvK_[:, 0, :]
                wukvV_ = load_w(st, "wukvV", W["wukvV"], 1, 256)
                wukvV = wukvV_[:, 0, :]
                S.barrier()
                rfk = RmsFeat(st)
                psAk = [pst(st, "psAk%d" % i, [128, 512], F32) for i in range(4)]
                psV = [pst(st, "psV%d" % i, [128, 512], F32) for i in range(2)]
                kst = [sbt(st, "kst%d" % i, [128, 512], BF16) for i in range(3)]
                vst = [sbt(st, "vst%d" % i, [128, 768], BF16) for i in range(2)]
                vst2 = [sbt(st, "vst2_%d" % i, [128, 256], BF16) for i in range(2)]
                ckvn = [sbt(st, "ckvn%d" % i, [128, 512], BF16) for i in range(2)]
                xa = sbt(st, "xa", [32, 512], F32)
                xb = sbt(st, "xb", [32, 512], F32)
                ckt = [sbt(st, "ckt%d" % i, [32, 512], F32) for i in range(2)]
                skt = [sbt(st, "skt%d" % i, [32, 512], F32) for i in range(2)]
                rst = [sbt(st, "rst%d" % i, [32, 512], BF16) for i in range(2)]
                acnt = [0]
                kcnt = [0]

                def nextAk():
                    i = acnt[0] % 4
                    acnt[0] += 1
                    return i

                def proj(ai, col0, ncol, T0, n=512):
                    for kc in range(8):
                        S.op('pe', lambda e, kc=kc: e.matmul(psAk[ai][0:ncol, 0:n], lhsT=wK[:, kc, col0:col0 + ncol], rhs=hTa[:, kc, T0:T0 + n],
                                                             start=(kc == 0), stop=(kc == 7)),
                             writes=['psAk%d' % ai], sig=(kc == 7))

                pipe = Pipe()
                for ch in range(8):
                    T0 = ch * 512
                    cb = ch % 2
                    tb = ch % 2
                    for pi in range(6):
                        ai = nextAk()
                        kb = kcnt[0] % 3
                        kcnt[0] += 1

                        def A(ai=ai, pi=pi, T0=T0):
                            proj(ai, K_K + pi * 128, 128, T0)

                        def B(ai=ai, pi=pi, T0=T0, kb=kb, ch=ch):
                            rfk([(psAk[ai][:, :], 'psAk%d' % ai)], 1, 128, bd64, [vecs[:, V_GK + pi:V_GK + pi + 1]], 0.0,
                                [(kst[kb][:, :], 'kst%d' % kb)])
                            hd = 4 * (pi // 2) + 2 * (pi % 2)
                            for hh in range(2):
                                S.dma('sp', lambda e, hh=hh: e.dma_start(out=KT_s[hd + hh, 0:64, T0:T0 + 512], in_=kst[kb][hh * 64:(hh + 1) * 64, :]),
                                      reads=['kst%d' % kb])
                            if pi // 2 == 1:
                                pp = pi % 2
                                S.op('dve', lambda e: e.tensor_reduce(out=kmsum[pp][:, 2 * ch:2 * ch + 2],
                                                                      in_=kst[kb][:, :].rearrange("p (a b) -> p a b", a=2), axis=AX.X, op=ALU.add),
                                     reads=['kst%d' % kb], writes=['kmsum%d' % pp])
                        pipe.add(A, B)
                    for tt in range(4):
                        def A(tt=tt, T0=T0, ch=ch):
                            tok = T0 + tt * 128
                            vb = (ch * 4 + tt) % 2
                            for half, (c0, cw) in enumerate(((0, 512), (512, 256))):
                                for kc in range(8):
                                    S.op('pe', lambda e, kc=kc, half=half, c0=c0, cw=cw: e.matmul(
                                        psV[half][:, 0:cw], lhsT=hTa[:, kc, tok:tok + 128], rhs=wK[:, kc, K_V + c0:K_V + c0 + cw],
                                        start=(kc == 0), stop=(kc == 7)), writes=['psV%d' % half], sig=(kc == 7))
                            S.op('act', lambda e: e.activation(out=vst[vb][:, 0:512], in_=psV[0][:, 0:512], func=AF.Copy), reads=['psV0'], writes=['vst%d' % vb])
                            S.op('dve', lambda e: e.tensor_copy(out=vst[vb][:, 512:768], in_=psV[1][:, 0:256]), reads=['psV1'], writes=['vst%d' % vb])
                            S.dma('sp', lambda e: e.dma_start(out=V_s[tok:tok + 128, 0:768], in_=vst[vb][:, :]), reads=['vst%d' % vb])
                        pipe.add(A)
                    ai = nextAk()

                    def A(ai=ai, T0=T0):
                        proj(ai, K_CKV, 128, T0)

                    def B(ai=ai, cb=cb):
                        rfk([(psAk[ai][:, :], 'psAk%d' % ai)], 1, 128, b128, [vecs[:, V_GKV:V_GKV + 1]], 0.0, [(ckvn[cb][:, :], 'ckvn%d' % cb)])
                    pipe.add(A, B)
                    for pp in range(2):
                        ai = nextAk()
                        kb = kcnt[0] % 3
                        kcnt[0] += 1

                        def A(ai=ai, pp=pp, cb=cb):
                            S.op('pe', lambda e: e.matmul(psAk[ai][:, :], lhsT=wukvK[:, pp * 128:(pp + 1) * 128], rhs=ckvn[cb][:, :], start=True, stop=True),
                                 reads=['ckvn%d' % cb], writes=['psAk%d' % ai])

                        def B(ai=ai, pp=pp, kb=kb, T0=T0):
                            rfk([(psAk[ai][:, :], 'psAk%d' % ai)], 1, 128, bd64, [vecs[:, V_GKN:V_GKN + 1]], 0.0, [(kst[kb][:, :], 'kst%d' % kb)])
                            for hh in range(2):
                                S.dma('sp', lambda e, hh=hh: e.dma_start(out=KT_s[12 + 2 * pp + hh, 0:64, T0:T0 + 512], in_=kst[kb][hh * 64:(hh + 1) * 64, :]),
                                      reads=['kst%d' % kb])
                        pipe.add(A, B, dep_prev=(pp == 0))
                    for tt in range(4):
                        def A(tt=tt, T0=T0, ch=ch, cb=cb):
                            tok = T0 + tt * 128
                            vb = (ch * 4 + tt) % 2
                            S.op('pe', lambda e: e.matmul(psV[1][:, 0:256], lhsT=ckvn[cb][:, tt * 128:(tt + 1) * 128], rhs=wukvV, start=True, stop=True),
                                 reads=['ckvn%d' % cb], writes=['psV1'])
                            S.op('dve', lambda e: e.tensor_copy(out=vst2[vb][:, :], in_=psV[1][:, 0:256]), reads=['psV1'], writes=['vst2_%d' % vb])
                            S.dma('sp', lambda e: e.dma_start(out=V_s[tok:tok + 128, 768:1024], in_=vst2[vb][:, :]), reads=['vst2_%d' % vb])
                        pipe.add(A)
                    aiA = nextAk()
                    aiB = nextAk()

                    def A(aiA=aiA, aiB=aiB, T0=T0, tb=tb):
                        proj(aiA, K_KR, 32, T0)
                        proj(aiB, K_KRS, 32, T0)
                        S.dma('sp', lambda e: e.dma_start(out=ckt[tb][:, :], in_=CK_d[:, T0:T0 + 512]), writes=['ckt%d' % tb])
                        S.dma('sp', lambda e: e.dma_start(out=skt[tb][:, :], in_=SK_d[:, T0:T0 + 512]), writes=['skt%d' % tb])

                    def B(aiA=aiA, aiB=aiB, T0=T0, tb=tb):
                        rfk([(psAk[aiA][0:32, :], 'psAk%d' % aiA), (psAk[aiB][0:32, :], 'psAk%d' % aiB)], 1, 32, b32,
                            [vecs[0:32, V_GR:V_GR + 1], vecs[0:32, V_GR + 1:V_GR + 2]], 0.0, [(xa[:, :], 'xa'), (xb[:, :], 'xb')])
                        S.op('pool', lambda e: e.tensor_tensor(out=xa[:, :], in0=xa[:, :], in1=ckt[tb][:, :], op=ALU.mult), reads=['xa', 'ckt%d' % tb], writes=['xa'])
                        S.op('pool', lambda e: e.tensor_tensor(out=xb[:, :], in0=xb[:, :], in1=skt[tb][:, :], op=ALU.mult), reads=['xb', 'skt%d' % tb], writes=['xb'])
                        S.op('pool', lambda e: e.tensor_tensor(out=rst[tb][:, :], in0=xa[:, :], in1=xb[:, :], op=ALU.add), reads=['xa', 'xb'], writes=['rst%d' % tb])
                        for h in range(4):
                            S.dma('sp', lambda e, h=h: e.dma_start(out=KT_s[12 + h, 64:96, T0:T0 + 512], in_=rst[tb][:, :]), reads=['rst%d' % tb])
                    pipe.add(A, B)
                    ai = nextAk()

                    def A(ai=ai, T0=T0):
                        proj(ai, K_FF, 4, T0)

                    def B(ai=ai, T0=T0):
                        S.op('act', lambda e: e.activation(out=ffT[0:4, T0:T0 + 512], in_=psAk[ai][0:4, :], func=AF.Copy), reads=['psAk%d' % ai], writes=['ffT'])
                    pipe.add(A, B)
                pipe.run()
                S.barrier()
            with stKC as st:
                onesb = sbt(st, "onesb", [4, SEQ], BF16)
                cT4 = sbt(st, "cT4", [4, SEQ], F32)
                S.op('pool', lambda e: e.memset(onesb[:], 1.0), writes=['onesb'])
                S.op('dve', lambda e: e.tensor_scalar(out=negb[0:4, :], in0=vecs[0:4, V_BF:V_BF + 1], scalar1=-1.0, scalar2=None, op0=ALU.mult), writes=['negb'])
                S.op('act', lambda e: e.activation(out=ffT[:, :], in_=ffT[:, :], func=AF.Exp, scale=-1.0, bias=negb[0:4, 0:1]), reads=['ffT', 'negb'], writes=['ffT'])
                S.op('act', lambda e: e.activation(out=ffT[:, :], in_=ffT[:, :], func=AF.Ln, bias=1.0), reads=['ffT'], writes=['ffT'])
                S.op('dve', lambda e: e.tensor_tensor_scan(out=cT4[:, :], data0=onesb[:, :], data1=ffT[:, :], initial=0.0, op0=ALU.mult, op1=ALU.add),
                     reads=['onesb', 'ffT'], writes=['ncsrc'])
                hi, mid, lo = split3(st, "nc", cT4[:, :], 4, SEQ)
                for part, row in ((hi, 67), (mid, 68), (lo, 69)):
                    S.dma('sp', lambda e, part=part, row=row: e.dma_start(out=KT_s[0:4, row, :], in_=part[:, :]), reads=['nchi', 'ncmid', 'nclo'])
                for h in range(4):
                    S.dma('sp', lambda e, h=h: e.dma_start(out=KT_s[h, 64:67, :], in_=ones3_d[:, :]))
                    S.dma('sp', lambda e, h=h: e.dma_start(out=KT_s[4 + h, 64:80, :], in_=oh16_d[:, :]))
                psC = pst(st, "psC", [128, 128], F32)
                for blk in range(32):
                    S.op('pe', lambda e, blk=blk: e.transpose(out=psC[:, blk * 4:(blk + 1) * 4], in_=cT4[0:4, blk * 128:(blk + 1) * 128], identity=identf[0:4, 0:4]),
                         reads=['ncsrc'], writes=['psC'], sig=(blk == 31))
                S.op('dve', lambda e: e.tensor_copy(out=cTm[:, :], in_=psC[:, :]), reads=['psC'], writes=['cTm'])
                for pp in range(2):
                    S.op('dve', lambda e, pp=pp: e.tensor_scalar(out=kmT[pp][:, :], in0=kmsum[pp][:, :], scalar1=1.0 / 256, scalar2=None, op0=ALU.mult),
                         reads=['kmsum%d' % pp], writes=['kmT%d' % pp])
                    S.op('dve', lambda e, pp=pp: e.memset(kmBD[pp][:, :], 0.0), writes=['kmBD%d' % pp])
                    S.op('dve', lambda e, pp=pp: e.tensor_copy(out=kmBD[pp][0:64, 0:16], in_=kmT[pp][0:64, :]), reads=['kmT%d' % pp], writes=['kmBD%d' % pp])
                    S.op('dve', lambda e, pp=pp: e.tensor_copy(out=kmBD[pp][64:128, 16:32], in_=kmT[pp][64:128, :]), reads=['kmT%d' % pp], writes=['kmBD%d' % pp])
                S.barrier()

        def phaseQ(xload, W, C, vecs, GT):
            with contextlib.ExitStack() as st:
                hTo = sbt(st, "hTo", [128, 8, OWN], BF16)
                with contextlib.ExitStack() as st1:
                    norm_phase(st1, xload, OWN // 128, hTo, "no_")
                    S.barrier()
                wQ = load_w(st, "wQ", W["wQ"], 8, NWQ, V_LNG, vecs)
                wuqA = load_w(st, "wuqA", W["wuqA"], 2, 384)
                wuqB = load_w(st, "wuqB", W["wuqB"], 2, 384)
                VB = sbt(st, "VB", [128, 16, 16], F32)
                VM = sbt(st, "VM", [128, 16, 16], F32)
                S.dma('sp', lambda e: e.dma_start(out=VB[:], in_=C["VB"][:]))
                S.dma('sp', lambda e: e.dma_start(out=VM[:], in_=C["VM"][:]))
                S.barrier()
                rfq = RmsFeat(st)
                psAq = [pst(st, "qpsA%d" % i, [128, 512], F32) for i in range(4)]
                psG = pst(st, "psG", [128, 512], F32)
                psM = pst(st, "psM", [128, 1024], BF16)
                qst = [sbt(st, "qst%d" % i, [128, 512], BF16) for i in range(3)]
                cqn = sbt(st, "cqn", [128, 2, 512], BF16)
                qa = sbt(st, "qa", [96, 512], F32)
                qb = sbt(st, "qb", [96, 512], F32)
                ctt = [sbt(st, "ctt%d" % i, [96, 512], F32) for i in range(2)]
                stt = [sbt(st, "stt%d" % i, [96, 512], F32) for i in range(2)]
                qmst = [sbt(st, "qmst%d" % i, [96, 512], BF16) for i in range(2)]
                gvs = sbt(st, "gvs", [128, 2, 4, 16], F32)
                m8 = sbt(st, "m8", [128, 2, 4, 8], F32)
                Mf = sbt(st, "Mf", [128, 2, 4, 16], F32)
                Mb = [sbt(st, "Mb%d" % i, [128, 2, 4, 16], BF16) for i in range(2)]
                mst = [sbt(st, "mst%d" % i, [16, 1024], BF16) for i in range(2)]
                acnt = [0]
                qcnt = [0]
                mcnt = [0]

                def nextAq():
                    i = acnt[0] % 4
                    acnt[0] += 1
                    return i

                def projq(ai, col0, ncol, T0):
                    for kc in range(8):
                        S.op('pe', lambda e, kc=kc: e.matmul(psAq[ai][0:ncol, :], lhsT=wQ[:, kc, col0:col0 + ncol], rhs=hTo[:, kc, T0:T0 + 512],
                                                             start=(kc == 0), stop=(kc == 7)),
                             writes=['qpsA%d' % ai], sig=(kc == 7))

                pipe = Pipe()
                for ch in range(4):
                    T0 = ch * 512
                    tb = ch % 2
                    fins = []
                    for pi in range(6):
                        ai = nextAq()
                        qbuf = qcnt[0] % 3
                        qcnt[0] += 1
                        mbs = None
                        if pi // 2 == 1:
                            mbs = (mcnt[0] % 2, mcnt[0] % 2)
                            mcnt[0] += 1

                        def A(ai=ai, pi=pi, T0=T0):
                            projq(ai, Q_Q + pi * 128, 128, T0)

                        def B(ai=ai, pi=pi, T0=T0, qbuf=qbuf, ch=ch, mbs=mbs):
                            rfq([(psAq[ai][:, :], 'qpsA%d' % ai)], 1, 128, bd64, [vecs[:, V_GQ + pi:V_GQ + pi + 1]], math.log(0.125),
                                [(qst[qbuf][:, :], 'qst%d' % qbuf)])
                            hd = 4 * (pi // 2) + 2 * (pi % 2)
                            for hh in range(2):
                                S.dma('sp', lambda e, hh=hh: e.dma_start(out=QT_s[hd + hh, 0:64, T0:T0 + 512], in_=qst[qbuf][hh * 64:(hh + 1) * 64, :]),
                                      reads=['qst%d' % qbuf])
                            if pi // 2 == 1:
                                pp = pi % 2
                                mbi = mbs[0]
                                for tt in range(4):
                                    S.op('pe', lambda e, tt=tt: e.matmul(
                                        psG[:, tt * 32:(tt + 1) * 32], lhsT=qst[qbuf][:, tt * 128:(tt + 1) * 128],
                                        rhs=kmBD[pp][:, 0:32], start=True, stop=True),
                                        reads=['qst%d' % qbuf], writes=['psG'], sig=(tt == 3))
                                for hh in range(2):
                                    S.op('dve', lambda e, hh=hh: e.tensor_tensor(
                                        out=gvs[:, hh, :, :], in0=psG[:, 0:128].rearrange("p (t h n) -> p t h n", t=4, h=2)[:, :, hh, :],
                                        in1=VB[:, ch * 4:(ch + 1) * 4, :], op=ALU.add), reads=['psG'], writes=['gvs'])
                                    for tt in range(4):
                                        S.op('dve', lambda e, hh=hh, tt=tt: e.max(out=m8[:, hh, tt, :], in_=gvs[:, hh, tt, :]), reads=['gvs'], writes=['m8'])
                                    S.op('dve', lambda e, hh=hh: e.tensor_tensor(out=Mf[:, hh, :, :], in0=gvs[:, hh, :, :],
                                                                               in1=m8[:, hh, :, 2:3].to_broadcast([128, 4, 16]), op=ALU.is_ge),
                                         reads=['gvs', 'm8'], writes=['Mf'])
                                    S.op('dve', lambda e, hh=hh: e.tensor_scalar(out=Mf[:, hh, :, :], in0=Mf[:, hh, :, :], scalar1=1.0, scalar2=-NEG,
                                                                               op0=ALU.subtract, op1=ALU.mult), reads=['Mf'], writes=['Mf'])
                                    S.op('dve', lambda e, hh=hh: e.tensor_tensor(out=Mb[mbi][:, hh, :, :], in0=Mf[:, hh, :, :], in1=VM[:, ch * 4:(ch + 1) * 4, :], op=ALU.mult),
                                         reads=['Mf'], writes=['Mb%d' % mbi])

                        fin = None
                        if pi // 2 == 1:
                            def fin(pi=pi, T0=T0, mbs=mbs):
                                pp = pi % 2
                                mbi = mbs[0]
                                for hh in range(2):
                                    for tt in range(4):
                                        g = hh * 4 + tt
                                        S.op('pe', lambda e, hh=hh, tt=tt, g=g: e.transpose(out=psM[0:16, g * 128:(g + 1) * 128], in_=Mb[mbi][:, hh, tt, :], identity=ident),
                                             reads=['Mb%d' % mbi], writes=['psM'], sig=(g == 7))
                                S.op('dve', lambda e: e.tensor_copy(out=mst[mbi][:, :], in_=psM[0:16, :]), reads=['psM'], writes=['mst%d' % mbi])
                                for hh in range(2):
                                    S.dma('sp', lambda e, hh=hh: e.dma_start(out=QT_s[4 + 2 * pp + hh, 64:80, T0:T0 + 512], in_=mst[mbi][:, hh * 512:(hh + 1) * 512]),
                                          reads=['mst%d' % mbi])
                        fins.append(fin)
                        pipe.add(A, B)
                        if pi >= 2 and fins[-3] is not None:
                            pipe.add(lambda: None, fins[-3])
                    for f_ in fins[-2:]:
                        if f_ is not None:
                            pipe.add(lambda: None, f_)
                    a0 = nextAq()
                    a1 = nextAq()

                    def A(a0=a0, a1=a1, T0=T0, tb=tb):
                        projq(a0, Q_CQ, 128, T0)
                        projq(a1, Q_CQ + 128, 128, T0)
                        S.dma('sp', lambda e: e.dma_start(out=ctt[tb][:, :], in_=C["CTq"][:, T0:T0 + 512]), writes=['ctt%d' % tb])
                        S.dma('sp', lambda e: e.dma_start(out=stt[tb][:, :], in_=C["STq"][:, T0:T0 + 512]), writes=['stt%d' % tb])

                    def B(a0=a0, a1=a1):
                        rfq([(psAq[a0][:, :], 'qpsA%d' % a0), (psAq[a1][:, :], 'qpsA%d' % a1)], 2, 128, b256,
                            [vecs[:, V_GQN:V_GQN + 1], vecs[:, V_GQN + 1:V_GQN + 2]], 0.0,
                            [(cqn[:, 0, :], 'cqn'), (cqn[:, 1, :], 'cqn')])
                    pipe.add(A, B)
                    for h in range(4):
                        aA = nextAq()
                        aB = nextAq()
                        qmb = (ch * 4 + h) % 2

                        def A(aA=aA, aB=aB, h=h):
                            for c in range(2):
                                S.op('pe', lambda e, c=c: e.matmul(psAq[aA][0:96, :], lhsT=wuqA[:, c, 96 * h:96 * h + 96], rhs=cqn[:, c, :], start=(c == 0), stop=(c == 1)),
                                     reads=['cqn'], writes=['qpsA%d' % aA], sig=(c == 1))
                            for c in range(2):
                                S.op('pe', lambda e, c=c: e.matmul(psAq[aB][0:96, :], lhsT=wuqB[:, c, 96 * h:96 * h + 96], rhs=cqn[:, c, :], start=(c == 0), stop=(c == 1)),
                                     reads=['cqn'], writes=['qpsA%d' % aB], sig=(c == 1))

                        def B(aA=aA, aB=aB, h=h, qmb=qmb, tb=tb, T0=T0):
                            rfq([(psAq[aA][0:96, :], 'qpsA%d' % aA), (psAq[aB][0:96, :], 'qpsA%d' % aB)], 1, 96, bdmla,
                                [vecs[0:96, V_GA:V_GA + 1], vecs[0:96, V_GB:V_GB + 1]], 0.0, [(qa[:, :], 'qa'), (qb[:, :], 'qb')])
                            S.op('pool', lambda e: e.tensor_tensor(out=qa[:, :], in0=qa[:, :], in1=ctt[tb][:, :], op=ALU.mult), reads=['qa', 'ctt%d' % tb], writes=['qa'])
                            S.op('pool', lambda e: e.tensor_tensor(out=qb[:, :], in0=qb[:, :], in1=stt[tb][:, :], op=ALU.mult), reads=['qb', 'stt%d' % tb], writes=['qb'])
                            S.op('pool', lambda e: e.tensor_tensor(out=qmst[qmb][:, :], in0=qa[:, :], in1=qb[:, :], op=ALU.add), reads=['qa', 'qb'], writes=['qmst%d' % qmb])
                            S.dma('sp', lambda e: e.dma_start(out=QT_s[12 + h, 0:96, T0:T0 + 512], in_=qmst[qmb][:, :]), reads=['qmst%d' % qmb])
                        pipe.add(A, B, dep_prev=(h == 0))
                    for g in range(8):
                        ai = nextAq()

                        def A(ai=ai, g=g, T0=T0):
                            projq(ai, Q_GATE + g * 128, 128, T0)

                        def B(ai=ai, g=g, T0=T0):
                            S.op('act', lambda e: e.activation(out=GT[:, g, T0:T0 + 512], in_=psAq[ai][:, :], func=AF.Silu), reads=['qpsA%d' % ai])
                        pipe.add(A, B)
                pipe.run()
                S.barrier()
            with contextlib.ExitStack() as st:
                Sel = sbt(st, "Sel", [128, 4, 256], F32)
                S.dma("sp", lambda e: e.dma_start(out=Sel[:], in_=C["Sel"][:]), writes=["Sel"])
                psG2 = pst(st, "psG2", [128, 512], F32)
                cown = sbt(st, "cown", [4, OWN], F32)
                for s in range(8):
                    for jj in range(4):
                        blk = 4 * s + jj
                        S.op('pe', lambda e, blk=blk, jj=jj: e.matmul(psG2[0:4, 0:256], lhsT=cTm[:, blk * 4:(blk + 1) * 4], rhs=Sel[:, jj, :], start=(jj == 0), stop=(jj == 3)),
                             reads=['Sel'], writes=['psG2'], sig=(jj == 3))
                    S.op('dve', lambda e, s=s: e.tensor_scalar(out=cown[0:4, s * 256:(s + 1) * 256], in0=psG2[0:4, 0:256], scalar1=-1.0, scalar2=None, op0=ALU.mult),
                         reads=['psG2'], writes=['cosrc'])
                hi, mid, lo = split3(st, "co", cown[:, :], 4, OWN)
                for part, row in ((hi, 64), (mid, 65), (lo, 66)):
                    S.dma('sp', lambda e, part=part, row=row: e.dma_start(out=QT_s[0:4, row, :], in_=part[:, :]), reads=['cohi', 'comid', 'colo'])
                for h in range(4):
                    S.dma('sp', lambda e, h=h: e.dma_start(out=QT_s[h, 67:70, :], in_=ones3_d[:, 0:OWN]))
                S.barrier()

        def phaseA(GT, mixT, F_s):
            with contextlib.ExitStack() as st:
                kt = [[sbt(st, "kt%d_%d" % (b, hh), [128, SEQ], BF16) for hh in range(2)] for b in range(2)]
                qt_ = [[sbt(st, "qt%d_%d" % (b, hh), [128, OWN], BF16) for hh in range(2)] for b in range(2)]
                vt = [sbt(st, "vt%d" % b, [128, 32, 192], BF16) for b in range(2)]
                gstage = [sbt(st, "gstage%d" % hh, [128, WG], BF16) for hh in range(2)]
                gtab = [sbt(st, "gtab%d" % b, [128, 2, WG], BF16) for b in range(2)]
                gc = sbt(st, "gc", [128, WC], BF16)
                NPB = 6
                pb = [sbt(st, "pb%d" % i, [128, 512], BF16) for i in range(NPB)]
                rd = sbt(st, "rd", [128, 512], F32)
                tmpo = sbt(st, "tmpo", [128, 512], F32)
                psS = [pst(st, "psS%d" % i, [128, 512], F32) for i in range(NPB)]
                psO = [[pst(st, "psO%d_%d" % (i, hh), [128, 512], F32) for hh in range(2)] for i in range(1)]
                Vv = V_s.rearrange("(j p) c -> p j c", p=128)
                S.dma('sp', lambda e: e.dma_start(out=gc[:, :], in_=bass.AP(tensor=F_s.tensor, offset=8 * FL, ap=[[1, 128], [1, WC]])), writes=['gc'])
                for b in range(2):
                    S.op('pool', lambda e, b=b: e.memset(vt[b][:, :, 64:128], 1.0), writes=['vt%d' % b])
                KDs = (70, 80, 64, 96)
                KDM = (70, 80, 128, 96)
                for b_ in range(2):
                    for hh_ in range(2):
                        S.op('dve', lambda e, b_=b_, hh_=hh_: e.memset(kt[b_][hh_][64:128, :], 0.0), writes=['kt%d_%d' % (b_, hh_)])
                        S.op('dve', lambda e, b_=b_, hh_=hh_: e.memset(qt_[b_][hh_][64:128, :], 0.0), writes=['qt%d_%d' % (b_, hh_)])
                scnt = [0]
                ocnt = [0]
                for p8 in range(8):
                    m, pp = p8 // 2, p8 % 2
                    KD = KDs[m]
                    KDq = KDM[m]
                    b = p8 % 2
                    if m == 2:
                        for hh in range(2):
                            S.op('dve', lambda e, b=b, hh=hh: e.memset(qt_[b][hh][64:128, :], 0.0), writes=['qt%d_%d' % (b, hh)])
                    for hh in range(2):
                        hd = 4 * m + 2 * pp + hh
                        for half in range(2):
                            S.dma('sp', lambda e, b=b, hh=hh, hd=hd, half=half, KD=KD: e.dma_start(
                                out=kt[b][hh][0:KD, half * 2048:(half + 1) * 2048], in_=KT_s[hd, 0:KD, half * 2048:(half + 1) * 2048]),
                                writes=['kt%d_%d' % (b, hh)])
                        S.dma('sp', lambda e, b=b, hh=hh, hd=hd, KD=KD: e.dma_start(out=qt_[b][hh][0:KD, :], in_=QT_s[hd, 0:KD, :]), writes=['qt%d_%d' % (b, hh)])
                    for q4 in range(4):
                        for hh in range(2):
                            c0 = m * 256 + pp * 128 + hh * 64
                            S.dma('sp', lambda e, b=b, q4=q4, hh=hh, c0=c0: e.dma_start(
                                out=vt[b][:, q4 * 8:(q4 + 1) * 8, hh * 128:hh * 128 + 64], in_=Vv[:, q4 * 8:(q4 + 1) * 8, c0:c0 + 64]),
                                writes=['vt%d' % b])
                    if m in (1, 2):
                        for hh in range(2):
                            row = (m - 1) * 4 + 2 * pp + hh
                            S.dma('sp', lambda e, hh=hh, row=row: e.dma_start(
                                out=gstage[hh][:, :], in_=bass.AP(tensor=F_s.tensor, offset=row * FL, ap=[[1, 128], [1, WG]])), writes=['gstage%d' % hh])
                            for c0 in range(0, WG, 512):
                                cw = min(512, WG - c0)
                                bi = scnt[0] % NPB
                                scnt[0] += 1
                                S.op('pe', lambda e, bi=bi, hh=hh, c0=c0, cw=cw: e.matmul(psS[bi][:, 0:cw], lhsT=Jm, rhs=gstage[hh][:, c0:c0 + cw], start=True, stop=True),
                                     reads=['gstage%d' % hh], writes=['psS%d' % bi])
                                S.op('act', lambda e, bi=bi, hh=hh, c0=c0, cw=cw, b=b: e.activation(out=gtab[b][:, hh, c0:c0 + cw], in_=psS[bi][:, 0:cw], func=AF.Exp),
                                     reads=['psS%d' % bi], writes=['gtab%d' % b])
                    RK = ['kt%d_%d' % (b, hh) for hh in range(2)] + ['qt%d_%d' % (b, hh) for hh in range(2)]
                    for u in range(4):
                        s0, s1 = 2 * u, 2 * u + 1
                        nk = (4 * s0 + 4, 4 * s1 + 4)
                        jlo = (max(0, 4 * s0 - 16), max(0, 4 * s1 - 16)) if m == 2 else (0, 0)
                        js = list(range(jlo[0], nk[1]))
                        ob = 0
                        sbufs = {}

                        def active(j):
                            a0 = (jlo[0] <= j < nk[0])
                            a1 = (jlo[1] <= j < nk[1])
                            c0 = 0 if a0 else 256
                            c1 = 512 if a1 else 256
                            return a0, a1, c0, c1

                        def emit_S(j):
                            a0, a1, c0, c1 = active(j)
                            bis = []
                            for hh in range(2):
                                bi = scnt[0] % NPB
                                scnt[0] += 1
                                bis.append(bi)
                                jadd = None
                                if m in (0, 3):
                                    for si, sl in enumerate((s0, s1)):
                                        o = 512 * sl - 128 * j + 384
                                        if (a0, a1)[si] and o < 512:
                                            jadd = (si, o)
                                S.op('pe', lambda e, bi=bi, hh=hh, j=j, b=b, KD=KDq, c0=c0, c1=c1, jadd=jadd, s0=s0: e.matmul(
                                    psS[bi][:, c0:c1], lhsT=kt[b][hh][0:KD, j * 128:(j + 1) * 128],
                                    rhs=qt_[b][hh][0:KD, s0 * 256 + c0:s0 * 256 + c1], start=True, stop=(jadd is None)),
                                    reads=RK, writes=['psS%d' % bi], sig=(jadd is None))
                                if jadd is not None:
                                    si, o = jadd
                                    S.op('pe', lambda e, bi=bi, si=si, o=o: e.matmul(psS[bi][:, si * 256:(si + 1) * 256], lhsT=Jm, rhs=gc[:, o:o + 256],
                                                                                   start=False, stop=True),
                                         reads=RK + ['gc'], writes=['psS%d' % bi])
                            sbufs[j] = bis

                        def emit_PV(j):
                            a0, a1, c0, c1 = active(j)
                            bis = sbufs[j]
                            first, last = (j == js[0]), (j == js[-1])
                            for hh in range(2):
                                bi = bis[hh]
                                S.op('act', lambda e, bi=bi, c0=c0, c1=c1: e.activation(out=pb[bi][:, c0:c1], in_=psS[bi][:, c0:c1], func=AF.Exp),
                                     reads=['psS%d' % bi], writes=['pb%d' % bi])
                                if m in (1, 2):
                                    for si, sl in enumerate((s0, s1)):
                                        if not (a0, a1)[si]:
                                            continue
                                        o = 512 * sl - 128 * j + 384
                                        oe = min(o, 2048) if m == 1 else o
                                        S.op('dve', lambda e, bi=bi, oe=oe, b=b, hh=hh, si=si: e.tensor_tensor(
                                            out=pb[bi][:, si * 256:(si + 1) * 256], in0=pb[bi][:, si * 256:(si + 1) * 256],
                                            in1=gtab[b][:, hh, oe:oe + 256], op=ALU.mult),
                                            reads=['pb%d' % bi, 'gtab%d' % b], writes=['pb%d' % bi])
                                S.op('pe', lambda e, bi=bi, hh=hh, j=j, ob=ob, b=b, first=first, last=last, c0=c0, c1=c1: e.matmul(
                                    psO[ob][hh][:, c0:c1], lhsT=vt[b][:, j, hh * 64:hh * 64 + 128],
                                    rhs=pb[bi][:, c0:c1], start=first, stop=last),
                                    reads=['pb%d' % bi, 'vt%d' % b], writes=['psO%d_%d' % (ob, hh)], sig=True)

                        LA = 2
                        for jj in js[:LA]:
                            emit_S(jj)
                        for idx, j in enumerate(js):
                            if idx + LA < len(js):
                                emit_S(js[idx + LA])
                            emit_PV(j)
                        S.op('act', lambda e, ob=ob: e.activation(out=rd[0:64, :], in_=psO[ob][0][64:128, :], func=AF.Ln), reads=['psO%d_0' % ob], writes=['rd'])
                        S.op('act', lambda e, ob=ob: e.activation(out=rd[64:128, :], in_=psO[ob][1][0:64, :], func=AF.Ln), reads=['psO%d_1' % ob], writes=['rd'])
                        S.op('act', lambda e: e.activation(out=rd[:, :], in_=rd[:, :], func=AF.Exp, scale=-1.0), reads=['rd'], writes=['rd'])
                        S.op('dve', lambda e, ob=ob: e.tensor_tensor(out=tmpo[0:64, :], in0=psO[ob][0][0:64, :], in1=rd[0:64, :], op=ALU.mult),
                             reads=['psO%d_0' % ob, 'rd'], writes=['tmpo'])
                        S.op('dve', lambda e, ob=ob: e.tensor_tensor(out=tmpo[64:128, :], in0=psO[ob][1][64:128, :], in1=rd[64:128, :], op=ALU.mult),
                             reads=['psO%d_1' % ob, 'psO%d_0' % ob, 'rd'], writes=['tmpo'])
                        S.op('pool', lambda e, p8=p8, s0=s0: e.tensor_tensor(out=mixT[:, p8, s0 * 256:s0 * 256 + 512], in0=tmpo[:, :], in1=GT[:, p8, s0 * 256:s0 * 256 + 512], op=ALU.mult),
                             reads=['tmpo'])
                S.barrier()

        def phaseO(xload, pown_d, W, vecs, mixT, out_ap):
            with contextlib.ExitStack() as st:
                wout = load_w(st, "wout", W["wout"], 8, 1024)
                wpp = load_w(st, "wpp", W["wpp"], 2, 1024)
                wpg = load_w(st, "wpg", W["wpg"], 8, 1024, V_PLEG, vecs)
                S.barrier()
                NX = 3
                xt = [sbt(st, "oxt%d" % i, [128, 1024], F32) for i in range(NX)]
                pt = [sbt(st, "opt%d" % i, [128, 256], F32) for i in range(2)]
                ptb = [sbt(st, "optb%d" % i, [128, 256], BF16) for i in range(2)]
                pTs = [sbt(st, "opTs%d" % i, [128, 2, 128], BF16) for i in range(2)]
                x1 = [sbt(st, "ox1%d" % i, [128, 1024], F32) for i in range(NX)]
                junk = sbt(st, "ojunk", [128, 1024], BF16)
                xn = [sbt(st, "oxn%d" % i, [128, 1024], BF16) for i in range(2)]
                ss = [sbt(st, "oss%d" % i, [128, 4], F32) for i in range(2)]
                gTs = [sbt(st, "ogT%d" % i, [128, 8, 128], BF16) for i in range(2)]
                sg = [sbt(st, "osg%d" % i, [128, 1024], F32) for i in range(2)]
                psY = [pst(st, "psY%d" % i, [128, 512], F32) for i in range(2)]
                psT = pst(st, "opsT", [128, 1024], BF16)
                psP = pst(st, "opsP", [128, 512], BF16)
                psGt = [pst(st, "psGt%d" % i, [128, 512], F32) for i in range(2)]
                psPP = [pst(st, "psPP%d" % i, [128, 512], F32) for i in range(2)]
                NTO = OWN // 128

                def stage1(t):
                    b, b3, tok = t % 2, t % NX, t * 128
                    xload(t, xt[b3], 'oxt%d' % b3)
                    S.dma('sp', lambda e: e.dma_start(out=pt[b][:, :], in_=pown_d[tok:tok + 128, :]), writes=['opt%d' % b])
                    for half in range(2):
                        for kc in range(8):
                            S.op('pe', lambda e, half=half, kc=kc: e.matmul(psY[half][:, :], lhsT=mixT[:, kc, tok:tok + 128], rhs=wout[:, kc, half * 512:(half + 1) * 512],
                                                                          start=(kc == 0), stop=(kc == 7)), writes=['psY%d' % half], sig=(kc == 7))
                        S.op('dve', lambda e, half=half: e.tensor_tensor(out=x1[b3][:, half * 512:(half + 1) * 512], in0=psY[half][:, :], in1=xt[b3][:, half * 512:(half + 1) * 512], op=ALU.add),
                             reads=['psY%d' % half, 'oxt%d' % b3], writes=['ox1%d' % b3])
                    S.op('act', lambda e: e.activation(out=junk[:, :], in_=x1[b3][:, :], func=AF.Square, accum_out=ss[b][:, 0:1]), reads=['ox1%d' % b3], writes=['ojunk', 'oss%d' % b])
                    S.op('act', lambda e: e.activation(out=ss[b][:, 1:2], in_=ss[b][:, 0:1], func=AF.Ln, scale=1.0 / 1024, bias=EPS), reads=['oss%d' % b], writes=['oss%d' % b])
                    S.op('act', lambda e: e.activation(out=ss[b][:, 2:3], in_=ss[b][:, 1:2], func=AF.Exp, scale=-0.5), reads=['oss%d' % b], writes=['oss%d' % b])
                    S.op('act', lambda e: e.activation(out=xn[b][:, :], in_=x1[b3][:, :], func=AF.Copy, scale=ss[b][:, 2:3]),
                         reads=['ox1%d' % b3, 'oss%d' % b], writes=['oxn%d' % b])
                    S.op('dve', lambda e: e.tensor_copy(out=ptb[b][:, :], in_=pt[b][:, :]), reads=['opt%d' % b], writes=['optb%d' % b])

                def stage2(t):
                    b = t % 2
                    for c in range(8):
                        S.op('pe', lambda e, c=c: e.transpose(out=psT[:, c * 128:(c + 1) * 128], in_=xn[b][:, c * 128:(c + 1) * 128], identity=ident),
                             reads=['oxn%d' % b], writes=['opsT'], sig=(c == 7))
                    S.op('dve', lambda e: e.tensor_copy(out=gTs[b][:, :, :], in_=psT[:, :].rearrange("p (c n) -> p c n", c=8)), reads=['opsT'], writes=['ogT%d' % b])
                    for c in range(2):
                        S.op('pe', lambda e, c=c: e.transpose(out=psP[:, c * 128:(c + 1) * 128], in_=ptb[b][:, c * 128:(c + 1) * 128], identity=ident),
                             reads=['optb%d' % b], writes=['opsP'], sig=(c == 1))
                    S.op('act', lambda e: e.activation(out=pTs[b][:, :, :], in_=psP[:, 0:256].rearrange("p (c n) -> p c n", c=2), func=AF.Copy), reads=['opsP'], writes=['opTs%d' % b])

                def stage3(t):
                    b, b3, tok = t % 2, t % NX, t * 128
                    for half in range(2):
                        for kc in range(8):
                            S.op('pe', lambda e, half=half, kc=kc: e.matmul(psGt[half][:, :], lhsT=gTs[b][:, kc, :], rhs=wpg[:, kc, half * 512:(half + 1) * 512],
                                                                          start=(kc == 0), stop=(kc == 7)), reads=['ogT%d' % b], writes=['psGt%d' % half], sig=(kc == 7))
                        S.op('act', lambda e, half=half: e.activation(out=sg[b][:, half * 512:(half + 1) * 512], in_=psGt[half][:, :], func=AF.Sigmoid),
                             reads=['psGt%d' % half], writes=['osg%d' % b])
                        for c in range(2):
                            S.op('pe', lambda e, half=half, c=c: e.matmul(psPP[half][:, :], lhsT=pTs[b][:, c, :], rhs=wpp[:, c, half * 512:(half + 1) * 512],
                                                                        start=(c == 0), stop=(c == 1)), reads=['opTs%d' % b], writes=['psPP%d' % half], sig=(c == 1))
                        S.op('dve', lambda e, half=half: e.tensor_tensor(out=sg[b][:, half * 512:(half + 1) * 512], in0=psPP[half][:, :], in1=sg[b][:, half * 512:(half + 1) * 512], op=ALU.mult),
                             reads=['psPP%d' % half, 'osg%d' % b], writes=['osg%d' % b])
                    S.op('dve', lambda e: e.tensor_tensor(out=sg[b][:, :], in0=sg[b][:, :], in1=x1[b3][:, :], op=ALU.add), reads=['osg%d' % b, 'ox1%d' % b3], writes=['osg%d' % b])
                    S.dma('sp', lambda e: e.dma_start(out=out_ap[tok:tok + 128, :], in_=sg[b][:, :]), reads=['osg%d' % b])

                for i in range(NTO + 2):
                    if i < NTO:
                        stage1(i)
                    if 1 <= i <= NTO:
                        stage2(i - 1)
                    if i >= 2:
                        stage3(i - 2)
                S.barrier()

        def dram_loader(src):
            def f(t, dst, res):
                S.dma('sp', lambda e: e.dma_start(out=dst[:, :], in_=src[t * 128:(t + 1) * 128, :]), writes=[res])
            return f

        def x1_nat_loader(t, dst, res):
            a = (t % 4) // 2
            r0 = (t // 4) * 256 + (t % 2) * 128
            S.dma('sp', lambda e: e.dma_start(out=dst[:, :], in_=x1_s[a][r0:r0 + 128, :]), writes=[res])

        for c in ("a0", "a1", "m"):
            phase0(CS[c]["OH"], F_sL[c])
        phaseK(dram_loader(x_all), WL[0], vecsL[0])
        for a in range(2):
            C = CS["a%d" % a]
            with contextlib.ExitStack() as stg:
                GT = sbt(stg, "GT", [128, 8, OWN], BF16)
                phaseQ(dram_loader(xown_d[a]), WL[0], C, vecsL[0], GT)
                mixT = sbt(stg, "mixT", [128, 8, OWN], BF16)
                phaseA(GT, mixT, F_sL["a%d" % a])
                phaseO(dram_loader(xown_d[a]), C["p"], WL[0], vecsL[0], mixT, x1_s[a])
        phaseK(x1_nat_loader, WL[1], vecsL[1])
        C = CS["m"]
        with contextlib.ExitStack() as stg:
            selt = [sbt(stg, "selt%d" % i, [128, 1024], F32) for i in range(2)]

            def x1_own_loader(t, dst, res):
                for a in range(2):
                    S.dma('sp', lambda e, a=a: e.dma_start(out=selt[a][:, :], in_=x1_s[a][t * 128:(t + 1) * 128, :]), writes=['selt%d' % a])
                S.op('act', lambda e: e.activation(out=selt[0][:, :], in_=selt[0][:, :], func=AF.Copy, scale=msel[:, 0:1]),
                     reads=['selt0'], writes=['selt0'])
                S.op('dve', lambda e: e.scalar_tensor_tensor(out=dst[:, :], in0=selt[1][:, :], scalar=msel[:, 1:2], in1=selt[0][:, :], op0=ALU.mult, op1=ALU.add),
                     reads=['selt0', 'selt1'], writes=[res])

            GT = sbt(stg, "GT", [128, 8, OWN], BF16)
            phaseQ(x1_own_loader, WL[1], C, vecsL[1], GT)
            mixT = sbt(stg, "mixT", [128, 8, OWN], BF16)
            phaseA(GT, mixT, F_sL["m"])
            phaseO(x1_own_loader, C["p"], WL[1], vecsL[1], mixT, out_d)
        S.emit()
    return nc


def _t5_bucket(d):
    d = np.maximum(d, 0)
    df = np.maximum(d, 1).astype(np.float32)
    large = 16 + (np.log(df / np.float32(16)) / np.float32(math.log(2048 / 16)) * np.float32(16)).astype(np.int32)
    large = np.minimum(large, 31)
    return np.where(d < 16, d, large)


def _core_consts(par):
    bf = ml_dtypes.bfloat16
    c = {}
    i = np.arange(FL)
    d = i + 256 * par - 511
    OH = np.zeros((34, FL), np.float32)
    bk = _t5_bucket(d)
    valid = d >= 0
    OH[bk[valid], i[valid]] = 1.0
    OH[32, ~valid] = NEG
    mult = ((d <= 128).astype(np.float32) + ((d % 4 == 0) & (d <= 512)).astype(np.float32)
            + ((d % 16 == 0) & (d <= 2048)).astype(np.float32))
    ok = valid & (mult > 0)
    OH[33, :] = NEG
    OH[33, ok] = np.log(mult[ok]).astype(np.float32)
    c["OH"] = OH
    Sel = np.zeros((128, 4, 256), np.float32)
    for jj in range(4):
        for k in range(128):
            q = 128 * jj + k - 256 * par
            if 0 <= q < 256:
                Sel[k, jj, q] = 1.0
    c["Sel"] = Sel
    VB = np.zeros((128, 16, 16), np.float32)
    VM = np.zeros((128, 16, 16), np.float32)
    for qt in range(16):
        own = 2 * (qt // 2) + par
        VB[:, qt, own:] = -1e9
        VM[:, qt, :own] = 1.0
    c["VB"], c["VM"] = VB, VM
    half = 16
    inv = (1.0 / (np.float32(10000.0) ** (np.arange(half, dtype=np.float32) * np.float32(2.0) / np.float32(32)))).astype(np.float32)
    pos = np.arange(SEQ).astype(np.float32)
    ang = pos[:, None] * inv[None, :]
    cos, sin = np.cos(ang).astype(np.float32).T, np.sin(ang).astype(np.float32).T
    c["CK"] = np.ascontiguousarray(np.concatenate([cos, cos], 0))
    c["SK"] = np.ascontiguousarray(np.concatenate([-sin, sin], 0))
    own_idx = np.concatenate([512 * s + 256 * par + np.arange(256) for s in range(8)])
    sc = np.float32(96 ** -0.5)
    CT = np.full((96, OWN), sc, np.float32)
    ST = np.zeros((96, OWN), np.float32)
    CT[64:96] = np.concatenate([cos, cos], 0)[:, own_idx] * sc
    ST[64:96] = np.concatenate([-sin, sin], 0)[:, own_idx] * sc
    c["CTq"], c["STq"] = CT, ST
    c["own_idx"] = own_idx
    return c


def _shared_consts():
    bf = ml_dtypes.bfloat16
    cb = np.zeros((128, NCB), np.float32)
    for g in range(2):
        cb[g * 64:(g + 1) * 64, C_BD64 + g * 64:C_BD64 + (g + 1) * 64] = 1.0 / 64
    cb[:, C_B128:C_B128 + 128] = 1.0 / 128
    cb[:, C_B256:C_B256 + 128] = 1.0 / 256
    cb[0:64, C_BDMLA:C_BDMLA + 64] = 1.0 / 64
    cb[64:96, C_BDMLA + 64:C_BDMLA + 96] = 1.0 / 32
    cb[0:32, C_B32:C_B32 + 32] = 1.0 / 32
    cb[:, C_ONES:C_ONES + 64] = 1.0
    cb[:, C_J:C_J + 128] = np.eye(128, dtype=np.float32)[::-1]
    cb[:, C_ID:C_ID + 128] = np.eye(128, dtype=np.float32)
    oh16 = np.zeros((16, SEQ), np.float32)
    for n in range(16):
        oh16[n, n * 256:(n + 1) * 256] = 1.0
    return {"cbf": cb.astype(bf), "oh16": oh16.astype(bf), "ones3": np.ones((3, SEQ), bf),
            "identf": np.eye(128, dtype=np.float32)}


def _kc(w):
    return np.ascontiguousarray(w.reshape(8, 128, -1).transpose(1, 0, 2))


def _layer_weights(l, ln_g, w_in, b_forget, qk_gain, mla_q_norm, mla_kv_norm, mla_nope_gain, mla_rope_gain,
                   w_uq, w_ukv, w_out, rel_bias, ple_norm_g, w_ple_gate, w_ple_proj):
    W = w_in[l]
    fq, fk, fv, ff = W[:, 0:256], W[:, 256:512], W[:, 512:768], W[:, 768:772]
    mq, mk, mv = W[:, 772:1028], W[:, 1028:1284], W[:, 1284:1540]
    dq, dk, dv = W[:, 1540:1796], W[:, 1796:2052], W[:, 2052:2308]
    cq, ckv, kr, gate = W[:, 2308:2564], W[:, 2564:2692], W[:, 2692:2724], W[:, 2724:3748]
    kr_sw = np.concatenate([kr[:, 16:32], kr[:, 0:16]], 1)
    wK = np.concatenate([fk, mk, dk, fv, mv, dv, ckv, kr, kr_sw, ff, np.zeros((1024, 4), np.float32)], 1)
    wQ = np.concatenate([fq, mq, dq, cq, gate], 1)
    uq = w_uq[l]
    uqB = uq.copy()
    for h in range(4):
        uqB[:, 96 * h + 64:96 * h + 80] = uq[:, 96 * h + 80:96 * h + 96]
        uqB[:, 96 * h + 80:96 * h + 96] = uq[:, 96 * h + 64:96 * h + 80]
    ukv = w_ukv[l]
    ukvK = np.concatenate([ukv[:, 128 * h:128 * h + 64] for h in range(4)], 1)
    ukvV = np.concatenate([ukv[:, 128 * h + 64:128 * h + 128] for h in range(4)], 1)
    vec = np.zeros((128, NV), np.float32)
    vec[:, V_LNG:V_LNG + 8] = ln_g[l].reshape(8, 128).T
    for pi in range(6):
        m = pi // 2
        vec[:, V_GK + pi] = np.tile(qk_gain[l, 2 * m + 1], 2)
        vec[:, V_GQ + pi] = np.tile(qk_gain[l, 2 * m], 2)
    vec[:, V_GQN:V_GQN + 2] = mla_q_norm[l].reshape(2, 128).T
    vec[:, V_GKV] = mla_kv_norm[l]
    vec[:, V_GKN] = np.tile(mla_nope_gain[l, 1], 2)
    rg0, rg1 = mla_rope_gain[l, 0], mla_rope_gain[l, 1]
    vec[0:96, V_GA] = np.concatenate([mla_nope_gain[l, 0], rg0])
    vec[0:96, V_GB] = np.concatenate([mla_nope_gain[l, 0], rg0[16:32], rg0[0:16]])
    vec[0:32, V_GR] = rg1
    vec[0:32, V_GR + 1] = np.concatenate([rg1[16:32], rg1[0:16]])
    vec[0:4, V_BF] = b_forget[l]
    vec[:, V_PLEG:V_PLEG + 8] = ple_norm_g[l].reshape(8, 128).T
    tabX = np.zeros((34, 9), np.float32)
    tabX[0:32, 0:8] = rel_bias
    tabX[32, 0:4] = 1.0
    tabX[33, 4:8] = 1.0
    tabX[32, 8] = 1.0
    return {"wK": _kc(wK), "wQ": _kc(wQ),
            "wuqA": np.ascontiguousarray(uq.reshape(2, 128, 384).transpose(1, 0, 2)),
            "wuqB": np.ascontiguousarray(uqB.reshape(2, 128, 384).transpose(1, 0, 2)),
            "wukvK": np.ascontiguousarray(ukvK.reshape(128, 1, 256)), "wukvV": np.ascontiguousarray(ukvV.reshape(128, 1, 256)),
            "wout": _kc(w_out[l]), "wpg": _kc(w_ple_gate[l]),
            "wpp": np.ascontiguousarray(w_ple_proj[l].reshape(2, 128, 1024).transpose(1, 0, 2)),
            "vecs": vec, "tabX": tabX}


_NC = None


def kernel(x, p, ln_g, w_in, b_forget, qk_gain, mla_q_norm, mla_kv_norm, mla_nope_gain, mla_rope_gain,
           w_uq, w_ukv, w_out, rel_bias, ple_norm_g, w_ple_gate, w_ple_proj):
    global _NC
    args = [np.asarray(a, dtype=np.float32) for a in (ln_g, w_in, b_forget, qk_gain, mla_q_norm, mla_kv_norm, mla_nope_gain,
                                                     mla_rope_gain, w_uq, w_ukv, w_out, rel_bias, ple_norm_g, w_ple_gate, w_ple_proj)]
    x = np.asarray(x, dtype=np.float32)
    p = np.asarray(p, dtype=np.float32)
    if _NC is None:
        _NC = build_fused()
    shared = _shared_consts()
    cc = [_core_consts(par) for par in range(2)]
    base = dict(shared)
    for l in range(2):
        lw = _layer_weights(l, *args)
        base["tabX"] = lw.pop("tabX")
        for k, v in lw.items():
            base[k + "_l%d" % l] = v
    base["CK"], base["SK"] = cc[0]["CK"], cc[0]["SK"]
    for a in range(2):
        for k in ("OH", "Sel", "VB", "VM", "CTq", "STq"):
            base[k + "_a%d" % a] = cc[a][k]
    in_maps = []
    for core in range(8):
        b, par = core // 2, core % 2
        mp = dict(base)
        mp["x_all"] = np.ascontiguousarray(x[b])
        for a in range(2):
            mp["x_own_a%d" % a] = np.ascontiguousarray(x[b][cc[a]["own_idx"]])
            mp["p_a%d" % a] = np.ascontiguousarray(p[0, b][cc[a]["own_idx"]])
        mp["p_m"] = np.ascontiguousarray(p[1, b][cc[par]["own_idx"]])
        for k in ("OH", "Sel", "VB", "VM", "CTq", "STq"):
            mp[k + "_m"] = cc[par][k]
        ms = np.zeros((128, 2), np.float32)
        ms[:, par] = 1.0
        mp["msel"] = ms
        in_maps.append(mp)
    res = run_bass_kernel_spmd(_NC, in_maps, core_ids=list(range(8)))
    out = np.empty_like(x)
    for core in range(8):
        b, par = core // 2, core % 2
        out[b][cc[par]["own_idx"]] = np.asarray(res.results[core]["out"], dtype=np.float32)
    return out
```

```python
import contextlib
import math
import numpy as np
import ml_dtypes
import concourse.bass as bass
import concourse.mybir as mybir
from concourse.bass_utils import run_bass_kernel_spmd

F32 = mybir.dt.float32
BF16 = mybir.dt.bfloat16
AF = mybir.ActivationFunctionType
ALU = mybir.AluOpType
AX = mybir.AxisListType

COMPUTE = ('pe', 'act', 'dve', 'pool')
NDMA_SEM = 8
SEQ = 4096
OWN = 2048
NEG = -30000.0
WG = 2688
FL = WG + 128
WC = 640
EPS = 1e-6


class Sched:
    def __init__(self, nc, stack):
        self.nc = nc
        self.ops = {e: [] for e in ('pe', 'act', 'dve', 'pool', 'sp')}
        self.psem = {e: stack.enter_context(nc.semaphore("pg_" + e)) for e in COMPUTE}
        self.pcnt = {e: 0 for e in COMPUTE}
        self.dsem = {q: [stack.enter_context(nc.semaphore("dq_%s%d" % (q, i))) for i in range(NDMA_SEM)]
                     for q in ('sp', 'pool')}
        self.dval = {q: [0] * NDMA_SEM for q in ('sp', 'pool')}
        self.didx = {q: 0 for q in ('sp', 'pool')}
        self.waited = {e: {} for e in self.ops}
        self.res = {}

    def _need(self, eng, tok, waits):
        if tok is None:
            return
        key, sem, val, prod = tok
        if prod == eng and eng == 'pe':
            return
        if self.waited[eng].get(key, 0) >= val:
            return
        self.waited[eng][key] = val
        waits.append((sem, val))

    def _deps(self, eng, reads, writes):
        waits = []
        for r in reads:
            st = self.res.get(r)
            if st is not None:
                self._need(eng, st[0], waits)
        for w in writes:
            st = self.res.get(w)
            if st is not None:
                self._need(eng, st[0], waits)
                for t in st[1]:
                    self._need(eng, t, waits)
        return waits

    def _record(self, tok, reads, writes):
        for r in reads:
            st = self.res.setdefault(r, [None, []])
            st[1].append(tok)
        for w in writes:
            self.res[w] = [tok, []]

    def op(self, eng, fn, reads=(), writes=(), sig=True):
        waits = self._deps(eng, reads, writes)
        tok = None
        if sig:
            self.pcnt[eng] += 1
            tok = ('p' + eng, self.psem[eng], self.pcnt[eng], eng)
            self._record(tok, reads, writes)
        self.ops[eng].append((waits, fn, (self.psem[eng], 1) if sig else None))
        return tok

    def dma(self, q, fn, reads=(), writes=()):
        waits = self._deps(q, reads, writes)
        i = self.didx[q]
        self.didx[q] = (i + 1) % NDMA_SEM
        sem = self.dsem[q][i]
        key = 'd%s%d' % (q, i)
        prev = self.dval[q][i]
        if prev > 0 and self.waited[q].get(key, 0) < prev:
            self.waited[q][key] = prev
            waits.append((sem, prev))
        self.dval[q][i] = prev + 16
        tok = (key, sem, prev + 16, 'dma')
        self._record(tok, reads, writes)
        self.ops[q].append((waits, fn, (sem, 16)))
        return tok

    def barrier(self):
        toks = []
        for e in COMPUTE:
            if self.pcnt[e] > 0:
                toks.append(('p' + e, self.psem[e], self.pcnt[e], e))
        for q in ('sp', 'pool'):
            for i in range(NDMA_SEM):
                if self.dval[q][i] > 0:
                    toks.append(('d%s%d' % (q, i), self.dsem[q][i], self.dval[q][i], 'dma'))
        for e in self.ops:
            waits = []
            for t in toks:
                key, sem, val, prod = t
                if self.waited[e].get(key, 0) >= val:
                    continue
                self.waited[e][key] = val
                waits.append((sem, val))
            if waits:
                self.ops[e].append((waits, None, None))
        self.res = {}

    def emit(self):
        nc = self.nc
        with nc.Block() as block:
            def mk(name):
                def body(eng):
                    for waits, fn, sig in self.ops[name]:
                        for sem, val in waits:
                            eng.wait_ge(sem, val)
                        if fn is None:
                            continue
                        ins = fn(eng)
                        if sig is not None:
                            ins.then_inc(sig[0], sig[1])
                return body
            block.tensor(mk('pe'))
            block.scalar(mk('act'))
            block.vector(mk('dve'))
            block.gpsimd(mk('pool'))
            block.sync(mk('sp'))


V_LNG, V_GK, V_GQ, V_GQN, V_GKV, V_GKN, V_GA, V_GB, V_GR, V_BF, V_PLEG, NV = 0, 8, 14, 20, 22, 23, 24, 25, 26, 28, 29, 40
C_BD64, C_B128, C_B256, C_BDMLA, C_B32, C_ONES, C_J, C_ID, NCB = 0, 128, 256, 384, 480, 512, 576, 704, 832
K_K, K_V, K_CKV, K_KR, K_KRS, K_FF, NWK = 0, 768, 1536, 1664, 1696, 1728, 1736
Q_Q, Q_CQ, Q_GATE, NWQ = 0, 768, 1024, 2048


def build_fused():
    nc = bass.Bass("TRN2", target_bir_lowering=False)

    def din(name, shape, dt=F32):
        return nc.dram_tensor(name, shape, dt, kind="ExternalInput").ap()

    x_all = din("x_all", [SEQ, 1024])
    identf_d = din("identf", [128, 128])
    CK_d = din("CK", [32, SEQ])
    SK_d = din("SK", [32, SEQ])
    oh16_d = din("oh16", [16, SEQ], BF16)
    ones3_d = din("ones3", [3, SEQ], BF16)
    cbf_d = din("cbf", [128, NCB], BF16)
    tabX_d = din("tabX", [34, 9])
    msel_d = din("msel", [128, 2])
    WL = []
    for l in range(2):
        sfx = "_l%d" % l
        WL.append({"wK": din("wK" + sfx, [128, 8, NWK]), "wQ": din("wQ" + sfx, [128, 8, NWQ]),
                   "wuqA": din("wuqA" + sfx, [128, 2, 384]), "wuqB": din("wuqB" + sfx, [128, 2, 384]),
                   "wukvK": din("wukvK" + sfx, [128, 1, 256]), "wukvV": din("wukvV" + sfx, [128, 1, 256]),
                   "wout": din("wout" + sfx, [128, 8, 1024]), "wpg": din("wpg" + sfx, [128, 8, 1024]),
                   "wpp": din("wpp" + sfx, [128, 2, 1024]), "vecs": din("vecs" + sfx, [128, NV])})
    CS = {}
    for c in ("a0", "a1", "m"):
        CS[c] = {"OH": din("OH_" + c, [34, FL]), "Sel": din("Sel_" + c, [128, 4, 256]), "VB": din("VB_" + c, [128, 16, 16]),
                 "VM": din("VM_" + c, [128, 16, 16]), "CTq": din("CTq_" + c, [96, OWN]), "STq": din("STq_" + c, [96, OWN]),
                 "p": din("p_" + c, [OWN, 256])}
    xown_d = [din("x_own_a%d" % a, [OWN, 1024]) for a in range(2)]
    out_d = nc.dram_tensor("out", [OWN, 1024], F32, kind="ExternalOutput").ap()

    KT_s = nc.dram_tensor("KT_s", [16, 128, SEQ], BF16, kind="Internal").ap()
    QT_s = nc.dram_tensor("QT_s", [16, 128, OWN], BF16, kind="Internal").ap()
    V_s = nc.dram_tensor("V_s", [SEQ, 1024], BF16, kind="Internal").ap()
    F_sL = {c: nc.dram_tensor("F_s_" + c, [9, FL], BF16, kind="Internal").ap() for c in ("a0", "a1", "m")}
    x1_s = [nc.dram_tensor("x1_s%d" % a, [OWN, 1024], F32, kind="Internal").ap() for a in range(2)]

    with contextlib.ExitStack() as top:
        S = Sched(nc, top)

        uid = [0]

        def sbt(st, n, shp, dt):
            uid[0] += 1
            return st.enter_context(nc.sbuf_tensor('s%d_%s' % (uid[0], n), shp, dt))

        def pst(st, n, shp, dt):
            uid[0] += 1
            return st.enter_context(nc.psum_tensor('p%d_%s' % (uid[0], n), shp, dt))

        vecsL = [sbt(top, "vecs%d" % l, [128, NV], F32) for l in range(2)]
        msel = sbt(top, "msel", [128, 2], F32)
        cbf = sbt(top, "cbf", [128, NCB], BF16)
        identf = sbt(top, "identf", [128, 128], F32)
        kmT = [sbt(top, "kmT%d" % i, [128, 16], BF16) for i in range(2)]
        kmBD = [sbt(top, "kmBD%d" % i, [128, 32], BF16) for i in range(2)]
        cTm = sbt(top, "cTm", [128, 128], F32)
        negb = sbt(top, "negb", [128, 1], F32)

        for l in range(2):
            S.dma('sp', lambda e, l=l: e.dma_start(out=vecsL[l][:], in_=WL[l]["vecs"][:]))
        S.dma('sp', lambda e: e.dma_start(out=msel[:], in_=msel_d[:]))
        S.dma('sp', lambda e: e.dma_start(out=cbf[:], in_=cbf_d[:]))
        S.dma('sp', lambda e: e.dma_start(out=identf[:], in_=identf_d[:]))
        S.barrier()
        ident = cbf[:, C_ID:C_ID + 128]
        bd64 = cbf[:, C_BD64:C_BD64 + 128]
        b128 = cbf[:, C_B128:C_B128 + 128]
        b256 = cbf[:, C_B256:C_B256 + 128]
        bdmla = cbf[0:96, C_BDMLA:C_BDMLA + 96]
        b32 = cbf[0:32, C_B32:C_B32 + 32]
        ones64 = cbf[:, C_ONES:C_ONES + 64]
        Jm = cbf[:, C_J:C_J + 128]

        def load_w(st, name, src, nk, ncols, gcol=None, vecs=None):
            w = sbt(st, name, [128, nk, ncols], BF16)
            stg = [sbt(st, name + "_stg%d" % i, [128, ncols], F32) for i in range(2)]
            for kc in range(nk):
                b = kc % 2
                S.dma('sp', lambda e, b=b, kc=kc: e.dma_start(out=stg[b][:, :], in_=src[:, kc, :]), writes=[name + 'stg%d' % b])
                if kc % 2 == 0:
                    if gcol is None:
                        S.op('dve', lambda e, b=b, kc=kc: e.tensor_copy(out=w[:, kc, :], in_=stg[b][:, :]), reads=[name + 'stg%d' % b])
                    else:
                        S.op('dve', lambda e, b=b, kc=kc: e.tensor_scalar(out=w[:, kc, :], in0=stg[b][:, :],
                                                                         scalar1=vecs[:, gcol + kc:gcol + kc + 1], scalar2=None, op0=ALU.mult),
                             reads=[name + 'stg%d' % b])
                else:
                    if gcol is None:
                        S.op('act', lambda e, b=b, kc=kc: e.activation(out=w[:, kc, :], in_=stg[b][:, :], func=AF.Copy), reads=[name + 'stg%d' % b])
                    else:
                        S.op('act', lambda e, b=b, kc=kc: e.activation(out=w[:, kc, :], in_=stg[b][:, :], func=AF.Copy,
                                                                       scale=vecs[:, gcol + kc:gcol + kc + 1]),
                             reads=[name + 'stg%d' % b])
            return w

        def norm_phase(st, xload, ntiles, hT, pfx):
            NB = 4
            xt = [sbt(st, pfx + "xt%d" % i, [128, 1024], F32) for i in range(NB)]
            junk = sbt(st, pfx + "junk", [128, 1024], BF16)
            xn = [sbt(st, pfx + "xn%d" % i, [128, 1024], BF16) for i in range(NB)]
            ss = [sbt(st, pfx + "ss%d" % i, [128, 4], F32) for i in range(NB)]
            pT = [pst(st, pfx + "pT%d" % i, [128, 1024], BF16) for i in range(2)]

            def front(t):
                b = t % NB
                X, N, SS = pfx + 'xt%d' % b, pfx + 'xn%d' % b, pfx + 'ss%d' % b
                xload(t, xt[b], X)
                S.op('act', lambda e: e.activation(out=junk[:], in_=xt[b][:], func=AF.Square, accum_out=ss[b][:, 0:1]),
                     reads=[X], writes=[pfx + 'junk', SS])
                S.op('act', lambda e: e.activation(out=ss[b][:, 1:2], in_=ss[b][:, 0:1], func=AF.Ln, scale=1.0 / 1024, bias=EPS),
                     reads=[SS], writes=[SS])
                S.op('act', lambda e: e.activation(out=ss[b][:, 2:3], in_=ss[b][:, 1:2], func=AF.Exp, scale=-0.5),
                     reads=[SS], writes=[SS])
                S.op('dve', lambda e: e.tensor_scalar(out=xn[b][:], in0=xt[b][:], scalar1=ss[b][:, 2:3], scalar2=None, op0=ALU.mult),
                     reads=[X, SS], writes=[N])

            def back(t):
                b, pb_ = t % NB, t % 2
                N, P = pfx + 'xn%d' % b, pfx + 'pT%d' % pb_
                for c in range(8):
                    S.op('pe', lambda e, c=c: e.transpose(out=pT[pb_][:, c * 128:(c + 1) * 128], in_=xn[b][:, c * 128:(c + 1) * 128], identity=ident),
                         reads=[N], writes=[P], sig=(c == 7))
                if t % 2 == 0:
                    S.op('dve', lambda e: e.tensor_copy(out=hT[:, :, t * 128:(t + 1) * 128], in_=pT[pb_][:].rearrange("p (c n) -> p c n", c=8)), reads=[P])
                else:
                    S.op('act', lambda e: e.activation(out=hT[:, :, t * 128:(t + 1) * 128], in_=pT[pb_][:].rearrange("p (c n) -> p c n", c=8), func=AF.Copy), reads=[P])

            LA = 2
            for i in range(ntiles + LA):
                if i < ntiles:
                    front(i)
                if i >= LA:
                    back(i - LA)

        class RmsFeat:
            def __init__(self, st):
                self.sq = [[sbt(st, "rf_sq%d_%d" % (i, a), [128, 512], BF16) for a in range(2)] for i in range(2)]
                self.lnv = [sbt(st, "rf_ln%d" % i, [128, 512], F32) for i in range(2)]
                self.rstd = [sbt(st, "rf_rs%d" % i, [128, 512], F32) for i in range(2)]
                self.psB = [pst(st, "rf_psB%d" % i, [128, 512], F32) for i in range(2)]
                self.cnt = 0

            def __call__(self, As, nstat, P, bm, gains, ebias, outs, n=512):
                i = self.cnt % 2
                self.cnt += 1
                for a in range(nstat):
                    S.op('act', lambda e, a=a: e.activation(out=self.sq[i][a][0:P, 0:n], in_=As[a][0], func=AF.Square),
                         reads=[As[a][1]], writes=['rf_sq%d_%d' % (i, a)])
                for a in range(nstat):
                    S.op('pe', lambda e, a=a: e.matmul(self.psB[i][0:P, 0:n], lhsT=bm, rhs=self.sq[i][a][0:P, 0:n], start=(a == 0), stop=(a == nstat - 1)),
                         reads=['rf_sq%d_%d' % (i, aa) for aa in range(nstat)], writes=['rf_psB%d' % i], sig=(a == nstat - 1))
                S.op('act', lambda e: e.activation(out=self.lnv[i][0:P, 0:n], in_=self.psB[i][0:P, 0:n], func=AF.Ln, bias=EPS),
                     reads=['rf_psB%d' % i], writes=['rf_ln%d' % i])
                S.op('act', lambda e: e.activation(out=self.rstd[i][0:P, 0:n], in_=self.lnv[i][0:P, 0:n], func=AF.Exp, scale=-0.5, bias=ebias),
                     reads=['rf_ln%d' % i], writes=['rf_rs%d' % i])
                for a in range(len(As)):
                    S.op('dve', lambda e, a=a: e.scalar_tensor_tensor(out=outs[a][0], in0=As[a][0], scalar=gains[a], in1=self.rstd[i][0:P, 0:n],
                                                                      op0=ALU.mult, op1=ALU.mult),
                         reads=[As[a][1], 'rf_rs%d' % i], writes=[outs[a][1]])

        class Pipe:
            def __init__(self):
                self.items = []

            def add(self, A, B=None, dep_prev=False):
                self.items.append((A, B, dep_prev))

            def run(self):
                n = len(self.items)
                done_a = 0
                for i in range(n):
                    while done_a < min(n, i + 2):
                        if done_a > i and self.items[done_a][2]:
                            break
                        self.items[done_a][0]()
                        done_a += 1
                    if self.items[i][1] is not None:
                        self.items[i][1]()
                self.items = []

        def split3(st, pfx, src, npart, n):
            hi = sbt(st, pfx + "hi", [npart, n], BF16)
            mid = sbt(st, pfx + "mid", [npart, n], BF16)
            lo = sbt(st, pfx + "lo", [npart, n], BF16)
            r1 = sbt(st, pfx + "r1", [npart, n], F32)
            S.op('dve', lambda e: e.tensor_copy(out=hi[:], in_=src), reads=[pfx + 'src'], writes=[pfx + 'hi'])
            S.op('dve', lambda e: e.tensor_tensor(out=r1[:], in0=src, in1=hi[:], op=ALU.subtract), reads=[pfx + 'src', pfx + 'hi'], writes=[pfx + 'r1'])
            S.op('dve', lambda e: e.tensor_copy(out=mid[:], in_=r1[:]), reads=[pfx + 'r1'], writes=[pfx + 'mid'])
            S.op('dve', lambda e: e.tensor_tensor(out=r1[:], in0=r1[:], in1=mid[:], op=ALU.subtract), reads=[pfx + 'r1', pfx + 'mid'], writes=[pfx + 'r1'])
            S.op('dve', lambda e: e.tensor_copy(out=lo[:], in_=r1[:]), reads=[pfx + 'r1'], writes=[pfx + 'lo'])
            return hi, mid, lo

        def phase0(OH_d, F_s):
            with contextlib.ExitStack() as st:
                tabX = sbt(st, "tabX", [34, 9], F32)
                OH = sbt(st, "OH", [34, FL], F32)
                Fsb = sbt(st, "Fsb", [9, FL], BF16)
                psF = [pst(st, "psF%d" % i, [128, 512], F32) for i in range(2)]
                S.dma('sp', lambda e: e.dma_start(out=tabX[:], in_=tabX_d[:]), writes=['tabX'])
                S.dma('sp', lambda e: e.dma_start(out=OH[:], in_=OH_d[:]), writes=['OH'])
                nch = (FL + 511) // 512
                for ci in range(nch):
                    c0 = ci * 512
                    cw = min(512, FL - c0)
                    b = ci % 2
                    S.op('pe', lambda e, b=b, c0=c0, cw=cw: e.matmul(psF[b][0:9, 0:cw], lhsT=tabX[0:34, 0:9], rhs=OH[0:34, c0:c0 + cw], start=True, stop=True),
                         reads=['tabX', 'OH'], writes=['psF%d' % b])
                    S.op('dve', lambda e, b=b, c0=c0, cw=cw: e.tensor_copy(out=Fsb[0:9, c0:c0 + cw], in_=psF[b][0:9, 0:cw]),
                         reads=['psF%d' % b], writes=['Fsb'])
                S.dma('sp', lambda e: e.dma_start(out=F_s[:], in_=Fsb[:]), reads=['Fsb'])

        def phaseK(xload, W, vecs):
            stKC = contextlib.ExitStack()
            ffT = sbt(stKC, "ffT", [4, SEQ], F32)
            kmsum = [sbt(stKC, "kmsum%d" % i, [128, 16], F32) for i in range(2)]
            with contextlib.ExitStack() as st:
                hTa = sbt(st, "hTa", [128, 8, SEQ], BF16)
                with contextlib.ExitStack() as st1:
                    norm_phase(st1, xload, SEQ // 128, hTa, "na_")
                    S.barrier()
                wK = load_w(st, "wK", W["wK"], 8, NWK, V_LNG, vecs)
                wukvK_ = load_w(st, "wukvK", W["wukvK"], 1, 256)
                wukvK = wukvK_[:, 0, :]
                wukvV_ = load_w(st, "wukvV", W["wukvV"], 1, 256)
                wukvV = wukvV_[:, 0, :]
                S.barrier()
                rfk = RmsFeat(st)
                psAk = [pst(st, "psAk%d" % i, [128, 512], F32) for i in range(4)]
                psV = [pst(st, "psV%d" % i, [128, 512], F32) for i in range(2)]
                kst = [sbt(st, "kst%d" % i, [128, 512], BF16) for i in range(3)]
                vst = [sbt(st, "vst%d" % i, [128, 768], BF16) for i in range(2)]
                vst2 = [sbt(st, "vst2_%d" % i, [128, 256], BF16) for i in range(2)]
                ckvn = [sbt(st, "ckvn%d" % i, [128, 512], BF16) for i in range(2)]
                xa = sbt(st, "xa", [32, 512], F32)
                xb = sbt(st, "xb", [32, 512], F32)
                ckt = [sbt(st, "ckt%d" % i, [32, 512], F32) for i in range(2)]
                skt = [sbt(st, "skt%d" % i, [32, 512], F32) for i in range(2)]
                rst = [sbt(st, "rst%d" % i, [32, 512], BF16) for i in range(2)]
                acnt = [0]
                kcnt = [0]

                def nextAk():
                    i = acnt[0] % 4
                    acnt[0] += 1
                    return i

                def proj(ai, col0, ncol, T0, n=512):
                    for kc in range(8):
                        S.op('pe', lambda e, kc=kc: e.matmul(psAk[ai][0:ncol, 0:n], lhsT=wK[:, kc, col0:col0 + ncol], rhs=hTa[:, kc, T0:T0 + n],
                                                             start=(kc == 0), stop=(kc == 7)),
                             writes=['psAk%d' % ai], sig=(kc == 7))

                pipe = Pipe()
                for ch in range(8):
                    T0 = ch * 512
                    cb = ch % 2
                    tb = ch % 2
                    for pi in range(6):
                        ai = nextAk()
                        kb = kcnt[0] % 3
                        kcnt[0] += 1

                        def A(ai=ai, pi=pi, T0=T0):
                            proj(ai, K_K + pi * 128, 128, T0)

                        def B(ai=ai, pi=pi, T0=T0, kb=kb, ch=ch):
                            rfk([(psAk[ai][:, :], 'psAk%d' % ai)], 1, 128, bd64, [vecs[:, V_GK + pi:V_GK + pi + 1]], 0.0,
                                [(kst[kb][:, :], 'kst%d' % kb)])
                            hd = 4 * (pi // 2) + 2 * (pi % 2)
                            for hh in range(2):
                                S.dma('pool', lambda e, hh=hh: e.dma_start(out=KT_s[hd + hh, 0:64, T0:T0 + 512], in_=kst[kb][hh * 64:(hh + 1) * 64, :]),
                                      reads=['kst%d' % kb])
                            if pi // 2 == 1:
                                pp = pi % 2
                                S.op('dve', lambda e: e.tensor_reduce(out=kmsum[pp][:, 2 * ch:2 * ch + 2],
                                                                      in_=kst[kb][:, :].rearrange("p (a b) -> p a b", a=2), axis=AX.X, op=ALU.add),
                                     reads=['kst%d' % kb], writes=['kmsum%d' % pp])
                        pipe.add(A, B)
                    for tt in range(4):
                        def A(tt=tt, T0=T0, ch=ch):
                            tok = T0 + tt * 128
                            vb = (ch * 4 + tt) % 2
                            for half, (c0, cw) in enumerate(((0, 512), (512, 256))):
                                for kc in range(8):
                                    S.op('pe', lambda e, kc=kc, half=half, c0=c0, cw=cw: e.matmul(
                                        psV[half][:, 0:cw], lhsT=hTa[:, kc, tok:tok + 128], rhs=wK[:, kc, K_V + c0:K_V + c0 + cw],
                                        start=(kc == 0), stop=(kc == 7)), writes=['psV%d' % half], sig=(kc == 7))
                            S.op('act', lambda e: e.activation(out=vst[vb][:, 0:512], in_=psV[0][:, 0:512], func=AF.Copy), reads=['psV0'], writes=['vst%d' % vb])
                            S.op('dve', lambda e: e.tensor_copy(out=vst[vb][:, 512:768], in_=psV[1][:, 0:256]), reads=['psV1'], writes=['vst%d' % vb])
                            S.dma('pool', lambda e: e.dma_start(out=V_s[tok:tok + 128, 0:768], in_=vst[vb][:, :]), reads=['vst%d' % vb])
                        pipe.add(A)
                    ai = nextAk()

                    def A(ai=ai, T0=T0):
                        proj(ai, K_CKV, 128, T0)

                    def B(ai=ai, cb=cb):
                        rfk([(psAk[ai][:, :], 'psAk%d' % ai)], 1, 128, b128, [vecs[:, V_GKV:V_GKV + 1]], 0.0, [(ckvn[cb][:, :], 'ckvn%d' % cb)])
                    pipe.add(A, B)
                    for pp in range(2):
                        ai = nextAk()
                        kb = kcnt[0] % 3
                        kcnt[0] += 1

                        def A(ai=ai, pp=pp, cb=cb):
                            S.op('pe', lambda e: e.matmul(psAk[ai][:, :], lhsT=wukvK[:, pp * 128:(pp + 1) * 128], rhs=ckvn[cb][:, :], start=True, stop=True),
                                 reads=['ckvn%d' % cb], writes=['psAk%d' % ai])

                        def B(ai=ai, pp=pp, kb=kb, T0=T0):
                            rfk([(psAk[ai][:, :], 'psAk%d' % ai)], 1, 128, bd64, [vecs[:, V_GKN:V_GKN + 1]], 0.0, [(kst[kb][:, :], 'kst%d' % kb)])
                            for hh in range(2):
                                S.dma('pool', lambda e, hh=hh: e.dma_start(out=KT_s[12 + 2 * pp + hh, 0:64, T0:T0 + 512], in_=kst[kb][hh * 64:(hh + 1) * 64, :]),
                                      reads=['kst%d' % kb])
                        pipe.add(A, B, dep_prev=(pp == 0))
                    for tt in range(4):
                        def A(tt=tt, T0=T0, ch=ch, cb=cb):
                            tok = T0 + tt * 128
                            vb = (ch * 4 + tt) % 2
                            S.op('pe', lambda e: e.matmul(psV[1][:, 0:256], lhsT=ckvn[cb][:, tt * 128:(tt + 1) * 128], rhs=wukvV, start=True, stop=True),
                                 reads=['ckvn%d' % cb], writes=['psV1'])
                            S.op('dve', lambda e: e.tensor_copy(out=vst2[vb][:, :], in_=psV[1][:, 0:256]), reads=['psV1'], writes=['vst2_%d' % vb])
                            S.dma('pool', lambda e: e.dma_start(out=V_s[tok:tok + 128, 768:1024], in_=vst2[vb][:, :]), reads=['vst2_%d' % vb])
                        pipe.add(A)
                    aiA = nextAk()
                    aiB = nextAk()

                    def A(aiA=aiA, aiB=aiB, T0=T0, tb=tb):
                        proj(aiA, K_KR, 32, T0)
                        proj(aiB, K_KRS, 32, T0)
                        S.dma('sp', lambda e: e.dma_start(out=ckt[tb][:, :], in_=CK_d[:, T0:T0 + 512]), writes=['ckt%d' % tb])
                        S.dma('sp', lambda e: e.dma_start(out=skt[tb][:, :], in_=SK_d[:, T0:T0 + 512]), writes=['skt%d' % tb])

                    def B(aiA=aiA, aiB=aiB, T0=T0, tb=tb):
                        rfk([(psAk[aiA][0:32, :], 'psAk%d' % aiA), (psAk[aiB][0:32, :], 'psAk%d' % aiB)], 1, 32, b32,
                            [vecs[0:32, V_GR:V_GR + 1], vecs[0:32, V_GR + 1:V_GR + 2]], 0.0, [(xa[:, :], 'xa'), (xb[:, :], 'xb')])
                        S.op('pool', lambda e: e.tensor_tensor(out=xa[:, :], in0=xa[:, :], in1=ckt[tb][:, :], op=ALU.mult), reads=['xa', 'ckt%d' % tb], writes=['xa'])
                        S.op('pool', lambda e: e.tensor_tensor(out=xb[:, :], in0=xb[:, :], in1=skt[tb][:, :], op=ALU.mult), reads=['xb', 'skt%d' % tb], writes=['xb'])
                        S.op('pool', lambda e: e.tensor_tensor(out=rst[tb][:, :], in0=xa[:, :], in1=xb[:, :], op=ALU.add), reads=['xa', 'xb'], writes=['rst%d' % tb])
                        for h in range(4):
                            S.dma('pool', lambda e, h=h: e.dma_start(out=KT_s[12 + h, 64:96, T0:T0 + 512], in_=rst[tb][:, :]), reads=['rst%d' % tb])
                    pipe.add(A, B)
                    ai = nextAk()

                    def A(ai=ai, T0=T0):
                        proj(ai, K_FF, 4, T0)

                    def B(ai=ai, T0=T0):
                        S.op('act', lambda e: e.activation(out=ffT[0:4, T0:T0 + 512], in_=psAk[ai][0:4, :], func=AF.Copy), reads=['psAk%d' % ai], writes=['ffT'])
                    pipe.add(A, B)
                pipe.run()
                S.barrier()
            with stKC as st:
                onesb = sbt(st, "onesb", [4, SEQ], BF16)
                cT4 = sbt(st, "cT4", [4, SEQ], F32)
                S.op('pool', lambda e: e.memset(onesb[:], 1.0), writes=['onesb'])
                S.op('dve', lambda e: e.tensor_scalar(out=negb[0:4, :], in0=vecs[0:4, V_BF:V_BF + 1], scalar1=-1.0, scalar2=None, op0=ALU.mult), writes=['negb'])
                S.op('act', lambda e: e.activation(out=ffT[:, :], in_=ffT[:, :], func=AF.Exp, scale=-1.0, bias=negb[0:4, 0:1]), reads=['ffT', 'negb'], writes=['ffT'])
                S.op('act', lambda e: e.activation(out=ffT[:, :], in_=ffT[:, :], func=AF.Ln, bias=1.0), reads=['ffT'], writes=['ffT'])
                S.op('dve', lambda e: e.tensor_tensor_scan(out=cT4[:, :], data0=onesb[:, :], data1=ffT[:, :], initial=0.0, op0=ALU.mult, op1=ALU.add),
                     reads=['onesb', 'ffT'], writes=['ncsrc'])
                hi, mid, lo = split3(st, "nc", cT4[:, :], 4, SEQ)
                for part, row in ((hi, 67), (mid, 68), (lo, 69)):
                    S.dma('sp', lambda e, part=part, row=row: e.dma_start(out=KT_s[0:4, row, :], in_=part[:, :]), reads=['nchi', 'ncmid', 'nclo'])
                for h in range(4):
                    S.dma('sp', lambda e, h=h: e.dma_start(out=KT_s[h, 64:67, :], in_=ones3_d[:, :]))
                    S.dma('sp', lambda e, h=h: e.dma_start(out=KT_s[4 + h, 64:80, :], in_=oh16_d[:, :]))
                psC = pst(st, "psC", [128, 128], F32)
                for blk in range(32):
                    S.op('pe', lambda e, blk=blk: e.transpose(out=psC[:, blk * 4:(blk + 1) * 4], in_=cT4[0:4, blk * 128:(blk + 1) * 128], identity=identf[0:4, 0:4]),
                         reads=['ncsrc'], writes=['psC'], sig=(blk == 31))
                S.op('dve', lambda e: e.tensor_copy(out=cTm[:, :], in_=psC[:, :]), reads=['psC'], writes=['cTm'])
                for pp in range(2):
                    S.op('dve', lambda e, pp=pp: e.tensor_scalar(out=kmT[pp][:, :], in0=kmsum[pp][:, :], scalar1=1.0 / 256, scalar2=None, op0=ALU.mult),
                         reads=['kmsum%d' % pp], writes=['kmT%d' % pp])
                    S.op('dve', lambda e, pp=pp: e.memset(kmBD[pp][:, :], 0.0), writes=['kmBD%d' % pp])
                    S.op('dve', lambda e, pp=pp: e.tensor_copy(out=kmBD[pp][0:64, 0:16], in_=kmT[pp][0:64, :]), reads=['kmT%d' % pp], writes=['kmBD%d' % pp])
                    S.op('dve', lambda e, pp=pp: e.tensor_copy(out=kmBD[pp][64:128, 16:32], in_=kmT[pp][64:128, :]), reads=['kmT%d' % pp], writes=['kmBD%d' % pp])
                S.barrier()

        def phaseQ(xload, W, C, vecs, GT):
            with contextlib.ExitStack() as st:
                hTo = sbt(st, "hTo", [128, 8, OWN], BF16)
                with contextlib.ExitStack() as st1:
                    norm_phase(st1, xload, OWN // 128, hTo, "no_")
                    S.barrier()
                wQ = load_w(st, "wQ", W["wQ"], 8, NWQ, V_LNG, vecs)
                wuqA = load_w(st, "wuqA", W["wuqA"], 2, 384)
                wuqB = load_w(st, "wuqB", W["wuqB"], 2, 384)
                VB = sbt(st, "VB", [128, 16, 16], F32)
                VM = sbt(st, "VM", [128, 16, 16], F32)
                S.dma('sp', lambda e: e.dma_start(out=VB[:], in_=C["VB"][:]))
                S.dma('sp', lambda e: e.dma_start(out=VM[:], in_=C["VM"][:]))
                S.barrier()
                rfq = RmsFeat(st)
                psAq = [pst(st, "qpsA%d" % i, [128, 512], F32) for i in range(4)]
                psG = pst(st, "psG", [128, 512], F32)
                psM = pst(st, "psM", [128, 1024], BF16)
                qst = [sbt(st, "qst%d" % i, [128, 512], BF16) for i in range(3)]
                cqn = sbt(st, "cqn", [128, 2, 512], BF16)
                qa = sbt(st, "qa", [96, 512], F32)
                qb = sbt(st, "qb", [96, 512], F32)
                ctt = [sbt(st, "ctt%d" % i, [96, 512], F32) for i in range(2)]
                stt = [sbt(st, "stt%d" % i, [96, 512], F32) for i in range(2)]
                qmst = [sbt(st, "qmst%d" % i, [96, 512], BF16) for i in range(2)]
                gvs = sbt(st, "gvs", [128, 2, 4, 16], F32)
                m8 = sbt(st, "m8", [128, 2, 4, 8], F32)
                Mf = sbt(st, "Mf", [128, 2, 4, 16], F32)
                Mb = [sbt(st, "Mb%d" % i, [128, 2, 4, 16], BF16) for i in range(2)]
                mst = [sbt(st, "mst%d" % i, [16, 1024], BF16) for i in range(2)]
                acnt = [0]
                qcnt = [0]
                mcnt = [0]

                def nextAq():
                    i = acnt[0] % 4
                    acnt[0] += 1
                    return i

                def projq(ai, col0, ncol, T0):
                    for kc in range(8):
                        S.op('pe', lambda e, kc=kc: e.matmul(psAq[ai][0:ncol, :], lhsT=wQ[:, kc, col0:col0 + ncol], rhs=hTo[:, kc, T0:T0 + 512],
                                                             start=(kc == 0), stop=(kc == 7)),
                             writes=['qpsA%d' % ai], sig=(kc == 7))

                pipe = Pipe()
                for ch in range(4):
                    T0 = ch * 512
                    tb = ch % 2
                    fins = []
                    for pi in range(6):
                        ai = nextAq()
                        qbuf = qcnt[0] % 3
                        qcnt[0] += 1
                        mbs = None
                        if pi // 2 == 1:
                            mbs = (mcnt[0] % 2, mcnt[0] % 2)
                            mcnt[0] += 1

                        def A(ai=ai, pi=pi, T0=T0):
                            projq(ai, Q_Q + pi * 128, 128, T0)

                        def B(ai=ai, pi=pi, T0=T0, qbuf=qbuf, ch=ch, mbs=mbs):
                            rfq([(psAq[ai][:, :], 'qpsA%d' % ai)], 1, 128, bd64, [vecs[:, V_GQ + pi:V_GQ + pi + 1]], math.log(0.125),
                                [(qst[qbuf][:, :], 'qst%d' % qbuf)])
                            hd = 4 * (pi // 2) + 2 * (pi % 2)
                            for hh in range(2):
                                S.dma('pool', lambda e, hh=hh: e.dma_start(out=QT_s[hd + hh, 0:64, T0:T0 + 512], in_=qst[qbuf][hh * 64:(hh + 1) * 64, :]),
                                      reads=['qst%d' % qbuf])
                            if pi // 2 == 1:
                                pp = pi % 2
                                mbi = mbs[0]
                                for tt in range(4):
                                    S.op('pe', lambda e, tt=tt: e.matmul(
                                        psG[:, tt * 32:(tt + 1) * 32], lhsT=qst[qbuf][:, tt * 128:(tt + 1) * 128],
                                        rhs=kmBD[pp][:, 0:32], start=True, stop=True),
                                        reads=['qst%d' % qbuf], writes=['psG'], sig=(tt == 3))
                                for hh in range(2):
                                    S.op('dve', lambda e, hh=hh: e.tensor_tensor(
                                        out=gvs[:, hh, :, :], in0=psG[:, 0:128].rearrange("p (t h n) -> p t h n", t=4, h=2)[:, :, hh, :],
                                        in1=VB[:, ch * 4:(ch + 1) * 4, :], op=ALU.add), reads=['psG'], writes=['gvs'])
                                    for tt in range(4):
                                        S.op('dve', lambda e, hh=hh, tt=tt: e.max(out=m8[:, hh, tt, :], in_=gvs[:, hh, tt, :]), reads=['gvs'], writes=['m8'])
                                    S.op('dve', lambda e, hh=hh: e.tensor_tensor(out=Mf[:, hh, :, :], in0=gvs[:, hh, :, :],
                                                                               in1=m8[:, hh, :, 2:3].to_broadcast([128, 4, 16]), op=ALU.is_ge),
                                         reads=['gvs', 'm8'], writes=['Mf'])
                                    S.op('dve', lambda e, hh=hh: e.tensor_scalar(out=Mf[:, hh, :, :], in0=Mf[:, hh, :, :], scalar1=1.0, scalar2=-NEG,
                                                                               op0=ALU.subtract, op1=ALU.mult), reads=['Mf'], writes=['Mf'])
                                    S.op('dve', lambda e, hh=hh: e.tensor_tensor(out=Mb[mbi][:, hh, :, :], in0=Mf[:, hh, :, :], in1=VM[:, ch * 4:(ch + 1) * 4, :], op=ALU.mult),
                                         reads=['Mf'], writes=['Mb%d' % mbi])

                        fin = None
                        if pi // 2 == 1:
                            def fin(pi=pi, T0=T0, mbs=mbs):
                                pp = pi % 2
                                mbi = mbs[0]
                                for hh in range(2):
                                    for tt in range(4):
                                        g = hh * 4 + tt
                                        S.op('pe', lambda e, hh=hh, tt=tt, g=g: e.transpose(out=psM[0:16, g * 128:(g + 1) * 128], in_=Mb[mbi][:, hh, tt, :], identity=ident),
                                             reads=['Mb%d' % mbi], writes=['psM'], sig=(g == 7))
                                S.op('dve', lambda e: e.tensor_copy(out=mst[mbi][:, :], in_=psM[0:16, :]), reads=['psM'], writes=['mst%d' % mbi])
                                for hh in range(2):
                                    S.dma('pool', lambda e, hh=hh: e.dma_start(out=QT_s[4 + 2 * pp + hh, 64:80, T0:T0 + 512], in_=mst[mbi][:, hh * 512:(hh + 1) * 512]),
                                          reads=['mst%d' % mbi])
                        fins.append(fin)
                        pipe.add(A, B)
                        if pi >= 2 and fins[-3] is not None:
                            pipe.add(lambda: None, fins[-3])
                    for f_ in fins[-2:]:
                        if f_ is not None:
                            pipe.add(lambda: None, f_)
                    a0 = nextAq()
                    a1 = nextAq()

                    def A(a0=a0, a1=a1, T0=T0, tb=tb):
                        projq(a0, Q_CQ, 128, T0)
                        projq(a1, Q_CQ + 128, 128, T0)
                        S.dma('sp', lambda e: e.dma_start(out=ctt[tb][:, :], in_=C["CTq"][:, T0:T0 + 512]), writes=['ctt%d' % tb])
                        S.dma('sp', lambda e: e.dma_start(out=stt[tb][:, :], in_=C["STq"][:, T0:T0 + 512]), writes=['stt%d' % tb])

                    def B(a0=a0, a1=a1):
                        rfq([(psAq[a0][:, :], 'qpsA%d' % a0), (psAq[a1][:, :], 'qpsA%d' % a1)], 2, 128, b256,
                            [vecs[:, V_GQN:V_GQN + 1], vecs[:, V_GQN + 1:V_GQN + 2]], 0.0,
                            [(cqn[:, 0, :], 'cqn'), (cqn[:, 1, :], 'cqn')])
                    pipe.add(A, B)
                    for h in range(4):
                        aA = nextAq()
                        aB = nextAq()
                        qmb = (ch * 4 + h) % 2

                        def A(aA=aA, aB=aB, h=h):
                            for c in range(2):
                                S.op('pe', lambda e, c=c: e.matmul(psAq[aA][0:96, :], lhsT=wuqA[:, c, 96 * h:96 * h + 96], rhs=cqn[:, c, :], start=(c == 0), stop=(c == 1)),
                                     reads=['cqn'], writes=['qpsA%d' % aA], sig=(c == 1))
                            for c in range(2):
                                S.op('pe', lambda e, c=c: e.matmul(psAq[aB][0:96, :], lhsT=wuqB[:, c, 96 * h:96 * h + 96], rhs=cqn[:, c, :], start=(c == 0), stop=(c == 1)),
                                     reads=['cqn'], writes=['qpsA%d' % aB], sig=(c == 1))

                        def B(aA=aA, aB=aB, h=h, qmb=qmb, tb=tb, T0=T0):
                            rfq([(psAq[aA][0:96, :], 'qpsA%d' % aA), (psAq[aB][0:96, :], 'qpsA%d' % aB)], 1, 96, bdmla,
                                [vecs[0:96, V_GA:V_GA + 1], vecs[0:96, V_GB:V_GB + 1]], 0.0, [(qa[:, :], 'qa'), (qb[:, :], 'qb')])
                            S.op('pool', lambda e: e.tensor_tensor(out=qa[:, :], in0=qa[:, :], in1=ctt[tb][:, :], op=ALU.mult), reads=['qa', 'ctt%d' % tb], writes=['qa'])
                            S.op('pool', lambda e: e.tensor_tensor(out=qb[:, :], in0=qb[:, :], in1=stt[tb][:, :], op=ALU.mult), reads=['qb', 'stt%d' % tb], writes=['qb'])
                            S.op('pool', lambda e: e.tensor_tensor(out=qmst[qmb][:, :], in0=qa[:, :], in1=qb[:, :], op=ALU.add), reads=['qa', 'qb'], writes=['qmst%d' % qmb])
                            S.dma('pool', lambda e: e.dma_start(out=QT_s[12 + h, 0:96, T0:T0 + 512], in_=qmst[qmb][:, :]), reads=['qmst%d' % qmb])
                        pipe.add(A, B, dep_prev=(h == 0))
                    for g in range(8):
                        ai = nextAq()

                        def A(ai=ai, g=g, T0=T0):
                            projq(ai, Q_GATE + g * 128, 128, T0)

                        def B(ai=ai, g=g, T0=T0):
                            S.op('act', lambda e: e.activation(out=GT[:, g, T0:T0 + 512], in_=psAq[ai][:, :], func=AF.Silu), reads=['qpsA%d' % ai])
                        pipe.add(A, B)
                pipe.run()
                S.barrier()
            with contextlib.ExitStack() as st:
                Sel = sbt(st, "Sel", [128, 4, 256], F32)
                S.dma("sp", lambda e: e.dma_start(out=Sel[:], in_=C["Sel"][:]), writes=["Sel"])
                psG2 = pst(st, "psG2", [128, 512], F32)
                cown = sbt(st, "cown", [4, OWN], F32)
                for s in range(8):
                    for jj in range(4):
                        blk = 4 * s + jj
                        S.op('pe', lambda e, blk=blk, jj=jj: e.matmul(psG2[0:4, 0:256], lhsT=cTm[:, blk * 4:(blk + 1) * 4], rhs=Sel[:, jj, :], start=(jj == 0), stop=(jj == 3)),
                             reads=['Sel'], writes=['psG2'], sig=(jj == 3))
                    S.op('dve', lambda e, s=s: e.tensor_scalar(out=cown[0:4, s * 256:(s + 1) * 256], in0=psG2[0:4, 0:256], scalar1=-1.0, scalar2=None, op0=ALU.mult),
                         reads=['psG2'], writes=['cosrc'])
                hi, mid, lo = split3(st, "co", cown[:, :], 4, OWN)
                for part, row in ((hi, 64), (mid, 65), (lo, 66)):
                    S.dma('sp', lambda e, part=part, row=row: e.dma_start(out=QT_s[0:4, row, :], in_=part[:, :]), reads=['cohi', 'comid', 'colo'])
                for h in range(4):
                    S.dma('sp', lambda e, h=h: e.dma_start(out=QT_s[h, 67:70, :], in_=ones3_d[:, 0:OWN]))
                S.barrier()

        def phaseA(GT, mixT, F_s):
            with contextlib.ExitStack() as st:
                kt = [[sbt(st, "kt%d_%d" % (b, hh), [128, SEQ], BF16) for hh in range(2)] for b in range(2)]
                qt_ = [[sbt(st, "qt%d_%d" % (b, hh), [128, OWN], BF16) for hh in range(2)] for b in range(2)]
                vt = [sbt(st, "vt%d" % b, [128, 32, 192], BF16) for b in range(2)]
                gstage = [sbt(st, "gstage%d" % hh, [128, WG], BF16) for hh in range(2)]
                gtab = [sbt(st, "gtab%d" % b, [128, 2, WG], BF16) for b in range(2)]
                gc = sbt(st, "gc", [128, WC], BF16)
                NPB = 6
                pb = [sbt(st, "pb%d" % i, [128, 512], BF16) for i in range(NPB)]
                rd = sbt(st, "rd", [128, 512], F32)
                tmpo = sbt(st, "tmpo", [128, 512], F32)
                psS = [pst(st, "psS%d" % i, [128, 512], F32) for i in range(NPB)]
                psO = [[pst(st, "psO%d_%d" % (i, hh), [128, 512], F32) for hh in range(2)] for i in range(1)]
                Vv = V_s.rearrange("(j p) c -> p j c", p=128)
                S.dma('sp', lambda e: e.dma_start(out=gc[:, :], in_=bass.AP(tensor=F_s.tensor, offset=8 * FL, ap=[[1, 128], [1, WC]])), writes=['gc'])
                for b in range(2):
                    S.op('pool', lambda e, b=b: e.memset(vt[b][:, :, 64:128], 1.0), writes=['vt%d' % b])
                KDs = (70, 80, 64, 96)
                KDM = (70, 80, 128, 96)
                for b_ in range(2):
                    for hh_ in range(2):
                        S.op('dve', lambda e, b_=b_, hh_=hh_: e.memset(kt[b_][hh_][64:128, :], 0.0), writes=['kt%d_%d' % (b_, hh_)])
                        S.op('dve', lambda e, b_=b_, hh_=hh_: e.memset(qt_[b_][hh_][64:128, :], 0.0), writes=['qt%d_%d' % (b_, hh_)])
                scnt = [0]
                ocnt = [0]
                for p8 in range(8):
                    m, pp = p8 // 2, p8 % 2
                    KD = KDs[m]
                    KDq = KDM[m]
                    b = p8 % 2
                    if m == 2:
                        for hh in range(2):
                            S.op('dve', lambda e, b=b, hh=hh: e.memset(qt_[b][hh][64:128, :], 0.0), writes=['qt%d_%d' % (b, hh)])
                    for hh in range(2):
                        hd = 4 * m + 2 * pp + hh
                        for half in range(2):
                            S.dma('sp', lambda e, b=b, hh=hh, hd=hd, half=half, KD=KD: e.dma_start(
                                out=kt[b][hh][0:KD, half * 2048:(half + 1) * 2048], in_=KT_s[hd, 0:KD, half * 2048:(half + 1) * 2048]),
                                writes=['kt%d_%d' % (b, hh)])
                        S.dma('sp', lambda e, b=b, hh=hh, hd=hd, KD=KD: e.dma_start(out=qt_[b][hh][0:KD, :], in_=QT_s[hd, 0:KD, :]), writes=['qt%d_%d' % (b, hh)])
                    for q4 in range(4):
                        for hh in range(2):
                            c0 = m * 256 + pp * 128 + hh * 64
                            S.dma('sp', lambda e, b=b, q4=q4, hh=hh, c0=c0: e.dma_start(
                                out=vt[b][:, q4 * 8:(q4 + 1) * 8, hh * 128:hh * 128 + 64], in_=Vv[:, q4 * 8:(q4 + 1) * 8, c0:c0 + 64]),
                                writes=['vt%d' % b])
                    if m in (1, 2):
                        for hh in range(2):
                            row = (m - 1) * 4 + 2 * pp + hh
                            S.dma('sp', lambda e, hh=hh, row=row: e.dma_start(
                                out=gstage[hh][:, :], in_=bass.AP(tensor=F_s.tensor, offset=row * FL, ap=[[1, 128], [1, WG]])), writes=['gstage%d' % hh])
                            for c0 in range(0, WG, 512):
                                cw = min(512, WG - c0)
                                bi = scnt[0] % NPB
                                scnt[0] += 1
                                S.op('pe', lambda e, bi=bi, hh=hh, c0=c0, cw=cw: e.matmul(psS[bi][:, 0:cw], lhsT=Jm, rhs=gstage[hh][:, c0:c0 + cw], start=True, stop=True),
                                     reads=['gstage%d' % hh], writes=['psS%d' % bi])
                                S.op('act', lambda e, bi=bi, hh=hh, c0=c0, cw=cw, b=b: e.activation(out=gtab[b][:, hh, c0:c0 + cw], in_=psS[bi][:, 0:cw], func=AF.Exp),
                                     reads=['psS%d' % bi], writes=['gtab%d' % b])
                    RK = ['kt%d_%d' % (b, hh) for hh in range(2)] + ['qt%d_%d' % (b, hh) for hh in range(2)]
                    for u in range(4):
                        s0, s1 = 2 * u, 2 * u + 1
                        nk = (4 * s0 + 4, 4 * s1 + 4)
                        jlo = (max(0, 4 * s0 - 16), max(0, 4 * s1 - 16)) if m == 2 else (0, 0)
                        js = list(range(jlo[0], nk[1]))
                        ob = 0
                        sbufs = {}

                        def active(j):
                            a0 = (jlo[0] <= j < nk[0])
                            a1 = (jlo[1] <= j < nk[1])
                            c0 = 0 if a0 else 256
                            c1 = 512 if a1 else 256
                            return a0, a1, c0, c1

                        def emit_S(j):
                            a0, a1, c0, c1 = active(j)
                            bis = []
                            for hh in range(2):
                                bi = scnt[0] % NPB
                                scnt[0] += 1
                                bis.append(bi)
                                jadd = None
                                if m in (0, 3):
                                    for si, sl in enumerate((s0, s1)):
                                        o = 512 * sl - 128 * j + 384
                                        if (a0, a1)[si] and o < 512:
                                            jadd = (si, o)
                                S.op('pe', lambda e, bi=bi, hh=hh, j=j, b=b, KD=KDq, c0=c0, c1=c1, jadd=jadd, s0=s0: e.matmul(
                                    psS[bi][:, c0:c1], lhsT=kt[b][hh][0:KD, j * 128:(j + 1) * 128],
                                    rhs=qt_[b][hh][0:KD, s0 * 256 + c0:s0 * 256 + c1], start=True, stop=(jadd is None)),
                                    reads=RK, writes=['psS%d' % bi], sig=(jadd is None))
                                if jadd is not None:
                                    si, o = jadd
                                    S.op('pe', lambda e, bi=bi, si=si, o=o: e.matmul(psS[bi][:, si * 256:(si + 1) * 256], lhsT=Jm, rhs=gc[:, o:o + 256],
                                                                                   start=False, stop=True),
                                         reads=RK + ['gc'], writes=['psS%d' % bi])
                            sbufs[j] = bis

                        def emit_PV(j):
                            a0, a1, c0, c1 = active(j)
                            bis = sbufs[j]
                            first, last = (j == js[0]), (j == js[-1])
                            for hh in range(2):
                                bi = bis[hh]
                                S.op('act', lambda e, bi=bi, c0=c0, c1=c1: e.activation(out=pb[bi][:, c0:c1], in_=psS[bi][:, c0:c1], func=AF.Exp),
                                     reads=['psS%d' % bi], writes=['pb%d' % bi])
                                if m in (1, 2):
                                    for si, sl in enumerate((s0, s1)):
                                        if not (a0, a1)[si]:
                                            continue
                                        o = 512 * sl - 128 * j + 384
                                        oe = min(o, 2048) if m == 1 else o
                                        S.op('dve', lambda e, bi=bi, oe=oe, b=b, hh=hh, si=si: e.tensor_tensor(
                                            out=pb[bi][:, si * 256:(si + 1) * 256], in0=pb[bi][:, si * 256:(si + 1) * 256],
                                            in1=gtab[b][:, hh, oe:oe + 256], op=ALU.mult),
                                            reads=['pb%d' % bi, 'gtab%d' % b], writes=['pb%d' % bi])
                                S.op('pe', lambda e, bi=bi, hh=hh, j=j, ob=ob, b=b, first=first, last=last, c0=c0, c1=c1: e.matmul(
                                    psO[ob][hh][:, c0:c1], lhsT=vt[b][:, j, hh * 64:hh * 64 + 128],
                                    rhs=pb[bi][:, c0:c1], start=first, stop=last),
                                    reads=['pb%d' % bi, 'vt%d' % b], writes=['psO%d_%d' % (ob, hh)], sig=True)

                        LA = 2
                        for jj in js[:LA]:
                            emit_S(jj)
                        for idx, j in enumerate(js):
                            if idx + LA < len(js):
                                emit_S(js[idx + LA])
                            emit_PV(j)
                        S.op('act', lambda e, ob=ob: e.activation(out=rd[0:64, :], in_=psO[ob][0][64:128, :], func=AF.Ln), reads=['psO%d_0' % ob], writes=['rd'])
                        S.op('act', lambda e, ob=ob: e.activation(out=rd[64:128, :], in_=psO[ob][1][0:64, :], func=AF.Ln), reads=['psO%d_1' % ob], writes=['rd'])
                        S.op('act', lambda e: e.activation(out=rd[:, :], in_=rd[:, :], func=AF.Exp, scale=-1.0), reads=['rd'], writes=['rd'])
                        S.op('dve', lambda e, ob=ob: e.tensor_tensor(out=tmpo[0:64, :], in0=psO[ob][0][0:64, :], in1=rd[0:64, :], op=ALU.mult),
                             reads=['psO%d_0' % ob, 'rd'], writes=['tmpo'])
                        S.op('dve', lambda e, ob=ob: e.tensor_tensor(out=tmpo[64:128, :], in0=psO[ob][1][64:128, :], in1=rd[64:128, :], op=ALU.mult),
                             reads=['psO%d_1' % ob, 'psO%d_0' % ob, 'rd'], writes=['tmpo'])
                        S.op('pool', lambda e, p8=p8, s0=s0: e.tensor_tensor(out=mixT[:, p8, s0 * 256:s0 * 256 + 512], in0=tmpo[:, :], in1=GT[:, p8, s0 * 256:s0 * 256 + 512], op=ALU.mult),
                             reads=['tmpo'])
                S.barrier()

        def phaseO(xload, pown_d, W, vecs, mixT, out_ap):
            with contextlib.ExitStack() as st:
                wout = load_w(st, "wout", W["wout"], 8, 1024)
                wpp = load_w(st, "wpp", W["wpp"], 2, 1024)
                wpg = load_w(st, "wpg", W["wpg"], 8, 1024, V_PLEG, vecs)
                S.barrier()
                NX = 3
                xt = [sbt(st, "oxt%d" % i, [128, 1024], F32) for i in range(NX)]
                pt = [sbt(st, "opt%d" % i, [128, 256], F32) for i in range(2)]
                ptb = [sbt(st, "optb%d" % i, [128, 256], BF16) for i in range(2)]
                pTs = [sbt(st, "opTs%d" % i, [128, 2, 128], BF16) for i in range(2)]
                x1 = [sbt(st, "ox1%d" % i, [128, 1024], F32) for i in range(NX)]
                junk = sbt(st, "ojunk", [128, 1024], BF16)
                xn = [sbt(st, "oxn%d" % i, [128, 1024], BF16) for i in range(2)]
                ss = [sbt(st, "oss%d" % i, [128, 4], F32) for i in range(2)]
                gTs = [sbt(st, "ogT%d" % i, [128, 8, 128], BF16) for i in range(2)]
                sg = [sbt(st, "osg%d" % i, [128, 1024], F32) for i in range(2)]
                psY = [pst(st, "psY%d" % i, [128, 512], F32) for i in range(2)]
                psT = pst(st, "opsT", [128, 1024], BF16)
                psP = pst(st, "opsP", [128, 512], BF16)
                psGt = [pst(st, "psGt%d" % i, [128, 512], F32) for i in range(2)]
                psPP = [pst(st, "psPP%d" % i, [128, 512], F32) for i in range(2)]
                NTO = OWN // 128

                def stage1(t):
                    b, b3, tok = t % 2, t % NX, t * 128
                    xload(t, xt[b3], 'oxt%d' % b3)
                    S.dma('sp', lambda e: e.dma_start(out=pt[b][:, :], in_=pown_d[tok:tok + 128, :]), writes=['opt%d' % b])
                    for half in range(2):
                        for kc in range(8):
                            S.op('pe', lambda e, half=half, kc=kc: e.matmul(psY[half][:, :], lhsT=mixT[:, kc, tok:tok + 128], rhs=wout[:, kc, half * 512:(half + 1) * 512],
                                                                          start=(kc == 0), stop=(kc == 7)), writes=['psY%d' % half], sig=(kc == 7))
                        S.op('dve', lambda e, half=half: e.tensor_tensor(out=x1[b3][:, half * 512:(half + 1) * 512], in0=psY[half][:, :], in1=xt[b3][:, half * 512:(half + 1) * 512], op=ALU.add),
                             reads=['psY%d' % half, 'oxt%d' % b3], writes=['ox1%d' % b3])
                    S.op('act', lambda e: e.activation(out=junk[:, :], in_=x1[b3][:, :], func=AF.Square, accum_out=ss[b][:, 0:1]), reads=['ox1%d' % b3], writes=['ojunk', 'oss%d' % b])
                    S.op('act', lambda e: e.activation(out=ss[b][:, 1:2], in_=ss[b][:, 0:1], func=AF.Ln, scale=1.0 / 1024, bias=EPS), reads=['oss%d' % b], writes=['oss%d' % b])
                    S.op('act', lambda e: e.activation(out=ss[b][:, 2:3], in_=ss[b][:, 1:2], func=AF.Exp, scale=-0.5), reads=['oss%d' % b], writes=['oss%d' % b])
                    S.op('act', lambda e: e.activation(out=xn[b][:, :], in_=x1[b3][:, :], func=AF.Copy, scale=ss[b][:, 2:3]),
                         reads=['ox1%d' % b3, 'oss%d' % b], writes=['oxn%d' % b])
                    S.op('dve', lambda e: e.tensor_copy(out=ptb[b][:, :], in_=pt[b][:, :]), reads=['opt%d' % b], writes=['optb%d' % b])

                def stage2(t):
                    b = t % 2
                    for c in range(8):
                        S.op('pe', lambda e, c=c: e.transpose(out=psT[:, c * 128:(c + 1) * 128], in_=xn[b][:, c * 128:(c + 1) * 128], identity=ident),
                             reads=['oxn%d' % b], writes=['opsT'], sig=(c == 7))
                    S.op('dve', lambda e: e.tensor_copy(out=gTs[b][:, :, :], in_=psT[:, :].rearrange("p (c n) -> p c n", c=8)), reads=['opsT'], writes=['ogT%d' % b])
                    for c in range(2):
                        S.op('pe', lambda e, c=c: e.transpose(out=psP[:, c * 128:(c + 1) * 128], in_=ptb[b][:, c * 128:(c + 1) * 128], identity=ident),
                             reads=['optb%d' % b], writes=['opsP'], sig=(c == 1))
                    S.op('act', lambda e: e.activation(out=pTs[b][:, :, :], in_=psP[:, 0:256].rearrange("p (c n) -> p c n", c=2), func=AF.Copy), reads=['opsP'], writes=['opTs%d' % b])

                def stage3(t):
                    b, b3, tok = t % 2, t % NX, t * 128
                    for half in range(2):
                        for kc in range(8):
                            S.op('pe', lambda e, half=half, kc=kc: e.matmul(psGt[half][:, :], lhsT=gTs[b][:, kc, :], rhs=wpg[:, kc, half * 512:(half + 1) * 512],
                                                                          start=(kc == 0), stop=(kc == 7)), reads=['ogT%d' % b], writes=['psGt%d' % half], sig=(kc == 7))
                        S.op('act', lambda e, half=half: e.activation(out=sg[b][:, half * 512:(half + 1) * 512], in_=psGt[half][:, :], func=AF.Sigmoid),
                             reads=['psGt%d' % half], writes=['osg%d' % b])
                        for c in range(2):
                            S.op('pe', lambda e, half=half, c=c: e.matmul(psPP[half][:, :], lhsT=pTs[b][:, c, :], rhs=wpp[:, c, half * 512:(half + 1) * 512],
                                                                        start=(c == 0), stop=(c == 1)), reads=['opTs%d' % b], writes=['psPP%d' % half], sig=(c == 1))
                        S.op('dve', lambda e, half=half: e.tensor_tensor(out=sg[b][:, half * 512:(half + 1) * 512], in0=psPP[half][:, :], in1=sg[b][:, half * 512:(half + 1) * 512], op=ALU.mult),
                             reads=['psPP%d' % half, 'osg%d' % b], writes=['osg%d' % b])
                    S.op('dve', lambda e: e.tensor_tensor(out=sg[b][:, :], in0=sg[b][:, :], in1=x1[b3][:, :], op=ALU.add), reads=['osg%d' % b, 'ox1%d' % b3], writes=['osg%d' % b])
                    S.dma('pool', lambda e: e.dma_start(out=out_ap[tok:tok + 128, :], in_=sg[b][:, :]), reads=['osg%d' % b])

                for i in range(NTO + 2):
                    if i < NTO:
                        stage1(i)
                    if 1 <= i <= NTO:
                        stage2(i - 1)
                    if i >= 2:
                        stage3(i - 2)
                S.barrier()

        def dram_loader(src):
            def f(t, dst, res):
                S.dma('sp', lambda e: e.dma_start(out=dst[:, :], in_=src[t * 128:(t + 1) * 128, :]), writes=[res])
            return f

        def x1_nat_loader(t, dst, res):
            a = (t % 4) // 2
            r0 = (t // 4) * 256 + (t % 2) * 128
            S.dma('sp', lambda e: e.dma_start(out=dst[:, :], in_=x1_s[a][r0:r0 + 128, :]), writes=[res])

        for c in ("a0", "a1", "m"):
            phase0(CS[c]["OH"], F_sL[c])
        phaseK(dram_loader(x_all), WL[0], vecsL[0])
        for a in range(2):
            C = CS["a%d" % a]
            with contextlib.ExitStack() as stg:
                GT = sbt(stg, "GT", [128, 8, OWN], BF16)
                phaseQ(dram_loader(xown_d[a]), WL[0], C, vecsL[0], GT)
                mixT = sbt(stg, "mixT", [128, 8, OWN], BF16)
                phaseA(GT, mixT, F_sL["a%d" % a])
                phaseO(dram_loader(xown_d[a]), C["p"], WL[0], vecsL[0], mixT, x1_s[a])
        phaseK(x1_nat_loader, WL[1], vecsL[1])
        C = CS["m"]
        with contextlib.ExitStack() as stg:
            selt = [sbt(stg, "selt%d" % i, [128, 1024], F32) for i in range(2)]

            def x1_own_loader(t, dst, res):
                for a in range(2):
                    S.dma('sp', lambda e, a=a: e.dma_start(out=selt[a][:, :], in_=x1_s[a][t * 128:(t + 1) * 128, :]), writes=['selt%d' % a])
                S.op('act', lambda e: e.activation(out=selt[0][:, :], in_=selt[0][:, :], func=AF.Copy, scale=msel[:, 0:1]),
                     reads=['selt0'], writes=['selt0'])
                S.op('dve', lambda e: e.scalar_tensor_tensor(out=dst[:, :], in0=selt[1][:, :], scalar=msel[:, 1:2], in1=selt[0][:, :], op0=ALU.mult, op1=ALU.add),
                     reads=['selt0', 'selt1'], writes=[res])

            GT = sbt(stg, "GT", [128, 8, OWN], BF16)
            phaseQ(x1_own_loader, WL[1], C, vecsL[1], GT)
            mixT = sbt(stg, "mixT", [128, 8, OWN], BF16)
            phaseA(GT, mixT, F_sL["m"])
            phaseO(x1_own_loader, C["p"], WL[1], vecsL[1], mixT, out_d)
        S.emit()
    return nc


def _t5_bucket(d):
    d = np.maximum(d, 0)
    df = np.maximum(d, 1).astype(np.float32)
    large = 16 + (np.log(df / np.float32(16)) / np.float32(math.log(2048 / 16)) * np.float32(16)).astype(np.int32)
    large = np.minimum(large, 31)
    return np.where(d < 16, d, large)


def _core_consts(par):
    bf = ml_dtypes.bfloat16
    c = {}
    i = np.arange(FL)
    d = i + 256 * par - 511
    OH = np.zeros((34, FL), np.float32)
    bk = _t5_bucket(d)
    valid = d >= 0
    OH[bk[valid], i[valid]] = 1.0
    OH[32, ~valid] = NEG
    mult = ((d <= 128).astype(np.float32) + ((d % 4 == 0) & (d <= 512)).astype(np.float32)
            + ((d % 16 == 0) & (d <= 2048)).astype(np.float32))
    ok = valid & (mult > 0)
    OH[33, :] = NEG
    OH[33, ok] = np.log(mult[ok]).astype(np.float32)
    c["OH"] = OH
    Sel = np.zeros((128, 4, 256), np.float32)
    for jj in range(4):
        for k in range(128):
            q = 128 * jj + k - 256 * par
            if 0 <= q < 256:
                Sel[k, jj, q] = 1.0
    c["Sel"] = Sel
    VB = np.zeros((128, 16, 16), np.float32)
    VM = np.zeros((128, 16, 16), np.float32)
    for qt in range(16):
        own = 2 * (qt // 2) + par
        VB[:, qt, own:] = -1e9
        VM[:, qt, :own] = 1.0
    c["VB"], c["VM"] = VB, VM
    half = 16
    inv = (1.0 / (np.float32(10000.0) ** (np.arange(half, dtype=np.float32) * np.float32(2.0) / np.float32(32)))).astype(np.float32)
    pos = np.arange(SEQ).astype(np.float32)
    ang = pos[:, None] * inv[None, :]
    cos, sin = np.cos(ang).astype(np.float32).T, np.sin(ang).astype(np.float32).T
    c["CK"] = np.ascontiguousarray(np.concatenate([cos, cos], 0))
    c["SK"] = np.ascontiguousarray(np.concatenate([-sin, sin], 0))
    own_idx = np.concatenate([512 * s + 256 * par + np.arange(256) for s in range(8)])
    sc = np.float32(96 ** -0.5)
    CT = np.full((96, OWN), sc, np.float32)
    ST = np.zeros((96, OWN), np.float32)
    CT[64:96] = np.concatenate([cos, cos], 0)[:, own_idx] * sc
    ST[64:96] = np.concatenate([-sin, sin], 0)[:, own_idx] * sc
    c["CTq"], c["STq"] = CT, ST
    c["own_idx"] = own_idx
    return c


def _shared_consts():
    bf = ml_dtypes.bfloat16
    cb = np.zeros((128, NCB), np.float32)
    for g in range(2):
        cb[g * 64:(g + 1) * 64, C_BD64 + g * 64:C_BD64 + (g + 1) * 64] = 1.0 / 64
    cb[:, C_B128:C_B128 + 128] = 1.0 / 128
    cb[:, C_B256:C_B256 + 128] = 1.0 / 256
    cb[0:64, C_BDMLA:C_BDMLA + 64] = 1.0 / 64
    cb[64:96, C_BDMLA + 64:C_BDMLA + 96] = 1.0 / 32
    cb[0:32, C_B32:C_B32 + 32] = 1.0 / 32
    cb[:, C_ONES:C_ONES + 64] = 1.0
    cb[:, C_J:C_J + 128] = np.eye(128, dtype=np.float32)[::-1]
    cb[:, C_ID:C_ID + 128] = np.eye(128, dtype=np.float32)
    oh16 = np.zeros((16, SEQ), np.float32)
    for n in range(16):
        oh16[n, n * 256:(n + 1) * 256] = 1.0
    return {"cbf": cb.astype(bf), "oh16": oh16.astype(bf), "ones3": np.ones((3, SEQ), bf),
            "identf": np.eye(128, dtype=np.float32)}


def _kc(w):
    return np.ascontiguousarray(w.reshape(8, 128, -1).transpose(1, 0, 2))


def _layer_weights(l, ln_g, w_in, b_forget, qk_gain, mla_q_norm, mla_kv_norm, mla_nope_gain, mla_rope_gain,
                   w_uq, w_ukv, w_out, rel_bias, ple_norm_g, w_ple_gate, w_ple_proj):
    W = w_in[l]
    fq, fk, fv, ff = W[:, 0:256], W[:, 256:512], W[:, 512:768], W[:, 768:772]
    mq, mk, mv = W[:, 772:1028], W[:, 1028:1284], W[:, 1284:1540]
    dq, dk, dv = W[:, 1540:1796], W[:, 1796:2052], W[:, 2052:2308]
    cq, ckv, kr, gate = W[:, 2308:2564], W[:, 2564:2692], W[:, 2692:2724], W[:, 2724:3748]
    kr_sw = np.concatenate([kr[:, 16:32], kr[:, 0:16]], 1)
    wK = np.concatenate([fk, mk, dk, fv, mv, dv, ckv, kr, kr_sw, ff, np.zeros((1024, 4), np.float32)], 1)
    wQ = np.concatenate([fq, mq, dq, cq, gate], 1)
    uq = w_uq[l]
    uqB = uq.copy()
    for h in range(4):
        uqB[:, 96 * h + 64:96 * h + 80] = uq[:, 96 * h + 80:96 * h + 96]
        uqB[:, 96 * h + 80:96 * h + 96] = uq[:, 96 * h + 64:96 * h + 80]
    ukv = w_ukv[l]
    ukvK = np.concatenate([ukv[:, 128 * h:128 * h + 64] for h in range(4)], 1)
    ukvV = np.concatenate([ukv[:, 128 * h + 64:128 * h + 128] for h in range(4)], 1)
    vec = np.zeros((128, NV), np.float32)
    vec[:, V_LNG:V_LNG + 8] = ln_g[l].reshape(8, 128).T
    for pi in range(6):
        m = pi // 2
        vec[:, V_GK + pi] = np.tile(qk_gain[l, 2 * m + 1], 2)
        vec[:, V_GQ + pi] = np.tile(qk_gain[l, 2 * m], 2)
    vec[:, V_GQN:V_GQN + 2] = mla_q_norm[l].reshape(2, 128).T
    vec[:, V_GKV] = mla_kv_norm[l]
    vec[:, V_GKN] = np.tile(mla_nope_gain[l, 1], 2)
    rg0, rg1 = mla_rope_gain[l, 0], mla_rope_gain[l, 1]
    vec[0:96, V_GA] = np.concatenate([mla_nope_gain[l, 0], rg0])
    vec[0:96, V_GB] = np.concatenate([mla_nope_gain[l, 0], rg0[16:32], rg0[0:16]])
    vec[0:32, V_GR] = rg1
    vec[0:32, V_GR + 1] = np.concatenate([rg1[16:32], rg1[0:16]])
    vec[0:4, V_BF] = b_forget[l]
    vec[:, V_PLEG:V_PLEG + 8] = ple_norm_g[l].reshape(8, 128).T
    tabX = np.zeros((34, 9), np.float32)
    tabX[0:32, 0:8] = rel_bias
    tabX[32, 0:4] = 1.0
    tabX[33, 4:8] = 1.0
    tabX[32, 8] = 1.0
    return {"wK": _kc(wK), "wQ": _kc(wQ),
            "wuqA": np.ascontiguousarray(uq.reshape(2, 128, 384).transpose(1, 0, 2)),
            "wuqB": np.ascontiguousarray(uqB.reshape(2, 128, 384).transpose(1, 0, 2)),
            "wukvK": np.ascontiguousarray(ukvK.reshape(128, 1, 256)), "wukvV": np.ascontiguousarray(ukvV.reshape(128, 1, 256)),
            "wout": _kc(w_out[l]), "wpg": _kc(w_ple_gate[l]),
            "wpp": np.ascontiguousarray(w_ple_proj[l].reshape(2, 128, 1024).transpose(1, 0, 2)),
            "vecs": vec, "tabX": tabX}


_NC = None


def kernel(x, p, ln_g, w_in, b_forget, qk_gain, mla_q_norm, mla_kv_norm, mla_nope_gain, mla_rope_gain,
           w_uq, w_ukv, w_out, rel_bias, ple_norm_g, w_ple_gate, w_ple_proj):
    global _NC
    args = [np.asarray(a, dtype=np.float32) for a in (ln_g, w_in, b_forget, qk_gain, mla_q_norm, mla_kv_norm, mla_nope_gain,
                                                     mla_rope_gain, w_uq, w_ukv, w_out, rel_bias, ple_norm_g, w_ple_gate, w_ple_proj)]
    x = np.asarray(x, dtype=np.float32)
    p = np.asarray(p, dtype=np.float32)
    if _NC is None:
        _NC = build_fused()
    shared = _shared_consts()
    cc = [_core_consts(par) for par in range(2)]
    base = dict(shared)
    for l in range(2):
        lw = _layer_weights(l, *args)
        base["tabX"] = lw.pop("tabX")
        for k, v in lw.items():
            base[k + "_l%d" % l] = v
    base["CK"], base["SK"] = cc[0]["CK"], cc[0]["SK"]
    for a in range(2):
        for k in ("OH", "Sel", "VB", "VM", "CTq", "STq"):
            base[k + "_a%d" % a] = cc[a][k]
    in_maps = []
    for core in range(8):
        b, par = core // 2, core % 2
        mp = dict(base)
        mp["x_all"] = np.ascontiguousarray(x[b])
        for a in range(2):
            mp["x_own_a%d" % a] = np.ascontiguousarray(x[b][cc[a]["own_idx"]])
            mp["p_a%d" % a] = np.ascontiguousarray(p[0, b][cc[a]["own_idx"]])
        mp["p_m"] = np.ascontiguousarray(p[1, b][cc[par]["own_idx"]])
        for k in ("OH", "Sel", "VB", "VM", "CTq", "STq"):
            mp[k + "_m"] = cc[par][k]
        ms = np.zeros((128, 2), np.float32)
        ms[:, par] = 1.0
        mp["msel"] = ms
        in_maps.append(mp)
    res = run_bass_kernel_spmd(_NC, in_maps, core_ids=list(range(8)))
    out = np.empty_like(x)
    for core in range(8):
        b, par = core // 2, core % 2
        out[b][cc[par]["own_idx"]] = np.asarray(res.results[core]["out"], dtype=np.float32)
    return out
```

```python
import contextlib
import math
import numpy as np
import ml_dtypes
import concourse.bass as bass
import concourse.mybir as mybir
from concourse.bass_utils import run_bass_kernel_spmd

F32 = mybir.dt.float32
BF16 = mybir.dt.bfloat16
AF = mybir.ActivationFunctionType
ALU = mybir.AluOpType
AX = mybir.AxisListType

COMPUTE = ('pe', 'act', 'dve', 'pool')
NDMA_SEM = 8
SEQ = 4096
OWN = 2048
NEG = -30000.0
WG = 2688
FL = WG + 128
WC = 640
EPS = 1e-6


class Sched:
    def __init__(self, nc, stack):
        self.nc = nc
        self.ops = {e: [] for e in ('pe', 'act', 'dve', 'pool', 'sp')}
        self.psem = {e: stack.enter_context(nc.semaphore("pg_" + e)) for e in COMPUTE}
        self.pcnt = {e: 0 for e in COMPUTE}
        self.dsem = {q: [stack.enter_context(nc.semaphore("dq_%s%d" % (q, i))) for i in range(NDMA_SEM)]
                     for q in ('sp', 'pool')}
        self.dval = {q: [0] * NDMA_SEM for q in ('sp', 'pool')}
        self.didx = {q: 0 for q in ('sp', 'pool')}
        self.waited = {e: {} for e in self.ops}
        self.res = {}

    def _need(self, eng, tok, waits):
        if tok is None:
            return
        key, sem, val, prod = tok
        if prod == eng and eng == 'pe':
            return
        if self.waited[eng].get(key, 0) >= val:
            return
        self.waited[eng][key] = val
        waits.append((sem, val))

    def _deps(self, eng, reads, writes):
        waits = []
        for r in reads:
            st = self.res.get(r)
            if st is not None:
                self._need(eng, st[0], waits)
        for w in writes:
            st = self.res.get(w)
            if st is not None:
                self._need(eng, st[0], waits)
                for t in st[1]:
                    self._need(eng, t, waits)
        return waits

    def _record(self, tok, reads, writes):
        for r in reads:
            st = self.res.setdefault(r, [None, []])
            st[1].append(tok)
        for w in writes:
            self.res[w] = [tok, []]

    def op(self, eng, fn, reads=(), writes=(), sig=True):
        waits = self._deps(eng, reads, writes)
        tok = None
        if sig:
            self.pcnt[eng] += 1
            tok = ('p' + eng, self.psem[eng], self.pcnt[eng], eng)
            self._record(tok, reads, writes)
        self.ops[eng].append((waits, fn, (self.psem[eng], 1) if sig else None))
        return tok

    def dma(self, q, fn, reads=(), writes=()):
        waits = self._deps(q, reads, writes)
        i = self.didx[q]
        self.didx[q] = (i + 1) % NDMA_SEM
        sem = self.dsem[q][i]
        key = 'd%s%d' % (q, i)
        prev = self.dval[q][i]
        if prev > 0 and self.waited[q].get(key, 0) < prev:
            self.waited[q][key] = prev
            waits.append((sem, prev))
        self.dval[q][i] = prev + 16
        tok = (key, sem, prev + 16, 'dma')
        self._record(tok, reads, writes)
        self.ops[q].append((waits, fn, (sem, 16)))
        return tok

    def barrier(self):
        toks = []
        for e in COMPUTE:
            if self.pcnt[e] > 0:
                toks.append(('p' + e, self.psem[e], self.pcnt[e], e))
        for q in ('sp', 'pool'):
            for i in range(NDMA_SEM):
                if self.dval[q][i] > 0:
                    toks.append(('d%s%d' % (q, i), self.dsem[q][i], self.dval[q][i], 'dma'))
        for e in self.ops:
            waits = []
            for t in toks:
                key, sem, val, prod = t
                if self.waited[e].get(key, 0) >= val:
                    continue
                self.waited[e][key] = val
                waits.append((sem, val))
            if waits:
                self.ops[e].append((waits, None, None))
        self.res = {}

    def emit(self):
        nc = self.nc
        with nc.Block() as block:
            def mk(name):
                def body(eng):
                    for waits, fn, sig in self.ops[name]:
                        for sem, val in waits:
                            eng.wait_ge(sem, val)
                        if fn is None:
                            continue
                        ins = fn(eng)
                        if sig is not None:
                            ins.then_inc(sig[0], sig[1])
                return body
            block.tensor(mk('pe'))
            block.scalar(mk('act'))
            block.vector(mk('dve'))
            block.gpsimd(mk('pool'))
            block.sync(mk('sp'))


V_LNG, V_GK, V_GQ, V_GQN, V_GKV, V_GKN, V_GA, V_GB, V_GR, V_BF, V_PLEG, NV = 0, 8, 14, 20, 22, 23, 24, 25, 26, 28, 29, 40
C_BD64, C_B128, C_B256, C_BDMLA, C_B32, C_ONES, C_J, C_ID, NCB = 0, 128, 256, 384, 480, 512, 576, 704, 832
K_K, K_V, K_CKV, K_KR, K_KRS, K_FF, NWK = 0, 768, 1536, 1664, 1696, 1728, 1736
Q_Q, Q_CQ, Q_GATE, NWQ = 0, 768, 1024, 2048


def build_fused():
    nc = bass.Bass("TRN2", target_bir_lowering=False)

    def din(name, shape, dt=F32):
        return nc.dram_tensor(name, shape, dt, kind="ExternalInput").ap()

    x_all = din("x_all", [SEQ, 1024])
    identf_d = din("identf", [128, 128])
    CK_d = din("CK", [32, SEQ])
    SK_d = din("SK", [32, SEQ])
    oh16_d = din("oh16", [16, SEQ], BF16)
    ones3_d = din("ones3", [3, SEQ], BF16)
    cbf_d = din("cbf", [128, NCB], BF16)
    tabX_d = din("tabX", [34, 9])
    msel_d = din("msel", [128, 2])
    WL = []
    for l in range(2):
        sfx = "_l%d" % l
        WL.append({"wK": din("wK" + sfx, [128, 8, NWK]), "wQ": din("wQ" + sfx, [128, 8, NWQ]),
                   "wuqA": din("wuqA" + sfx, [128, 2, 384]), "wuqB": din("wuqB" + sfx, [128, 2, 384]),
                   "wukvK": din("wukvK" + sfx, [128, 1, 256]), "wukvV": din("wukvV" + sfx, [128, 1, 256]),
                   "wout": din("wout" + sfx, [128, 8, 1024]), "wpg": din("wpg" + sfx, [128, 8, 1024]),
                   "wpp": din("wpp" + sfx, [128, 2, 1024]), "vecs": din("vecs" + sfx, [128, NV])})
    CS = {}
    for c in ("a0", "a1", "m"):
        CS[c] = {"OH": din("OH_" + c, [34, FL]), "Sel": din("Sel_" + c, [128, 4, 256]), "VB": din("VB_" + c, [128, 16, 16]),
                 "VM": din("VM_" + c, [128, 16, 16]), "CTq": din("CTq_" + c, [96, OWN]), "STq": din("STq_" + c, [96, OWN]),
                 "p": din("p_" + c, [OWN, 256])}
    xown_d = [din("x_own_a%d" % a, [OWN, 1024]) for a in range(2)]
    out_d = nc.dram_tensor("out", [OWN, 1024], F32, kind="ExternalOutput").ap()

    KT_s = nc.dram_tensor("KT_s", [16, 128, SEQ], BF16, kind="Internal").ap()
    QT_s = nc.dram_tensor("QT_s", [16, 128, OWN], BF16, kind="Internal").ap()
    V_s = nc.dram_tensor("V_s", [SEQ, 1024], BF16, kind="Internal").ap()
    F_sL = {c: nc.dram_tensor("F_s_" + c, [9, FL], BF16, kind="Internal").ap() for c in ("a0", "a1", "m")}
    x1_s = [nc.dram_tensor("x1_s%d" % a, [OWN, 1024], F32, kind="Internal").ap() for a in range(2)]

    with contextlib.ExitStack() as top:
        S = Sched(nc, top)

        uid = [0]

        def sbt(st, n, shp, dt):
            uid[0] += 1
            return st.enter_context(nc.sbuf_tensor('s%d_%s' % (uid[0], n), shp, dt))

        def pst(st, n, shp, dt):
            uid[0] += 1
            return st.enter_context(nc.psum_tensor('p%d_%s' % (uid[0], n), shp, dt))

        vecsL = [sbt(top, "vecs%d" % l, [128, NV], F32) for l in range(2)]
        msel = sbt(top, "msel", [128, 2], F32)
        cbf = sbt(top, "cbf", [128, NCB], BF16)
        identf = sbt(top, "identf", [128, 128], F32)
        kmT = [sbt(top, "kmT%d" % i, [128, 16], BF16) for i in range(2)]
        kmBD = [sbt(top, "kmBD%d" % i, [128, 32], BF16) for i in range(2)]
        cTm = sbt(top, "cTm", [128, 128], F32)
        negb = sbt(top, "negb", [128, 1], F32)

        for l in range(2):
            S.dma('sp', lambda e, l=l: e.dma_start(out=vecsL[l][:], in_=WL[l]["vecs"][:]))
        S.dma('sp', lambda e: e.dma_start(out=msel[:], in_=msel_d[:]))
        S.dma('sp', lambda e: e.dma_start(out=cbf[:], in_=cbf_d[:]))
        S.dma('sp', lambda e: e.dma_start(out=identf[:], in_=identf_d[:]))
        S.barrier()
        ident = cbf[:, C_ID:C_ID + 128]
        bd64 = cbf[:, C_BD64:C_BD64 + 128]
        b128 = cbf[:, C_B128:C_B128 + 128]
        b256 = cbf[:, C_B256:C_B256 + 128]
        bdmla = cbf[0:96, C_BDMLA:C_BDMLA + 96]
        b32 = cbf[0:32, C_B32:C_B32 + 32]
        ones64 = cbf[:, C_ONES:C_ONES + 64]
        Jm = cbf[:, C_J:C_J + 128]

        def load_w(st, name, src, nk, ncols, gcol=None, vecs=None):
            w = sbt(st, name, [128, nk, ncols], BF16)
            stg = [sbt(st, name + "_stg%d" % i, [128, ncols], F32) for i in range(2)]
            for kc in range(nk):
                b = kc % 2
                S.dma('sp', lambda e, b=b, kc=kc: e.dma_start(out=stg[b][:, :], in_=src[:, kc, :]), writes=[name + 'stg%d' % b])
                if kc % 2 == 0:
                    if gcol is None:
                        S.op('dve', lambda e, b=b, kc=kc: e.tensor_copy(out=w[:, kc, :], in_=stg[b][:, :]), reads=[name + 'stg%d' % b])
                    else:
                        S.op('dve', lambda e, b=b, kc=kc: e.tensor_scalar(out=w[:, kc, :], in0=stg[b][:, :],
                                                                         scalar1=vecs[:, gcol + kc:gcol + kc + 1], scalar2=None, op0=ALU.mult),
                             reads=[name + 'stg%d' % b])
                else:
                    if gcol is None:
                        S.op('act', lambda e, b=b, kc=kc: e.activation(out=w[:, kc, :], in_=stg[b][:, :], func=AF.Copy), reads=[name + 'stg%d' % b])
                    else:
                        S.op('act', lambda e, b=b, kc=kc: e.activation(out=w[:, kc, :], in_=stg[b][:, :], func=AF.Copy,
                                                                       scale=vecs[:, gcol + kc:gcol + kc + 1]),
                             reads=[name + 'stg%d' % b])
            return w

        def norm_funcs(st, xload, hT, pfx, hres=None):
            NB = 4
            xt = [sbt(st, pfx + "xt%d" % i, [128, 1024], F32) for i in range(NB)]
            junk = sbt(st, pfx + "junk", [128, 1024], BF16)
            xn = [sbt(st, pfx + "xn%d" % i, [128, 1024], BF16) for i in range(NB)]
            ss = [sbt(st, pfx + "ss%d" % i, [128, 4], F32) for i in range(NB)]
            pT = [pst(st, pfx + "pT%d" % i, [128, 1024], BF16) for i in range(2)]

            def front(t):
                b = t % NB
                X, N, SS = pfx + 'xt%d' % b, pfx + 'xn%d' % b, pfx + 'ss%d' % b
                xload(t, xt[b], X)
                S.op('act', lambda e: e.activation(out=junk[:], in_=xt[b][:], func=AF.Square, accum_out=ss[b][:, 0:1]),
                     reads=[X], writes=[pfx + 'junk', SS])
                S.op('act', lambda e: e.activation(out=ss[b][:, 1:2], in_=ss[b][:, 0:1], func=AF.Ln, scale=1.0 / 1024, bias=EPS),
                     reads=[SS], writes=[SS])
                S.op('act', lambda e: e.activation(out=ss[b][:, 2:3], in_=ss[b][:, 1:2], func=AF.Exp, scale=-0.5),
                     reads=[SS], writes=[SS])
                S.op('dve', lambda e: e.tensor_scalar(out=xn[b][:], in0=xt[b][:], scalar1=ss[b][:, 2:3], scalar2=None, op0=ALU.mult),
                     reads=[X, SS], writes=[N])

            def back(t):
                b, pb_ = t % NB, t % 2
                N, P = pfx + 'xn%d' % b, pfx + 'pT%d' % pb_
                wr = [hres % t] if hres else []
                for c in range(8):
                    S.op('pe', lambda e, c=c: e.transpose(out=pT[pb_][:, c * 128:(c + 1) * 128], in_=xn[b][:, c * 128:(c + 1) * 128], identity=ident),
                         reads=[N], writes=[P], sig=(c == 7))
                if t % 2 == 0:
                    S.op('dve', lambda e: e.tensor_copy(out=hT[:, :, t * 128:(t + 1) * 128], in_=pT[pb_][:].rearrange("p (c n) -> p c n", c=8)), reads=[P], writes=wr)
                else:
                    S.op('act', lambda e: e.activation(out=hT[:, :, t * 128:(t + 1) * 128], in_=pT[pb_][:].rearrange("p (c n) -> p c n", c=8), func=AF.Copy), reads=[P], writes=wr)
            return front, back

        def norm_phase(st, xload, ntiles, hT, pfx):
            front, back = norm_funcs(st, xload, hT, pfx)
            LA = 2
            for i in range(ntiles + LA):
                if i < ntiles:
                    front(i)
                if i >= LA:
                    back(i - LA)

        class RmsFeat:
            def __init__(self, st):
                self.sq = [[sbt(st, "rf_sq%d_%d" % (i, a), [128, 512], BF16) for a in range(2)] for i in range(2)]
                self.lnv = [sbt(st, "rf_ln%d" % i, [128, 512], F32) for i in range(2)]
                self.rstd = [sbt(st, "rf_rs%d" % i, [128, 512], F32) for i in range(2)]
                self.psB = [pst(st, "rf_psB%d" % i, [128, 512], F32) for i in range(2)]
                self.cnt = 0

            def __call__(self, As, nstat, P, bm, gains, ebias, outs, n=512):
                i = self.cnt % 2
                self.cnt += 1
                for a in range(nstat):
                    S.op('act', lambda e, a=a: e.activation(out=self.sq[i][a][0:P, 0:n], in_=As[a][0], func=AF.Square),
                         reads=[As[a][1]], writes=['rf_sq%d_%d' % (i, a)])
                for a in range(nstat):
                    S.op('pe', lambda e, a=a: e.matmul(self.psB[i][0:P, 0:n], lhsT=bm, rhs=self.sq[i][a][0:P, 0:n], start=(a == 0), stop=(a == nstat - 1)),
                         reads=['rf_sq%d_%d' % (i, aa) for aa in range(nstat)], writes=['rf_psB%d' % i], sig=(a == nstat - 1))
                S.op('act', lambda e: e.activation(out=self.lnv[i][0:P, 0:n], in_=self.psB[i][0:P, 0:n], func=AF.Ln, bias=EPS),
                     reads=['rf_psB%d' % i], writes=['rf_ln%d' % i])
                S.op('act', lambda e: e.activation(out=self.rstd[i][0:P, 0:n], in_=self.lnv[i][0:P, 0:n], func=AF.Exp, scale=-0.5, bias=ebias),
                     reads=['rf_ln%d' % i], writes=['rf_rs%d' % i])
                for a in range(len(As)):
                    S.op('dve', lambda e, a=a: e.scalar_tensor_tensor(out=outs[a][0], in0=As[a][0], scalar=gains[a], in1=self.rstd[i][0:P, 0:n],
                                                                      op0=ALU.mult, op1=ALU.mult),
                         reads=[As[a][1], 'rf_rs%d' % i], writes=[outs[a][1]])

        class Pipe:
            def __init__(self):
                self.items = []

            def add(self, A, B=None, dep_prev=False, depth=1):
                self.items.append((A, B, 0 if dep_prev else depth))

            def run(self):
                n = len(self.items)
                slots = [[] for _ in range(n)]
                for i, (A, B, d) in enumerate(self.items):
                    slots[max(0, i - d)].append(A)
                for i in range(n):
                    for A in slots[i]:
                        A()
                    if self.items[i][1] is not None:
                        self.items[i][1]()
                self.items = []

        def split3(st, pfx, src, npart, n):
            hi = sbt(st, pfx + "hi", [npart, n], BF16)
            mid = sbt(st, pfx + "mid", [npart, n], BF16)
            lo = sbt(st, pfx + "lo", [npart, n], BF16)
            r1 = sbt(st, pfx + "r1", [npart, n], F32)
            S.op('dve', lambda e: e.tensor_copy(out=hi[:], in_=src), reads=[pfx + 'src'], writes=[pfx + 'hi'])
            S.op('dve', lambda e: e.tensor_tensor(out=r1[:], in0=src, in1=hi[:], op=ALU.subtract), reads=[pfx + 'src', pfx + 'hi'], writes=[pfx + 'r1'])
            S.op('dve', lambda e: e.tensor_copy(out=mid[:], in_=r1[:]), reads=[pfx + 'r1'], writes=[pfx + 'mid'])
            S.op('dve', lambda e: e.tensor_tensor(out=r1[:], in0=r1[:], in1=mid[:], op=ALU.subtract), reads=[pfx + 'r1', pfx + 'mid'], writes=[pfx + 'r1'])
            S.op('dve', lambda e: e.tensor_copy(out=lo[:], in_=r1[:]), reads=[pfx + 'r1'], writes=[pfx + 'lo'])
            return hi, mid, lo

        def phase0(OH_d, F_s):
            with contextlib.ExitStack() as st:
                tabX = sbt(st, "tabX", [34, 9], F32)
                OH = sbt(st, "OH", [34, FL], F32)
                Fsb = sbt(st, "Fsb", [9, FL], BF16)
                psF = [pst(st, "psF%d" % i, [128, 512], F32) for i in range(2)]
                S.dma('sp', lambda e: e.dma_start(out=tabX[:], in_=tabX_d[:]), writes=['tabX'])
                S.dma('sp', lambda e: e.dma_start(out=OH[:], in_=OH_d[:]), writes=['OH'])
                nch = (FL + 511) // 512
                for ci in range(nch):
                    c0 = ci * 512
                    cw = min(512, FL - c0)
                    b = ci % 2
                    S.op('pe', lambda e, b=b, c0=c0, cw=cw: e.matmul(psF[b][0:9, 0:cw], lhsT=tabX[0:34, 0:9], rhs=OH[0:34, c0:c0 + cw], start=True, stop=True),
                         reads=['tabX', 'OH'], writes=['psF%d' % b])
                    S.op('dve', lambda e, b=b, c0=c0, cw=cw: e.tensor_copy(out=Fsb[0:9, c0:c0 + cw], in_=psF[b][0:9, 0:cw]),
                         reads=['psF%d' % b], writes=['Fsb'])
                S.dma('sp', lambda e: e.dma_start(out=F_s[:], in_=Fsb[:]), reads=['Fsb'])

        def phaseK(xload, W, vecs):
            stKC = contextlib.ExitStack()
            ffT = sbt(stKC, "ffT", [4, SEQ], F32)
            kmsum = [sbt(stKC, "kmsum%d" % i, [128, 16], F32) for i in range(2)]
            with contextlib.ExitStack() as st:
                hTa = sbt(st, "hTa", [128, 8, SEQ], BF16)
                nfront, nback = norm_funcs(st, xload, hTa, "na_", hres="hTa_t%d")
                wK = load_w(st, "wK", W["wK"], 8, NWK, V_LNG, vecs)
                wukvK_ = load_w(st, "wukvK", W["wukvK"], 1, 256)
                wukvK = wukvK_[:, 0, :]
                wukvV_ = load_w(st, "wukvV", W["wukvV"], 1, 256)
                wukvV = wukvV_[:, 0, :]
                S.barrier()
                rfk = RmsFeat(st)
                psAk = [pst(st, "psAk%d" % i, [128, 512], F32) for i in range(4)]
                kst = [sbt(st, "kst%d" % i, [128, 512], BF16) for i in range(3)]
                vst = [sbt(st, "vst%d" % i, [128, 768], BF16) for i in range(2)]
                vst2 = [sbt(st, "vst2_%d" % i, [128, 256], BF16) for i in range(2)]
                ckvn = [sbt(st, "ckvn%d" % i, [128, 512], BF16) for i in range(2)]
                xa = sbt(st, "xa", [32, 512], F32)
                xb = sbt(st, "xb", [32, 512], F32)
                ckt = [sbt(st, "ckt%d" % i, [32, 512], F32) for i in range(2)]
                skt = [sbt(st, "skt%d" % i, [32, 512], F32) for i in range(2)]
                rst = [sbt(st, "rst%d" % i, [32, 512], BF16) for i in range(2)]
                acnt = [0]
                kcnt = [0]

                def nextAk():
                    i = acnt[0] % 4
                    acnt[0] += 1
                    return i

                def proj(ai, col0, ncol, T0, n=512):
                    hr = ['hTa_t%d' % t for t in range(T0 // 128, (T0 + n) // 128)]
                    for kc in range(8):
                        S.op('pe', lambda e, kc=kc: e.matmul(psAk[ai][0:ncol, 0:n], lhsT=wK[:, kc, col0:col0 + ncol], rhs=hTa[:, kc, T0:T0 + n],
                                                             start=(kc == 0), stop=(kc == 7)),
                             reads=hr, writes=['psAk%d' % ai], sig=(kc == 7))

                pipe = Pipe()

                def add_norm(t):
                    pipe.add(lambda t=t: nfront(t), lambda t=t: nback(t), depth=3)
                for t in range(4):
                    add_norm(t)
                for ch in range(8):
                    T0 = ch * 512
                    cb = ch % 2
                    tb = ch % 2
                    nxt = [4 * (ch + 1) + i for i in range(4)] if ch < 7 else []
                    for pi in range(6):
                        ai = nextAk()
                        kb = kcnt[0] % 3
                        kcnt[0] += 1

                        def A(ai=ai, pi=pi, T0=T0):
                            proj(ai, K_K + pi * 128, 128, T0)

                        def B(ai=ai, pi=pi, T0=T0, kb=kb, ch=ch):
                            rfk([(psAk[ai][:, :], 'psAk%d' % ai)], 1, 128, bd64, [vecs[:, V_GK + pi:V_GK + pi + 1]], 0.0,
                                [(kst[kb][:, :], 'kst%d' % kb)])
                            hd = 4 * (pi // 2) + 2 * (pi % 2)
                            for hh in range(2):
                                S.dma('pool', lambda e, hh=hh: e.dma_start(out=KT_s[hd + hh, 0:64, T0:T0 + 512], in_=kst[kb][hh * 64:(hh + 1) * 64, :]),
                                      reads=['kst%d' % kb])
                            if pi // 2 == 1:
                                pp = pi % 2
                                S.op('dve', lambda e: e.tensor_reduce(out=kmsum[pp][:, 2 * ch:2 * ch + 2],
                                                                      in_=kst[kb][:, :].rearrange("p (a b) -> p a b", a=2), axis=AX.X, op=ALU.add),
                                     reads=['kst%d' % kb], writes=['kmsum%d' % pp])
                        pipe.add(A, B, dep_prev=(ch == 0 and pi == 0))
                        if pi in (1, 3) and nxt:
                            add_norm(nxt.pop(0))
                    for tt in range(4):
                        v0 = nextAk()
                        v1 = nextAk()

                        def A(tt=tt, T0=T0, ch=ch, v0=v0, v1=v1):
                            tok = T0 + tt * 128
                            vb = (ch * 4 + tt) % 2
                            for (vi, c0, cw) in ((v0, 0, 512), (v1, 512, 256)):
                                for kc in range(8):
                                    S.op('pe', lambda e, kc=kc, vi=vi, c0=c0, cw=cw: e.matmul(
                                        psAk[vi][:, 0:cw], lhsT=hTa[:, kc, tok:tok + 128], rhs=wK[:, kc, K_V + c0:K_V + c0 + cw],
                                        start=(kc == 0), stop=(kc == 7)), reads=['hTa_t%d' % (tok // 128)], writes=['psAk%d' % vi], sig=(kc == 7))
                            S.op('act', lambda e: e.activation(out=vst[vb][:, 0:512], in_=psAk[v0][:, 0:512], func=AF.Copy), reads=['psAk%d' % v0], writes=['vst%d' % vb])
                            S.op('dve', lambda e: e.tensor_copy(out=vst[vb][:, 512:768], in_=psAk[v1][:, 0:256]), reads=['psAk%d' % v1], writes=['vst%d' % vb])
                            S.dma('pool', lambda e: e.dma_start(out=V_s[tok:tok + 128, 0:768], in_=vst[vb][:, :]), reads=['vst%d' % vb])
                        pipe.add(A)
                        if tt in (0, 2) and nxt:
                            add_norm(nxt.pop(0))
                    ai = nextAk()

                    def A(ai=ai, T0=T0):
                        proj(ai, K_CKV, 128, T0)

                    def B(ai=ai, cb=cb):
                        rfk([(psAk[ai][:, :], 'psAk%d' % ai)], 1, 128, b128, [vecs[:, V_GKV:V_GKV + 1]], 0.0, [(ckvn[cb][:, :], 'ckvn%d' % cb)])
                    pipe.add(A, B)
                    for pp in range(2):
                        ai = nextAk()
                        kb = kcnt[0] % 3
                        kcnt[0] += 1

                        def A(ai=ai, pp=pp, cb=cb):
                            S.op('pe', lambda e: e.matmul(psAk[ai][:, :], lhsT=wukvK[:, pp * 128:(pp + 1) * 128], rhs=ckvn[cb][:, :], start=True, stop=True),
                                 reads=['ckvn%d' % cb], writes=['psAk%d' % ai])

                        def B(ai=ai, pp=pp, kb=kb, T0=T0):
                            rfk([(psAk[ai][:, :], 'psAk%d' % ai)], 1, 128, bd64, [vecs[:, V_GKN:V_GKN + 1]], 0.0, [(kst[kb][:, :], 'kst%d' % kb)])
                            for hh in range(2):
                                S.dma('pool', lambda e, hh=hh: e.dma_start(out=KT_s[12 + 2 * pp + hh, 0:64, T0:T0 + 512], in_=kst[kb][hh * 64:(hh + 1) * 64, :]),
                                      reads=['kst%d' % kb])
                        pipe.add(A, B, dep_prev=(pp == 0))
                    for tt in range(4):
                        v1 = nextAk()

                        def A(tt=tt, T0=T0, ch=ch, cb=cb, v1=v1):
                            tok = T0 + tt * 128
                            vb = (ch * 4 + tt) % 2
                            S.op('pe', lambda e: e.matmul(psAk[v1][:, 0:256], lhsT=ckvn[cb][:, tt * 128:(tt + 1) * 128], rhs=wukvV, start=True, stop=True),
                                 reads=['ckvn%d' % cb], writes=['psAk%d' % v1])
                            S.op('dve', lambda e: e.tensor_copy(out=vst2[vb][:, :], in_=psAk[v1][:, 0:256]), reads=['psAk%d' % v1], writes=['vst2_%d' % vb])
                            S.dma('pool', lambda e: e.dma_start(out=V_s[tok:tok + 128, 768:1024], in_=vst2[vb][:, :]), reads=['vst2_%d' % vb])
                        pipe.add(A)
                    aiA = nextAk()
                    aiB = nextAk()

                    def A(aiA=aiA, aiB=aiB, T0=T0, tb=tb):
                        proj(aiA, K_KR, 32, T0)
                        proj(aiB, K_KRS, 32, T0)
                        S.dma('sp', lambda e: e.dma_start(out=ckt[tb][:, :], in_=CK_d[:, T0:T0 + 512]), writes=['ckt%d' % tb])
                        S.dma('sp', lambda e: e.dma_start(out=skt[tb][:, :], in_=SK_d[:, T0:T0 + 512]), writes=['skt%d' % tb])

                    def B(aiA=aiA, aiB=aiB, T0=T0, tb=tb):
                        rfk([(psAk[aiA][0:32, :], 'psAk%d' % aiA), (psAk[aiB][0:32, :], 'psAk%d' % aiB)], 1, 32, b32,
                            [vecs[0:32, V_GR:V_GR + 1], vecs[0:32, V_GR + 1:V_GR + 2]], 0.0, [(xa[:, :], 'xa'), (xb[:, :], 'xb')])
                        S.op('pool', lambda e: e.tensor_tensor(out=xa[:, :], in0=xa[:, :], in1=ckt[tb][:, :], op=ALU.mult), reads=['xa', 'ckt%d' % tb], writes=['xa'])
                        S.op('pool', lambda e: e.tensor_tensor(out=xb[:, :], in0=xb[:, :], in1=skt[tb][:, :], op=ALU.mult), reads=['xb', 'skt%d' % tb], writes=['xb'])
                        S.op('pool', lambda e: e.tensor_tensor(out=rst[tb][:, :], in0=xa[:, :], in1=xb[:, :], op=ALU.add), reads=['xa', 'xb'], writes=['rst%d' % tb])
                        for h in range(4):
                            S.dma('pool', lambda e, h=h: e.dma_start(out=KT_s[12 + h, 64:96, T0:T0 + 512], in_=rst[tb][:, :]), reads=['rst%d' % tb])
                    pipe.add(A, B)
                    ai = nextAk()

                    def A(ai=ai, T0=T0):
                        proj(ai, K_FF, 4, T0)

                    def B(ai=ai, T0=T0):
                        S.op('act', lambda e: e.activation(out=ffT[0:4, T0:T0 + 512], in_=psAk[ai][0:4, :], func=AF.Copy), reads=['psAk%d' % ai], writes=['ffT'])
                    pipe.add(A, B)
                pipe.run()
                S.barrier()
            with stKC as st:
                onesb = sbt(st, "onesb", [4, SEQ], BF16)
                cT4 = sbt(st, "cT4", [4, SEQ], F32)
                S.op('pool', lambda e: e.memset(onesb[:], 1.0), writes=['onesb'])
                S.op('dve', lambda e: e.tensor_scalar(out=negb[0:4, :], in0=vecs[0:4, V_BF:V_BF + 1], scalar1=-1.0, scalar2=None, op0=ALU.mult), writes=['negb'])
                S.op('act', lambda e: e.activation(out=ffT[:, :], in_=ffT[:, :], func=AF.Exp, scale=-1.0, bias=negb[0:4, 0:1]), reads=['ffT', 'negb'], writes=['ffT'])
                S.op('act', lambda e: e.activation(out=ffT[:, :], in_=ffT[:, :], func=AF.Ln, bias=1.0), reads=['ffT'], writes=['ffT'])
                S.op('dve', lambda e: e.tensor_tensor_scan(out=cT4[:, :], data0=onesb[:, :], data1=ffT[:, :], initial=0.0, op0=ALU.mult, op1=ALU.add),
                     reads=['onesb', 'ffT'], writes=['ncsrc'])
                hi, mid, lo = split3(st, "nc", cT4[:, :], 4, SEQ)
                for part, row in ((hi, 67), (mid, 68), (lo, 69)):
                    S.dma('sp', lambda e, part=part, row=row: e.dma_start(out=KT_s[0:4, row, :], in_=part[:, :]), reads=['nchi', 'ncmid', 'nclo'])
                for h in range(4):
                    S.dma('sp', lambda e, h=h: e.dma_start(out=KT_s[h, 64:67, :], in_=ones3_d[:, :]))
                    S.dma('sp', lambda e, h=h: e.dma_start(out=KT_s[4 + h, 64:80, :], in_=oh16_d[:, :]))
                psC = pst(st, "psC", [128, 128], F32)
                for blk in range(32):
                    S.op('pe', lambda e, blk=blk: e.transpose(out=psC[:, blk * 4:(blk + 1) * 4], in_=cT4[0:4, blk * 128:(blk + 1) * 128], identity=identf[0:4, 0:4]),
                         reads=['ncsrc'], writes=['psC'], sig=(blk == 31))
                S.op('dve', lambda e: e.tensor_copy(out=cTm[:, :], in_=psC[:, :]), reads=['psC'], writes=['cTm'])
                for pp in range(2):
                    S.op('dve', lambda e, pp=pp: e.tensor_scalar(out=kmT[pp][:, :], in0=kmsum[pp][:, :], scalar1=1.0 / 256, scalar2=None, op0=ALU.mult),
                         reads=['kmsum%d' % pp], writes=['kmT%d' % pp])
                    S.op('dve', lambda e, pp=pp: e.memset(kmBD[pp][:, :], 0.0), writes=['kmBD%d' % pp])
                    S.op('dve', lambda e, pp=pp: e.tensor_copy(out=kmBD[pp][0:64, 0:16], in_=kmT[pp][0:64, :]), reads=['kmT%d' % pp], writes=['kmBD%d' % pp])
                    S.op('dve', lambda e, pp=pp: e.tensor_copy(out=kmBD[pp][64:128, 16:32], in_=kmT[pp][64:128, :]), reads=['kmT%d' % pp], writes=['kmBD%d' % pp])
                S.barrier()

        def phaseQ(xload, W, C, vecs, GT):
            with contextlib.ExitStack() as st:
                hTo = sbt(st, "hTo", [128, 8, OWN], BF16)
                with contextlib.ExitStack() as st1:
                    norm_phase(st1, xload, OWN // 128, hTo, "no_")
                    S.barrier()
                wQ = load_w(st, "wQ", W["wQ"], 8, NWQ, V_LNG, vecs)
                wuqA = load_w(st, "wuqA", W["wuqA"], 2, 384)
                wuqB = load_w(st, "wuqB", W["wuqB"], 2, 384)
                VB = sbt(st, "VB", [128, 16, 16], F32)
                VM = sbt(st, "VM", [128, 16, 16], F32)
                S.dma('sp', lambda e: e.dma_start(out=VB[:], in_=C["VB"][:]))
                S.dma('sp', lambda e: e.dma_start(out=VM[:], in_=C["VM"][:]))
                S.barrier()
                rfq = RmsFeat(st)
                psAq = [pst(st, "qpsA%d" % i, [128, 512], F32) for i in range(4)]
                psG = pst(st, "psG", [128, 512], F32)
                psM = pst(st, "psM", [128, 1024], BF16)
                qst = [sbt(st, "qst%d" % i, [128, 512], BF16) for i in range(3)]
                cqn = sbt(st, "cqn", [128, 2, 512], BF16)
                qa = sbt(st, "qa", [96, 512], F32)
                qb = sbt(st, "qb", [96, 512], F32)
                ctt = [sbt(st, "ctt%d" % i, [96, 512], F32) for i in range(2)]
                stt = [sbt(st, "stt%d" % i, [96, 512], F32) for i in range(2)]
                qmst = [sbt(st, "qmst%d" % i, [96, 512], BF16) for i in range(2)]
                gvs = sbt(st, "gvs", [128, 2, 4, 16], F32)
                m8 = sbt(st, "m8", [128, 2, 4, 8], F32)
                Mf = sbt(st, "Mf", [128, 2, 4, 16], F32)
                Mb = [sbt(st, "Mb%d" % i, [128, 2, 4, 16], BF16) for i in range(2)]
                mst = [sbt(st, "mst%d" % i, [16, 1024], BF16) for i in range(2)]
                acnt = [0]
                qcnt = [0]
                mcnt = [0]

                def nextAq():
                    i = acnt[0] % 4
                    acnt[0] += 1
                    return i

                def projq(ai, col0, ncol, T0):
                    for kc in range(8):
                        S.op('pe', lambda e, kc=kc: e.matmul(psAq[ai][0:ncol, :], lhsT=wQ[:, kc, col0:col0 + ncol], rhs=hTo[:, kc, T0:T0 + 512],
                                                             start=(kc == 0), stop=(kc == 7)),
                             writes=['qpsA%d' % ai], sig=(kc == 7))

                pipe = Pipe()
                for ch in range(4):
                    T0 = ch * 512
                    tb = ch % 2
                    fins = []
                    for pi in range(6):
                        ai = nextAq()
                        qbuf = qcnt[0] % 3
                        qcnt[0] += 1
                        mbs = None
                        if pi // 2 == 1:
                            mbs = (mcnt[0] % 2, mcnt[0] % 2)
                            mcnt[0] += 1

                        def A(ai=ai, pi=pi, T0=T0):
                            projq(ai, Q_Q + pi * 128, 128, T0)

                        def B(ai=ai, pi=pi, T0=T0, qbuf=qbuf, ch=ch, mbs=mbs):
                            rfq([(psAq[ai][:, :], 'qpsA%d' % ai)], 1, 128, bd64, [vecs[:, V_GQ + pi:V_GQ + pi + 1]], math.log(0.125),
                                [(qst[qbuf][:, :], 'qst%d' % qbuf)])
                            hd = 4 * (pi // 2) + 2 * (pi % 2)
                            for hh in range(2):
                                S.dma('pool', lambda e, hh=hh: e.dma_start(out=QT_s[hd + hh, 0:64, T0:T0 + 512], in_=qst[qbuf][hh * 64:(hh + 1) * 64, :]),
                                      reads=['qst%d' % qbuf])
                            if pi // 2 == 1:
                                pp = pi % 2
                                mbi = mbs[0]
                                for tt in range(4):
                                    S.op('pe', lambda e, tt=tt: e.matmul(
                                        psG[:, tt * 32:(tt + 1) * 32], lhsT=qst[qbuf][:, tt * 128:(tt + 1) * 128],
                                        rhs=kmBD[pp][:, 0:32], start=True, stop=True),
                                        reads=['qst%d' % qbuf], writes=['psG'], sig=(tt == 3))
                                for hh in range(2):
                                    S.op('dve', lambda e, hh=hh: e.tensor_tensor(
                                        out=gvs[:, hh, :, :], in0=psG[:, 0:128].rearrange("p (t h n) -> p t h n", t=4, h=2)[:, :, hh, :],
                                        in1=VB[:, ch * 4:(ch + 1) * 4, :], op=ALU.add), reads=['psG'], writes=['gvs'])
                                    for tt in range(4):
                                        S.op('dve', lambda e, hh=hh, tt=tt: e.max(out=m8[:, hh, tt, :], in_=gvs[:, hh, tt, :]), reads=['gvs'], writes=['m8'])
                                    S.op('dve', lambda e, hh=hh: e.tensor_tensor(out=Mf[:, hh, :, :], in0=gvs[:, hh, :, :],
                                                                               in1=m8[:, hh, :, 2:3].to_broadcast([128, 4, 16]), op=ALU.is_ge),
                                         reads=['gvs', 'm8'], writes=['Mf'])
                                    S.op('dve', lambda e, hh=hh: e.tensor_scalar(out=Mf[:, hh, :, :], in0=Mf[:, hh, :, :], scalar1=1.0, scalar2=-NEG,
                                                                               op0=ALU.subtract, op1=ALU.mult), reads=['Mf'], writes=['Mf'])
                                    S.op('dve', lambda e, hh=hh: e.tensor_tensor(out=Mb[mbi][:, hh, :, :], in0=Mf[:, hh, :, :], in1=VM[:, ch * 4:(ch + 1) * 4, :], op=ALU.mult),
                                         reads=['Mf'], writes=['Mb%d' % mbi])

                        fin = None
                        if pi // 2 == 1:
                            def fin(pi=pi, T0=T0, mbs=mbs):
                                pp = pi % 2
                                mbi = mbs[0]
                                for hh in range(2):
                                    for tt in range(4):
                                        g = hh * 4 + tt
                                        S.op('pe', lambda e, hh=hh, tt=tt, g=g: e.transpose(out=psM[0:16, g * 128:(g + 1) * 128], in_=Mb[mbi][:, hh, tt, :], identity=ident),
                                             reads=['Mb%d' % mbi], writes=['psM'], sig=(g == 7))
                                S.op('dve', lambda e: e.tensor_copy(out=mst[mbi][:, :], in_=psM[0:16, :]), reads=['psM'], writes=['mst%d' % mbi])
                                for hh in range(2):
                                    S.dma('pool', lambda e, hh=hh: e.dma_start(out=QT_s[4 + 2 * pp + hh, 64:80, T0:T0 + 512], in_=mst[mbi][:, hh * 512:(hh + 1) * 512]),
                                          reads=['mst%d' % mbi])
                        fins.append(fin)
                        pipe.add(A, B)
                        if pi >= 2 and fins[-3] is not None:
                            pipe.add(lambda: None, fins[-3])
                    for f_ in fins[-2:]:
                        if f_ is not None:
                            pipe.add(lambda: None, f_)
                    a0 = nextAq()
                    a1 = nextAq()

                    def A(a0=a0, a1=a1, T0=T0, tb=tb):
                        projq(a0, Q_CQ, 128, T0)
                        projq(a1, Q_CQ + 128, 128, T0)
                        S.dma('sp', lambda e: e.dma_start(out=ctt[tb][:, :], in_=C["CTq"][:, T0:T0 + 512]), writes=['ctt%d' % tb])
                        S.dma('sp', lambda e: e.dma_start(out=stt[tb][:, :], in_=C["STq"][:, T0:T0 + 512]), writes=['stt%d' % tb])

                    def B(a0=a0, a1=a1):
                        rfq([(psAq[a0][:, :], 'qpsA%d' % a0), (psAq[a1][:, :], 'qpsA%d' % a1)], 2, 128, b256,
                            [vecs[:, V_GQN:V_GQN + 1], vecs[:, V_GQN + 1:V_GQN + 2]], 0.0,
                            [(cqn[:, 0, :], 'cqn'), (cqn[:, 1, :], 'cqn')])
                    pipe.add(A, B)
                    for h in range(4):
                        aA = nextAq()
                        aB = nextAq()
                        qmb = (ch * 4 + h) % 2

                        def A(aA=aA, aB=aB, h=h):
                            for c in range(2):
                                S.op('pe', lambda e, c=c: e.matmul(psAq[aA][0:96, :], lhsT=wuqA[:, c, 96 * h:96 * h + 96], rhs=cqn[:, c, :], start=(c == 0), stop=(c == 1)),
                                     reads=['cqn'], writes=['qpsA%d' % aA], sig=(c == 1))
                            for c in range(2):
                                S.op('pe', lambda e, c=c: e.matmul(psAq[aB][0:96, :], lhsT=wuqB[:, c, 96 * h:96 * h + 96], rhs=cqn[:, c, :], start=(c == 0), stop=(c == 1)),
                                     reads=['cqn'], writes=['qpsA%d' % aB], sig=(c == 1))

                        def B(aA=aA, aB=aB, h=h, qmb=qmb, tb=tb, T0=T0):
                            rfq([(psAq[aA][0:96, :], 'qpsA%d' % aA), (psAq[aB][0:96, :], 'qpsA%d' % aB)], 1, 96, bdmla,
                                [vecs[0:96, V_GA:V_GA + 1], vecs[0:96, V_GB:V_GB + 1]], 0.0, [(qa[:, :], 'qa'), (qb[:, :], 'qb')])
                            S.op('pool', lambda e: e.tensor_tensor(out=qa[:, :], in0=qa[:, :], in1=ctt[tb][:, :], op=ALU.mult), reads=['qa', 'ctt%d' % tb], writes=['qa'])
                            S.op('pool', lambda e: e.tensor_tensor(out=qb[:, :], in0=qb[:, :], in1=stt[tb][:, :], op=ALU.mult), reads=['qb', 'stt%d' % tb], writes=['qb'])
                            S.op('pool', lambda e: e.tensor_tensor(out=qmst[qmb][:, :], in0=qa[:, :], in1=qb[:, :], op=ALU.add), reads=['qa', 'qb'], writes=['qmst%d' % qmb])
                            S.dma('pool', lambda e: e.dma_start(out=QT_s[12 + h, 0:96, T0:T0 + 512], in_=qmst[qmb][:, :]), reads=['qmst%d' % qmb])
                        pipe.add(A, B, dep_prev=(h == 0))
                    for g in range(8):
                        ai = nextAq()

                        def A(ai=ai, g=g, T0=T0):
                            projq(ai, Q_GATE + g * 128, 128, T0)

                        def B(ai=ai, g=g, T0=T0):
                            S.op('act', lambda e: e.activation(out=GT[:, g, T0:T0 + 512], in_=psAq[ai][:, :], func=AF.Silu), reads=['qpsA%d' % ai])
                        pipe.add(A, B)
                pipe.run()
                S.barrier()
            with contextlib.ExitStack() as st:
                Sel = sbt(st, "Sel", [128, 4, 256], F32)
                S.dma("sp", lambda e: e.dma_start(out=Sel[:], in_=C["Sel"][:]), writes=["Sel"])
                psG2 = pst(st, "psG2", [128, 512], F32)
                cown = sbt(st, "cown", [4, OWN], F32)
                for s in range(8):
                    for jj in range(4):
                        blk = 4 * s + jj
                        S.op('pe', lambda e, blk=blk, jj=jj: e.matmul(psG2[0:4, 0:256], lhsT=cTm[:, blk * 4:(blk + 1) * 4], rhs=Sel[:, jj, :], start=(jj == 0), stop=(jj == 3)),
                             reads=['Sel'], writes=['psG2'], sig=(jj == 3))
                    S.op('dve', lambda e, s=s: e.tensor_scalar(out=cown[0:4, s * 256:(s + 1) * 256], in0=psG2[0:4, 0:256], scalar1=-1.0, scalar2=None, op0=ALU.mult),
                         reads=['psG2'], writes=['cosrc'])
                hi, mid, lo = split3(st, "co", cown[:, :], 4, OWN)
                for part, row in ((hi, 64), (mid, 65), (lo, 66)):
                    S.dma('sp', lambda e, part=part, row=row: e.dma_start(out=QT_s[0:4, row, :], in_=part[:, :]), reads=['cohi', 'comid', 'colo'])
                for h in range(4):
                    S.dma('sp', lambda e, h=h: e.dma_start(out=QT_s[h, 67:70, :], in_=ones3_d[:, 0:OWN]))
                S.barrier()

        def phaseA(GT, mixT, F_s):
            with contextlib.ExitStack() as st:
                kt = [[sbt(st, "kt%d_%d" % (b, hh), [128, SEQ], BF16) for hh in range(2)] for b in range(2)]
                qt_ = [[sbt(st, "qt%d_%d" % (b, hh), [128, OWN], BF16) for hh in range(2)] for b in range(2)]
                vt = [sbt(st, "vt%d" % b, [128, 32, 192], BF16) for b in range(2)]
                gstage = [sbt(st, "gstage%d" % hh, [128, WG], BF16) for hh in range(2)]
                gtab = [sbt(st, "gtab%d" % b, [128, 2, WG], BF16) for b in range(2)]
                gc = sbt(st, "gc", [128, WC], BF16)
                NPB = 6
                pb = [sbt(st, "pb%d" % i, [128, 512], BF16) for i in range(NPB)]
                rd = sbt(st, "rd", [128, 512], F32)
                tmpo = sbt(st, "tmpo", [128, 512], F32)
                psS = [pst(st, "psS%d" % i, [128, 512], F32) for i in range(NPB)]
                psO = [[pst(st, "psO%d_%d" % (i, hh), [128, 512], F32) for hh in range(2)] for i in range(1)]
                Vv = V_s.rearrange("(j p) c -> p j c", p=128)
                S.dma('sp', lambda e: e.dma_start(out=gc[:, :], in_=bass.AP(tensor=F_s.tensor, offset=8 * FL, ap=[[1, 128], [1, WC]])), writes=['gc'])
                for b in range(2):
                    S.op('pool', lambda e, b=b: e.memset(vt[b][:, :, 64:128], 1.0), writes=['vt%d' % b])
                KDs = (70, 80, 64, 96)
                KDM = (70, 80, 128, 96)
                for b_ in range(2):
                    for hh_ in range(2):
                        S.op('dve', lambda e, b_=b_, hh_=hh_: e.memset(kt[b_][hh_][64:128, :], 0.0), writes=['kt%d_%d' % (b_, hh_)])
                        S.op('dve', lambda e, b_=b_, hh_=hh_: e.memset(qt_[b_][hh_][64:128, :], 0.0), writes=['qt%d_%d' % (b_, hh_)])
                scnt = [0]
                ocnt = [0]
                for p8 in range(8):
                    m, pp = p8 // 2, p8 % 2
                    KD = KDs[m]
                    KDq = KDM[m]
                    b = p8 % 2
                    if m == 2:
                        for hh in range(2):
                            S.op('dve', lambda e, b=b, hh=hh: e.memset(qt_[b][hh][64:128, :], 0.0), writes=['qt%d_%d' % (b, hh)])
                    for hh in range(2):
                        hd = 4 * m + 2 * pp + hh
                        for half in range(2):
                            S.dma('sp', lambda e, b=b, hh=hh, hd=hd, half=half, KD=KD: e.dma_start(
                                out=kt[b][hh][0:KD, half * 2048:(half + 1) * 2048], in_=KT_s[hd, 0:KD, half * 2048:(half + 1) * 2048]),
                                writes=['kt%d_%d' % (b, hh)])
                        S.dma('sp', lambda e, b=b, hh=hh, hd=hd, KD=KD: e.dma_start(out=qt_[b][hh][0:KD, :], in_=QT_s[hd, 0:KD, :]), writes=['qt%d_%d' % (b, hh)])
                    for q4 in range(4):
                        for hh in range(2):
                            c0 = m * 256 + pp * 128 + hh * 64
                            S.dma('sp', lambda e, b=b, q4=q4, hh=hh, c0=c0: e.dma_start(
                                out=vt[b][:, q4 * 8:(q4 + 1) * 8, hh * 128:hh * 128 + 64], in_=Vv[:, q4 * 8:(q4 + 1) * 8, c0:c0 + 64]),
                                writes=['vt%d' % b])
                    if m in (1, 2):
                        for hh in range(2):
                            row = (m - 1) * 4 + 2 * pp + hh
                            S.dma('sp', lambda e, hh=hh, row=row: e.dma_start(
                                out=gstage[hh][:, :], in_=bass.AP(tensor=F_s.tensor, offset=row * FL, ap=[[1, 128], [1, WG]])), writes=['gstage%d' % hh])
                            for c0 in range(0, WG, 512):
                                cw = min(512, WG - c0)
                                bi = scnt[0] % NPB
                                scnt[0] += 1
                                S.op('pe', lambda e, bi=bi, hh=hh, c0=c0, cw=cw: e.matmul(psS[bi][:, 0:cw], lhsT=Jm, rhs=gstage[hh][:, c0:c0 + cw], start=True, stop=True),
                                     reads=['gstage%d' % hh], writes=['psS%d' % bi])
                                S.op('act', lambda e, bi=bi, hh=hh, c0=c0, cw=cw, b=b: e.activation(out=gtab[b][:, hh, c0:c0 + cw], in_=psS[bi][:, 0:cw], func=AF.Exp),
                                     reads=['psS%d' % bi], writes=['gtab%d' % b])
                    RK = ['kt%d_%d' % (b, hh) for hh in range(2)] + ['qt%d_%d' % (b, hh) for hh in range(2)]
                    for u in range(4):
                        s0, s1 = 2 * u, 2 * u + 1
                        nk = (4 * s0 + 4, 4 * s1 + 4)
                        jlo = (max(0, 4 * s0 - 16), max(0, 4 * s1 - 16)) if m == 2 else (0, 0)
                        js = list(range(jlo[0], nk[1]))
                        ob = 0
                        sbufs = {}

                        def active(j):
                            a0 = (jlo[0] <= j < nk[0])
                            a1 = (jlo[1] <= j < nk[1])
                            c0 = 0 if a0 else 256
                            c1 = 512 if a1 else 256
                            return a0, a1, c0, c1

                        def emit_S(j):
                            a0, a1, c0, c1 = active(j)
                            bis = []
                            for hh in range(2):
                                bi = scnt[0] % NPB
                                scnt[0] += 1
                                bis.append(bi)
                                jadd = None
                                if m in (0, 3):
                                    for si, sl in enumerate((s0, s1)):
                                        o = 512 * sl - 128 * j + 384
                                        if (a0, a1)[si] and o < 512:
                                            jadd = (si, o)
                                S.op('pe', lambda e, bi=bi, hh=hh, j=j, b=b, KD=KDq, c0=c0, c1=c1, jadd=jadd, s0=s0: e.matmul(
                                    psS[bi][:, c0:c1], lhsT=kt[b][hh][0:KD, j * 128:(j + 1) * 128],
                                    rhs=qt_[b][hh][0:KD, s0 * 256 + c0:s0 * 256 + c1], start=True, stop=(jadd is None)),
                                    reads=RK, writes=['psS%d' % bi], sig=(jadd is None))
                                if jadd is not None:
                                    si, o = jadd
                                    S.op('pe', lambda e, bi=bi, si=si, o=o: e.matmul(psS[bi][:, si * 256:(si + 1) * 256], lhsT=Jm, rhs=gc[:, o:o + 256],
                                                                                   start=False, stop=True),
                                         reads=RK + ['gc'], writes=['psS%d' % bi])
                            sbufs[j] = bis

                        def emit_PV(j):
                            a0, a1, c0, c1 = active(j)
                            bis = sbufs[j]
                            first, last = (j == js[0]), (j == js[-1])
                            for hh in range(2):
                                bi = bis[hh]
                                S.op('act', lambda e, bi=bi, c0=c0, c1=c1: e.activation(out=pb[bi][:, c0:c1], in_=psS[bi][:, c0:c1], func=AF.Exp),
                                     reads=['psS%d' % bi], writes=['pb%d' % bi])
                                if m in (1, 2):
                                    for si, sl in enumerate((s0, s1)):
                                        if not (a0, a1)[si]:
                                            continue
                                        o = 512 * sl - 128 * j + 384
                                        oe = min(o, 2048) if m == 1 else o
                                        S.op('dve', lambda e, bi=bi, oe=oe, b=b, hh=hh, si=si: e.tensor_tensor(
                                            out=pb[bi][:, si * 256:(si + 1) * 256], in0=pb[bi][:, si * 256:(si + 1) * 256],
                                            in1=gtab[b][:, hh, oe:oe + 256], op=ALU.mult),
                                            reads=['pb%d' % bi, 'gtab%d' % b], writes=['pb%d' % bi])
                                S.op('pe', lambda e, bi=bi, hh=hh, j=j, ob=ob, b=b, first=first, last=last, c0=c0, c1=c1: e.matmul(
                                    psO[ob][hh][:, c0:c1], lhsT=vt[b][:, j, hh * 64:hh * 64 + 128],
                                    rhs=pb[bi][:, c0:c1], start=first, stop=last),
                                    reads=['pb%d' % bi, 'vt%d' % b], writes=['psO%d_%d' % (ob, hh)], sig=True)

                        LA = 2
                        for jj in js[:LA]:
                            emit_S(jj)
                        for idx, j in enumerate(js):
                            if idx + LA < len(js):
                                emit_S(js[idx + LA])
                            emit_PV(j)
                        S.op('act', lambda e, ob=ob: e.activation(out=rd[0:64, :], in_=psO[ob][0][64:128, :], func=AF.Ln), reads=['psO%d_0' % ob], writes=['rd'])
                        S.op('act', lambda e, ob=ob: e.activation(out=rd[64:128, :], in_=psO[ob][1][0:64, :], func=AF.Ln), reads=['psO%d_1' % ob], writes=['rd'])
                        S.op('act', lambda e: e.activation(out=rd[:, :], in_=rd[:, :], func=AF.Exp, scale=-1.0), reads=['rd'], writes=['rd'])
                        S.op('dve', lambda e, ob=ob: e.tensor_tensor(out=tmpo[0:64, :], in0=psO[ob][0][0:64, :], in1=rd[0:64, :], op=ALU.mult),
                             reads=['psO%d_0' % ob, 'rd'], writes=['tmpo'])
                        S.op('dve', lambda e, ob=ob: e.tensor_tensor(out=tmpo[64:128, :], in0=psO[ob][1][64:128, :], in1=rd[64:128, :], op=ALU.mult),
                             reads=['psO%d_1' % ob, 'psO%d_0' % ob, 'rd'], writes=['tmpo'])
                        S.op('pool', lambda e, p8=p8, s0=s0: e.tensor_tensor(out=mixT[:, p8, s0 * 256:s0 * 256 + 512], in0=tmpo[:, :], in1=GT[:, p8, s0 * 256:s0 * 256 + 512], op=ALU.mult),
                             reads=['tmpo'])
                S.barrier()

        def phaseO(xload, pown_d, W, vecs, mixT, out_ap):
            with contextlib.ExitStack() as st:
                wout = load_w(st, "wout", W["wout"], 8, 1024)
                wpp = load_w(st, "wpp", W["wpp"], 2, 1024)
                wpg = load_w(st, "wpg", W["wpg"], 8, 1024, V_PLEG, vecs)
                S.barrier()
                NX = 3
                xt = [sbt(st, "oxt%d" % i, [128, 1024], F32) for i in range(NX)]
                pt = [sbt(st, "opt%d" % i, [128, 256], F32) for i in range(2)]
                ptb = [sbt(st, "optb%d" % i, [128, 256], BF16) for i in range(2)]
                pTs = [sbt(st, "opTs%d" % i, [128, 2, 128], BF16) for i in range(2)]
                x1 = [sbt(st, "ox1%d" % i, [128, 1024], F32) for i in range(NX)]
                junk = sbt(st, "ojunk", [128, 1024], BF16)
                xn = [sbt(st, "oxn%d" % i, [128, 1024], BF16) for i in range(2)]
                ss = [sbt(st, "oss%d" % i, [128, 4], F32) for i in range(2)]
                gTs = [sbt(st, "ogT%d" % i, [128, 8, 128], BF16) for i in range(2)]
                sg = [sbt(st, "osg%d" % i, [128, 1024], F32) for i in range(2)]
                psY = [pst(st, "psY%d" % i, [128, 512], F32) for i in range(2)]
                psT = pst(st, "opsT", [128, 1024], BF16)
                psP = pst(st, "opsP", [128, 512], BF16)
                psGt = [pst(st, "psGt%d" % i, [128, 512], F32) for i in range(2)]
                psPP = [pst(st, "psPP%d" % i, [128, 512], F32) for i in range(2)]
                NTO = OWN // 128

                def stage1(t):
                    b, b3, tok = t % 2, t % NX, t * 128
                    xload(t, xt[b3], 'oxt%d' % b3)
                    S.dma('sp', lambda e: e.dma_start(out=pt[b][:, :], in_=pown_d[tok:tok + 128, :]), writes=['opt%d' % b])
                    for half in range(2):
                        for kc in range(8):
                            S.op('pe', lambda e, half=half, kc=kc: e.matmul(psY[half][:, :], lhsT=mixT[:, kc, tok:tok + 128], rhs=wout[:, kc, half * 512:(half + 1) * 512],
                                                                          start=(kc == 0), stop=(kc == 7)), writes=['psY%d' % half], sig=(kc == 7))
                        S.op('dve', lambda e, half=half: e.tensor_tensor(out=x1[b3][:, half * 512:(half + 1) * 512], in0=psY[half][:, :], in1=xt[b3][:, half * 512:(half + 1) * 512], op=ALU.add),
                             reads=['psY%d' % half, 'oxt%d' % b3], writes=['ox1%d' % b3])
                    S.op('act', lambda e: e.activation(out=junk[:, :], in_=x1[b3][:, :], func=AF.Square, accum_out=ss[b][:, 0:1]), reads=['ox1%d' % b3], writes=['ojunk', 'oss%d' % b])
                    S.op('act', lambda e: e.activation(out=ss[b][:, 1:2], in_=ss[b][:, 0:1], func=AF.Ln, scale=1.0 / 1024, bias=EPS), reads=['oss%d' % b], writes=['oss%d' % b])
                    S.op('act', lambda e: e.activation(out=ss[b][:, 2:3], in_=ss[b][:, 1:2], func=AF.Exp, scale=-0.5), reads=['oss%d' % b], writes=['oss%d' % b])
                    S.op('act', lambda e: e.activation(out=xn[b][:, :], in_=x1[b3][:, :], func=AF.Copy, scale=ss[b][:, 2:3]),
                         reads=['ox1%d' % b3, 'oss%d' % b], writes=['oxn%d' % b])
                    S.op('dve', lambda e: e.tensor_copy(out=ptb[b][:, :], in_=pt[b][:, :]), reads=['opt%d' % b], writes=['optb%d' % b])

                def stage2(t):
                    b = t % 2
                    for c in range(8):
                        S.op('pe', lambda e, c=c: e.transpose(out=psT[:, c * 128:(c + 1) * 128], in_=xn[b][:, c * 128:(c + 1) * 128], identity=ident),
                             reads=['oxn%d' % b], writes=['opsT'], sig=(c == 7))
                    S.op('dve', lambda e: e.tensor_copy(out=gTs[b][:, :, :], in_=psT[:, :].rearrange("p (c n) -> p c n", c=8)), reads=['opsT'], writes=['ogT%d' % b])
                    for c in range(2):
                        S.op('pe', lambda e, c=c: e.transpose(out=psP[:, c * 128:(c + 1) * 128], in_=ptb[b][:, c * 128:(c + 1) * 128], identity=ident),
                             reads=['optb%d' % b], writes=['opsP'], sig=(c == 1))
                    S.op('act', lambda e: e.activation(out=pTs[b][:, :, :], in_=psP[:, 0:256].rearrange("p (c n) -> p c n", c=2), func=AF.Copy), reads=['opsP'], writes=['opTs%d' % b])

                def stage3(t):
                    b, b3, tok = t % 2, t % NX, t * 128
                    for half in range(2):
                        for kc in range(8):
                            S.op('pe', lambda e, half=half, kc=kc: e.matmul(psGt[half][:, :], lhsT=gTs[b][:, kc, :], rhs=wpg[:, kc, half * 512:(half + 1) * 512],
                                                                          start=(kc == 0), stop=(kc == 7)), reads=['ogT%d' % b], writes=['psGt%d' % half], sig=(kc == 7))
                        S.op('act', lambda e, half=half: e.activation(out=sg[b][:, half * 512:(half + 1) * 512], in_=psGt[half][:, :], func=AF.Sigmoid),
                             reads=['psGt%d' % half], writes=['osg%d' % b])
                        for c in range(2):
                            S.op('pe', lambda e, half=half, c=c: e.matmul(psPP[half][:, :], lhsT=pTs[b][:, c, :], rhs=wpp[:, c, half * 512:(half + 1) * 512],
                                                                        start=(c == 0), stop=(c == 1)), reads=['opTs%d' % b], writes=['psPP%d' % half], sig=(c == 1))
                        S.op('dve', lambda e, half=half: e.tensor_tensor(out=sg[b][:, half * 512:(half + 1) * 512], in0=psPP[half][:, :], in1=sg[b][:, half * 512:(half + 1) * 512], op=ALU.mult),
                             reads=['psPP%d' % half, 'osg%d' % b], writes=['osg%d' % b])
                    S.op('dve', lambda e: e.tensor_tensor(out=sg[b][:, :], in0=sg[b][:, :], in1=x1[b3][:, :], op=ALU.add), reads=['osg%d' % b, 'ox1%d' % b3], writes=['osg%d' % b])
                    S.dma('pool', lambda e: e.dma_start(out=out_ap[tok:tok + 128, :], in_=sg[b][:, :]), reads=['osg%d' % b])

                for i in range(NTO + 2):
                    if i < NTO:
                        stage1(i)
                    if 1 <= i <= NTO:
                        stage2(i - 1)
                    if i >= 2:
                        stage3(i - 2)
                S.barrier()

        def dram_loader(src):
            def f(t, dst, res):
                S.dma('sp', lambda e: e.dma_start(out=dst[:, :], in_=src[t * 128:(t + 1) * 128, :]), writes=[res])
            return f

        def x1_nat_loader(t, dst, res):
            a = (t % 4) // 2
            r0 = (t // 4) * 256 + (t % 2) * 128
            S.dma('sp', lambda e: e.dma_start(out=dst[:, :], in_=x1_s[a][r0:r0 + 128, :]), writes=[res])

        for c in ("a0", "a1", "m"):
            phase0(CS[c]["OH"], F_sL[c])
        phaseK(dram_loader(x_all), WL[0], vecsL[0])
        for a in range(2):
            C = CS["a%d" % a]
            with contextlib.ExitStack() as stg:
                GT = sbt(stg, "GT", [128, 8, OWN], BF16)
                phaseQ(dram_loader(xown_d[a]), WL[0], C, vecsL[0], GT)
                mixT = sbt(stg, "mixT", [128, 8, OWN], BF16)
                phaseA(GT, mixT, F_sL["a%d" % a])
                phaseO(dram_loader(xown_d[a]), C["p"], WL[0], vecsL[0], mixT, x1_s[a])
        phaseK(x1_nat_loader, WL[1], vecsL[1])
        C = CS["m"]
        with contextlib.ExitStack() as stg:
            selt = [sbt(stg, "selt%d" % i, [128, 1024], F32) for i in range(2)]

            def x1_own_loader(t, dst, res):
                for a in range(2):
                    S.dma('sp', lambda e, a=a: e.dma_start(out=selt[a][:, :], in_=x1_s[a][t * 128:(t + 1) * 128, :]), writes=['selt%d' % a])
                S.op('act', lambda e: e.activation(out=selt[0][:, :], in_=selt[0][:, :], func=AF.Copy, scale=msel[:, 0:1]),
                     reads=['selt0'], writes=['selt0'])
                S.op('dve', lambda e: e.scalar_tensor_tensor(out=dst[:, :], in0=selt[1][:, :], scalar=msel[:, 1:2], in1=selt[0][:, :], op0=ALU.mult, op1=ALU.add),
                     reads=['selt0', 'selt1'], writes=[res])

            GT = sbt(stg, "GT", [128, 8, OWN], BF16)
            phaseQ(x1_own_loader, WL[1], C, vecsL[1], GT)
            mixT = sbt(stg, "mixT", [128, 8, OWN], BF16)
            phaseA(GT, mixT, F_sL["m"])
            phaseO(x1_own_loader, C["p"], WL[1], vecsL[1], mixT, out_d)
        S.emit()
    return nc


def _t5_bucket(d):
    d = np.maximum(d, 0)
    df = np.maximum(d, 1).astype(np.float32)
    large = 16 + (np.log(df / np.float32(16)) / np.float32(math.log(2048 / 16)) * np.float32(16)).astype(np.int32)
    large = np.minimum(large, 31)
    return np.where(d < 16, d, large)


def _core_consts(par):
    bf = ml_dtypes.bfloat16
    c = {}
    i = np.arange(FL)
    d = i + 256 * par - 511
    OH = np.zeros((34, FL), np.float32)
    bk = _t5_bucket(d)
    valid = d >= 0
    OH[bk[valid], i[valid]] = 1.0
    OH[32, ~valid] = NEG
    mult = ((d <= 128).astype(np.float32) + ((d % 4 == 0) & (d <= 512)).astype(np.float32)
            + ((d % 16 == 0) & (d <= 2048)).astype(np.float32))
    ok = valid & (mult > 0)
    OH[33, :] = NEG
    OH[33, ok] = np.log(mult[ok]).astype(np.float32)
    c["OH"] = OH
    Sel = np.zeros((128, 4, 256), np.float32)
    for jj in range(4):
        for k in range(128):
            q = 128 * jj + k - 256 * par
            if 0 <= q < 256:
                Sel[k, jj, q] = 1.0
    c["Sel"] = Sel
    VB = np.zeros((128, 16, 16), np.float32)
    VM = np.zeros((128, 16, 16), np.float32)
    for qt in range(16):
        own = 2 * (qt // 2) + par
        VB[:, qt, own:] = -1e9
        VM[:, qt, :own] = 1.0
    c["VB"], c["VM"] = VB, VM
    half = 16
    inv = (1.0 / (np.float32(10000.0) ** (np.arange(half, dtype=np.float32) * np.float32(2.0) / np.float32(32)))).astype(np.float32)
    pos = np.arange(SEQ).astype(np.float32)
    ang = pos[:, None] * inv[None, :]
    cos, sin = np.cos(ang).astype(np.float32).T, np.sin(ang).astype(np.float32).T
    c["CK"] = np.ascontiguousarray(np.concatenate([cos, cos], 0))
    c["SK"] = np.ascontiguousarray(np.concatenate([-sin, sin], 0))
    own_idx = np.concatenate([512 * s + 256 * par + np.arange(256) for s in range(8)])
    sc = np.float32(96 ** -0.5)
    CT = np.full((96, OWN), sc, np.float32)
    ST = np.zeros((96, OWN), np.float32)
    CT[64:96] = np.concatenate([cos, cos], 0)[:, own_idx] * sc
    ST[64:96] = np.concatenate([-sin, sin], 0)[:, own_idx] * sc
    c["CTq"], c["STq"] = CT, ST
    c["own_idx"] = own_idx
    return c


def _shared_consts():
    bf = ml_dtypes.bfloat16
    cb = np.zeros((128, NCB), np.float32)
    for g in range(2):
        cb[g * 64:(g + 1) * 64, C_BD64 + g * 64:C_BD64 + (g + 1) * 64] = 1.0 / 64
    cb[:, C_B128:C_B128 + 128] = 1.0 / 128
    cb[:, C_B256:C_B256 + 128] = 1.0 / 256
    cb[0:64, C_BDMLA:C_BDMLA + 64] = 1.0 / 64
    cb[64:96, C_BDMLA + 64:C_BDMLA + 96] = 1.0 / 32
    cb[0:32, C_B32:C_B32 + 32] = 1.0 / 32
    cb[:, C_ONES:C_ONES + 64] = 1.0
    cb[:, C_J:C_J + 128] = np.eye(128, dtype=np.float32)[::-1]
    cb[:, C_ID:C_ID + 128] = np.eye(128, dtype=np.float32)
    oh16 = np.zeros((16, SEQ), np.float32)
    for n in range(16):
        oh16[n, n * 256:(n + 1) * 256] = 1.0
    return {"cbf": cb.astype(bf), "oh16": oh16.astype(bf), "ones3": np.ones((3, SEQ), bf),
            "identf": np.eye(128, dtype=np.float32)}


def _kc(w):
    return np.ascontiguousarray(w.reshape(8, 128, -1).transpose(1, 0, 2))


def _layer_weights(l, ln_g, w_in, b_forget, qk_gain, mla_q_norm, mla_kv_norm, mla_nope_gain, mla_rope_gain,
                   w_uq, w_ukv, w_out, rel_bias, ple_norm_g, w_ple_gate, w_ple_proj):
    W = w_in[l]
    fq, fk, fv, ff = W[:, 0:256], W[:, 256:512], W[:, 512:768], W[:, 768:772]
    mq, mk, mv = W[:, 772:1028], W[:, 1028:1284], W[:, 1284:1540]
    dq, dk, dv = W[:, 1540:1796], W[:, 1796:2052], W[:, 2052:2308]
    cq, ckv, kr, gate = W[:, 2308:2564], W[:, 2564:2692], W[:, 2692:2724], W[:, 2724:3748]
    kr_sw = np.concatenate([kr[:, 16:32], kr[:, 0:16]], 1)
    wK = np.concatenate([fk, mk, dk, fv, mv, dv, ckv, kr, kr_sw, ff, np.zeros((1024, 4), np.float32)], 1)
    wQ = np.concatenate([fq, mq, dq, cq, gate], 1)
    uq = w_uq[l]
    uqB = uq.copy()
    for h in range(4):
        uqB[:, 96 * h + 64:96 * h + 80] = uq[:, 96 * h + 80:96 * h + 96]
        uqB[:, 96 * h + 80:96 * h + 96] = uq[:, 96 * h + 64:96 * h + 80]
    ukv = w_ukv[l]
    ukvK = np.concatenate([ukv[:, 128 * h:128 * h + 64] for h in range(4)], 1)
    ukvV = np.concatenate([ukv[:, 128 * h + 64:128 * h + 128] for h in range(4)], 1)
    vec = np.zeros((128, NV), np.float32)
    vec[:, V_LNG:V_LNG + 8] = ln_g[l].reshape(8, 128).T
    for pi in range(6):
        m = pi // 2
        vec[:, V_GK + pi] = np.tile(qk_gain[l, 2 * m + 1], 2)
        vec[:, V_GQ + pi] = np.tile(qk_gain[l, 2 * m], 2)
    vec[:, V_GQN:V_GQN + 2] = mla_q_norm[l].reshape(2, 128).T
    vec[:, V_GKV] = mla_kv_norm[l]
    vec[:, V_GKN] = np.tile(mla_nope_gain[l, 1], 2)
    rg0, rg1 = mla_rope_gain[l, 0], mla_rope_gain[l, 1]
    vec[0:96, V_GA] = np.concatenate([mla_nope_gain[l, 0], rg0])
    vec[0:96, V_GB] = np.concatenate([mla_nope_gain[l, 0], rg0[16:32], rg0[0:16]])
    vec[0:32, V_GR] = rg1
    vec[0:32, V_GR + 1] = np.concatenate([rg1[16:32], rg1[0:16]])
    vec[0:4, V_BF] = b_forget[l]
    vec[:, V_PLEG:V_PLEG + 8] = ple_norm_g[l].reshape(8, 128).T
    tabX = np.zeros((34, 9), np.float32)
    tabX[0:32, 0:8] = rel_bias
    tabX[32, 0:4] = 1.0
    tabX[33, 4:8] = 1.0
    tabX[32, 8] = 1.0
    return {"wK": _kc(wK), "wQ": _kc(wQ),
            "wuqA": np.ascontiguousarray(uq.reshape(2, 128, 384).transpose(1, 0, 2)),
            "wuqB": np.ascontiguousarray(uqB.reshape(2, 128, 384).transpose(1, 0, 2)),
            "wukvK": np.ascontiguousarray(ukvK.reshape(128, 1, 256)), "wukvV": np.ascontiguousarray(ukvV.reshape(128, 1, 256)),
            "wout": _kc(w_out[l]), "wpg": _kc(w_ple_gate[l]),
            "wpp": np.ascontiguousarray(w_ple_proj[l].reshape(2, 128, 1024).transpose(1, 0, 2)),
            "vecs": vec, "tabX": tabX}


_NC = None


def kernel(x, p, ln_g, w_in, b_forget, qk_gain, mla_q_norm, mla_kv_norm, mla_nope_gain, mla_rope_gain,
           w_uq, w_ukv, w_out, rel_bias, ple_norm_g, w_ple_gate, w_ple_proj):
    global _NC
    args = [np.asarray(a, dtype=np.float32) for a in (ln_g, w_in, b_forget, qk_gain, mla_q_norm, mla_kv_norm, mla_nope_gain,
                                                     mla_rope_gain, w_uq, w_ukv, w_out, rel_bias, ple_norm_g, w_ple_gate, w_ple_proj)]
    x = np.asarray(x, dtype=np.float32)
    p = np.asarray(p, dtype=np.float32)
    if _NC is None:
        _NC = build_fused()
    shared = _shared_consts()
    cc = [_core_consts(par) for par in range(2)]
    base = dict(shared)
    for l in range(2):
        lw = _layer_weights(l, *args)
        base["tabX"] = lw.pop("tabX")
        for k, v in lw.items():
            base[k + "_l%d" % l] = v
    base["CK"], base["SK"] = cc[0]["CK"], cc[0]["SK"]
    for a in range(2):
        for k in ("OH", "Sel", "VB", "VM", "CTq", "STq"):
            base[k + "_a%d" % a] = cc[a][k]
    in_maps = []
    for core in range(8):
        b, par = core // 2, core % 2
        mp = dict(base)
        mp["x_all"] = np.ascontiguousarray(x[b])
        for a in range(2):
            mp["x_own_a%d" % a] = np.ascontiguousarray(x[b][cc[a]["own_idx"]])
            mp["p_a%d" % a] = np.ascontiguousarray(p[0, b][cc[a]["own_idx"]])
        mp["p_m"] = np.ascontiguousarray(p[1, b][cc[par]["own_idx"]])
        for k in ("OH", "Sel", "VB", "VM", "CTq", "STq"):
            mp[k + "_m"] = cc[par][k]
        ms = np.zeros((128, 2), np.float32)
        ms[:, par] = 1.0
        mp["msel"] = ms
        in_maps.append(mp)
    res = run_bass_kernel_spmd(_NC, in_maps, core_ids=list(range(8)))
    out = np.empty_like(x)
    for core in range(8):
        b, par = core // 2, core % 2
        out[b][cc[par]["own_idx"]] = np.asarray(res.results[core]["out"], dtype=np.float32)
    return out
```

```python
import contextlib
import math
import numpy as np
import ml_dtypes
import concourse.bass as bass
import concourse.mybir as mybir
from concourse.bass_utils import run_bass_kernel_spmd

F32 = mybir.dt.float32
BF16 = mybir.dt.bfloat16
AF = mybir.ActivationFunctionType
ALU = mybir.AluOpType
AX = mybir.AxisListType

COMPUTE = ('pe', 'act', 'dve', 'pool')
NDMA_SEM = 16
SEQ = 4096
OWN = 2048
NEG = -30000.0
WG = 2688
FL = WG + 128
WC = 640
EPS = 1e-6


class Sched:
    def __init__(self, nc, stack):
        self.nc = nc
        self.ops = {e: [] for e in ('pe', 'act', 'dve', 'pool', 'sp')}
        self.psem = {e: stack.enter_context(nc.semaphore("pg_" + e)) for e in COMPUTE}
        self.pcnt = {e: 0 for e in COMPUTE}
        self.dsem = {q: [stack.enter_context(nc.semaphore("dq_%s%d" % (q, i))) for i in range(NDMA_SEM)]
                     for q in ('sp', 'pool')}
        self.dval = {q: [0] * NDMA_SEM for q in ('sp', 'pool')}
        self.didx = {q: 0 for q in ('sp', 'pool')}
        self.waited = {e: {} for e in self.ops}
        self.res = {}

    def _need(self, eng, tok, waits):
        if tok is None:
            return
        key, sem, val, prod = tok
        if prod == eng and eng == 'pe':
            return
        if self.waited[eng].get(key, 0) >= val:
            return
        self.waited[eng][key] = val
        waits.append((sem, val))

    def _deps(self, eng, reads, writes):
        waits = []
        for r in reads:
            st = self.res.get(r)
            if st is not None:
                self._need(eng, st[0], waits)
        for w in writes:
            st = self.res.get(w)
            if st is not None:
                self._need(eng, st[0], waits)
                for t in st[1]:
                    self._need(eng, t, waits)
        return waits

    def _record(self, tok, reads, writes):
        for r in reads:
            st = self.res.setdefault(r, [None, []])
            st[1].append(tok)
        for w in writes:
            self.res[w] = [tok, []]

    def op(self, eng, fn, reads=(), writes=(), sig=True):
        waits = self._deps(eng, reads, writes)
        tok = None
        if sig:
            self.pcnt[eng] += 1
            tok = ('p' + eng, self.psem[eng], self.pcnt[eng], eng)
            self._record(tok, reads, writes)
        self.ops[eng].append((waits, fn, (self.psem[eng], 1) if sig else None))
        return tok

    def dma(self, q, fn, reads=(), writes=()):
        waits = self._deps(q, reads, writes)
        i = self.didx[q]
        self.didx[q] = (i + 1) % NDMA_SEM
        sem = self.dsem[q][i]
        key = 'd%s%d' % (q, i)
        prev = self.dval[q][i]
        if prev > 0 and self.waited[q].get(key, 0) < prev:
            self.waited[q][key] = prev
            waits.append((sem, prev))
        self.dval[q][i] = prev + 16
        tok = (key, sem, prev + 16, 'dma')
        self._record(tok, reads, writes)
        self.ops[q].append((waits, fn, (sem, 16)))
        return tok

    def barrier(self):
        toks = []
        for e in COMPUTE:
            if self.pcnt[e] > 0:
                toks.append(('p' + e, self.psem[e], self.pcnt[e], e))
        for q in ('sp', 'pool'):
            for i in range(NDMA_SEM):
                if self.dval[q][i] > 0:
                    toks.append(('d%s%d' % (q, i), self.dsem[q][i], self.dval[q][i], 'dma'))
        for e in self.ops:
            waits = []
            for t in toks:
                key, sem, val, prod = t
                if self.waited[e].get(key, 0) >= val:
                    continue
                self.waited[e][key] = val
                waits.append((sem, val))
            if waits:
                self.ops[e].append((waits, None, None))
        self.res = {}

    def emit(self):
        nc = self.nc
        with nc.Block() as block:
            def mk(name):
                def body(eng):
                    for waits, fn, sig in self.ops[name]:
                        for sem, val in waits:
                            eng.wait_ge(sem, val)
                        if fn is None:
                            continue
                        ins = fn(eng)
                        if sig is not None:
                            ins.then_inc(sig[0], sig[1])
                return body
            block.tensor(mk('pe'))
            block.scalar(mk('act'))
            block.vector(mk('dve'))
            block.gpsimd(mk('pool'))
            block.sync(mk('sp'))


V_LNG, V_GK, V_GQ, V_GQN, V_GKV, V_GKN, V_GA, V_GB, V_GR, V_BF, V_PLEG, NV = 0, 8, 14, 20, 22, 23, 24, 25, 26, 28, 29, 40
C_BD64, C_B128, C_B256, C_BDMLA, C_B32, C_ONES, C_J, C_ID, NCB = 0, 128, 256, 384, 480, 512, 576, 704, 832
K_K, K_V, K_CKV, K_KR, K_KRS, K_FF, NWK = 0, 768, 1536, 1664, 1696, 1728, 1736
Q_Q, Q_CQ, Q_GATE, NWQ = 0, 768, 1024, 2048


def build_fused():
    nc = bass.Bass("TRN2", target_bir_lowering=False)

    def din(name, shape, dt=F32):
        return nc.dram_tensor(name, shape, dt, kind="ExternalInput").ap()

    x_all = din("x_all", [SEQ, 1024])
    identf_d = din("identf", [128, 128])
    CK_d = din("CK", [32, SEQ])
    SK_d = din("SK", [32, SEQ])
    oh16_d = din("oh16", [16, SEQ], BF16)
    ones3_d = din("ones3", [3, SEQ], BF16)
    cbf_d = din("cbf", [128, NCB], BF16)
    tabX_d = din("tabX", [34, 9])
    msel_d = din("msel", [128, 2])
    WL = []
    for l in range(2):
        sfx = "_l%d" % l
        WL.append({"wK": din("wK" + sfx, [128, 8, NWK]), "wQ": din("wQ" + sfx, [128, 8, NWQ]),
                   "wuqA": din("wuqA" + sfx, [128, 2, 384]), "wuqB": din("wuqB" + sfx, [128, 2, 384]),
                   "wukvK": din("wukvK" + sfx, [128, 1, 256]), "wukvV": din("wukvV" + sfx, [128, 1, 256]),
                   "wout": din("wout" + sfx, [128, 8, 1024]), "wpg": din("wpg" + sfx, [128, 8, 1024]),
                   "wpp": din("wpp" + sfx, [128, 2, 1024]), "vecs": din("vecs" + sfx, [128, NV])})
    CS = {}
    for c in ("a0", "a1", "m"):
        CS[c] = {"OH": din("OH_" + c, [34, FL]), "Sel": din("Sel_" + c, [128, 4, 256]), "VB": din("VB_" + c, [128, 16, 16]),
                 "VM": din("VM_" + c, [128, 16, 16]), "CTq": din("CTq_" + c, [96, OWN]), "STq": din("STq_" + c, [96, OWN]),
                 "p": din("p_" + c, [OWN, 256])}
    xown_d = [din("x_own_a%d" % a, [OWN, 1024]) for a in range(2)]
    out_d = nc.dram_tensor("out", [OWN, 1024], F32, kind="ExternalOutput").ap()

    KT_s = nc.dram_tensor("KT_s", [16, 128, SEQ], BF16, kind="Internal").ap()
    QT_s = nc.dram_tensor("QT_s", [16, 128, OWN], BF16, kind="Internal").ap()
    V_s = nc.dram_tensor("V_s", [SEQ, 1024], BF16, kind="Internal").ap()
    F_sL = {c: nc.dram_tensor("F_s_" + c, [9, FL], BF16, kind="Internal").ap() for c in ("a0", "a1", "m")}
    x1_s = [nc.dram_tensor("x1_s%d" % a, [OWN, 1024], F32, kind="Internal").ap() for a in range(2)]

    with contextlib.ExitStack() as top:
        S = Sched(nc, top)

        uid = [0]

        def sbt(st, n, shp, dt):
            uid[0] += 1
            return st.enter_context(nc.sbuf_tensor('s%d_%s' % (uid[0], n), shp, dt))

        def pst(st, n, shp, dt):
            uid[0] += 1
            return st.enter_context(nc.psum_tensor('p%d_%s' % (uid[0], n), shp, dt))

        vecsL = [sbt(top, "vecs%d" % l, [128, NV], F32) for l in range(2)]
        msel = sbt(top, "msel", [128, 2], F32)
        cbf = sbt(top, "cbf", [128, NCB], BF16)
        identf = sbt(top, "identf", [128, 128], F32)
        kmT = [sbt(top, "kmT%d" % i, [128, 16], BF16) for i in range(2)]
        kmBD = [sbt(top, "kmBD%d" % i, [128, 32], BF16) for i in range(2)]
        cTm = sbt(top, "cTm", [128, 128], F32)
        negb = sbt(top, "negb", [128, 1], F32)

        for l in range(2):
            S.dma('sp', lambda e, l=l: e.dma_start(out=vecsL[l][:], in_=WL[l]["vecs"][:]))
        S.dma('sp', lambda e: e.dma_start(out=msel[:], in_=msel_d[:]))
        S.dma('sp', lambda e: e.dma_start(out=cbf[:], in_=cbf_d[:]))
        S.dma('sp', lambda e: e.dma_start(out=identf[:], in_=identf_d[:]))
        S.barrier()
        ident = cbf[:, C_ID:C_ID + 128]
        bd64 = cbf[:, C_BD64:C_BD64 + 128]
        b128 = cbf[:, C_B128:C_B128 + 128]
        b256 = cbf[:, C_B256:C_B256 + 128]
        bdmla = cbf[0:96, C_BDMLA:C_BDMLA + 96]
        b32 = cbf[0:32, C_B32:C_B32 + 32]
        ones64 = cbf[:, C_ONES:C_ONES + 64]
        Jm = cbf[:, C_J:C_J + 128]

        def load_w(st, name, src, nk, ncols, gcol=None, vecs=None):
            w = sbt(st, name, [128, nk, ncols], BF16)
            stg = [sbt(st, name + "_stg%d" % i, [128, ncols], F32) for i in range(2)]
            for kc in range(nk):
                b = kc % 2
                S.dma('sp', lambda e, b=b, kc=kc: e.dma_start(out=stg[b][:, :], in_=src[:, kc, :]), writes=[name + 'stg%d' % b])
                if kc % 2 == 0:
                    if gcol is None:
                        S.op('dve', lambda e, b=b, kc=kc: e.tensor_copy(out=w[:, kc, :], in_=stg[b][:, :]), reads=[name + 'stg%d' % b])
                    else:
                        S.op('dve', lambda e, b=b, kc=kc: e.tensor_scalar(out=w[:, kc, :], in0=stg[b][:, :],
                                                                         scalar1=vecs[:, gcol + kc:gcol + kc + 1], scalar2=None, op0=ALU.mult),
                             reads=[name + 'stg%d' % b])
                else:
                    if gcol is None:
                        S.op('act', lambda e, b=b, kc=kc: e.activation(out=w[:, kc, :], in_=stg[b][:, :], func=AF.Copy), reads=[name + 'stg%d' % b])
                    else:
                        S.op('act', lambda e, b=b, kc=kc: e.activation(out=w[:, kc, :], in_=stg[b][:, :], func=AF.Copy,
                                                                       scale=vecs[:, gcol + kc:gcol + kc + 1]),
                             reads=[name + 'stg%d' % b])
            return w

        def norm_funcs(st, xload, hT, pfx, hres=None):
            NB = 4
            xt = [sbt(st, pfx + "xt%d" % i, [128, 1024], F32) for i in range(NB)]
            junk = sbt(st, pfx + "junk", [128, 1024], BF16)
            xn = [sbt(st, pfx + "xn%d" % i, [128, 1024], BF16) for i in range(NB)]
            ss = [sbt(st, pfx + "ss%d" % i, [128, 4], F32) for i in range(NB)]
            pT = [pst(st, pfx + "pT%d" % i, [128, 1024], BF16) for i in range(2)]

            def front(t):
                b = t % NB
                X, N, SS = pfx + 'xt%d' % b, pfx + 'xn%d' % b, pfx + 'ss%d' % b
                xload(t, xt[b], X)
                S.op('act', lambda e: e.activation(out=junk[:], in_=xt[b][:], func=AF.Square, accum_out=ss[b][:, 0:1]),
                     reads=[X], writes=[pfx + 'junk', SS])
                S.op('act', lambda e: e.activation(out=ss[b][:, 1:2], in_=ss[b][:, 0:1], func=AF.Ln, scale=1.0 / 1024, bias=EPS),
                     reads=[SS], writes=[SS])
                S.op('act', lambda e: e.activation(out=ss[b][:, 2:3], in_=ss[b][:, 1:2], func=AF.Exp, scale=-0.5),
                     reads=[SS], writes=[SS])
                S.op('dve', lambda e: e.tensor_scalar(out=xn[b][:], in0=xt[b][:], scalar1=ss[b][:, 2:3], scalar2=None, op0=ALU.mult),
                     reads=[X, SS], writes=[N])

            def back(t):
                b, pb_ = t % NB, t % 2
                N, P = pfx + 'xn%d' % b, pfx + 'pT%d' % pb_
                wr = [hres % t] if hres else []
                for c in range(8):
                    S.op('pe', lambda e, c=c: e.transpose(out=pT[pb_][:, c * 128:(c + 1) * 128], in_=xn[b][:, c * 128:(c + 1) * 128], identity=ident),
                         reads=[N], writes=[P], sig=(c == 7))
                if t % 2 == 0:
                    S.op('dve', lambda e: e.tensor_copy(out=hT[:, :, t * 128:(t + 1) * 128], in_=pT[pb_][:].rearrange("p (c n) -> p c n", c=8)), reads=[P], writes=wr)
                else:
                    S.op('act', lambda e: e.activation(out=hT[:, :, t * 128:(t + 1) * 128], in_=pT[pb_][:].rearrange("p (c n) -> p c n", c=8), func=AF.Copy), reads=[P], writes=wr)
            return front, back

        def norm_phase(st, xload, ntiles, hT, pfx):
            front, back = norm_funcs(st, xload, hT, pfx)
            LA = 3
            for i in range(ntiles + LA):
                if i < ntiles:
                    front(i)
                if i >= LA:
                    back(i - LA)

        class RmsFeat:
            def __init__(self, st):
                self.sq = [[sbt(st, "rf_sq%d_%d" % (i, a), [128, 512], BF16) for a in range(2)] for i in range(2)]
                self.lnv = [sbt(st, "rf_ln%d" % i, [128, 512], F32) for i in range(2)]
                self.rstd = [sbt(st, "rf_rs%d" % i, [128, 512], F32) for i in range(2)]
                self.psB = [pst(st, "rf_psB%d" % i, [128, 512], F32) for i in range(2)]
                self.cnt = 0

            def __call__(self, As, nstat, P, bm, gains, ebias, outs, n=512):
                i = self.cnt % 2
                self.cnt += 1
                for a in range(nstat):
                    S.op('act', lambda e, a=a: e.activation(out=self.sq[i][a][0:P, 0:n], in_=As[a][0], func=AF.Square),
                         reads=[As[a][1]], writes=['rf_sq%d_%d' % (i, a)])
                for a in range(nstat):
                    S.op('pe', lambda e, a=a: e.matmul(self.psB[i][0:P, 0:n], lhsT=bm, rhs=self.sq[i][a][0:P, 0:n], start=(a == 0), stop=(a == nstat - 1)),
                         reads=['rf_sq%d_%d' % (i, aa) for aa in range(nstat)], writes=['rf_psB%d' % i], sig=(a == nstat - 1))
                S.op('act', lambda e: e.activation(out=self.lnv[i][0:P, 0:n], in_=self.psB[i][0:P, 0:n], func=AF.Ln, bias=EPS),
                     reads=['rf_psB%d' % i], writes=['rf_ln%d' % i])
                S.op('act', lambda e: e.activation(out=self.rstd[i][0:P, 0:n], in_=self.lnv[i][0:P, 0:n], func=AF.Exp, scale=-0.5, bias=ebias),
                     reads=['rf_ln%d' % i], writes=['rf_rs%d' % i])
                for a in range(len(As)):
                    S.op('dve', lambda e, a=a: e.scalar_tensor_tensor(out=outs[a][0], in0=As[a][0], scalar=gains[a], in1=self.rstd[i][0:P, 0:n],
                                                                      op0=ALU.mult, op1=ALU.mult),
                         reads=[As[a][1], 'rf_rs%d' % i], writes=[outs[a][1]])

        class Pipe:
            def __init__(self):
                self.items = []

            def add(self, A, B=None, dep_prev=False, depth=1):
                self.items.append((A, B, 0 if dep_prev else depth))

            def run(self):
                n = len(self.items)
                slots = [[] for _ in range(n)]
                for i, (A, B, d) in enumerate(self.items):
                    slots[max(0, i - d)].append(A)
                for i in range(n):
                    for A in slots[i]:
                        A()
                    if self.items[i][1] is not None:
                        self.items[i][1]()
                self.items = []

        def split3(st, pfx, src, npart, n):
            hi = sbt(st, pfx + "hi", [npart, n], BF16)
            mid = sbt(st, pfx + "mid", [npart, n], BF16)
            lo = sbt(st, pfx + "lo", [npart, n], BF16)
            r1 = sbt(st, pfx + "r1", [npart, n], F32)
            S.op('dve', lambda e: e.tensor_copy(out=hi[:], in_=src), reads=[pfx + 'src'], writes=[pfx + 'hi'])
            S.op('dve', lambda e: e.tensor_tensor(out=r1[:], in0=src, in1=hi[:], op=ALU.subtract), reads=[pfx + 'src', pfx + 'hi'], writes=[pfx + 'r1'])
            S.op('dve', lambda e: e.tensor_copy(out=mid[:], in_=r1[:]), reads=[pfx + 'r1'], writes=[pfx + 'mid'])
            S.op('dve', lambda e: e.tensor_tensor(out=r1[:], in0=r1[:], in1=mid[:], op=ALU.subtract), reads=[pfx + 'r1', pfx + 'mid'], writes=[pfx + 'r1'])
            S.op('dve', lambda e: e.tensor_copy(out=lo[:], in_=r1[:]), reads=[pfx + 'r1'], writes=[pfx + 'lo'])
            return hi, mid, lo

        def phase0(OH_d, F_s):
            with contextlib.ExitStack() as st:
                tabX = sbt(st, "tabX", [34, 9], F32)
                OH = sbt(st, "OH", [34, FL], F32)
                Fsb = sbt(st, "Fsb", [9, FL], BF16)
                psF = [pst(st, "psF%d" % i, [128, 512], F32) for i in range(2)]
                S.dma('sp', lambda e: e.dma_start(out=tabX[:], in_=tabX_d[:]), writes=['tabX'])
                S.dma('sp', lambda e: e.dma_start(out=OH[:], in_=OH_d[:]), writes=['OH'])
                nch = (FL + 511) // 512
                for ci in range(nch):
                    c0 = ci * 512
                    cw = min(512, FL - c0)
                    b = ci % 2
                    S.op('pe', lambda e, b=b, c0=c0, cw=cw: e.matmul(psF[b][0:9, 0:cw], lhsT=tabX[0:34, 0:9], rhs=OH[0:34, c0:c0 + cw], start=True, stop=True),
                         reads=['tabX', 'OH'], writes=['psF%d' % b])
                    S.op('dve', lambda e, b=b, c0=c0, cw=cw: e.tensor_copy(out=Fsb[0:9, c0:c0 + cw], in_=psF[b][0:9, 0:cw]),
                         reads=['psF%d' % b], writes=['Fsb'])
                S.dma('sp', lambda e: e.dma_start(out=F_s[:], in_=Fsb[:]), reads=['Fsb'])

        def phaseK(xload, W, vecs):
            stKC = contextlib.ExitStack()
            ffT = sbt(stKC, "ffT", [4, SEQ], F32)
            kmsum = [sbt(stKC, "kmsum%d" % i, [128, 16], F32) for i in range(2)]
            with contextlib.ExitStack() as st:
                hTa = sbt(st, "hTa", [128, 8, SEQ], BF16)
                nfront, nback = norm_funcs(st, xload, hTa, "na_", hres="hTa_t%d")
                wK = load_w(st, "wK", W["wK"], 8, NWK, V_LNG, vecs)
                wukvK_ = load_w(st, "wukvK", W["wukvK"], 1, 256)
                wukvK = wukvK_[:, 0, :]
                wukvV_ = load_w(st, "wukvV", W["wukvV"], 1, 256)
                wukvV = wukvV_[:, 0, :]
                S.barrier()
                rfk = RmsFeat(st)
                psAk = [pst(st, "psAk%d" % i, [128, 512], F32) for i in range(4)]
                kst = [sbt(st, "kst%d" % i, [128, 512], BF16) for i in range(3)]
                vst = [sbt(st, "vst%d" % i, [128, 768], BF16) for i in range(2)]
                vst2 = [sbt(st, "vst2_%d" % i, [128, 256], BF16) for i in range(2)]
                ckvn = [sbt(st, "ckvn%d" % i, [128, 512], BF16) for i in range(2)]
                xa = sbt(st, "xa", [32, 512], F32)
                xb = sbt(st, "xb", [32, 512], F32)
                ckt = [sbt(st, "ckt%d" % i, [32, 512], F32) for i in range(2)]
                skt = [sbt(st, "skt%d" % i, [32, 512], F32) for i in range(2)]
                rst = [sbt(st, "rst%d" % i, [32, 512], BF16) for i in range(2)]
                acnt = [0]
                kcnt = [0]

                def nextAk():
                    i = acnt[0] % 4
                    acnt[0] += 1
                    return i

                def proj(ai, col0, ncol, T0, n=512):
                    hr = ['hTa_t%d' % t for t in range(T0 // 128, (T0 + n) // 128)]
                    for kc in range(8):
                        S.op('pe', lambda e, kc=kc: e.matmul(psAk[ai][0:ncol, 0:n], lhsT=wK[:, kc, col0:col0 + ncol], rhs=hTa[:, kc, T0:T0 + n],
                                                             start=(kc == 0), stop=(kc == 7)),
                             reads=hr, writes=['psAk%d' % ai], sig=(kc == 7))

                pipe = Pipe()

                def add_norm(t):
                    pipe.add(lambda t=t: nfront(t), lambda t=t: nback(t), depth=3)
                for t in range(4):
                    add_norm(t)
                for ch in range(8):
                    T0 = ch * 512
                    cb = ch % 2
                    tb = ch % 2
                    nxt = [4 * (ch + 1) + i for i in range(4)] if ch < 7 else []
                    for pi in range(6):
                        ai = nextAk()
                        kb = kcnt[0] % 3
                        kcnt[0] += 1

                        def A(ai=ai, pi=pi, T0=T0):
                            proj(ai, K_K + pi * 128, 128, T0)

                        def B(ai=ai, pi=pi, T0=T0, kb=kb, ch=ch):
                            rfk([(psAk[ai][:, :], 'psAk%d' % ai)], 1, 128, bd64, [vecs[:, V_GK + pi:V_GK + pi + 1]], 0.0,
                                [(kst[kb][:, :], 'kst%d' % kb)])
                            hd = 4 * (pi // 2) + 2 * (pi % 2)
                            for hh in range(2):
                                S.dma('pool', lambda e, hh=hh: e.dma_start(out=KT_s[hd + hh, 0:64, T0:T0 + 512], in_=kst[kb][hh * 64:(hh + 1) * 64, :]),
                                      reads=['kst%d' % kb])
                            if pi // 2 == 1:
                                pp = pi % 2
                                S.op('dve', lambda e: e.tensor_reduce(out=kmsum[pp][:, 2 * ch:2 * ch + 2],
                                                                      in_=kst[kb][:, :].rearrange("p (a b) -> p a b", a=2), axis=AX.X, op=ALU.add),
                                     reads=['kst%d' % kb], writes=['kmsum%d' % pp])
                        pipe.add(A, B, dep_prev=(ch == 0 and pi == 0))
                        if pi in (1, 3) and nxt:
                            add_norm(nxt.pop(0))
                    for tt in range(4):
                        v0 = nextAk()
                        v1 = nextAk()

                        def A(tt=tt, T0=T0, ch=ch, v0=v0, v1=v1):
                            tok = T0 + tt * 128
                            vb = (ch * 4 + tt) % 2
                            for (vi, c0, cw) in ((v0, 0, 512), (v1, 512, 256)):
                                for kc in range(8):
                                    S.op('pe', lambda e, kc=kc, vi=vi, c0=c0, cw=cw: e.matmul(
                                        psAk[vi][:, 0:cw], lhsT=hTa[:, kc, tok:tok + 128], rhs=wK[:, kc, K_V + c0:K_V + c0 + cw],
                                        start=(kc == 0), stop=(kc == 7)), reads=['hTa_t%d' % (tok // 128)], writes=['psAk%d' % vi], sig=(kc == 7))
                            S.op('act', lambda e: e.activation(out=vst[vb][:, 0:512], in_=psAk[v0][:, 0:512], func=AF.Copy), reads=['psAk%d' % v0], writes=['vst%d' % vb])
                            S.op('dve', lambda e: e.tensor_copy(out=vst[vb][:, 512:768], in_=psAk[v1][:, 0:256]), reads=['psAk%d' % v1], writes=['vst%d' % vb])
                            S.dma('pool', lambda e: e.dma_start(out=V_s[tok:tok + 128, 0:768], in_=vst[vb][:, :]), reads=['vst%d' % vb])
                        pipe.add(A)
                        if tt in (0, 2) and nxt:
                            add_norm(nxt.pop(0))
                    ai = nextAk()

                    def A(ai=ai, T0=T0):
                        proj(ai, K_CKV, 128, T0)

                    def B(ai=ai, cb=cb):
                        rfk([(psAk[ai][:, :], 'psAk%d' % ai)], 1, 128, b128, [vecs[:, V_GKV:V_GKV + 1]], 0.0, [(ckvn[cb][:, :], 'ckvn%d' % cb)])
                    pipe.add(A, B)
                    for pp in range(2):
                        ai = nextAk()
                        kb = kcnt[0] % 3
                        kcnt[0] += 1

                        def A(ai=ai, pp=pp, cb=cb):
                            S.op('pe', lambda e: e.matmul(psAk[ai][:, :], lhsT=wukvK[:, pp * 128:(pp + 1) * 128], rhs=ckvn[cb][:, :], start=True, stop=True),
                                 reads=['ckvn%d' % cb], writes=['psAk%d' % ai])

                        def B(ai=ai, pp=pp, kb=kb, T0=T0):
                            rfk([(psAk[ai][:, :], 'psAk%d' % ai)], 1, 128, bd64, [vecs[:, V_GKN:V_GKN + 1]], 0.0, [(kst[kb][:, :], 'kst%d' % kb)])
                            for hh in range(2):
                                S.dma('pool', lambda e, hh=hh: e.dma_start(out=KT_s[12 + 2 * pp + hh, 0:64, T0:T0 + 512], in_=kst[kb][hh * 64:(hh + 1) * 64, :]),
                                      reads=['kst%d' % kb])
                        pipe.add(A, B, dep_prev=(pp == 0))
                    for tt in range(4):
                        v1 = nextAk()

                        def A(tt=tt, T0=T0, ch=ch, cb=cb, v1=v1):
                            tok = T0 + tt * 128
                            vb = (ch * 4 + tt) % 2
                            S.op('pe', lambda e: e.matmul(psAk[v1][:, 0:256], lhsT=ckvn[cb][:, tt * 128:(tt + 1) * 128], rhs=wukvV, start=True, stop=True),
                                 reads=['ckvn%d' % cb], writes=['psAk%d' % v1])
                            S.op('dve', lambda e: e.tensor_copy(out=vst2[vb][:, :], in_=psAk[v1][:, 0:256]), reads=['psAk%d' % v1], writes=['vst2_%d' % vb])
                            S.dma('pool', lambda e: e.dma_start(out=V_s[tok:tok + 128, 768:1024], in_=vst2[vb][:, :]), reads=['vst2_%d' % vb])
                        pipe.add(A)
                    aiA = nextAk()
                    aiB = nextAk()

                    def A(aiA=aiA, aiB=aiB, T0=T0, tb=tb):
                        proj(aiA, K_KR, 32, T0)
                        proj(aiB, K_KRS, 32, T0)
                        S.dma('sp', lambda e: e.dma_start(out=ckt[tb][:, :], in_=CK_d[:, T0:T0 + 512]), writes=['ckt%d' % tb])
                        S.dma('sp', lambda e: e.dma_start(out=skt[tb][:, :], in_=SK_d[:, T0:T0 + 512]), writes=['skt%d' % tb])

                    def B(aiA=aiA, aiB=aiB, T0=T0, tb=tb):
                        rfk([(psAk[aiA][0:32, :], 'psAk%d' % aiA), (psAk[aiB][0:32, :], 'psAk%d' % aiB)], 1, 32, b32,
                            [vecs[0:32, V_GR:V_GR + 1], vecs[0:32, V_GR + 1:V_GR + 2]], 0.0, [(xa[:, :], 'xa'), (xb[:, :], 'xb')])
                        S.op('pool', lambda e: e.tensor_tensor(out=xa[:, :], in0=xa[:, :], in1=ckt[tb][:, :], op=ALU.mult), reads=['xa', 'ckt%d' % tb], writes=['xa'])
                        S.op('pool', lambda e: e.tensor_tensor(out=xb[:, :], in0=xb[:, :], in1=skt[tb][:, :], op=ALU.mult), reads=['xb', 'skt%d' % tb], writes=['xb'])
                        S.op('pool', lambda e: e.tensor_tensor(out=rst[tb][:, :], in0=xa[:, :], in1=xb[:, :], op=ALU.add), reads=['xa', 'xb'], writes=['rst%d' % tb])
                        for h in range(4):
                            S.dma('pool', lambda e, h=h: e.dma_start(out=KT_s[12 + h, 64:96, T0:T0 + 512], in_=rst[tb][:, :]), reads=['rst%d' % tb])
                    pipe.add(A, B)
                    ai = nextAk()

                    def A(ai=ai, T0=T0):
                        proj(ai, K_FF, 4, T0)

                    def B(ai=ai, T0=T0):
                        S.op('act', lambda e: e.activation(out=ffT[0:4, T0:T0 + 512], in_=psAk[ai][0:4, :], func=AF.Copy), reads=['psAk%d' % ai], writes=['ffT'])
                    pipe.add(A, B)
                pipe.run()
                S.barrier()
            with stKC as st:
                onesb = sbt(st, "onesb", [4, SEQ], BF16)
                cT4 = sbt(st, "cT4", [4, SEQ], F32)
                S.op('pool', lambda e: e.memset(onesb[:], 1.0), writes=['onesb'])
                S.op('dve', lambda e: e.tensor_scalar(out=negb[0:4, :], in0=vecs[0:4, V_BF:V_BF + 1], scalar1=-1.0, scalar2=None, op0=ALU.mult), writes=['negb'])
                S.op('act', lambda e: e.activation(out=ffT[:, :], in_=ffT[:, :], func=AF.Exp, scale=-1.0, bias=negb[0:4, 0:1]), reads=['ffT', 'negb'], writes=['ffT'])
                S.op('act', lambda e: e.activation(out=ffT[:, :], in_=ffT[:, :], func=AF.Ln, bias=1.0), reads=['ffT'], writes=['ffT'])
                S.op('dve', lambda e: e.tensor_tensor_scan(out=cT4[:, :], data0=onesb[:, :], data1=ffT[:, :], initial=0.0, op0=ALU.mult, op1=ALU.add),
                     reads=['onesb', 'ffT'], writes=['ncsrc'])
                hi, mid, lo = split3(st, "nc", cT4[:, :], 4, SEQ)
                for part, row in ((hi, 67), (mid, 68), (lo, 69)):
                    S.dma('sp', lambda e, part=part, row=row: e.dma_start(out=KT_s[0:4, row, :], in_=part[:, :]), reads=['nchi', 'ncmid', 'nclo'])
                for h in range(4):
                    S.dma('sp', lambda e, h=h: e.dma_start(out=KT_s[h, 64:67, :], in_=ones3_d[:, :]))
                    S.dma('sp', lambda e, h=h: e.dma_start(out=KT_s[4 + h, 64:80, :], in_=oh16_d[:, :]))
                psC = pst(st, "psC", [128, 128], F32)
                for blk in range(32):
                    S.op('pe', lambda e, blk=blk: e.transpose(out=psC[:, blk * 4:(blk + 1) * 4], in_=cT4[0:4, blk * 128:(blk + 1) * 128], identity=identf[0:4, 0:4]),
                         reads=['ncsrc'], writes=['psC'], sig=(blk == 31))
                S.op('dve', lambda e: e.tensor_copy(out=cTm[:, :], in_=psC[:, :]), reads=['psC'], writes=['cTm'])
                for pp in range(2):
                    S.op('dve', lambda e, pp=pp: e.tensor_scalar(out=kmT[pp][:, :], in0=kmsum[pp][:, :], scalar1=1.0 / 256, scalar2=None, op0=ALU.mult),
                         reads=['kmsum%d' % pp], writes=['kmT%d' % pp])
                    S.op('dve', lambda e, pp=pp: e.memset(kmBD[pp][:, :], 0.0), writes=['kmBD%d' % pp])
                    S.op('dve', lambda e, pp=pp: e.tensor_copy(out=kmBD[pp][0:64, 0:16], in_=kmT[pp][0:64, :]), reads=['kmT%d' % pp], writes=['kmBD%d' % pp])
                    S.op('dve', lambda e, pp=pp: e.tensor_copy(out=kmBD[pp][64:128, 16:32], in_=kmT[pp][64:128, :]), reads=['kmT%d' % pp], writes=['kmBD%d' % pp])
                S.barrier()

        def phaseQ(xload, W, C, vecs, GT):
            with contextlib.ExitStack() as st:
                hTo = sbt(st, "hTo", [128, 8, OWN], BF16)
                with contextlib.ExitStack() as st1:
                    norm_phase(st1, xload, OWN // 128, hTo, "no_")
                    S.barrier()
                wQ = load_w(st, "wQ", W["wQ"], 8, NWQ, V_LNG, vecs)
                wuqA = load_w(st, "wuqA", W["wuqA"], 2, 384)
                wuqB = load_w(st, "wuqB", W["wuqB"], 2, 384)
                VB = sbt(st, "VB", [128, 16, 16], F32)
                VM = sbt(st, "VM", [128, 16, 16], F32)
                S.dma('sp', lambda e: e.dma_start(out=VB[:], in_=C["VB"][:]))
                S.dma('sp', lambda e: e.dma_start(out=VM[:], in_=C["VM"][:]))
                S.barrier()
                rfq = RmsFeat(st)
                psAq = [pst(st, "qpsA%d" % i, [128, 512], F32) for i in range(4)]
                psG = pst(st, "psG", [128, 512], F32)
                psM = pst(st, "psM", [128, 1024], BF16)
                qst = [sbt(st, "qst%d" % i, [128, 512], BF16) for i in range(3)]
                cqn = sbt(st, "cqn", [128, 2, 512], BF16)
                qa = sbt(st, "qa", [96, 512], F32)
                qb = sbt(st, "qb", [96, 512], F32)
                ctt = [sbt(st, "ctt%d" % i, [96, 512], F32) for i in range(2)]
                stt = [sbt(st, "stt%d" % i, [96, 512], F32) for i in range(2)]
                qmst = [sbt(st, "qmst%d" % i, [96, 512], BF16) for i in range(2)]
                gvs = sbt(st, "gvs", [128, 2, 4, 16], F32)
                m8 = sbt(st, "m8", [128, 2, 4, 8], F32)
                Mf = sbt(st, "Mf", [128, 2, 4, 16], F32)
                Mb = [sbt(st, "Mb%d" % i, [128, 2, 4, 16], BF16) for i in range(2)]
                mst = [sbt(st, "mst%d" % i, [16, 1024], BF16) for i in range(2)]
                acnt = [0]
                qcnt = [0]
                mcnt = [0]

                def nextAq():
                    i = acnt[0] % 4
                    acnt[0] += 1
                    return i

                def projq(ai, col0, ncol, T0):
                    for kc in range(8):
                        S.op('pe', lambda e, kc=kc: e.matmul(psAq[ai][0:ncol, :], lhsT=wQ[:, kc, col0:col0 + ncol], rhs=hTo[:, kc, T0:T0 + 512],
                                                             start=(kc == 0), stop=(kc == 7)),
                             writes=['qpsA%d' % ai], sig=(kc == 7))

                pipe = Pipe()
                for ch in range(4):
                    T0 = ch * 512
                    tb = ch % 2
                    fins = []
                    for pi in range(6):
                        ai = nextAq()
                        qbuf = qcnt[0] % 3
                        qcnt[0] += 1
                        mbs = None
                        if pi // 2 == 1:
                            mbs = (mcnt[0] % 2, mcnt[0] % 2)
                            mcnt[0] += 1

                        def A(ai=ai, pi=pi, T0=T0):
                            projq(ai, Q_Q + pi * 128, 128, T0)

                        def B(ai=ai, pi=pi, T0=T0, qbuf=qbuf, ch=ch, mbs=mbs):
                            rfq([(psAq[ai][:, :], 'qpsA%d' % ai)], 1, 128, bd64, [vecs[:, V_GQ + pi:V_GQ + pi + 1]], math.log(0.125),
                                [(qst[qbuf][:, :], 'qst%d' % qbuf)])
                            hd = 4 * (pi // 2) + 2 * (pi % 2)
                            for hh in range(2):
                                S.dma('pool', lambda e, hh=hh: e.dma_start(out=QT_s[hd + hh, 0:64, T0:T0 + 512], in_=qst[qbuf][hh * 64:(hh + 1) * 64, :]),
                                      reads=['qst%d' % qbuf])
                            if pi // 2 == 1:
                                pp = pi % 2
                                mbi = mbs[0]
                                for tt in range(4):
                                    S.op('pe', lambda e, tt=tt: e.matmul(
                                        psG[:, tt * 32:(tt + 1) * 32], lhsT=qst[qbuf][:, tt * 128:(tt + 1) * 128],
                                        rhs=kmBD[pp][:, 0:32], start=True, stop=True),
                                        reads=['qst%d' % qbuf], writes=['psG'], sig=(tt == 3))
                                for hh in range(2):
                                    S.op('dve', lambda e, hh=hh: e.tensor_tensor(
                                        out=gvs[:, hh, :, :], in0=psG[:, 0:128].rearrange("p (t h n) -> p t h n", t=4, h=2)[:, :, hh, :],
                                        in1=VB[:, ch * 4:(ch + 1) * 4, :], op=ALU.add), reads=['psG'], writes=['gvs'])
                                    for tt in range(4):
                                        S.op('dve', lambda e, hh=hh, tt=tt: e.max(out=m8[:, hh, tt, :], in_=gvs[:, hh, tt, :]), reads=['gvs'], writes=['m8'])
                                    S.op('dve', lambda e, hh=hh: e.tensor_tensor(out=Mf[:, hh, :, :], in0=gvs[:, hh, :, :],
                                                                               in1=m8[:, hh, :, 2:3].to_broadcast([128, 4, 16]), op=ALU.is_ge),
                                         reads=['gvs', 'm8'], writes=['Mf'])
                                    S.op('dve', lambda e, hh=hh: e.tensor_scalar(out=Mf[:, hh, :, :], in0=Mf[:, hh, :, :], scalar1=1.0, scalar2=-NEG,
                                                                               op0=ALU.subtract, op1=ALU.mult), reads=['Mf'], writes=['Mf'])
                                    S.op('dve', lambda e, hh=hh: e.tensor_tensor(out=Mb[mbi][:, hh, :, :], in0=Mf[:, hh, :, :], in1=VM[:, ch * 4:(ch + 1) * 4, :], op=ALU.mult),
                                         reads=['Mf'], writes=['Mb%d' % mbi])

                        fin = None
                        if pi // 2 == 1:
                            def fin(pi=pi, T0=T0, mbs=mbs):
                                pp = pi % 2
                                mbi = mbs[0]
                                for hh in range(2):
                                    for tt in range(4):
                                        g = hh * 4 + tt
                                        S.op('pe', lambda e, hh=hh, tt=tt, g=g: e.transpose(out=psM[0:16, g * 128:(g + 1) * 128], in_=Mb[mbi][:, hh, tt, :], identity=ident),
                                             reads=['Mb%d' % mbi], writes=['psM'], sig=(g == 7))
                                S.op('dve', lambda e: e.tensor_copy(out=mst[mbi][:, :], in_=psM[0:16, :]), reads=['psM'], writes=['mst%d' % mbi])
                                for hh in range(2):
                                    S.dma('pool', lambda e, hh=hh: e.dma_start(out=QT_s[4 + 2 * pp + hh, 64:80, T0:T0 + 512], in_=mst[mbi][:, hh * 512:(hh + 1) * 512]),
                                          reads=['mst%d' % mbi])
                        fins.append(fin)
                        pipe.add(A, B)
                        if pi >= 2 and fins[-3] is not None:
                            pipe.add(lambda: None, fins[-3])
                    for f_ in fins[-2:]:
                        if f_ is not None:
                            pipe.add(lambda: None, f_)
                    a0 = nextAq()
                    a1 = nextAq()

                    def A(a0=a0, a1=a1, T0=T0, tb=tb):
                        projq(a0, Q_CQ, 128, T0)
                        projq(a1, Q_CQ + 128, 128, T0)
                        S.dma('sp', lambda e: e.dma_start(out=ctt[tb][:, :], in_=C["CTq"][:, T0:T0 + 512]), writes=['ctt%d' % tb])
                        S.dma('sp', lambda e: e.dma_start(out=stt[tb][:, :], in_=C["STq"][:, T0:T0 + 512]), writes=['stt%d' % tb])

                    def B(a0=a0, a1=a1):
                        rfq([(psAq[a0][:, :], 'qpsA%d' % a0), (psAq[a1][:, :], 'qpsA%d' % a1)], 2, 128, b256,
                            [vecs[:, V_GQN:V_GQN + 1], vecs[:, V_GQN + 1:V_GQN + 2]], 0.0,
                            [(cqn[:, 0, :], 'cqn'), (cqn[:, 1, :], 'cqn')])
                    pipe.add(A, B)
                    for h in range(4):
                        aA = nextAq()
                        aB = nextAq()
                        qmb = (ch * 4 + h) % 2

                        def A(aA=aA, aB=aB, h=h):
                            for c in range(2):
                                S.op('pe', lambda e, c=c: e.matmul(psAq[aA][0:96, :], lhsT=wuqA[:, c, 96 * h:96 * h + 96], rhs=cqn[:, c, :], start=(c == 0), stop=(c == 1)),
                                     reads=['cqn'], writes=['qpsA%d' % aA], sig=(c == 1))
                            for c in range(2):
                                S.op('pe', lambda e, c=c: e.matmul(psAq[aB][0:96, :], lhsT=wuqB[:, c, 96 * h:96 * h + 96], rhs=cqn[:, c, :], start=(c == 0), stop=(c == 1)),
                                     reads=['cqn'], writes=['qpsA%d' % aB], sig=(c == 1))

                        def B(aA=aA, aB=aB, h=h, qmb=qmb, tb=tb, T0=T0):
                            rfq([(psAq[aA][0:96, :], 'qpsA%d' % aA), (psAq[aB][0:96, :], 'qpsA%d' % aB)], 1, 96, bdmla,
                                [vecs[0:96, V_GA:V_GA + 1], vecs[0:96, V_GB:V_GB + 1]], 0.0, [(qa[:, :], 'qa'), (qb[:, :], 'qb')])
                            S.op('pool', lambda e: e.tensor_tensor(out=qa[:, :], in0=qa[:, :], in1=ctt[tb][:, :], op=ALU.mult), reads=['qa', 'ctt%d' % tb], writes=['qa'])
                            S.op('pool', lambda e: e.tensor_tensor(out=qb[:, :], in0=qb[:, :], in1=stt[tb][:, :], op=ALU.mult), reads=['qb', 'stt%d' % tb], writes=['qb'])
                            S.op('pool', lambda e: e.tensor_tensor(out=qmst[qmb][:, :], in0=qa[:, :], in1=qb[:, :], op=ALU.add), reads=['qa', 'qb'], writes=['qmst%d' % qmb])
                            S.dma('pool', lambda e: e.dma_start(out=QT_s[12 + h, 0:96, T0:T0 + 512], in_=qmst[qmb][:, :]), reads=['qmst%d' % qmb])
                        pipe.add(A, B, dep_prev=(h == 0))
                    for g in range(8):
                        ai = nextAq()

                        def A(ai=ai, g=g, T0=T0):
                            projq(ai, Q_GATE + g * 128, 128, T0)

                        def B(ai=ai, g=g, T0=T0):
                            S.op('act', lambda e: e.activation(out=GT[:, g, T0:T0 + 512], in_=psAq[ai][:, :], func=AF.Silu), reads=['qpsA%d' % ai])
                        pipe.add(A, B)
                pipe.run()
                S.barrier()
            with contextlib.ExitStack() as st:
                Sel = sbt(st, "Sel", [128, 4, 256], F32)
                S.dma("sp", lambda e: e.dma_start(out=Sel[:], in_=C["Sel"][:]), writes=["Sel"])
                psG2 = pst(st, "psG2", [128, 512], F32)
                cown = sbt(st, "cown", [4, OWN], F32)
                for s in range(8):
                    for jj in range(4):
                        blk = 4 * s + jj
                        S.op('pe', lambda e, blk=blk, jj=jj: e.matmul(psG2[0:4, 0:256], lhsT=cTm[:, blk * 4:(blk + 1) * 4], rhs=Sel[:, jj, :], start=(jj == 0), stop=(jj == 3)),
                             reads=['Sel'], writes=['psG2'], sig=(jj == 3))
                    S.op('dve', lambda e, s=s: e.tensor_scalar(out=cown[0:4, s * 256:(s + 1) * 256], in0=psG2[0:4, 0:256], scalar1=-1.0, scalar2=None, op0=ALU.mult),
                         reads=['psG2'], writes=['cosrc'])
                hi, mid, lo = split3(st, "co", cown[:, :], 4, OWN)
                for part, row in ((hi, 64), (mid, 65), (lo, 66)):
                    S.dma('sp', lambda e, part=part, row=row: e.dma_start(out=QT_s[0:4, row, :], in_=part[:, :]), reads=['cohi', 'comid', 'colo'])
                for h in range(4):
                    S.dma('sp', lambda e, h=h: e.dma_start(out=QT_s[h, 67:70, :], in_=ones3_d[:, 0:OWN]))
                S.barrier()

        def phaseA(GT, mixT, F_s):
            with contextlib.ExitStack() as st:
                kt = [[sbt(st, "kt%d_%d" % (b, hh), [128, SEQ], BF16) for hh in range(2)] for b in range(2)]
                qt_ = [[sbt(st, "qt%d_%d" % (b, hh), [128, OWN], BF16) for hh in range(2)] for b in range(2)]
                vt = [sbt(st, "vt%d" % b, [128, 32, 192], BF16) for b in range(2)]
                gstage = [sbt(st, "gstage%d" % hh, [128, WG], BF16) for hh in range(2)]
                gtab = [sbt(st, "gtab%d" % b, [128, 2, WG], BF16) for b in range(2)]
                gc = sbt(st, "gc", [128, WC], BF16)
                NPB = 6
                pb = [sbt(st, "pb%d" % i, [128, 512], BF16) for i in range(NPB)]
                rd = sbt(st, "rd", [128, 512], F32)
                tmpo = sbt(st, "tmpo", [128, 512], F32)
                psS = [pst(st, "psS%d" % i, [128, 512], F32) for i in range(NPB)]
                psO = [[pst(st, "psO%d_%d" % (i, hh), [128, 512], F32) for hh in range(2)] for i in range(1)]
                Vv = V_s.rearrange("(j p) c -> p j c", p=128)
                S.dma('sp', lambda e: e.dma_start(out=gc[:, :], in_=bass.AP(tensor=F_s.tensor, offset=8 * FL, ap=[[1, 128], [1, WC]])), writes=['gc'])
                for b in range(2):
                    S.op('pool', lambda e, b=b: e.memset(vt[b][:, :, 64:128], 1.0), writes=['vt%d' % b])
                KDs = (70, 80, 64, 96)
                KDM = (70, 80, 128, 96)
                for b_ in range(2):
                    for hh_ in range(2):
                        S.op('dve', lambda e, b_=b_, hh_=hh_: e.memset(kt[b_][hh_][64:128, :], 0.0), writes=['kt%d_%d' % (b_, hh_)])
                        S.op('dve', lambda e, b_=b_, hh_=hh_: e.memset(qt_[b_][hh_][64:128, :], 0.0), writes=['qt%d_%d' % (b_, hh_)])
                scnt = [0]
                ocnt = [0]
                for p8 in range(8):
                    m, pp = p8 // 2, p8 % 2
                    KD = KDs[m]
                    KDq = KDM[m]
                    b = p8 % 2
                    if m == 2:
                        for hh in range(2):
                            S.op('dve', lambda e, b=b, hh=hh: e.memset(qt_[b][hh][64:128, :], 0.0), writes=['qt%d_%d' % (b, hh)])
                    for hh in range(2):
                        hd = 4 * m + 2 * pp + hh
                        for half in range(2):
                            S.dma('sp', lambda e, b=b, hh=hh, hd=hd, half=half, KD=KD: e.dma_start(
                                out=kt[b][hh][0:KD, half * 2048:(half + 1) * 2048], in_=KT_s[hd, 0:KD, half * 2048:(half + 1) * 2048]),
                                writes=['kt%d_%d' % (b, hh)])
                        S.dma('sp', lambda e, b=b, hh=hh, hd=hd, KD=KD: e.dma_start(out=qt_[b][hh][0:KD, :], in_=QT_s[hd, 0:KD, :]), writes=['qt%d_%d' % (b, hh)])
                    for q4 in range(4):
                        for hh in range(2):
                            c0 = m * 256 + pp * 128 + hh * 64
                            S.dma('sp', lambda e, b=b, q4=q4, hh=hh, c0=c0: e.dma_start(
                                out=vt[b][:, q4 * 8:(q4 + 1) * 8, hh * 128:hh * 128 + 64], in_=Vv[:, q4 * 8:(q4 + 1) * 8, c0:c0 + 64]),
                                writes=['vt%d' % b])
                    if m in (1, 2):
                        for hh in range(2):
                            row = (m - 1) * 4 + 2 * pp + hh
                            S.dma('sp', lambda e, hh=hh, row=row: e.dma_start(
                                out=gstage[hh][:, :], in_=bass.AP(tensor=F_s.tensor, offset=row * FL, ap=[[1, 128], [1, WG]])), writes=['gstage%d' % hh])
                            for c0 in range(0, WG, 512):
                                cw = min(512, WG - c0)
                                bi = scnt[0] % NPB
                                scnt[0] += 1
                                S.op('pe', lambda e, bi=bi, hh=hh, c0=c0, cw=cw: e.matmul(psS[bi][:, 0:cw], lhsT=Jm, rhs=gstage[hh][:, c0:c0 + cw], start=True, stop=True),
                                     reads=['gstage%d' % hh], writes=['psS%d' % bi])
                                S.op('act', lambda e, bi=bi, hh=hh, c0=c0, cw=cw, b=b: e.activation(out=gtab[b][:, hh, c0:c0 + cw], in_=psS[bi][:, 0:cw], func=AF.Exp),
                                     reads=['psS%d' % bi], writes=['gtab%d' % b])
                    RK = ['kt%d_%d' % (b, hh) for hh in range(2)] + ['qt%d_%d' % (b, hh) for hh in range(2)]
                    for u in range(4):
                        s0, s1 = 2 * u, 2 * u + 1
                        nk = (4 * s0 + 4, 4 * s1 + 4)
                        jlo = (max(0, 4 * s0 - 16), max(0, 4 * s1 - 16)) if m == 2 else (0, 0)
                        js = list(range(jlo[0], nk[1]))
                        ob = 0
                        sbufs = {}

                        def active(j):
                            a0 = (jlo[0] <= j < nk[0])
                            a1 = (jlo[1] <= j < nk[1])
                            c0 = 0 if a0 else 256
                            c1 = 512 if a1 else 256
                            return a0, a1, c0, c1

                        def emit_S(j):
                            a0, a1, c0, c1 = active(j)
                            bis = []
                            for hh in range(2):
                                bi = scnt[0] % NPB
                                scnt[0] += 1
                                bis.append(bi)
                                jadd = None
                                if m in (0, 3):
                                    for si, sl in enumerate((s0, s1)):
                                        o = 512 * sl - 128 * j + 384
                                        if (a0, a1)[si] and o < 512:
                                            jadd = (si, o)
                                S.op('pe', lambda e, bi=bi, hh=hh, j=j, b=b, KD=KDq, c0=c0, c1=c1, jadd=jadd, s0=s0: e.matmul(
                                    psS[bi][:, c0:c1], lhsT=kt[b][hh][0:KD, j * 128:(j + 1) * 128],
                                    rhs=qt_[b][hh][0:KD, s0 * 256 + c0:s0 * 256 + c1], start=True, stop=(jadd is None)),
                                    reads=RK, writes=['psS%d' % bi], sig=(jadd is None))
                                if jadd is not None:
                                    si, o = jadd
                                    S.op('pe', lambda e, bi=bi, si=si, o=o: e.matmul(psS[bi][:, si * 256:(si + 1) * 256], lhsT=Jm, rhs=gc[:, o:o + 256],
                                                                                   start=False, stop=True),
                                         reads=RK + ['gc'], writes=['psS%d' % bi])
                            sbufs[j] = bis

                        def emit_PV(j):
                            a0, a1, c0, c1 = active(j)
                            bis = sbufs[j]
                            first, last = (j == js[0]), (j == js[-1])
                            for hh in range(2):
                                bi = bis[hh]
                                S.op('act', lambda e, bi=bi, c0=c0, c1=c1: e.activation(out=pb[bi][:, c0:c1], in_=psS[bi][:, c0:c1], func=AF.Exp),
                                     reads=['psS%d' % bi], writes=['pb%d' % bi])
                                if m in (1, 2):
                                    for si, sl in enumerate((s0, s1)):
                                        if not (a0, a1)[si]:
                                            continue
                                        o = 512 * sl - 128 * j + 384
                                        oe = min(o, 2048) if m == 1 else o
                                        S.op('dve', lambda e, bi=bi, oe=oe, b=b, hh=hh, si=si: e.tensor_tensor(
                                            out=pb[bi][:, si * 256:(si + 1) * 256], in0=pb[bi][:, si * 256:(si + 1) * 256],
                                            in1=gtab[b][:, hh, oe:oe + 256], op=ALU.mult),
                                            reads=['pb%d' % bi, 'gtab%d' % b], writes=['pb%d' % bi])
                                S.op('pe', lambda e, bi=bi, hh=hh, j=j, ob=ob, b=b, first=first, last=last, c0=c0, c1=c1: e.matmul(
                                    psO[ob][hh][:, c0:c1], lhsT=vt[b][:, j, hh * 64:hh * 64 + 128],
                                    rhs=pb[bi][:, c0:c1], start=first, stop=last),
                                    reads=['pb%d' % bi, 'vt%d' % b], writes=['psO%d_%d' % (ob, hh)], sig=True)

                        LA = 2
                        for jj in js[:LA]:
                            emit_S(jj)
                        for idx, j in enumerate(js):
                            if idx + LA < len(js):
                                emit_S(js[idx + LA])
                            emit_PV(j)
                        S.op('act', lambda e, ob=ob: e.activation(out=rd[0:64, :], in_=psO[ob][0][64:128, :], func=AF.Ln), reads=['psO%d_0' % ob], writes=['rd'])
                        S.op('act', lambda e, ob=ob: e.activation(out=rd[64:128, :], in_=psO[ob][1][0:64, :], func=AF.Ln), reads=['psO%d_1' % ob], writes=['rd'])
                        S.op('act', lambda e: e.activation(out=rd[:, :], in_=rd[:, :], func=AF.Exp, scale=-1.0), reads=['rd'], writes=['rd'])
                        S.op('dve', lambda e, ob=ob: e.tensor_tensor(out=tmpo[0:64, :], in0=psO[ob][0][0:64, :], in1=rd[0:64, :], op=ALU.mult),
                             reads=['psO%d_0' % ob, 'rd'], writes=['tmpo'])
                        S.op('dve', lambda e, ob=ob: e.tensor_tensor(out=tmpo[64:128, :], in0=psO[ob][1][64:128, :], in1=rd[64:128, :], op=ALU.mult),
                             reads=['psO%d_1' % ob, 'psO%d_0' % ob, 'rd'], writes=['tmpo'])
                        S.op('pool', lambda e, p8=p8, s0=s0: e.tensor_tensor(out=mixT[:, p8, s0 * 256:s0 * 256 + 512], in0=tmpo[:, :], in1=GT[:, p8, s0 * 256:s0 * 256 + 512], op=ALU.mult),
                             reads=['tmpo'])
                S.barrier()

        def phaseO(xload, pown_d, W, vecs, mixT, out_ap):
            with contextlib.ExitStack() as st:
                wout = load_w(st, "wout", W["wout"], 8, 1024)
                wpp = load_w(st, "wpp", W["wpp"], 2, 1024)
                wpg = load_w(st, "wpg", W["wpg"], 8, 1024, V_PLEG, vecs)
                S.barrier()
                NX = 3
                xt = [sbt(st, "oxt%d" % i, [128, 1024], F32) for i in range(NX)]
                pt = [sbt(st, "opt%d" % i, [128, 256], F32) for i in range(2)]
                ptb = [sbt(st, "optb%d" % i, [128, 256], BF16) for i in range(2)]
                pTs = [sbt(st, "opTs%d" % i, [128, 2, 128], BF16) for i in range(2)]
                x1 = [sbt(st, "ox1%d" % i, [128, 1024], F32) for i in range(NX)]
                junk = sbt(st, "ojunk", [128, 1024], BF16)
                xn = [sbt(st, "oxn%d" % i, [128, 1024], BF16) for i in range(2)]
                ss = [sbt(st, "oss%d" % i, [128, 4], F32) for i in range(2)]
                gTs = [sbt(st, "ogT%d" % i, [128, 8, 128], BF16) for i in range(2)]
                sg = [sbt(st, "osg%d" % i, [128, 1024], F32) for i in range(2)]
                psY = [pst(st, "psY%d" % i, [128, 512], F32) for i in range(2)]
                psT = pst(st, "opsT", [128, 1024], BF16)
                psP = pst(st, "opsP", [128, 512], BF16)
                psGt = [pst(st, "psGt%d" % i, [128, 512], F32) for i in range(2)]
                psPP = [pst(st, "psPP%d" % i, [128, 512], F32) for i in range(2)]
                NTO = OWN // 128

                def stage1(t):
                    b, b3, tok = t % 2, t % NX, t * 128
                    xload(t, xt[b3], 'oxt%d' % b3)
                    S.dma('sp', lambda e: e.dma_start(out=pt[b][:, :], in_=pown_d[tok:tok + 128, :]), writes=['opt%d' % b])
                    for half in range(2):
                        for kc in range(8):
                            S.op('pe', lambda e, half=half, kc=kc: e.matmul(psY[half][:, :], lhsT=mixT[:, kc, tok:tok + 128], rhs=wout[:, kc, half * 512:(half + 1) * 512],
                                                                          start=(kc == 0), stop=(kc == 7)), writes=['psY%d' % half], sig=(kc == 7))
                        S.op('dve', lambda e, half=half: e.tensor_tensor(out=x1[b3][:, half * 512:(half + 1) * 512], in0=psY[half][:, :], in1=xt[b3][:, half * 512:(half + 1) * 512], op=ALU.add),
                             reads=['psY%d' % half, 'oxt%d' % b3], writes=['ox1%d' % b3])
                    S.op('act', lambda e: e.activation(out=junk[:, :], in_=x1[b3][:, :], func=AF.Square, accum_out=ss[b][:, 0:1]), reads=['ox1%d' % b3], writes=['ojunk', 'oss%d' % b])
                    S.op('act', lambda e: e.activation(out=ss[b][:, 1:2], in_=ss[b][:, 0:1], func=AF.Ln, scale=1.0 / 1024, bias=EPS), reads=['oss%d' % b], writes=['oss%d' % b])
                    S.op('act', lambda e: e.activation(out=ss[b][:, 2:3], in_=ss[b][:, 1:2], func=AF.Exp, scale=-0.5), reads=['oss%d' % b], writes=['oss%d' % b])
                    S.op('act', lambda e: e.activation(out=xn[b][:, :], in_=x1[b3][:, :], func=AF.Copy, scale=ss[b][:, 2:3]),
                         reads=['ox1%d' % b3, 'oss%d' % b], writes=['oxn%d' % b])
                    S.op('dve', lambda e: e.tensor_copy(out=ptb[b][:, :], in_=pt[b][:, :]), reads=['opt%d' % b], writes=['optb%d' % b])

                def stage2(t):
                    b = t % 2
                    for c in range(8):
                        S.op('pe', lambda e, c=c: e.transpose(out=psT[:, c * 128:(c + 1) * 128], in_=xn[b][:, c * 128:(c + 1) * 128], identity=ident),
                             reads=['oxn%d' % b], writes=['opsT'], sig=(c == 7))
                    S.op('dve', lambda e: e.tensor_copy(out=gTs[b][:, :, :], in_=psT[:, :].rearrange("p (c n) -> p c n", c=8)), reads=['opsT'], writes=['ogT%d' % b])
                    for c in range(2):
                        S.op('pe', lambda e, c=c: e.transpose(out=psP[:, c * 128:(c + 1) * 128], in_=ptb[b][:, c * 128:(c + 1) * 128], identity=ident),
                             reads=['optb%d' % b], writes=['opsP'], sig=(c == 1))
                    S.op('act', lambda e: e.activation(out=pTs[b][:, :, :], in_=psP[:, 0:256].rearrange("p (c n) -> p c n", c=2), func=AF.Copy), reads=['opsP'], writes=['opTs%d' % b])

                def stage3(t):
                    b, b3, tok = t % 2, t % NX, t * 128
                    for half in range(2):
                        for kc in range(8):
                            S.op('pe', lambda e, half=half, kc=kc: e.matmul(psGt[half][:, :], lhsT=gTs[b][:, kc, :], rhs=wpg[:, kc, half * 512:(half + 1) * 512],
                                                                          start=(kc == 0), stop=(kc == 7)), reads=['ogT%d' % b], writes=['psGt%d' % half], sig=(kc == 7))
                        S.op('act', lambda e, half=half: e.activation(out=sg[b][:, half * 512:(half + 1) * 512], in_=psGt[half][:, :], func=AF.Sigmoid),
                             reads=['psGt%d' % half], writes=['osg%d' % b])
                        for c in range(2):
                            S.op('pe', lambda e, half=half, c=c: e.matmul(psPP[half][:, :], lhsT=pTs[b][:, c, :], rhs=wpp[:, c, half * 512:(half + 1) * 512],
                                                                        start=(c == 0), stop=(c == 1)), reads=['opTs%d' % b], writes=['psPP%d' % half], sig=(c == 1))
                        S.op('dve', lambda e, half=half: e.tensor_tensor(out=sg[b][:, half * 512:(half + 1) * 512], in0=psPP[half][:, :], in1=sg[b][:, half * 512:(half + 1) * 512], op=ALU.mult),
                             reads=['psPP%d' % half, 'osg%d' % b], writes=['osg%d' % b])
                    S.op('dve', lambda e: e.tensor_tensor(out=sg[b][:, :], in0=sg[b][:, :], in1=x1[b3][:, :], op=ALU.add), reads=['osg%d' % b, 'ox1%d' % b3], writes=['osg%d' % b])
                    S.dma('pool', lambda e: e.dma_start(out=out_ap[tok:tok + 128, :], in_=sg[b][:, :]), reads=['osg%d' % b])

                for i in range(NTO + 2):
                    if i < NTO:
                        stage1(i)
                    if 1 <= i <= NTO:
                        stage2(i - 1)
                    if i >= 2:
                        stage3(i - 2)
                S.barrier()

        def dram_loader(src):
            def f(t, dst, res):
                S.dma('sp', lambda e: e.dma_start(out=dst[:, :], in_=src[t * 128:(t + 1) * 128, :]), writes=[res])
            return f

        def x1_nat_loader(t, dst, res):
            a = (t % 4) // 2
            r0 = (t // 4) * 256 + (t % 2) * 128
            S.dma('sp', lambda e: e.dma_start(out=dst[:, :], in_=x1_s[a][r0:r0 + 128, :]), writes=[res])

        for c in ("a0", "a1", "m"):
            phase0(CS[c]["OH"], F_sL[c])
        phaseK(dram_loader(x_all), WL[0], vecsL[0])
        for a in range(2):
            C = CS["a%d" % a]
            with contextlib.ExitStack() as stg:
                GT = sbt(stg, "GT", [128, 8, OWN], BF16)
                phaseQ(dram_loader(xown_d[a]), WL[0], C, vecsL[0], GT)
                mixT = sbt(stg, "mixT", [128, 8, OWN], BF16)
                phaseA(GT, mixT, F_sL["a%d" % a])
                phaseO(dram_loader(xown_d[a]), C["p"], WL[0], vecsL[0], mixT, x1_s[a])
        phaseK(x1_nat_loader, WL[1], vecsL[1])
        C = CS["m"]
        with contextlib.ExitStack() as stg:
            selt = [sbt(stg, "selt%d" % i, [128, 1024], F32) for i in range(2)]

            def x1_own_loader(t, dst, res):
                for a in range(2):
                    S.dma('sp', lambda e, a=a: e.dma_start(out=selt[a][:, :], in_=x1_s[a][t * 128:(t + 1) * 128, :]), writes=['selt%d' % a])
                S.op('act', lambda e: e.activation(out=selt[0][:, :], in_=selt[0][:, :], func=AF.Copy, scale=msel[:, 0:1]),
                     reads=['selt0'], writes=['selt0'])
                S.op('dve', lambda e: e.scalar_tensor_tensor(out=dst[:, :], in0=selt[1][:, :], scalar=msel[:, 1:2], in1=selt[0][:, :], op0=ALU.mult, op1=ALU.add),
                     reads=['selt0', 'selt1'], writes=[res])

            GT = sbt(stg, "GT", [128, 8, OWN], BF16)
            phaseQ(x1_own_loader, WL[1], C, vecsL[1], GT)
            mixT = sbt(stg, "mixT", [128, 8, OWN], BF16)
            phaseA(GT, mixT, F_sL["m"])
            phaseO(x1_own_loader, C["p"], WL[1], vecsL[1], mixT, out_d)
        S.emit()
    return nc


def _t5_bucket(d):
    d = np.maximum(d, 0)
    df = np.maximum(d, 1).astype(np.float32)
    large = 16 + (np.log(df / np.float32(16)) / np.float32(math.log(2048 / 16)) * np.float32(16)).astype(np.int32)
    large = np.minimum(large, 31)
    return np.where(d < 16, d, large)


def _core_consts(par):
    bf = ml_dtypes.bfloat16
    c = {}
    i = np.arange(FL)
    d = i + 256 * par - 511
    OH = np.zeros((34, FL), np.float32)
    bk = _t5_bucket(d)
    valid = d >= 0
    OH[bk[valid], i[valid]] = 1.0
    OH[32, ~valid] = NEG
    mult = ((d <= 128).astype(np.float32) + ((d % 4 == 0) & (d <= 512)).astype(np.float32)
            + ((d % 16 == 0) & (d <= 2048)).astype(np.float32))
    ok = valid & (mult > 0)
    OH[33, :] = NEG
    OH[33, ok] = np.log(mult[ok]).astype(np.float32)
    c["OH"] = OH
    Sel = np.zeros((128, 4, 256), np.float32)
    for jj in range(4):
        for k in range(128):
            q = 128 * jj + k - 256 * par
            if 0 <= q < 256:
                Sel[k, jj, q] = 1.0
    c["Sel"] = Sel
    VB = np.zeros((128, 16, 16), np.float32)
    VM = np.zeros((128, 16, 16), np.float32)
    for qt in range(16):
        own = 2 * (qt // 2) + par
        VB[:, qt, own:] = -1e9
        VM[:, qt, :own] = 1.0
    c["VB"], c["VM"] = VB, VM
    half = 16
    inv = (1.0 / (np.float32(10000.0) ** (np.arange(half, dtype=np.float32) * np.float32(2.0) / np.float32(32)))).astype(np.float32)
    pos = np.arange(SEQ).astype(np.float32)
    ang = pos[:, None] * inv[None, :]
    cos, sin = np.cos(ang).astype(np.float32).T, np.sin(ang).astype(np.float32).T
    c["CK"] = np.ascontiguousarray(np.concatenate([cos, cos], 0))
    c["SK"] = np.ascontiguousarray(np.concatenate([-sin, sin], 0))
    own_idx = np.concatenate([512 * s + 256 * par + np.arange(256) for s in range(8)])
    sc = np.float32(96 ** -0.5)
    CT = np.full((96, OWN), sc, np.float32)
    ST = np.zeros((96, OWN), np.float32)
    CT[64:96] = np.concatenate([cos, cos], 0)[:, own_idx] * sc
    ST[64:96] = np.concatenate([-sin, sin], 0)[:, own_idx] * sc
    c["CTq"], c["STq"] = CT, ST
    c["own_idx"] = own_idx
    return c


def _shared_consts():
    bf = ml_dtypes.bfloat16
    cb = np.zeros((128, NCB), np.float32)
    for g in range(2):
        cb[g * 64:(g + 1) * 64, C_BD64 + g * 64:C_BD64 + (g + 1) * 64] = 1.0 / 64
    cb[:, C_B128:C_B128 + 128] = 1.0 / 128
    cb[:, C_B256:C_B256 + 128] = 1.0 / 256
    cb[0:64, C_BDMLA:C_BDMLA + 64] = 1.0 / 64
    cb[64:96, C_BDMLA + 64:C_BDMLA + 96] = 1.0 / 32
    cb[0:32, C_B32:C_B32 + 32] = 1.0 / 32
    cb[:, C_ONES:C_ONES + 64] = 1.0
    cb[:, C_J:C_J + 128] = np.eye(128, dtype=np.float32)[::-1]
    cb[:, C_ID:C_ID + 128] = np.eye(128, dtype=np.float32)
    oh16 = np.zeros((16, SEQ), np.float32)
    for n in range(16):
        oh16[n, n * 256:(n + 1) * 256] = 1.0
    return {"cbf": cb.astype(bf), "oh16": oh16.astype(bf), "ones3": np.ones((3, SEQ), bf),
            "identf": np.eye(128, dtype=np.float32)}


def _kc(w):
    return np.ascontiguousarray(w.reshape(8, 128, -1).transpose(1, 0, 2))


def _layer_weights(l, ln_g, w_in, b_forget, qk_gain, mla_q_norm, mla_kv_norm, mla_nope_gain, mla_rope_gain,
                   w_uq, w_ukv, w_out, rel_bias, ple_norm_g, w_ple_gate, w_ple_proj):
    W = w_in[l]
    fq, fk, fv, ff = W[:, 0:256], W[:, 256:512], W[:, 512:768], W[:, 768:772]
    mq, mk, mv = W[:, 772:1028], W[:, 1028:1284], W[:, 1284:1540]
    dq, dk, dv = W[:, 1540:1796], W[:, 1796:2052], W[:, 2052:2308]
    cq, ckv, kr, gate = W[:, 2308:2564], W[:, 2564:2692], W[:, 2692:2724], W[:, 2724:3748]
    kr_sw = np.concatenate([kr[:, 16:32], kr[:, 0:16]], 1)
    wK = np.concatenate([fk, mk, dk, fv, mv, dv, ckv, kr, kr_sw, ff, np.zeros((1024, 4), np.float32)], 1)
    wQ = np.concatenate([fq, mq, dq, cq, gate], 1)
    uq = w_uq[l]
    uqB = uq.copy()
    for h in range(4):
        uqB[:, 96 * h + 64:96 * h + 80] = uq[:, 96 * h + 80:96 * h + 96]
        uqB[:, 96 * h + 80:96 * h + 96] = uq[:, 96 * h + 64:96 * h + 80]
    ukv = w_ukv[l]
    ukvK = np.concatenate([ukv[:, 128 * h:128 * h + 64] for h in range(4)], 1)
    ukvV = np.concatenate([ukv[:, 128 * h + 64:128 * h + 128] for h in range(4)], 1)
    vec = np.zeros((128, NV), np.float32)
    vec[:, V_LNG:V_LNG + 8] = ln_g[l].reshape(8, 128).T
    for pi in range(6):
        m = pi // 2
        vec[:, V_GK + pi] = np.tile(qk_gain[l, 2 * m + 1], 2)
        vec[:, V_GQ + pi] = np.tile(qk_gain[l, 2 * m], 2)
    vec[:, V_GQN:V_GQN + 2] = mla_q_norm[l].reshape(2, 128).T
    vec[:, V_GKV] = mla_kv_norm[l]
    vec[:, V_GKN] = np.tile(mla_nope_gain[l, 1], 2)
    rg0, rg1 = mla_rope_gain[l, 0], mla_rope_gain[l, 1]
    vec[0:96, V_GA] = np.concatenate([mla_nope_gain[l, 0], rg0])
    vec[0:96, V_GB] = np.concatenate([mla_nope_gain[l, 0], rg0[16:32], rg0[0:16]])
    vec[0:32, V_GR] = rg1
    vec[0:32, V_GR + 1] = np.concatenate([rg1[16:32], rg1[0:16]])
    vec[0:4, V_BF] = b_forget[l]
    vec[:, V_PLEG:V_PLEG + 8] = ple_norm_g[l].reshape(8, 128).T
    tabX = np.zeros((34, 9), np.float32)
    tabX[0:32, 0:8] = rel_bias
    tabX[32, 0:4] = 1.0
    tabX[33, 4:8] = 1.0
    tabX[32, 8] = 1.0
    return {"wK": _kc(wK), "wQ": _kc(wQ),
            "wuqA": np.ascontiguousarray(uq.reshape(2, 128, 384).transpose(1, 0, 2)),
            "wuqB": np.ascontiguousarray(uqB.reshape(2, 128, 384).transpose(1, 0, 2)),
            "wukvK": np.ascontiguousarray(ukvK.reshape(128, 1, 256)), "wukvV": np.ascontiguousarray(ukvV.reshape(128, 1, 256)),
            "wout": _kc(w_out[l]), "wpg": _kc(w_ple_gate[l]),
            "wpp": np.ascontiguousarray(w_ple_proj[l].reshape(2, 128, 1024).transpose(1, 0, 2)),
            "vecs": vec, "tabX": tabX}


_NC = None


def kernel(x, p, ln_g, w_in, b_forget, qk_gain, mla_q_norm, mla_kv_norm, mla_nope_gain, mla_rope_gain,
           w_uq, w_ukv, w_out, rel_bias, ple_norm_g, w_ple_gate, w_ple_proj):
    global _NC
    args = [np.asarray(a, dtype=np.float32) for a in (ln_g, w_in, b_forget, qk_gain, mla_q_norm, mla_kv_norm, mla_nope_gain,
                                                     mla_rope_gain, w_uq, w_ukv, w_out, rel_bias, ple_norm_g, w_ple_gate, w_ple_proj)]
    x = np.asarray(x, dtype=np.float32)
    p = np.asarray(p, dtype=np.float32)
    if _NC is None:
        _NC = build_fused()
    shared = _shared_consts()
    cc = [_core_consts(par) for par in range(2)]
    base = dict(shared)
    for l in range(2):
        lw = _layer_weights(l, *args)
        base["tabX"] = lw.pop("tabX")
        for k, v in lw.items():
            base[k + "_l%d" % l] = v
    base["CK"], base["SK"] = cc[0]["CK"], cc[0]["SK"]
    for a in range(2):
        for k in ("OH", "Sel", "VB", "VM", "CTq", "STq"):
            base[k + "_a%d" % a] = cc[a][k]
    in_maps = []
    for core in range(8):
        b, par = core // 2, core % 2
        mp = dict(base)
        mp["x_all"] = np.ascontiguousarray(x[b])
        for a in range(2):
            mp["x_own_a%d" % a] = np.ascontiguousarray(x[b][cc[a]["own_idx"]])
            mp["p_a%d" % a] = np.ascontiguousarray(p[0, b][cc[a]["own_idx"]])
        mp["p_m"] = np.ascontiguousarray(p[1, b][cc[par]["own_idx"]])
        for k in ("OH", "Sel", "VB", "VM", "CTq", "STq"):
            mp[k + "_m"] = cc[par][k]
        ms = np.zeros((128, 2), np.float32)
        ms[:, par] = 1.0
        mp["msel"] = ms
        in_maps.append(mp)
    res = run_bass_kernel_spmd(_NC, in_maps, core_ids=list(range(8)))
    out = np.empty_like(x)
    for core in range(8):
        b, par = core // 2, core % 2
        out[b][cc[par]["own_idx"]] = np.asarray(res.results[core]["out"], dtype=np.float32)
    return out
```

```python
import contextlib
import math
import numpy as np
import ml_dtypes
import concourse.bass as bass
import concourse.mybir as mybir
from concourse.bass_utils import run_bass_kernel_spmd

F32 = mybir.dt.float32
BF16 = mybir.dt.bfloat16
AF = mybir.ActivationFunctionType
ALU = mybir.AluOpType
AX = mybir.AxisListType

COMPUTE = ('pe', 'act', 'dve', 'pool')
NDMA_SEM = 8
SEQ = 4096
OWN = 2048
NEG = -30000.0
WG = 2688
FL = WG + 128
WC = 640
EPS = 1e-6


class Sched:
    def __init__(self, nc, stack):
        self.nc = nc
        self.ops = {e: [] for e in ('pe', 'act', 'dve', 'pool', 'sp')}
        self.psem = {e: stack.enter_context(nc.semaphore("pg_" + e)) for e in COMPUTE}
        self.pcnt = {e: 0 for e in COMPUTE}
        self.dsem = {q: [stack.enter_context(nc.semaphore("dq_%s%d" % (q, i))) for i in range(NDMA_SEM)]
                     for q in ('sp', 'pool')}
        self.dval = {q: [0] * NDMA_SEM for q in ('sp', 'pool')}
        self.didx = {q: 0 for q in ('sp', 'pool')}
        self.waited = {e: {} for e in self.ops}
        self.res = {}

    def _need(self, eng, tok, waits):
        if tok is None:
            return
        key, sem, val, prod = tok
        if prod == eng and eng == 'pe':
            return
        if self.waited[eng].get(key, 0) >= val:
            return
        self.waited[eng][key] = val
        waits.append((sem, val))

    def _deps(self, eng, reads, writes):
        waits = []
        for r in reads:
            st = self.res.get(r)
            if st is not None:
                self._need(eng, st[0], waits)
        for w in writes:
            st = self.res.get(w)
            if st is not None:
                self._need(eng, st[0], waits)
                for t in st[1]:
                    self._need(eng, t, waits)
        return waits

    def _record(self, tok, reads, writes):
        for r in reads:
            st = self.res.setdefault(r, [None, []])
            st[1].append(tok)
        for w in writes:
            self.res[w] = [tok, []]

    def op(self, eng, fn, reads=(), writes=(), sig=True):
        waits = self._deps(eng, reads, writes)
        tok = None
        if sig:
            self.pcnt[eng] += 1
            tok = ('p' + eng, self.psem[eng], self.pcnt[eng], eng)
            self._record(tok, reads, writes)
        self.ops[eng].append((waits, fn, (self.psem[eng], 1) if sig else None))
        return tok

    def dma(self, q, fn, reads=(), writes=()):
        waits = self._deps(q, reads, writes)
        i = self.didx[q]
        self.didx[q] = (i + 1) % NDMA_SEM
        sem = self.dsem[q][i]
        key = 'd%s%d' % (q, i)
        prev = self.dval[q][i]
        if prev > 0 and self.waited[q].get(key, 0) < prev:
            self.waited[q][key] = prev
            waits.append((sem, prev))
        self.dval[q][i] = prev + 16
        tok = (key, sem, prev + 16, 'dma')
        self._record(tok, reads, writes)
        self.ops[q].append((waits, fn, (sem, 16)))
        return tok

    def barrier(self):
        toks = []
        for e in COMPUTE:
            if self.pcnt[e] > 0:
                toks.append(('p' + e, self.psem[e], self.pcnt[e], e))
        for q in ('sp', 'pool'):
            for i in range(NDMA_SEM):
                if self.dval[q][i] > 0:
                    toks.append(('d%s%d' % (q, i), self.dsem[q][i], self.dval[q][i], 'dma'))
        for e in self.ops:
            waits = []
            for t in toks:
                key, sem, val, prod = t
                if self.waited[e].get(key, 0) >= val:
                    continue
                self.waited[e][key] = val
                waits.append((sem, val))
            if waits:
                self.ops[e].append((waits, None, None))
        self.res = {}

    def emit(self):
        nc = self.nc
        with nc.Block() as block:
            def mk(name):
                def body(eng):
                    for waits, fn, sig in self.ops[name]:
                        for sem, val in waits:
                            eng.wait_ge(sem, val)
                        if fn is None:
                            continue
                        ins = fn(eng)
                        if sig is not None:
                            ins.then_inc(sig[0], sig[1])
                return body
            block.tensor(mk('pe'))
            block.scalar(mk('act'))
            block.vector(mk('dve'))
            block.gpsimd(mk('pool'))
            block.sync(mk('sp'))


V_LNG, V_GK, V_GQ, V_GQN, V_GKV, V_GKN, V_GA, V_GB, V_GR, V_BF, V_PLEG, NV = 0, 8, 14, 20, 22, 23, 24, 25, 26, 28, 29, 40
C_BD64, C_B128, C_B256, C_BDMLA, C_B32, C_ONES, C_J, C_ID, NCB = 0, 128, 256, 384, 480, 512, 576, 704, 832
K_K, K_V, K_CKV, K_KR, K_KRS, K_FF, NWK = 0, 768, 1536, 1664, 1696, 1728, 1736
Q_Q, Q_CQ, Q_GATE, NWQ = 0, 768, 1024, 2048


def build_fused():
    nc = bass.Bass("TRN2", target_bir_lowering=False)

    def din(name, shape, dt=F32):
        return nc.dram_tensor(name, shape, dt, kind="ExternalInput").ap()

    x_all = din("x_all", [SEQ, 1024])
    identf_d = din("identf", [128, 128])
    CK_d = din("CK", [32, SEQ])
    SK_d = din("SK", [32, SEQ])
    oh16_d = din("oh16", [16, SEQ], BF16)
    ones3_d = din("ones3", [3, SEQ], BF16)
    cbf_d = din("cbf", [128, NCB], BF16)
    tabX_d = din("tabX", [34, 9])
    msel_d = din("msel", [128, 2])
    WL = []
    for l in range(2):
        sfx = "_l%d" % l
        WL.append({"wK": din("wK" + sfx, [128, 8, NWK]), "wQ": din("wQ" + sfx, [128, 8, NWQ]),
                   "wuqA": din("wuqA" + sfx, [128, 2, 384]), "wuqB": din("wuqB" + sfx, [128, 2, 384]),
                   "wukvK": din("wukvK" + sfx, [128, 1, 256]), "wukvV": din("wukvV" + sfx, [128, 1, 256]),
                   "wout": din("wout" + sfx, [128, 8, 1024]), "wpg": din("wpg" + sfx, [128, 8, 1024]),
                   "wpp": din("wpp" + sfx, [128, 2, 1024]), "vecs": din("vecs" + sfx, [128, NV])})
    CS = {}
    for c in ("a0", "a1", "m"):
        CS[c] = {"OH": din("OH_" + c, [34, FL]), "Sel": din("Sel_" + c, [128, 4, 256]), "VB": din("VB_" + c, [128, 16, 16]),
                 "VM": din("VM_" + c, [128, 16, 16]), "CTq": din("CTq_" + c, [96, OWN]), "STq": din("STq_" + c, [96, OWN]),
                 "p": din("p_" + c, [OWN, 256])}
    xown_d = [din("x_own_a%d" % a, [OWN, 1024]) for a in range(2)]
    out_d = nc.dram_tensor("out", [OWN, 1024], F32, kind="ExternalOutput").ap()

    KT_s = nc.dram_tensor("KT_s", [16, 128, SEQ], BF16, kind="Internal").ap()
    QT_s = nc.dram_tensor("QT_s", [16, 128, OWN], BF16, kind="Internal").ap()
    V_s = nc.dram_tensor("V_s", [SEQ, 1024], BF16, kind="Internal").ap()
    F_sL = {c: nc.dram_tensor("F_s_" + c, [9, FL], BF16, kind="Internal").ap() for c in ("a0", "a1", "m")}
    x1_s = [nc.dram_tensor("x1_s%d" % a, [OWN, 1024], F32, kind="Internal").ap() for a in range(2)]

    with contextlib.ExitStack() as top:
        S = Sched(nc, top)

        uid = [0]

        def sbt(st, n, shp, dt):
            uid[0] += 1
            return st.enter_context(nc.sbuf_tensor('s%d_%s' % (uid[0], n), shp, dt))

        def pst(st, n, shp, dt):
            uid[0] += 1
            return st.enter_context(nc.psum_tensor('p%d_%s' % (uid[0], n), shp, dt))

        vecsL = [sbt(top, "vecs%d" % l, [128, NV], F32) for l in range(2)]
        msel = sbt(top, "msel", [128, 2], F32)
        cbf = sbt(top, "cbf", [128, NCB], BF16)
        identf = sbt(top, "identf", [128, 128], F32)
        kmT = [sbt(top, "kmT%d" % i, [128, 16], BF16) for i in range(2)]
        kmBD = [sbt(top, "kmBD%d" % i, [128, 32], BF16) for i in range(2)]
        cTm = sbt(top, "cTm", [128, 128], F32)
        negb = sbt(top, "negb", [128, 1], F32)

        for l in range(2):
            S.dma('sp', lambda e, l=l: e.dma_start(out=vecsL[l][:], in_=WL[l]["vecs"][:]))
        S.dma('sp', lambda e: e.dma_start(out=msel[:], in_=msel_d[:]))
        S.dma('sp', lambda e: e.dma_start(out=cbf[:], in_=cbf_d[:]))
        S.dma('sp', lambda e: e.dma_start(out=identf[:], in_=identf_d[:]))
        S.barrier()
        ident = cbf[:, C_ID:C_ID + 128]
        bd64 = cbf[:, C_BD64:C_BD64 + 128]
        b128 = cbf[:, C_B128:C_B128 + 128]
        b256 = cbf[:, C_B256:C_B256 + 128]
        bdmla = cbf[0:96, C_BDMLA:C_BDMLA + 96]
        b32 = cbf[0:32, C_B32:C_B32 + 32]
        ones64 = cbf[:, C_ONES:C_ONES + 64]
        Jm = cbf[:, C_J:C_J + 128]

        def load_w(st, name, src, nk, ncols, gcol=None, vecs=None):
            w = sbt(st, name, [128, nk, ncols], BF16)
            stg = [sbt(st, name + "_stg%d" % i, [128, ncols], F32) for i in range(2)]
            for kc in range(nk):
                b = kc % 2
                S.dma('sp', lambda e, b=b, kc=kc: e.dma_start(out=stg[b][:, :], in_=src[:, kc, :]), writes=[name + 'stg%d' % b])
                if kc % 2 == 0:
                    if gcol is None:
                        S.op('dve', lambda e, b=b, kc=kc: e.tensor_copy(out=w[:, kc, :], in_=stg[b][:, :]), reads=[name + 'stg%d' % b])
                    else:
                        S.op('dve', lambda e, b=b, kc=kc: e.tensor_scalar(out=w[:, kc, :], in0=stg[b][:, :],
                                                                         scalar1=vecs[:, gcol + kc:gcol + kc + 1], scalar2=None, op0=ALU.mult),
                             reads=[name + 'stg%d' % b])
                else:
                    if gcol is None:
                        S.op('act', lambda e, b=b, kc=kc: e.activation(out=w[:, kc, :], in_=stg[b][:, :], func=AF.Copy), reads=[name + 'stg%d' % b])
                    else:
                        S.op('act', lambda e, b=b, kc=kc: e.activation(out=w[:, kc, :], in_=stg[b][:, :], func=AF.Copy,
                                                                       scale=vecs[:, gcol + kc:gcol + kc + 1]),
                             reads=[name + 'stg%d' % b])
            return w

        def norm_funcs(st, xload, hT, pfx, hres=None):
            NB = 4
            xt = [sbt(st, pfx + "xt%d" % i, [128, 1024], F32) for i in range(NB)]
            junk = sbt(st, pfx + "junk", [128, 1024], BF16)
            xn = [sbt(st, pfx + "xn%d" % i, [128, 1024], BF16) for i in range(NB)]
            ss = [sbt(st, pfx + "ss%d" % i, [128, 4], F32) for i in range(NB)]
            pT = [pst(st, pfx + "pT%d" % i, [128, 1024], BF16) for i in range(2)]

            def front(t):
                b = t % NB
                X, N, SS = pfx + 'xt%d' % b, pfx + 'xn%d' % b, pfx + 'ss%d' % b
                xload(t, xt[b], X)
                S.op('act', lambda e: e.activation(out=junk[:], in_=xt[b][:], func=AF.Square, accum_out=ss[b][:, 0:1]),
                     reads=[X], writes=[pfx + 'junk', SS])
                S.op('act', lambda e: e.activation(out=ss[b][:, 1:2], in_=ss[b][:, 0:1], func=AF.Ln, scale=1.0 / 1024, bias=EPS),
                     reads=[SS], writes=[SS])
                S.op('act', lambda e: e.activation(out=ss[b][:, 2:3], in_=ss[b][:, 1:2], func=AF.Exp, scale=-0.5),
                     reads=[SS], writes=[SS])
                S.op('dve', lambda e: e.tensor_scalar(out=xn[b][:], in0=xt[b][:], scalar1=ss[b][:, 2:3], scalar2=None, op0=ALU.mult),
                     reads=[X, SS], writes=[N])

            def back(t):
                b, pb_ = t % NB, t % 2
                N, P = pfx + 'xn%d' % b, pfx + 'pT%d' % pb_
                wr = [hres % t] if hres else []
                for c in range(8):
                    S.op('pe', lambda e, c=c: e.transpose(out=pT[pb_][:, c * 128:(c + 1) * 128], in_=xn[b][:, c * 128:(c + 1) * 128], identity=ident),
                         reads=[N], writes=[P], sig=(c == 7))
                if t % 2 == 0:
                    S.op('dve', lambda e: e.tensor_copy(out=hT[:, :, t * 128:(t + 1) * 128], in_=pT[pb_][:].rearrange("p (c n) -> p c n", c=8)), reads=[P], writes=wr)
                else:
                    S.op('act', lambda e: e.activation(out=hT[:, :, t * 128:(t + 1) * 128], in_=pT[pb_][:].rearrange("p (c n) -> p c n", c=8), func=AF.Copy), reads=[P], writes=wr)
            return front, back

        def norm_phase(st, xload, ntiles, hT, pfx):
            front, back = norm_funcs(st, xload, hT, pfx)
            LA = 2
            for i in range(ntiles + LA):
                if i < ntiles:
                    front(i)
                if i >= LA:
                    back(i - LA)

        class RmsFeat:
            def __init__(self, st):
                self.sq = [[sbt(st, "rf_sq%d_%d" % (i, a), [128, 512], BF16) for a in range(2)] for i in range(2)]
                self.lnv = [sbt(st, "rf_ln%d" % i, [128, 512], F32) for i in range(2)]
                self.rstd = [sbt(st, "rf_rs%d" % i, [128, 512], F32) for i in range(2)]
                self.psB = [pst(st, "rf_psB%d" % i, [128, 512], F32) for i in range(2)]
                self.cnt = 0

            def __call__(self, As, nstat, P, bm, gains, ebias, outs, n=512):
                i = self.cnt % 2
                self.cnt += 1
                for a in range(nstat):
                    S.op('act', lambda e, a=a: e.activation(out=self.sq[i][a][0:P, 0:n], in_=As[a][0], func=AF.Square),
                         reads=[As[a][1]], writes=['rf_sq%d_%d' % (i, a)])
                for a in range(nstat):
                    S.op('pe', lambda e, a=a: e.matmul(self.psB[i][0:P, 0:n], lhsT=bm, rhs=self.sq[i][a][0:P, 0:n], start=(a == 0), stop=(a == nstat - 1)),
                         reads=['rf_sq%d_%d' % (i, aa) for aa in range(nstat)], writes=['rf_psB%d' % i], sig=(a == nstat - 1))
                S.op('act', lambda e: e.activation(out=self.lnv[i][0:P, 0:n], in_=self.psB[i][0:P, 0:n], func=AF.Ln, bias=EPS),
                     reads=['rf_psB%d' % i], writes=['rf_ln%d' % i])
                S.op('act', lambda e: e.activation(out=self.rstd[i][0:P, 0:n], in_=self.lnv[i][0:P, 0:n], func=AF.Exp, scale=-0.5, bias=ebias),
                     reads=['rf_ln%d' % i], writes=['rf_rs%d' % i])
                for a in range(len(As)):
                    S.op('dve', lambda e, a=a: e.scalar_tensor_tensor(out=outs[a][0], in0=As[a][0], scalar=gains[a], in1=self.rstd[i][0:P, 0:n],
                                                                      op0=ALU.mult, op1=ALU.mult),
                         reads=[As[a][1], 'rf_rs%d' % i], writes=[outs[a][1]])

        class Pipe:
            def __init__(self):
                self.items = []

            def add(self, A, B=None, dep_prev=False, depth=1):
                self.items.append((A, B, 0 if dep_prev else depth))

            def run(self):
                n = len(self.items)
                slots = [[] for _ in range(n)]
                for i, (A, B, d) in enumerate(self.items):
                    slots[max(0, i - d)].append(A)
                for i in range(n):
                    for A in slots[i]:
                        A()
                    if self.items[i][1] is not None:
                        self.items[i][1]()
                self.items = []

        def split3(st, pfx, src, npart, n):
            hi = sbt(st, pfx + "hi", [npart, n], BF16)
            mid = sbt(st, pfx + "mid", [npart, n], BF16)
            lo = sbt(st, pfx + "lo", [npart, n], BF16)
            r1 = sbt(st, pfx + "r1", [npart, n], F32)
            S.op('dve', lambda e: e.tensor_copy(out=hi[:], in_=src), reads=[pfx + 'src'], writes=[pfx + 'hi'])
            S.op('dve', lambda e: e.tensor_tensor(out=r1[:], in0=src, in1=hi[:], op=ALU.subtract), reads=[pfx + 'src', pfx + 'hi'], writes=[pfx + 'r1'])
            S.op('dve', lambda e: e.tensor_copy(out=mid[:], in_=r1[:]), reads=[pfx + 'r1'], writes=[pfx + 'mid'])
            S.op('dve', lambda e: e.tensor_tensor(out=r1[:], in0=r1[:], in1=mid[:], op=ALU.subtract), reads=[pfx + 'r1', pfx + 'mid'], writes=[pfx + 'r1'])
            S.op('dve', lambda e: e.tensor_copy(out=lo[:], in_=r1[:]), reads=[pfx + 'r1'], writes=[pfx + 'lo'])
            return hi, mid, lo

        def phase0(OH_d, F_s):
            with contextlib.ExitStack() as st:
                tabX = sbt(st, "tabX", [34, 9], F32)
                OH = sbt(st, "OH", [34, FL], F32)
                Fsb = sbt(st, "Fsb", [9, FL], BF16)
                psF = [pst(st, "psF%d" % i, [128, 512], F32) for i in range(2)]
                S.dma('sp', lambda e: e.dma_start(out=tabX[:], in_=tabX_d[:]), writes=['tabX'])
                S.dma('sp', lambda e: e.dma_start(out=OH[:], in_=OH_d[:]), writes=['OH'])
                nch = (FL + 511) // 512
                for ci in range(nch):
                    c0 = ci * 512
                    cw = min(512, FL - c0)
                    b = ci % 2
                    S.op('pe', lambda e, b=b, c0=c0, cw=cw: e.matmul(psF[b][0:9, 0:cw], lhsT=tabX[0:34, 0:9], rhs=OH[0:34, c0:c0 + cw], start=True, stop=True),
                         reads=['tabX', 'OH'], writes=['psF%d' % b])
                    S.op('dve', lambda e, b=b, c0=c0, cw=cw: e.tensor_copy(out=Fsb[0:9, c0:c0 + cw], in_=psF[b][0:9, 0:cw]),
                         reads=['psF%d' % b], writes=['Fsb'])
                S.dma('sp', lambda e: e.dma_start(out=F_s[:], in_=Fsb[:]), reads=['Fsb'])

        def phaseK(xload, W, vecs):
            stKC = contextlib.ExitStack()
            ffT = sbt(stKC, "ffT", [4, SEQ], F32)
            kmsum = [sbt(stKC, "kmsum%d" % i, [128, 16], F32) for i in range(2)]
            with contextlib.ExitStack() as st:
                hTa = sbt(st, "hTa", [128, 8, SEQ], BF16)
                nfront, nback = norm_funcs(st, xload, hTa, "na_", hres="hTa_t%d")
                wK = load_w(st, "wK", W["wK"], 8, NWK, V_LNG, vecs)
                wukvK_ = load_w(st, "wukvK", W["wukvK"], 1, 256)
                wukvK = wukvK_[:, 0, :]
                wukvV_ = load_w(st, "wukvV", W["wukvV"], 1, 256)
                wukvV = wukvV_[:, 0, :]
                S.barrier()
                rfk = RmsFeat(st)
                psAk = [pst(st, "psAk%d" % i, [128, 512], F32) for i in range(4)]
                kst = [sbt(st, "kst%d" % i, [128, 512], BF16) for i in range(3)]
                vst = [sbt(st, "vst%d" % i, [128, 768], BF16) for i in range(2)]
                vst2 = [sbt(st, "vst2_%d" % i, [128, 256], BF16) for i in range(2)]
                ckvn = [sbt(st, "ckvn%d" % i, [128, 512], BF16) for i in range(2)]
                xa = sbt(st, "xa", [32, 512], F32)
                xb = sbt(st, "xb", [32, 512], F32)
                ckt = [sbt(st, "ckt%d" % i, [32, 512], F32) for i in range(2)]
                skt = [sbt(st, "skt%d" % i, [32, 512], F32) for i in range(2)]
                rst = [sbt(st, "rst%d" % i, [32, 512], BF16) for i in range(2)]
                acnt = [0]
                kcnt = [0]

                def nextAk():
                    i = acnt[0] % 4
                    acnt[0] += 1
                    return i

                def proj(ai, col0, ncol, T0, n=512):
                    hr = ['hTa_t%d' % t for t in range(T0 // 128, (T0 + n) // 128)]
                    for kc in range(8):
                        S.op('pe', lambda e, kc=kc: e.matmul(psAk[ai][0:ncol, 0:n], lhsT=wK[:, kc, col0:col0 + ncol], rhs=hTa[:, kc, T0:T0 + n],
                                                             start=(kc == 0), stop=(kc == 7)),
                             reads=hr, writes=['psAk%d' % ai], sig=(kc == 7))

                pipe = Pipe()

                def add_norm(t):
                    pipe.add(lambda t=t: nfront(t), lambda t=t: nback(t), depth=3)
                for t in range(4):
                    add_norm(t)
                for ch in range(8):
                    T0 = ch * 512
                    cb = ch % 2
                    tb = ch % 2
                    nxt = [4 * (ch + 1) + i for i in range(4)] if ch < 7 else []
                    for pi in range(6):
                        ai = nextAk()
                        kb = kcnt[0] % 3
                        kcnt[0] += 1

                        def A(ai=ai, pi=pi, T0=T0):
                            proj(ai, K_K + pi * 128, 128, T0)

                        def B(ai=ai, pi=pi, T0=T0, kb=kb, ch=ch):
                            rfk([(psAk[ai][:, :], 'psAk%d' % ai)], 1, 128, bd64, [vecs[:, V_GK + pi:V_GK + pi + 1]], 0.0,
                                [(kst[kb][:, :], 'kst%d' % kb)])
                            hd = 4 * (pi // 2) + 2 * (pi % 2)
                            for hh in range(2):
                                S.dma('pool', lambda e, hh=hh: e.dma_start(out=KT_s[hd + hh, 0:64, T0:T0 + 512], in_=kst[kb][hh * 64:(hh + 1) * 64, :]),
                                      reads=['kst%d' % kb])
                            if pi // 2 == 1:
                                pp = pi % 2
                                S.op('dve', lambda e: e.tensor_reduce(out=kmsum[pp][:, 2 * ch:2 * ch + 2],
                                                                      in_=kst[kb][:, :].rearrange("p (a b) -> p a b", a=2), axis=AX.X, op=ALU.add),
                                     reads=['kst%d' % kb], writes=['kmsum%d' % pp])
                        pipe.add(A, B, dep_prev=(ch == 0 and pi == 0))
                        if pi in (1, 3) and nxt:
                            add_norm(nxt.pop(0))
                    for tt in range(4):
                        v0 = nextAk()
                        v1 = nextAk()

                        def A(tt=tt, T0=T0, ch=ch, v0=v0, v1=v1):
                            tok = T0 + tt * 128
                            vb = (ch * 4 + tt) % 2
                            for (vi, c0, cw) in ((v0, 0, 512), (v1, 512, 256)):
                                for kc in range(8):
                                    S.op('pe', lambda e, kc=kc, vi=vi, c0=c0, cw=cw: e.matmul(
                                        psAk[vi][:, 0:cw], lhsT=hTa[:, kc, tok:tok + 128], rhs=wK[:, kc, K_V + c0:K_V + c0 + cw],
                                        start=(kc == 0), stop=(kc == 7)), reads=['hTa_t%d' % (tok // 128)], writes=['psAk%d' % vi], sig=(kc == 7))
                            S.op('act', lambda e: e.activation(out=vst[vb][:, 0:512], in_=psAk[v0][:, 0:512], func=AF.Copy), reads=['psAk%d' % v0], writes=['vst%d' % vb])
                            S.op('dve', lambda e: e.tensor_copy(out=vst[vb][:, 512:768], in_=psAk[v1][:, 0:256]), reads=['psAk%d' % v1], writes=['vst%d' % vb])
                            S.dma('pool', lambda e: e.dma_start(out=V_s[tok:tok + 128, 0:768], in_=vst[vb][:, :]), reads=['vst%d' % vb])
                        pipe.add(A)
                        if tt in (0, 2) and nxt:
                            add_norm(nxt.pop(0))
                    ai = nextAk()

                    def A(ai=ai, T0=T0):
                        proj(ai, K_CKV, 128, T0)

                    def B(ai=ai, cb=cb):
                        rfk([(psAk[ai][:, :], 'psAk%d' % ai)], 1, 128, b128, [vecs[:, V_GKV:V_GKV + 1]], 0.0, [(ckvn[cb][:, :], 'ckvn%d' % cb)])
                    pipe.add(A, B)
                    for pp in range(2):
                        ai = nextAk()
                        kb = kcnt[0] % 3
                        kcnt[0] += 1

                        def A(ai=ai, pp=pp, cb=cb):
                            S.op('pe', lambda e: e.matmul(psAk[ai][:, :], lhsT=wukvK[:, pp * 128:(pp + 1) * 128], rhs=ckvn[cb][:, :], start=True, stop=True),
                                 reads=['ckvn%d' % cb], writes=['psAk%d' % ai])

                        def B(ai=ai, pp=pp, kb=kb, T0=T0):
                            rfk([(psAk[ai][:, :], 'psAk%d' % ai)], 1, 128, bd64, [vecs[:, V_GKN:V_GKN + 1]], 0.0, [(kst[kb][:, :], 'kst%d' % kb)])
                            for hh in range(2):
                                S.dma('pool', lambda e, hh=hh: e.dma_start(out=KT_s[12 + 2 * pp + hh, 0:64, T0:T0 + 512], in_=kst[kb][hh * 64:(hh + 1) * 64, :]),
                                      reads=['kst%d' % kb])
                        pipe.add(A, B, dep_prev=(pp == 0))
                    for tt in range(4):
                        v1 = nextAk()

                        def A(tt=tt, T0=T0, ch=ch, cb=cb, v1=v1):
                            tok = T0 + tt * 128
                            vb = (ch * 4 + tt) % 2
                            S.op('pe', lambda e: e.matmul(psAk[v1][:, 0:256], lhsT=ckvn[cb][:, tt * 128:(tt + 1) * 128], rhs=wukvV, start=True, stop=True),
                                 reads=['ckvn%d' % cb], writes=['psAk%d' % v1])
                            S.op('dve', lambda e: e.tensor_copy(out=vst2[vb][:, :], in_=psAk[v1][:, 0:256]), reads=['psAk%d' % v1], writes=['vst2_%d' % vb])
                            S.dma('pool', lambda e: e.dma_start(out=V_s[tok:tok + 128, 768:1024], in_=vst2[vb][:, :]), reads=['vst2_%d' % vb])
                        pipe.add(A)
                    aiA = nextAk()
                    aiB = nextAk()

                    def A(aiA=aiA, aiB=aiB, T0=T0, tb=tb):
                        proj(aiA, K_KR, 32, T0)
                        proj(aiB, K_KRS, 32, T0)
                        S.dma('sp', lambda e: e.dma_start(out=ckt[tb][:, :], in_=CK_d[:, T0:T0 + 512]), writes=['ckt%d' % tb])
                        S.dma('sp', lambda e: e.dma_start(out=skt[tb][:, :], in_=SK_d[:, T0:T0 + 512]), writes=['skt%d' % tb])

                    def B(aiA=aiA, aiB=aiB, T0=T0, tb=tb):
                        rfk([(psAk[aiA][0:32, :], 'psAk%d' % aiA), (psAk[aiB][0:32, :], 'psAk%d' % aiB)], 1, 32, b32,
                            [vecs[0:32, V_GR:V_GR + 1], vecs[0:32, V_GR + 1:V_GR + 2]], 0.0, [(xa[:, :], 'xa'), (xb[:, :], 'xb')])
                        S.op('pool', lambda e: e.tensor_tensor(out=xa[:, :], in0=xa[:, :], in1=ckt[tb][:, :], op=ALU.mult), reads=['xa', 'ckt%d' % tb], writes=['xa'])
                        S.op('pool', lambda e: e.tensor_tensor(out=xb[:, :], in0=xb[:, :], in1=skt[tb][:, :], op=ALU.mult), reads=['xb', 'skt%d' % tb], writes=['xb'])
                        S.op('pool', lambda e: e.tensor_tensor(out=rst[tb][:, :], in0=xa[:, :], in1=xb[:, :], op=ALU.add), reads=['xa', 'xb'], writes=['rst%d' % tb])
                        for h in range(4):
                            S.dma('pool', lambda e, h=h: e.dma_start(out=KT_s[12 + h, 64:96, T0:T0 + 512], in_=rst[tb][:, :]), reads=['rst%d' % tb])
                    pipe.add(A, B)
                    ai = nextAk()

                    def A(ai=ai, T0=T0):
                        proj(ai, K_FF, 4, T0)

                    def B(ai=ai, T0=T0):
                        S.op('act', lambda e: e.activation(out=ffT[0:4, T0:T0 + 512], in_=psAk[ai][0:4, :], func=AF.Copy), reads=['psAk%d' % ai], writes=['ffT'])
                    pipe.add(A, B)
                pipe.run()
                S.barrier()
            with stKC as st:
                onesb = sbt(st, "onesb", [4, SEQ], BF16)
                cT4 = sbt(st, "cT4", [4, SEQ], F32)
                S.op('pool', lambda e: e.memset(onesb[:], 1.0), writes=['onesb'])
                S.op('dve', lambda e: e.tensor_scalar(out=negb[0:4, :], in0=vecs[0:4, V_BF:V_BF + 1], scalar1=-1.0, scalar2=None, op0=ALU.mult), writes=['negb'])
                S.op('act', lambda e: e.activation(out=ffT[:, :], in_=ffT[:, :], func=AF.Exp, scale=-1.0, bias=negb[0:4, 0:1]), reads=['ffT', 'negb'], writes=['ffT'])
                S.op('act', lambda e: e.activation(out=ffT[:, :], in_=ffT[:, :], func=AF.Ln, bias=1.0), reads=['ffT'], writes=['ffT'])
                S.op('dve', lambda e: e.tensor_tensor_scan(out=cT4[:, :], data0=onesb[:, :], data1=ffT[:, :], initial=0.0, op0=ALU.mult, op1=ALU.add),
                     reads=['onesb', 'ffT'], writes=['ncsrc'])
                hi, mid, lo = split3(st, "nc", cT4[:, :], 4, SEQ)
                for part, row in ((hi, 67), (mid, 68), (lo, 69)):
                    S.dma('sp', lambda e, part=part, row=row: e.dma_start(out=KT_s[0:4, row, :], in_=part[:, :]), reads=['nchi', 'ncmid', 'nclo'])
                for h in range(4):
                    S.dma('sp', lambda e, h=h: e.dma_start(out=KT_s[h, 64:67, :], in_=ones3_d[:, :]))
                    S.dma('sp', lambda e, h=h: e.dma_start(out=KT_s[4 + h, 64:80, :], in_=oh16_d[:, :]))
                psC = pst(st, "psC", [128, 128], F32)
                for blk in range(32):
                    S.op('pe', lambda e, blk=blk: e.transpose(out=psC[:, blk * 4:(blk + 1) * 4], in_=cT4[0:4, blk * 128:(blk + 1) * 128], identity=identf[0:4, 0:4]),
                         reads=['ncsrc'], writes=['psC'], sig=(blk == 31))
                S.op('dve', lambda e: e.tensor_copy(out=cTm[:, :], in_=psC[:, :]), reads=['psC'], writes=['cTm'])
                for pp in range(2):
                    S.op('dve', lambda e, pp=pp: e.tensor_scalar(out=kmT[pp][:, :], in0=kmsum[pp][:, :], scalar1=1.0 / 256, scalar2=None, op0=ALU.mult),
                         reads=['kmsum%d' % pp], writes=['kmT%d' % pp])
                    S.op('dve', lambda e, pp=pp: e.memset(kmBD[pp][:, :], 0.0), writes=['kmBD%d' % pp])
                    S.op('dve', lambda e, pp=pp: e.tensor_copy(out=kmBD[pp][0:64, 0:16], in_=kmT[pp][0:64, :]), reads=['kmT%d' % pp], writes=['kmBD%d' % pp])
                    S.op('dve', lambda e, pp=pp: e.tensor_copy(out=kmBD[pp][64:128, 16:32], in_=kmT[pp][64:128, :]), reads=['kmT%d' % pp], writes=['kmBD%d' % pp])
                S.barrier()

        def phaseQ(xload, W, C, vecs, GT):
            with contextlib.ExitStack() as st:
                hTo = sbt(st, "hTo", [128, 8, OWN], BF16)
                with contextlib.ExitStack() as st1:
                    norm_phase(st1, xload, OWN // 128, hTo, "no_")
                    S.barrier()
                wQ = load_w(st, "wQ", W["wQ"], 8, NWQ, V_LNG, vecs)
                wuqA = load_w(st, "wuqA", W["wuqA"], 2, 384)
                wuqB = load_w(st, "wuqB", W["wuqB"], 2, 384)
                VB = sbt(st, "VB", [128, 16, 16], F32)
                VM = sbt(st, "VM", [128, 16, 16], F32)
                S.dma('sp', lambda e: e.dma_start(out=VB[:], in_=C["VB"][:]))
                S.dma('sp', lambda e: e.dma_start(out=VM[:], in_=C["VM"][:]))
                S.barrier()
                rfq = RmsFeat(st)
                psAq = [pst(st, "qpsA%d" % i, [128, 512], F32) for i in range(4)]
                psG = pst(st, "psG", [128, 512], F32)
                psM = pst(st, "psM", [128, 1024], BF16)
                qst = [sbt(st, "qst%d" % i, [128, 512], BF16) for i in range(3)]
                cqn = sbt(st, "cqn", [128, 2, 512], BF16)
                qa = sbt(st, "qa", [96, 512], F32)
                qb = sbt(st, "qb", [96, 512], F32)
                ctt = [sbt(st, "ctt%d" % i, [96, 512], F32) for i in range(2)]
                stt = [sbt(st, "stt%d" % i, [96, 512], F32) for i in range(2)]
                qmst = [sbt(st, "qmst%d" % i, [96, 512], BF16) for i in range(2)]
                gvs = sbt(st, "gvs", [128, 2, 4, 16], F32)
                m8 = sbt(st, "m8", [128, 2, 4, 8], F32)
                Mf = sbt(st, "Mf", [128, 2, 4, 16], F32)
                Mb = [sbt(st, "Mb%d" % i, [128, 2, 4, 16], BF16) for i in range(2)]
                mst = [sbt(st, "mst%d" % i, [16, 1024], BF16) for i in range(2)]
                acnt = [0]
                qcnt = [0]
                mcnt = [0]

                def nextAq():
                    i = acnt[0] % 4
                    acnt[0] += 1
                    return i

                def projq(ai, col0, ncol, T0):
                    for kc in range(8):
                        S.op('pe', lambda e, kc=kc: e.matmul(psAq[ai][0:ncol, :], lhsT=wQ[:, kc, col0:col0 + ncol], rhs=hTo[:, kc, T0:T0 + 512],
                                                             start=(kc == 0), stop=(kc == 7)),
                             writes=['qpsA%d' % ai], sig=(kc == 7))

                pipe = Pipe()
                for ch in range(4):
                    T0 = ch * 512
                    tb = ch % 2
                    fins = []
                    for pi in range(6):
                        ai = nextAq()
                        qbuf = qcnt[0] % 3
                        qcnt[0] += 1
                        mbs = None
                        if pi // 2 == 1:
                            mbs = (mcnt[0] % 2, mcnt[0] % 2)
                            mcnt[0] += 1

                        def A(ai=ai, pi=pi, T0=T0):
                            projq(ai, Q_Q + pi * 128, 128, T0)

                        def B(ai=ai, pi=pi, T0=T0, qbuf=qbuf, ch=ch, mbs=mbs):
                            rfq([(psAq[ai][:, :], 'qpsA%d' % ai)], 1, 128, bd64, [vecs[:, V_GQ + pi:V_GQ + pi + 1]], math.log(0.125),
                                [(qst[qbuf][:, :], 'qst%d' % qbuf)])
                            hd = 4 * (pi // 2) + 2 * (pi % 2)
                            for hh in range(2):
                                S.dma('pool', lambda e, hh=hh: e.dma_start(out=QT_s[hd + hh, 0:64, T0:T0 + 512], in_=qst[qbuf][hh * 64:(hh + 1) * 64, :]),
                                      reads=['qst%d' % qbuf])
                            if pi // 2 == 1:
                                pp = pi % 2
                                mbi = mbs[0]
                                for tt in range(4):
                                    S.op('pe', lambda e, tt=tt: e.matmul(
                                        psG[:, tt * 32:(tt + 1) * 32], lhsT=qst[qbuf][:, tt * 128:(tt + 1) * 128],
                                        rhs=kmBD[pp][:, 0:32], start=True, stop=True),
                                        reads=['qst%d' % qbuf], writes=['psG'], sig=(tt == 3))
                                for hh in range(2):
                                    S.op('dve', lambda e, hh=hh: e.tensor_tensor(
                                        out=gvs[:, hh, :, :], in0=psG[:, 0:128].rearrange("p (t h n) -> p t h n", t=4, h=2)[:, :, hh, :],
                                        in1=VB[:, ch * 4:(ch + 1) * 4, :], op=ALU.add), reads=['psG'], writes=['gvs'])
                                    for tt in range(4):
                                        S.op('dve', lambda e, hh=hh, tt=tt: e.max(out=m8[:, hh, tt, :], in_=gvs[:, hh, tt, :]), reads=['gvs'], writes=['m8'])
                                    S.op('dve', lambda e, hh=hh: e.tensor_tensor(out=Mf[:, hh, :, :], in0=gvs[:, hh, :, :],
                                                                               in1=m8[:, hh, :, 2:3].to_broadcast([128, 4, 16]), op=ALU.is_ge),
                                         reads=['gvs', 'm8'], writes=['Mf'])
                                    S.op('dve', lambda e, hh=hh: e.tensor_scalar(out=Mf[:, hh, :, :], in0=Mf[:, hh, :, :], scalar1=1.0, scalar2=-NEG,
                                                                               op0=ALU.subtract, op1=ALU.mult), reads=['Mf'], writes=['Mf'])
                                    S.op('dve', lambda e, hh=hh: e.tensor_tensor(out=Mb[mbi][:, hh, :, :], in0=Mf[:, hh, :, :], in1=VM[:, ch * 4:(ch + 1) * 4, :], op=ALU.mult),
                                         reads=['Mf'], writes=['Mb%d' % mbi])

                        fin = None
                        if pi // 2 == 1:
                            def fin(pi=pi, T0=T0, mbs=mbs):
                                pp = pi % 2
                                mbi = mbs[0]
                                for hh in range(2):
                                    for tt in range(4):
                                        g = hh * 4 + tt
                                        S.op('pe', lambda e, hh=hh, tt=tt, g=g: e.transpose(out=psM[0:16, g * 128:(g + 1) * 128], in_=Mb[mbi][:, hh, tt, :], identity=ident),
                                             reads=['Mb%d' % mbi], writes=['psM'], sig=(g == 7))
                                S.op('dve', lambda e: e.tensor_copy(out=mst[mbi][:, :], in_=psM[0:16, :]), reads=['psM'], writes=['mst%d' % mbi])
                                for hh in range(2):
                                    S.dma('pool', lambda e, hh=hh: e.dma_start(out=QT_s[4 + 2 * pp + hh, 64:80, T0:T0 + 512], in_=mst[mbi][:, hh * 512:(hh + 1) * 512]),
                                          reads=['mst%d' % mbi])
                        fins.append(fin)
                        pipe.add(A, B)
                        if pi >= 2 and fins[-3] is not None:
                            pipe.add(lambda: None, fins[-3])
                    for f_ in fins[-2:]:
                        if f_ is not None:
                            pipe.add(lambda: None, f_)
                    a0 = nextAq()
                    a1 = nextAq()

                    def A(a0=a0, a1=a1, T0=T0, tb=tb):
                        projq(a0, Q_CQ, 128, T0)
                        projq(a1, Q_CQ + 128, 128, T0)
                        S.dma('sp', lambda e: e.dma_start(out=ctt[tb][:, :], in_=C["CTq"][:, T0:T0 + 512]), writes=['ctt%d' % tb])
                        S.dma('sp', lambda e: e.dma_start(out=stt[tb][:, :], in_=C["STq"][:, T0:T0 + 512]), writes=['stt%d' % tb])

                    def B(a0=a0, a1=a1):
                        rfq([(psAq[a0][:, :], 'qpsA%d' % a0), (psAq[a1][:, :], 'qpsA%d' % a1)], 2, 128, b256,
                            [vecs[:, V_GQN:V_GQN + 1], vecs[:, V_GQN + 1:V_GQN + 2]], 0.0,
                            [(cqn[:, 0, :], 'cqn'), (cqn[:, 1, :], 'cqn')])
                    pipe.add(A, B)
                    for h in range(4):
                        aA = nextAq()
                        aB = nextAq()
                        qmb = (ch * 4 + h) % 2

                        def A(aA=aA, aB=aB, h=h):
                            for c in range(2):
                                S.op('pe', lambda e, c=c: e.matmul(psAq[aA][0:96, :], lhsT=wuqA[:, c, 96 * h:96 * h + 96], rhs=cqn[:, c, :], start=(c == 0), stop=(c == 1)),
                                     reads=['cqn'], writes=['qpsA%d' % aA], sig=(c == 1))
                            for c in range(2):
                                S.op('pe', lambda e, c=c: e.matmul(psAq[aB][0:96, :], lhsT=wuqB[:, c, 96 * h:96 * h + 96], rhs=cqn[:, c, :], start=(c == 0), stop=(c == 1)),
                                     reads=['cqn'], writes=['qpsA%d' % aB], sig=(c == 1))

                        def B(aA=aA, aB=aB, h=h, qmb=qmb, tb=tb, T0=T0):
                            rfq([(psAq[aA][0:96, :], 'qpsA%d' % aA), (psAq[aB][0:96, :], 'qpsA%d' % aB)], 1, 96, bdmla,
                                [vecs[0:96, V_GA:V_GA + 1], vecs[0:96, V_GB:V_GB + 1]], 0.0, [(qa[:, :], 'qa'), (qb[:, :], 'qb')])
                            S.op('pool', lambda e: e.tensor_tensor(out=qa[:, :], in0=qa[:, :], in1=ctt[tb][:, :], op=ALU.mult), reads=['qa', 'ctt%d' % tb], writes=['qa'])
                            S.op('pool', lambda e: e.tensor_tensor(out=qb[:, :], in0=qb[:, :], in1=stt[tb][:, :], op=ALU.mult), reads=['qb', 'stt%d' % tb], writes=['qb'])
                            S.op('pool', lambda e: e.tensor_tensor(out=qmst[qmb][:, :], in0=qa[:, :], in1=qb[:, :], op=ALU.add), reads=['qa', 'qb'], writes=['qmst%d' % qmb])
                            S.dma('pool', lambda e: e.dma_start(out=QT_s[12 + h, 0:96, T0:T0 + 512], in_=qmst[qmb][:, :]), reads=['qmst%d' % qmb])
                        pipe.add(A, B, dep_prev=(h == 0))
                    for g in range(8):
                        ai = nextAq()

                        def A(ai=ai, g=g, T0=T0):
                            projq(ai, Q_GATE + g * 128, 128, T0)

                        def B(ai=ai, g=g, T0=T0):
                            S.op('act', lambda e: e.activation(out=GT[:, g, T0:T0 + 512], in_=psAq[ai][:, :], func=AF.Silu), reads=['qpsA%d' % ai])
                        pipe.add(A, B)
                pipe.run()
                S.barrier()
            with contextlib.ExitStack() as st:
                Sel = sbt(st, "Sel", [128, 4, 256], F32)
                S.dma("sp", lambda e: e.dma_start(out=Sel[:], in_=C["Sel"][:]), writes=["Sel"])
                psG2 = pst(st, "psG2", [128, 512], F32)
                cown = sbt(st, "cown", [4, OWN], F32)
                for s in range(8):
                    for jj in range(4):
                        blk = 4 * s + jj
                        S.op('pe', lambda e, blk=blk, jj=jj: e.matmul(psG2[0:4, 0:256], lhsT=cTm[:, blk * 4:(blk + 1) * 4], rhs=Sel[:, jj, :], start=(jj == 0), stop=(jj == 3)),
                             reads=['Sel'], writes=['psG2'], sig=(jj == 3))
                    S.op('dve', lambda e, s=s: e.tensor_scalar(out=cown[0:4, s * 256:(s + 1) * 256], in0=psG2[0:4, 0:256], scalar1=-1.0, scalar2=None, op0=ALU.mult),
                         reads=['psG2'], writes=['cosrc'])
                hi, mid, lo = split3(st, "co", cown[:, :], 4, OWN)
                for part, row in ((hi, 64), (mid, 65), (lo, 66)):
                    S.dma('sp', lambda e, part=part, row=row: e.dma_start(out=QT_s[0:4, row, :], in_=part[:, :]), reads=['cohi', 'comid', 'colo'])
                for h in range(4):
                    S.dma('sp', lambda e, h=h: e.dma_start(out=QT_s[h, 67:70, :], in_=ones3_d[:, 0:OWN]))
                S.barrier()

        def phaseA(GT, mixT, F_s):
            with contextlib.ExitStack() as st:
                kt = [[sbt(st, "kt%d_%d" % (b, hh), [128, SEQ], BF16) for hh in range(2)] for b in range(2)]
                qt_ = [[sbt(st, "qt%d_%d" % (b, hh), [128, OWN], BF16) for hh in range(2)] for b in range(2)]
                vt = [sbt(st, "vt%d" % b, [128, 32, 192], BF16) for b in range(2)]
                gstage = [sbt(st, "gstage%d" % hh, [128, WG], BF16) for hh in range(2)]
                gtab = [sbt(st, "gtab%d" % b, [128, 2, WG], BF16) for b in range(2)]
                gc = sbt(st, "gc", [128, WC], BF16)
                NPB = 6
                pb = [sbt(st, "pb%d" % i, [128, 512], BF16) for i in range(NPB)]
                rd = sbt(st, "rd", [128, 512], F32)
                tmpo = sbt(st, "tmpo", [128, 512], F32)
                psS = [pst(st, "psS%d" % i, [128, 512], F32) for i in range(NPB)]
                psO = [[pst(st, "psO%d_%d" % (i, hh), [128, 512], F32) for hh in range(2)] for i in range(1)]
                Vv = V_s.rearrange("(j p) c -> p j c", p=128)
                S.dma('sp', lambda e: e.dma_start(out=gc[:, :], in_=bass.AP(tensor=F_s.tensor, offset=8 * FL, ap=[[1, 128], [1, WC]])), writes=['gc'])
                for b in range(2):
                    S.op('pool', lambda e, b=b: e.memset(vt[b][:, :, 64:128], 1.0), writes=['vt%d' % b])
                KDs = (70, 80, 64, 96)
                KDM = (70, 80, 128, 96)
                for b_ in range(2):
                    for hh_ in range(2):
                        S.op('dve', lambda e, b_=b_, hh_=hh_: e.memset(kt[b_][hh_][64:128, :], 0.0), writes=['kt%d_%d' % (b_, hh_)])
                        S.op('dve', lambda e, b_=b_, hh_=hh_: e.memset(qt_[b_][hh_][64:128, :], 0.0), writes=['qt%d_%d' % (b_, hh_)])
                scnt = [0]
                ocnt = [0]
                for p8 in range(8):
                    m, pp = p8 // 2, p8 % 2
                    KD = KDs[m]
                    KDq = KDM[m]
                    b = p8 % 2
                    if m == 2:
                        for hh in range(2):
                            S.op('dve', lambda e, b=b, hh=hh: e.memset(qt_[b][hh][64:128, :], 0.0), writes=['qt%d_%d' % (b, hh)])
                    for hh in range(2):
                        hd = 4 * m + 2 * pp + hh
                        for half in range(2):
                            S.dma('sp', lambda e, b=b, hh=hh, hd=hd, half=half, KD=KD: e.dma_start(
                                out=kt[b][hh][0:KD, half * 2048:(half + 1) * 2048], in_=KT_s[hd, 0:KD, half * 2048:(half + 1) * 2048]),
                                writes=['kt%d_%d' % (b, hh)])
                        S.dma('sp', lambda e, b=b, hh=hh, hd=hd, KD=KD: e.dma_start(out=qt_[b][hh][0:KD, :], in_=QT_s[hd, 0:KD, :]), writes=['qt%d_%d' % (b, hh)])
                    for q4 in range(4):
                        for hh in range(2):
                            c0 = m * 256 + pp * 128 + hh * 64
                            S.dma('sp', lambda e, b=b, q4=q4, hh=hh, c0=c0: e.dma_start(
                                out=vt[b][:, q4 * 8:(q4 + 1) * 8, hh * 128:hh * 128 + 64], in_=Vv[:, q4 * 8:(q4 + 1) * 8, c0:c0 + 64]),
                                writes=['vt%d' % b])
                    if m in (1, 2):
                        for hh in range(2):
                            row = (m - 1) * 4 + 2 * pp + hh
                            S.dma('sp', lambda e, hh=hh, row=row: e.dma_start(
                                out=gstage[hh][:, :], in_=bass.AP(tensor=F_s.tensor, offset=row * FL, ap=[[1, 128], [1, WG]])), writes=['gstage%d' % hh])
                            for c0 in range(0, WG, 512):
                                cw = min(512, WG - c0)
                                bi = scnt[0] % NPB
                                scnt[0] += 1
                                S.op('pe', lambda e, bi=bi, hh=hh, c0=c0, cw=cw: e.matmul(psS[bi][:, 0:cw], lhsT=Jm, rhs=gstage[hh][:, c0:c0 + cw], start=True, stop=True),
                                     reads=['gstage%d' % hh], writes=['psS%d' % bi])
                                S.op('act', lambda e, bi=bi, hh=hh, c0=c0, cw=cw, b=b: e.activation(out=gtab[b][:, hh, c0:c0 + cw], in_=psS[bi][:, 0:cw], func=AF.Exp),
                                     reads=['psS%d' % bi], writes=['gtab%d' % b])
                    RK = ['kt%d_%d' % (b, hh) for hh in range(2)] + ['qt%d_%d' % (b, hh) for hh in range(2)]
                    for u in range(4):
                        s0, s1 = 2 * u, 2 * u + 1
                        nk = (4 * s0 + 4, 4 * s1 + 4)
                        jlo = (max(0, 4 * s0 - 16), max(0, 4 * s1 - 16)) if m == 2 else (0, 0)
                        js = list(range(jlo[0], nk[1]))
                        ob = 0
                        sbufs = {}

                        def active(j):
                            a0 = (jlo[0] <= j < nk[0])
                            a1 = (jlo[1] <= j < nk[1])
                            c0 = 0 if a0 else 256
                            c1 = 512 if a1 else 256
                            return a0, a1, c0, c1

                        def emit_S(j):
                            a0, a1, c0, c1 = active(j)
                            bis = []
                            for hh in range(2):
                                bi = scnt[0] % NPB
                                scnt[0] += 1
                                bis.append(bi)
                                jadd = None
                                if m in (0, 3):
                                    for si, sl in enumerate((s0, s1)):
                                        o = 512 * sl - 128 * j + 384
                                        if (a0, a1)[si] and o < 512:
                                            jadd = (si, o)
                                S.op('pe', lambda e, bi=bi, hh=hh, j=j, b=b, KD=KDq, c0=c0, c1=c1, jadd=jadd, s0=s0: e.matmul(
                                    psS[bi][:, c0:c1], lhsT=kt[b][hh][0:KD, j * 128:(j + 1) * 128],
                                    rhs=qt_[b][hh][0:KD, s0 * 256 + c0:s0 * 256 + c1], start=True, stop=(jadd is None)),
                                    reads=RK, writes=['psS%d' % bi], sig=(jadd is None))
                                if jadd is not None:
                                    si, o = jadd
                                    S.op('pe', lambda e, bi=bi, si=si, o=o: e.matmul(psS[bi][:, si * 256:(si + 1) * 256], lhsT=Jm, rhs=gc[:, o:o + 256],
                                                                                   start=False, stop=True),
                                         reads=RK + ['gc'], writes=['psS%d' % bi])
                            sbufs[j] = bis

                        def emit_PV(j):
                            a0, a1, c0, c1 = active(j)
                            bis = sbufs[j]
                            first, last = (j == js[0]), (j == js[-1])
                            for hh in range(2):
                                bi = bis[hh]
                                S.op('act', lambda e, bi=bi, c0=c0, c1=c1: e.activation(out=pb[bi][:, c0:c1], in_=psS[bi][:, c0:c1], func=AF.Exp),
                                     reads=['psS%d' % bi], writes=['pb%d' % bi])
                                if m in (1, 2):
                                    for si, sl in enumerate((s0, s1)):
                                        if not (a0, a1)[si]:
                                            continue
                                        o = 512 * sl - 128 * j + 384
                                        oe = min(o, 2048) if m == 1 else o
                                        S.op('dve', lambda e, bi=bi, oe=oe, b=b, hh=hh, si=si: e.tensor_tensor(
                                            out=pb[bi][:, si * 256:(si + 1) * 256], in0=pb[bi][:, si * 256:(si + 1) * 256],
                                            in1=gtab[b][:, hh, oe:oe + 256], op=ALU.mult),
                                            reads=['pb%d' % bi, 'gtab%d' % b], writes=['pb%d' % bi])
                                S.op('pe', lambda e, bi=bi, hh=hh, j=j, ob=ob, b=b, first=first, last=last, c0=c0, c1=c1: e.matmul(
                                    psO[ob][hh][:, c0:c1], lhsT=vt[b][:, j, hh * 64:hh * 64 + 128],
                                    rhs=pb[bi][:, c0:c1], start=first, stop=last),
                                    reads=['pb%d' % bi, 'vt%d' % b], writes=['psO%d_%d' % (ob, hh)], sig=True)

                        LA = 2
                        for jj in js[:LA]:
                            emit_S(jj)
                        for idx, j in enumerate(js):
                            if idx + LA < len(js):
                                emit_S(js[idx + LA])
                            emit_PV(j)
                        S.op('act', lambda e, ob=ob: e.activation(out=rd[0:64, :], in_=psO[ob][0][64:128, :], func=AF.Ln), reads=['psO%d_0' % ob], writes=['rd'])
                        S.op('act', lambda e, ob=ob: e.activation(out=rd[64:128, :], in_=psO[ob][1][0:64, :], func=AF.Ln), reads=['psO%d_1' % ob], writes=['rd'])
                        S.op('act', lambda e: e.activation(out=rd[:, :], in_=rd[:, :], func=AF.Exp, scale=-1.0), reads=['rd'], writes=['rd'])
                        S.op('dve', lambda e, ob=ob: e.tensor_tensor(out=tmpo[0:64, :], in0=psO[ob][0][0:64, :], in1=rd[0:64, :], op=ALU.mult),
                             reads=['psO%d_0' % ob, 'rd'], writes=['tmpo'])
                        S.op('dve', lambda e, ob=ob: e.tensor_tensor(out=tmpo[64:128, :], in0=psO[ob][1][64:128, :], in1=rd[64:128, :], op=ALU.mult),
                             reads=['psO%d_1' % ob, 'psO%d_0' % ob, 'rd'], writes=['tmpo'])
                        S.op('pool', lambda e, p8=p8, s0=s0: e.tensor_tensor(out=mixT[:, p8, s0 * 256:s0 * 256 + 512], in0=tmpo[:, :], in1=GT[:, p8, s0 * 256:s0 * 256 + 512], op=ALU.mult),
                             reads=['tmpo'])
                S.barrier()

        def phaseO(xload, pown_d, W, vecs, mixT, out_ap):
            with contextlib.ExitStack() as st:
                wout = load_w(st, "wout", W["wout"], 8, 1024)
                wpp = load_w(st, "wpp", W["wpp"], 2, 1024)
                wpg = load_w(st, "wpg", W["wpg"], 8, 1024, V_PLEG, vecs)
                S.barrier()
                NX = 3
                xt = [sbt(st, "oxt%d" % i, [128, 1024], F32) for i in range(NX)]
                pt = [sbt(st, "opt%d" % i, [128, 256], F32) for i in range(2)]
                ptb = [sbt(st, "optb%d" % i, [128, 256], BF16) for i in range(2)]
                pTs = [sbt(st, "opTs%d" % i, [128, 2, 128], BF16) for i in range(2)]
                x1 = [sbt(st, "ox1%d" % i, [128, 1024], F32) for i in range(NX)]
                junk = sbt(st, "ojunk", [128, 1024], BF16)
                xn = [sbt(st, "oxn%d" % i, [128, 1024], BF16) for i in range(2)]
                ss = [sbt(st, "oss%d" % i, [128, 4], F32) for i in range(2)]
                gTs = [sbt(st, "ogT%d" % i, [128, 8, 128], BF16) for i in range(2)]
                sg = [sbt(st, "osg%d" % i, [128, 1024], F32) for i in range(2)]
                psY = [pst(st, "psY%d" % i, [128, 512], F32) for i in range(2)]
                psT = pst(st, "opsT", [128, 1024], BF16)
                psP = pst(st, "opsP", [128, 512], BF16)
                psGt = [pst(st, "psGt%d" % i, [128, 512], F32) for i in range(2)]
                psPP = [pst(st, "psPP%d" % i, [128, 512], F32) for i in range(2)]
                NTO = OWN // 128

                def stage1(t):
                    b, b3, tok = t % 2, t % NX, t * 128
                    xload(t, xt[b3], 'oxt%d' % b3)
                    S.dma('sp', lambda e: e.dma_start(out=pt[b][:, :], in_=pown_d[tok:tok + 128, :]), writes=['opt%d' % b])
                    for half in range(2):
                        for kc in range(8):
                            S.op('pe', lambda e, half=half, kc=kc: e.matmul(psY[half][:, :], lhsT=mixT[:, kc, tok:tok + 128], rhs=wout[:, kc, half * 512:(half + 1) * 512],
                                                                          start=(kc == 0), stop=(kc == 7)), writes=['psY%d' % half], sig=(kc == 7))
                        S.op('dve', lambda e, half=half: e.tensor_tensor(out=x1[b3][:, half * 512:(half + 1) * 512], in0=psY[half][:, :], in1=xt[b3][:, half * 512:(half + 1) * 512], op=ALU.add),
                             reads=['psY%d' % half, 'oxt%d' % b3], writes=['ox1%d' % b3])
                    S.op('act', lambda e: e.activation(out=junk[:, :], in_=x1[b3][:, :], func=AF.Square, accum_out=ss[b][:, 0:1]), reads=['ox1%d' % b3], writes=['ojunk', 'oss%d' % b])
                    S.op('act', lambda e: e.activation(out=ss[b][:, 1:2], in_=ss[b][:, 0:1], func=AF.Ln, scale=1.0 / 1024, bias=EPS), reads=['oss%d' % b], writes=['oss%d' % b])
                    S.op('act', lambda e: e.activation(out=ss[b][:, 2:3], in_=ss[b][:, 1:2], func=AF.Exp, scale=-0.5), reads=['oss%d' % b], writes=['oss%d' % b])
                    S.op('act', lambda e: e.activation(out=xn[b][:, :], in_=x1[b3][:, :], func=AF.Copy, scale=ss[b][:, 2:3]),
                         reads=['ox1%d' % b3, 'oss%d' % b], writes=['oxn%d' % b])
                    S.op('dve', lambda e: e.tensor_copy(out=ptb[b][:, :], in_=pt[b][:, :]), reads=['opt%d' % b], writes=['optb%d' % b])

                def stage2(t):
                    b = t % 2
                    for c in range(8):
                        S.op('pe', lambda e, c=c: e.transpose(out=psT[:, c * 128:(c + 1) * 128], in_=xn[b][:, c * 128:(c + 1) * 128], identity=ident),
                             reads=['oxn%d' % b], writes=['opsT'], sig=(c == 7))
                    S.op('dve', lambda e: e.tensor_copy(out=gTs[b][:, :, :], in_=psT[:, :].rearrange("p (c n) -> p c n", c=8)), reads=['opsT'], writes=['ogT%d' % b])
                    for c in range(2):
                        S.op('pe', lambda e, c=c: e.transpose(out=psP[:, c * 128:(c + 1) * 128], in_=ptb[b][:, c * 128:(c + 1) * 128], identity=ident),
                             reads=['optb%d' % b], writes=['opsP'], sig=(c == 1))
                    S.op('act', lambda e: e.activation(out=pTs[b][:, :, :], in_=psP[:, 0:256].rearrange("p (c n) -> p c n", c=2), func=AF.Copy), reads=['opsP'], writes=['opTs%d' % b])

                def stage3(t):
                    b, b3, tok = t % 2, t % NX, t * 128
                    for half in range(2):
                        for kc in range(8):
                            S.op('pe', lambda e, half=half, kc=kc: e.matmul(psGt[half][:, :], lhsT=gTs[b][:, kc, :], rhs=wpg[:, kc, half * 512:(half + 1) * 512],
                                                                          start=(kc == 0), stop=(kc == 7)), reads=['ogT%d' % b], writes=['psGt%d' % half], sig=(kc == 7))
                        S.op('act', lambda e, half=half: e.activation(out=sg[b][:, half * 512:(half + 1) * 512], in_=psGt[half][:, :], func=AF.Sigmoid),
                             reads=['psGt%d' % half], writes=['osg%d' % b])
                        for c in range(2):
                            S.op('pe', lambda e, half=half, c=c: e.matmul(psPP[half][:, :], lhsT=pTs[b][:, c, :], rhs=wpp[:, c, half * 512:(half + 1) * 512],
                                                                        start=(c == 0), stop=(c == 1)), reads=['opTs%d' % b], writes=['psPP%d' % half], sig=(c == 1))
                        S.op('dve', lambda e, half=half: e.tensor_tensor(out=sg[b][:, half * 512:(half + 1) * 512], in0=psPP[half][:, :], in1=sg[b][:, half * 512:(half + 1) * 512], op=ALU.mult),
                             reads=['psPP%d' % half, 'osg%d' % b], writes=['osg%d' % b])
                    S.op('dve', lambda e: e.tensor_tensor(out=sg[b][:, :], in0=sg[b][:, :], in1=x1[b3][:, :], op=ALU.add), reads=['osg%d' % b, 'ox1%d' % b3], writes=['osg%d' % b])
                    S.dma('pool', lambda e: e.dma_start(out=out_ap[tok:tok + 128, :], in_=sg[b][:, :]), reads=['osg%d' % b])

                for i in range(NTO + 2):
                    if i < NTO:
                        stage1(i)
                    if 1 <= i <= NTO:
                        stage2(i - 1)
                    if i >= 2:
                        stage3(i - 2)
                S.barrier()

        def dram_loader(src):
            def f(t, dst, res):
                S.dma('sp', lambda e: e.dma_start(out=dst[:, :], in_=src[t * 128:(t + 1) * 128, :]), writes=[res])
            return f

        def x1_nat_loader(t, dst, res):
            a = (t % 4) // 2
            r0 = (t // 4) * 256 + (t % 2) * 128
            S.dma('sp', lambda e: e.dma_start(out=dst[:, :], in_=x1_s[a][r0:r0 + 128, :]), writes=[res])

        for c in ("a0", "a1", "m"):
            phase0(CS[c]["OH"], F_sL[c])
        S.barrier()
        phaseK(dram_loader(x_all), WL[0], vecsL[0])
        for a in range(2):
            C = CS["a%d" % a]
            with contextlib.ExitStack() as stg:
                GT = sbt(stg, "GT", [128, 8, OWN], BF16)
                phaseQ(dram_loader(xown_d[a]), WL[0], C, vecsL[0], GT)
                mixT = sbt(stg, "mixT", [128, 8, OWN], BF16)
                phaseA(GT, mixT, F_sL["a%d" % a])
                phaseO(dram_loader(xown_d[a]), C["p"], WL[0], vecsL[0], mixT, x1_s[a])
        phaseK(x1_nat_loader, WL[1], vecsL[1])
        C = CS["m"]
        with contextlib.ExitStack() as stg:
            selt = [sbt(stg, "selt%d" % i, [128, 1024], F32) for i in range(2)]

            def x1_own_loader(t, dst, res):
                for a in range(2):
                    S.dma('sp', lambda e, a=a: e.dma_start(out=selt[a][:, :], in_=x1_s[a][t * 128:(t + 1) * 128, :]), writes=['selt%d' % a])
                S.op('act', lambda e: e.activation(out=selt[0][:, :], in_=selt[0][:, :], func=AF.Copy, scale=msel[:, 0:1]),
                     reads=['selt0'], writes=['selt0'])
                S.op('dve', lambda e: e.scalar_tensor_tensor(out=dst[:, :], in0=selt[1][:, :], scalar=msel[:, 1:2], in1=selt[0][:, :], op0=ALU.mult, op1=ALU.add),
                     reads=['selt0', 'selt1'], writes=[res])

            GT = sbt(stg, "GT", [128, 8, OWN], BF16)
            phaseQ(x1_own_loader, WL[1], C, vecsL[1], GT)
            mixT = sbt(stg, "mixT", [128, 8, OWN], BF16)
            phaseA(GT, mixT, F_sL["m"])
            phaseO(x1_own_loader, C["p"], WL[1], vecsL[1], mixT, out_d)
        S.emit()
    return nc


def _t5_bucket(d):
    d = np.maximum(d, 0)
    df = np.maximum(d, 1).astype(np.float32)
    large = 16 + (np.log(df / np.float32(16)) / np.float32(math.log(2048 / 16)) * np.float32(16)).astype(np.int32)
    large = np.minimum(large, 31)
    return np.where(d < 16, d, large)


def _core_consts(par):
    bf = ml_dtypes.bfloat16
    c = {}
    i = np.arange(FL)
    d = i + 256 * par - 511
    OH = np.zeros((34, FL), np.float32)
    bk = _t5_bucket(d)
    valid = d >= 0
    OH[bk[valid], i[valid]] = 1.0
    OH[32, ~valid] = NEG
    mult = ((d <= 128).astype(np.float32) + ((d % 4 == 0) & (d <= 512)).astype(np.float32)
            + ((d % 16 == 0) & (d <= 2048)).astype(np.float32))
    ok = valid & (mult > 0)
    OH[33, :] = NEG
    OH[33, ok] = np.log(mult[ok]).astype(np.float32)
    c["OH"] = OH
    Sel = np.zeros((128, 4, 256), np.float32)
    for jj in range(4):
        for k in range(128):
            q = 128 * jj + k - 256 * par
            if 0 <= q < 256:
                Sel[k, jj, q] = 1.0
    c["Sel"] = Sel
    VB = np.zeros((128, 16, 16), np.float32)
    VM = np.zeros((128, 16, 16), np.float32)
    for qt in range(16):
        own = 2 * (qt // 2) + par
        VB[:, qt, own:] = -1e9
        VM[:, qt, :own] = 1.0
    c["VB"], c["VM"] = VB, VM
    half = 16
    inv = (1.0 / (np.float32(10000.0) ** (np.arange(half, dtype=np.float32) * np.float32(2.0) / np.float32(32)))).astype(np.float32)
    pos = np.arange(SEQ).astype(np.float32)
    ang = pos[:, None] * inv[None, :]
    cos, sin = np.cos(ang).astype(np.float32).T, np.sin(ang).astype(np.float32).T
    c["CK"] = np.ascontiguousarray(np.concatenate([cos, cos], 0))
    c["SK"] = np.ascontiguousarray(np.concatenate([-sin, sin], 0))
    own_idx = np.concatenate([512 * s + 256 * par + np.arange(256) for s in range(8)])
    sc = np.float32(96 ** -0.5)
    CT = np.full((96, OWN), sc, np.float32)
    ST = np.zeros((96, OWN), np.float32)
    CT[64:96] = np.concatenate([cos, cos], 0)[:, own_idx] * sc
    ST[64:96] = np.concatenate([-sin, sin], 0)[:, own_idx] * sc
    c["CTq"], c["STq"] = CT, ST
    c["own_idx"] = own_idx
    return c


def _shared_consts():
    bf = ml_dtypes.bfloat16
    cb = np.zeros((128, NCB), np.float32)
    for g in range(2):
        cb[g * 64:(g + 1) * 64, C_BD64 + g * 64:C_BD64 + (g + 1) * 64] = 1.0 / 64
    cb[:, C_B128:C_B128 + 128] = 1.0 / 128
    cb[:, C_B256:C_B256 + 128] = 1.0 / 256
    cb[0:64, C_BDMLA:C_BDMLA + 64] = 1.0 / 64
    cb[64:96, C_BDMLA + 64:C_BDMLA + 96] = 1.0 / 32
    cb[0:32, C_B32:C_B32 + 32] = 1.0 / 32
    cb[:, C_ONES:C_ONES + 64] = 1.0
    cb[:, C_J:C_J + 128] = np.eye(128, dtype=np.float32)[::-1]
    cb[:, C_ID:C_ID + 128] = np.eye(128, dtype=np.float32)
    oh16 = np.zeros((16, SEQ), np.float32)
    for n in range(16):
        oh16[n, n * 256:(n + 1) * 256] = 1.0
    return {"cbf": cb.astype(bf), "oh16": oh16.astype(bf), "ones3": np.ones((3, SEQ), bf),
            "identf": np.eye(128, dtype=np.float32)}


def _kc(w):
    return np.ascontiguousarray(w.reshape(8, 128, -1).transpose(1, 0, 2))


def _layer_weights(l, ln_g, w_in, b_forget, qk_gain, mla_q_norm, mla_kv_norm, mla_nope_gain, mla_rope_gain,
                   w_uq, w_ukv, w_out, rel_bias, ple_norm_g, w_ple_gate, w_ple_proj):
    W = w_in[l]
    fq, fk, fv, ff = W[:, 0:256], W[:, 256:512], W[:, 512:768], W[:, 768:772]
    mq, mk, mv = W[:, 772:1028], W[:, 1028:1284], W[:, 1284:1540]
    dq, dk, dv = W[:, 1540:1796], W[:, 1796:2052], W[:, 2052:2308]
    cq, ckv, kr, gate = W[:, 2308:2564], W[:, 2564:2692], W[:, 2692:2724], W[:, 2724:3748]
    kr_sw = np.concatenate([kr[:, 16:32], kr[:, 0:16]], 1)
    wK = np.concatenate([fk, mk, dk, fv, mv, dv, ckv, kr, kr_sw, ff, np.zeros((1024, 4), np.float32)], 1)
    wQ = np.concatenate([fq, mq, dq, cq, gate], 1)
    uq = w_uq[l]
    uqB = uq.copy()
    for h in range(4):
        uqB[:, 96 * h + 64:96 * h + 80] = uq[:, 96 * h + 80:96 * h + 96]
        uqB[:, 96 * h + 80:96 * h + 96] = uq[:, 96 * h + 64:96 * h + 80]
    ukv = w_ukv[l]
    ukvK = np.concatenate([ukv[:, 128 * h:128 * h + 64] for h in range(4)], 1)
    ukvV = np.concatenate([ukv[:, 128 * h + 64:128 * h + 128] for h in range(4)], 1)
    vec = np.zeros((128, NV), np.float32)
    vec[:, V_LNG:V_LNG + 8] = ln_g[l].reshape(8, 128).T
    for pi in range(6):
        m = pi // 2
        vec[:, V_GK + pi] = np.tile(qk_gain[l, 2 * m + 1], 2)
        vec[:, V_GQ + pi] = np.tile(qk_gain[l, 2 * m], 2)
    vec[:, V_GQN:V_GQN + 2] = mla_q_norm[l].reshape(2, 128).T
    vec[:, V_GKV] = mla_kv_norm[l]
    vec[:, V_GKN] = np.tile(mla_nope_gain[l, 1], 2)
    rg0, rg1 = mla_rope_gain[l, 0], mla_rope_gain[l, 1]
    vec[0:96, V_GA] = np.concatenate([mla_nope_gain[l, 0], rg0])
    vec[0:96, V_GB] = np.concatenate([mla_nope_gain[l, 0], rg0[16:32], rg0[0:16]])
    vec[0:32, V_GR] = rg1
    vec[0:32, V_GR + 1] = np.concatenate([rg1[16:32], rg1[0:16]])
    vec[0:4, V_BF] = b_forget[l]
    vec[:, V_PLEG:V_PLEG + 8] = ple_norm_g[l].reshape(8, 128).T
    tabX = np.zeros((34, 9), np.float32)
    tabX[0:32, 0:8] = rel_bias
    tabX[32, 0:4] = 1.0
    tabX[33, 4:8] = 1.0
    tabX[32, 8] = 1.0
    return {"wK": _kc(wK), "wQ": _kc(wQ),
            "wuqA": np.ascontiguousarray(uq.reshape(2, 128, 384).transpose(1, 0, 2)),
            "wuqB": np.ascontiguousarray(uqB.reshape(2, 128, 384).transpose(1, 0, 2)),
            "wukvK": np.ascontiguousarray(ukvK.reshape(128, 1, 256)), "wukvV": np.ascontiguousarray(ukvV.reshape(128, 1, 256)),
            "wout": _kc(w_out[l]), "wpg": _kc(w_ple_gate[l]),
            "wpp": np.ascontiguousarray(w_ple_proj[l].reshape(2, 128, 1024).transpose(1, 0, 2)),
            "vecs": vec, "tabX": tabX}


_NC = None


def kernel(x, p, ln_g, w_in, b_forget, qk_gain, mla_q_norm, mla_kv_norm, mla_nope_gain, mla_rope_gain,
           w_uq, w_ukv, w_out, rel_bias, ple_norm_g, w_ple_gate, w_ple_proj):
    global _NC
    args = [np.asarray(a, dtype=np.float32) for a in (ln_g, w_in, b_forget, qk_gain, mla_q_norm, mla_kv_norm, mla_nope_gain,
                                                     mla_rope_gain, w_uq, w_ukv, w_out, rel_bias, ple_norm_g, w_ple_gate, w_ple_proj)]
    x = np.asarray(x, dtype=np.float32)
    p = np.asarray(p, dtype=np.float32)
    if _NC is None:
        _NC = build_fused()
    shared = _shared_consts()
    cc = [_core_consts(par) for par in range(2)]
    base = dict(shared)
    for l in range(2):
        lw = _layer_weights(l, *args)
        base["tabX"] = lw.pop("tabX")
        for k, v in lw.items():
            base[k + "_l%d" % l] = v
    base["CK"], base["SK"] = cc[0]["CK"], cc[0]["SK"]
    for a in range(2):
        for k in ("OH", "Sel", "VB", "VM", "CTq", "STq"):
            base[k + "_a%d" % a] = cc[a][k]
    in_maps = []
    for core in range(8):
        b, par = core // 2, core % 2
        mp = dict(base)
        mp["x_all"] = np.ascontiguousarray(x[b])
        for a in range(2):
            mp["x_own_a%d" % a] = np.ascontiguousarray(x[b][cc[a]["own_idx"]])
            mp["p_a%d" % a] = np.ascontiguousarray(p[0, b][cc[a]["own_idx"]])
        mp["p_m"] = np.ascontiguousarray(p[1, b][cc[par]["own_idx"]])
        for k in ("OH", "Sel", "VB", "VM", "CTq", "STq"):
            mp[k + "_m"] = cc[par][k]
        ms = np.zeros((128, 2), np.float32)
        ms[:, par] = 1.0
        mp["msel"] = ms
        in_maps.append(mp)
    res = run_bass_kernel_spmd(_NC, in_maps, core_ids=list(range(8)))
    out = np.empty_like(x)
    for core in range(8):
        b, par = core // 2, core % 2
        out[b][cc[par]["own_idx"]] = np.asarray(res.results[core]["out"], dtype=np.float32)
    return out
```
